# Optimizing a Trainium2 kernel written in Bass

```python
import math
import jax, jax.numpy as jnp
from jax import lax
import numpy as np

D_MODEL = 1024
BATCH = 8
SEQ = 2048
DEPTH = 4
DEC_BATCH = 128
DEC_SEQ = 8
PAST_LEN = 16384
PAGE_SIZE = 128

N_MIXERS = 3
N_LAYERS_A = (DEPTH + 2) // 3
N_LAYERS_B = (DEPTH + 1) // 3
N_LAYERS_C = DEPTH // 3
D_FF = 4 * D_MODEL
CONV_W = 4
NORM_EPS = 1e-6

D_LRU = D_MODEL
LRU_BLOCKS = 8
LRU_BLOCK = D_LRU // LRU_BLOCKS
LRU_C = 8.0

D_SSM = 2 * D_MODEL
SSM_HEADDIM = 64
SSM_HEADS = D_SSM // SSM_HEADDIM
SSM_GROUPS = 8
SSM_HPG = SSM_HEADS // SSM_GROUPS
SSM_STATE = 128
SSM_CONV_DIM = D_SSM + 2 * SSM_GROUPS * SSM_STATE
SSM_IN = D_SSM + SSM_CONV_DIM + SSM_HEADS
SSM_CHUNK = 128
SSM_NORM_EPS = 1e-5

RWKV_HEAD = 64
RWKV_HEADS = D_MODEL // RWKV_HEAD
D_DECAY_LORA = 64
D_AAA_LORA = 64
D_GATE_LORA = 128
GN_EPS = 64e-5

kernel_name = "hybrid_rglru_ssd_rwkv7_step"


def rmsnorm(x, w):
    xf = x.astype(jnp.float32)
    y = xf * lax.rsqrt(jnp.mean(xf * xf, axis=-1, keepdims=True) + NORM_EPS)
    return (y * w.astype(jnp.float32)).astype(x.dtype)


def causal_conv(x, buf, w, b):
    L = x.shape[1]
    xp = jnp.concatenate([buf.astype(x.dtype), x], axis=1)
    y = b + xp[:, 0:L] * w[0]
    for k in range(1, CONV_W):
        y = y + xp[:, k:k + L] * w[k]
    return y, xp[:, -(CONV_W - 1):]


def _lin_combine(e1, e2):
    a1, b1 = e1
    a2, b2 = e2
    return a1 * a2, a2 * b1 + b2


def rglru_mixer(u, conv_buf, h0, start_pos, w_in, conv_w, conv_b, w_r, b_r, w_i, b_i, lam, w_out):
    bsz, L, _ = u.shape
    f32 = jnp.float32
    xb, gate = jnp.split(u @ w_in, 2, axis=-1)
    xc, new_buf = causal_conv(xb, conv_buf, conv_w, conv_b)
    xblk = xc.reshape(bsz, L, LRU_BLOCKS, LRU_BLOCK)
    r = jax.nn.sigmoid(jnp.einsum("blhi,hij->blhj", xblk, w_r).reshape(bsz, L, D_LRU) + b_r)
    i = jax.nn.sigmoid(jnp.einsum("blhi,hij->blhj", xblk, w_i).reshape(bsz, L, D_LRU) + b_i)
    log_a = -LRU_C * r.astype(f32) * jax.nn.softplus(-lam.astype(f32))
    reset = ((start_pos + jnp.arange(L)) == 0)[None, :, None]
    a = jnp.where(reset, 0.0, jnp.exp(log_a))
    mult = jnp.where(reset, 1.0, jnp.sqrt(-jnp.expm1(2.0 * log_a)))
    bterm = mult * (i * xc).astype(f32)
    bterm = bterm.at[:, 0].add(a[:, 0] * h0.astype(f32))
    _, h = lax.associative_scan(_lin_combine, (a, bterm), axis=1)
    y = (h * jax.nn.gelu(gate.astype(f32))).astype(u.dtype)
    return y @ w_out, new_buf, h[:, -1].astype(h0.dtype)


def ssd_chunked(x, dt, A, B, C, s0):
    f32 = jnp.float32
    bsz, L = x.shape[:2]
    q = SSM_CHUNK if L % SSM_CHUNK == 0 else L
    nc = L // q
    xc = x.reshape(bsz, nc, q, SSM_GROUPS, SSM_HPG, SSM_HEADDIM).astype(f32)
    dtc = dt.reshape(bsz, nc, q, SSM_GROUPS, SSM_HPG).astype(f32)
    Bc = B.reshape(bsz, nc, q, SSM_GROUPS, SSM_STATE).astype(f32)
    Cc = C.reshape(bsz, nc, q, SSM_GROUPS, SSM_STATE).astype(f32)
    acs = jnp.cumsum(dtc * A, axis=2)
    xdt = xc * dtc[..., None]
    seg = acs[:, :, :, None] - acs[:, :, None, :]
    mask = jnp.tril(jnp.ones((q, q), bool))[:, :, None, None]
    Ldec = jnp.exp(jnp.where(mask, seg, -jnp.inf))
    CB = jnp.einsum("bcign,bcjgn->bcijg", Cc, Bc)
    y_diag = jnp.einsum("bcijg,bcijge,bcjgep->bcigep", CB, Ldec, xdt)
    decay_end = jnp.exp(acs[:, :, -1:] - acs)
    st = jnp.einsum("bcjgn,bcjge,bcjgep->bcgepn", Bc, decay_end, xdt)
    chunk_decay = jnp.exp(acs[:, :, -1])

    def step(s, inp):
        dec, st_c = inp
        return s * dec[..., None, None] + st_c, s

    s_fin, s_start = lax.scan(step, s0.astype(f32),
                              (jnp.moveaxis(chunk_decay, 1, 0), jnp.moveaxis(st, 1, 0)))
    s_start = jnp.moveaxis(s_start, 0, 1)
    y_off = jnp.einsum("bcign,bcige,bcgepn->bcigep", Cc, jnp.exp(acs), s_start)
    y = (y_diag + y_off).reshape(bsz, L, SSM_GROUPS, SSM_HPG, SSM_HEADDIM)
    return y, s_fin


def mamba2_mixer(u, conv_buf, ssm0, w_in, conv_w, conv_b, dt_bias, a_log, d_skip, norm_w, w_out):
    bsz, L, _ = u.shape
    f32 = jnp.float32
    z, xbc, dt = jnp.split(u @ w_in, [D_SSM, D_SSM + SSM_CONV_DIM], axis=-1)
    xbc, new_buf = causal_conv(xbc, conv_buf, conv_w, conv_b)
    xbc = jax.nn.silu(xbc)
    xs, Bm, Cm = jnp.split(xbc, [D_SSM, D_SSM + SSM_GROUPS * SSM_STATE], axis=-1)
    xs = xs.reshape(bsz, L, SSM_GROUPS, SSM_HPG, SSM_HEADDIM)
    Bm = Bm.reshape(bsz, L, SSM_GROUPS, SSM_STATE)
    Cm = Cm.reshape(bsz, L, SSM_GROUPS, SSM_STATE)
    dt = jax.nn.softplus(dt.astype(f32) + dt_bias.astype(f32)).reshape(bsz, L, SSM_GROUPS, SSM_HPG)
    A = -jnp.exp(a_log.astype(f32)).reshape(SSM_GROUPS, SSM_HPG)
    s0 = ssm0.reshape(bsz, SSM_GROUPS, SSM_HPG, SSM_HEADDIM, SSM_STATE)
    y, s_fin = ssd_chunked(xs, dt, A, Bm, Cm, s0)
    y = y + d_skip.astype(f32).reshape(SSM_GROUPS, SSM_HPG, 1) * xs.astype(f32)
    yg = (y.reshape(bsz, L, D_SSM) * jax.nn.silu(z.astype(f32))).reshape(bsz, L, SSM_GROUPS, D_SSM // SSM_GROUPS)
    yg = yg * lax.rsqrt(jnp.mean(yg * yg, axis=-1, keepdims=True) + SSM_NORM_EPS)
    yg = (yg.reshape(bsz, L, D_SSM) * norm_w.astype(f32)).astype(u.dtype)
    s_fin = s_fin.reshape(bsz, SSM_HEADS, SSM_HEADDIM, SSM_STATE).astype(ssm0.dtype)
    return yg @ w_out, new_buf, s_fin


def rwkv7_mixer(u, shift0, wkv0, mu, w_rkv, w0, w_w1, w_w2, a0, w_a1, w_a2, w_g1, w_g2,
                k_k, k_a, r_k, lnx_w, lnx_b, w_out):
    bsz, L, _ = u.shape
    f32 = jnp.float32
    prev = jnp.concatenate([shift0[:, None].astype(u.dtype), u[:, :-1]], axis=1)
    xm = u[None] + (prev - u)[None] * mu[:, None, None, :]
    r, k, v = jnp.einsum("sbld,sde->sble", xm[:3], w_rkv)
    xw, xa, xg = xm[3], xm[4], xm[5]
    w_log = -jax.nn.softplus(-(w0 + jnp.tanh(xw @ w_w1) @ w_w2)) - 0.5
    decay = jnp.exp(-jnp.exp(w_log.astype(f32)))
    a = jax.nn.sigmoid(a0 + (xa @ w_a1) @ w_a2)
    g = jax.nn.sigmoid(xg @ w_g1) @ w_g2

    def heads(t):
        return t.reshape(bsz, L, RWKV_HEADS, RWKV_HEAD).astype(f32)

    kk = heads(k * k_k)
    kk = kk / jnp.maximum(jnp.linalg.norm(kk, axis=-1, keepdims=True), 1e-12)
    k = k * (1.0 + (a - 1.0) * k_a)
    r_h, k_h, v_h, a_h, w_h = heads(r), heads(k), heads(v), heads(a), heads(decay)

    def step(S, inp):
        r_t, k_t, v_t, kk_t, a_t, w_t = inp
        sa = jnp.einsum("bhvk,bhk->bhv", S, kk_t)
        S = (S * w_t[:, :, None, :] - sa[..., None] * (kk_t * a_t)[:, :, None, :]
             + v_t[..., None] * k_t[:, :, None, :])
        return S, jnp.einsum("bhvk,bhk->bhv", S, r_t)

    seq = tuple(jnp.moveaxis(t, 1, 0) for t in (r_h, k_h, v_h, kk, a_h, w_h))
    S_fin, o = lax.scan(step, wkv0.astype(f32), seq)
    o = jnp.moveaxis(o, 0, 1)
    mean = jnp.mean(o, axis=-1, keepdims=True)
    var = jnp.mean(jnp.square(o - mean), axis=-1, keepdims=True)
    o = ((o - mean) * lax.rsqrt(var + GN_EPS)).reshape(bsz, L, D_MODEL) * lnx_w + lnx_b
    bonus = (jnp.sum(r_h * k_h * r_k, axis=-1, keepdims=True) * v_h).reshape(bsz, L, D_MODEL)
    out = ((o + bonus) * g).astype(u.dtype)
    return out @ w_out, u[:, -1], S_fin.astype(wkv0.dtype)


def sqrelu_mlp(u, w1, w2):
    return jnp.square(jax.nn.relu(u @ w1)) @ w2


def run_trunk(x, lru_conv, lru_h, ssm_conv, ssm_state, rw_shift, rw_wkv, start_pos, p):
    n_lc, n_lh, n_sc, n_ss, n_rs, n_rw = [], [], [], [], [], []
    ia = ib = ic = 0
    for layer in range(DEPTH):
        u = rmsnorm(x, p["norm_mix"][layer])
        kind = layer % N_MIXERS
        if kind == 0:
            out, cb, h = rglru_mixer(u, lru_conv[ia], lru_h[ia], start_pos,
                                     p["lru_w_in"][ia], p["lru_conv_w"][ia], p["lru_conv_b"][ia],
                                     p["lru_w_r"][ia], p["lru_b_r"][ia], p["lru_w_i"][ia], p["lru_b_i"][ia],
                                     p["lru_lambda"][ia], p["lru_w_out"][ia])
            n_lc.append(cb)
            n_lh.append(h)
            ia += 1
        elif kind == 1:
            out, cb, s = mamba2_mixer(u, ssm_conv[ib], ssm_state[ib],
                                      p["ssm_w_in"][ib], p["ssm_conv_w"][ib], p["ssm_conv_b"][ib],
                                      p["ssm_dt_bias"][ib], p["ssm_a_log"][ib], p["ssm_d"][ib],
                                      p["ssm_norm_w"][ib], p["ssm_w_out"][ib])
            n_sc.append(cb)
            n_ss.append(s)
            ib += 1
        else:
            out, sh, S = rwkv7_mixer(u, rw_shift[ic], rw_wkv[ic],
                                     p["rwkv_mu"][ic], p["rwkv_w_rkv"][ic], p["rwkv_w0"][ic],
                                     p["rwkv_w_w1"][ic], p["rwkv_w_w2"][ic], p["rwkv_a0"][ic],
                                     p["rwkv_w_a1"][ic], p["rwkv_w_a2"][ic], p["rwkv_w_g1"][ic],
                                     p["rwkv_w_g2"][ic], p["rwkv_k_k"][ic], p["rwkv_k_a"][ic],
                                     p["rwkv_r_k"][ic], p["rwkv_lnx_w"][ic], p["rwkv_lnx_b"][ic],
                                     p["rwkv_w_out"][ic])
            n_rs.append(sh)
            n_rw.append(S)
            ic += 1
        x = x + out.astype(x.dtype)
        x = x + sqrelu_mlp(rmsnorm(x, p["norm_ffn"][layer]), p["ffn_w1"][layer], p["ffn_w2"][layer]).astype(x.dtype)
    y = rmsnorm(x, p["norm_final"])
    return (y, jnp.stack(n_lc), jnp.stack(n_lh), jnp.stack(n_sc), jnp.stack(n_ss),
            jnp.stack(n_rs), jnp.stack(n_rw))


def setup_inputs(seed: int = 0) -> dict:
    key = jax.random.key(seed)
    ks = iter(jax.random.split(key, 64))
    f32 = jnp.float32

    def nrm(shape, scale):
        return jax.random.normal(next(ks), shape, f32) * scale

    def uni(shape, lo, hi):
        return jax.random.uniform(next(ks), shape, f32, lo, hi)

    nA, nB, nC = N_LAYERS_A, N_LAYERS_B, N_LAYERS_C
    a_init = uni((nA, D_LRU), 0.9, 0.999)
    s = a_init ** (1.0 / LRU_C)
    lru_lambda = jnp.log(s) - jnp.log1p(-s)
    dt_init = jnp.exp(uni((nB, SSM_HEADS), math.log(1e-3), math.log(1e-1)))
    ssm_dt_bias = dt_init + jnp.log(-jnp.expm1(-dt_init))
    return {
        "x_prompt": nrm((BATCH, SEQ, D_MODEL), 1.0),
        "x_sample": nrm((DEC_BATCH, DEC_SEQ, D_MODEL), 1.0),
        "state_lru_conv": nrm((nA, DEC_BATCH, CONV_W - 1, D_LRU), 1.0),
        "state_lru_h": nrm((nA, DEC_BATCH, D_LRU), 0.5),
        "state_ssm_conv": nrm((nB, DEC_BATCH, CONV_W - 1, SSM_CONV_DIM), 1.0),
        "state_ssm": nrm((nB, DEC_BATCH, SSM_HEADS, SSM_HEADDIM, SSM_STATE), 0.1),
        "state_rwkv_shift": nrm((nC, DEC_BATCH, D_MODEL), 1.0),
        "state_rwkv_wkv": nrm((nC, DEC_BATCH, RWKV_HEADS, RWKV_HEAD, RWKV_HEAD), 0.1),
        "norm_mix": 1.0 + nrm((DEPTH, D_MODEL), 0.01),
        "norm_ffn": 1.0 + nrm((DEPTH, D_MODEL), 0.01),
        "norm_final": 1.0 + nrm((D_MODEL,), 0.01),
        "lru_w_in": nrm((nA, D_MODEL, 2 * D_LRU), D_MODEL ** -0.5),
        "lru_conv_w": nrm((nA, CONV_W, D_LRU), CONV_W ** -0.5),
        "lru_conv_b": nrm((nA, D_LRU), 0.01),
        "lru_w_r": nrm((nA, LRU_BLOCKS, LRU_BLOCK, LRU_BLOCK), LRU_BLOCK ** -0.5),
        "lru_b_r": nrm((nA, D_LRU), 0.01),
        "lru_w_i": nrm((nA, LRU_BLOCKS, LRU_BLOCK, LRU_BLOCK), LRU_BLOCK ** -0.5),
        "lru_b_i": nrm((nA, D_LRU), 0.01),
        "lru_lambda": lru_lambda,
        "lru_w_out": nrm((nA, D_LRU, D_MODEL), D_LRU ** -0.5),
        "ssm_w_in": nrm((nB, D_MODEL, SSM_IN), D_MODEL ** -0.5),
        "ssm_conv_w": nrm((nB, CONV_W, SSM_CONV_DIM), CONV_W ** -0.5),
        "ssm_conv_b": nrm((nB, SSM_CONV_DIM), 0.01),
        "ssm_dt_bias": ssm_dt_bias,
        "ssm_a_log": jnp.log(uni((nB, SSM_HEADS), 1.0, 16.0)),
        "ssm_d": 1.0 + nrm((nB, SSM_HEADS), 0.01),
        "ssm_norm_w": 1.0 + nrm((nB, D_SSM), 0.01),
        "ssm_w_out": nrm((nB, D_SSM, D_MODEL), D_SSM ** -0.5),
        "rwkv_mu": uni((nC, 6, D_MODEL), 0.0, 1.0),
        "rwkv_w_rkv": nrm((nC, 3, D_MODEL, D_MODEL), D_MODEL ** -0.5),
        "rwkv_w0": -1.0 + nrm((nC, D_MODEL), 0.5),
        "rwkv_w_w1": nrm((nC, D_MODEL, D_DECAY_LORA), D_MODEL ** -0.5),
        "rwkv_w_w2": nrm((nC, D_DECAY_LORA, D_MODEL), 0.1 * D_DECAY_LORA ** -0.5),
        "rwkv_a0": nrm((nC, D_MODEL), 0.1),
        "rwkv_w_a1": nrm((nC, D_MODEL, D_AAA_LORA), D_MODEL ** -0.5),
        "rwkv_w_a2": nrm((nC, D_AAA_LORA, D_MODEL), 0.1 * D_AAA_LORA ** -0.5),
        "rwkv_w_g1": nrm((nC, D_MODEL, D_GATE_LORA), D_MODEL ** -0.5),
        "rwkv_w_g2": nrm((nC, D_GATE_LORA, D_MODEL), D_GATE_LORA ** -0.5),
        "rwkv_k_k": 1.0 + nrm((nC, D_MODEL), 0.1),
        "rwkv_k_a": 1.0 + nrm((nC, D_MODEL), 0.1),
        "rwkv_r_k": nrm((nC, RWKV_HEADS, RWKV_HEAD), 0.1),
        "rwkv_lnx_w": 1.0 + nrm((nC, D_MODEL), 0.01),
        "rwkv_lnx_b": nrm((nC, D_MODEL), 0.01),
        "rwkv_w_out": nrm((nC, D_MODEL, D_MODEL), D_MODEL ** -0.5),
        "ffn_w1": nrm((DEPTH, D_MODEL, D_FF), D_MODEL ** -0.5),
        "ffn_w2": nrm((DEPTH, D_FF, D_MODEL), 0.5 * D_FF ** -0.5),
    }


def reference(x_prompt, x_sample, state_lru_conv, state_lru_h, state_ssm_conv, state_ssm,
              state_rwkv_shift, state_rwkv_wkv, norm_mix, norm_ffn, norm_final,
              lru_w_in, lru_conv_w, lru_conv_b, lru_w_r, lru_b_r, lru_w_i, lru_b_i, lru_lambda, lru_w_out,
              ssm_w_in, ssm_conv_w, ssm_conv_b, ssm_dt_bias, ssm_a_log, ssm_d, ssm_norm_w, ssm_w_out,
              rwkv_mu, rwkv_w_rkv, rwkv_w0, rwkv_w_w1, rwkv_w_w2, rwkv_a0, rwkv_w_a1, rwkv_w_a2,
              rwkv_w_g1, rwkv_w_g2, rwkv_k_k, rwkv_k_a, rwkv_r_k, rwkv_lnx_w, rwkv_lnx_b, rwkv_w_out,
              ffn_w1, ffn_w2):
    p = dict(norm_mix=norm_mix, norm_ffn=norm_ffn, norm_final=norm_final,
             lru_w_in=lru_w_in, lru_conv_w=lru_conv_w, lru_conv_b=lru_conv_b, lru_w_r=lru_w_r,
             lru_b_r=lru_b_r, lru_w_i=lru_w_i, lru_b_i=lru_b_i, lru_lambda=lru_lambda, lru_w_out=lru_w_out,
             ssm_w_in=ssm_w_in, ssm_conv_w=ssm_conv_w, ssm_conv_b=ssm_conv_b, ssm_dt_bias=ssm_dt_bias,
             ssm_a_log=ssm_a_log, ssm_d=ssm_d, ssm_norm_w=ssm_norm_w, ssm_w_out=ssm_w_out,
             rwkv_mu=rwkv_mu, rwkv_w_rkv=rwkv_w_rkv, rwkv_w0=rwkv_w0, rwkv_w_w1=rwkv_w_w1,
             rwkv_w_w2=rwkv_w_w2, rwkv_a0=rwkv_a0, rwkv_w_a1=rwkv_w_a1, rwkv_w_a2=rwkv_w_a2,
             rwkv_w_g1=rwkv_w_g1, rwkv_w_g2=rwkv_w_g2, rwkv_k_k=rwkv_k_k, rwkv_k_a=rwkv_k_a,
             rwkv_r_k=rwkv_r_k, rwkv_lnx_w=rwkv_lnx_w, rwkv_lnx_b=rwkv_lnx_b, rwkv_w_out=rwkv_w_out,
             ffn_w1=ffn_w1, ffn_w2=ffn_w2)
    bp = x_prompt.shape[0]
    sdt = state_lru_h.dtype
    y_prompt, lc_p, lh_p, sc_p, ss_p, rs_p, rw_p = run_trunk(
        x_prompt,
        jnp.zeros((N_LAYERS_A, bp, CONV_W - 1, D_LRU), sdt),
        jnp.zeros((N_LAYERS_A, bp, D_LRU), sdt),
        jnp.zeros((N_LAYERS_B, bp, CONV_W - 1, SSM_CONV_DIM), sdt),
        jnp.zeros((N_LAYERS_B, bp, SSM_HEADS, SSM_HEADDIM, SSM_STATE), sdt),
        jnp.zeros((N_LAYERS_C, bp, D_MODEL), sdt),
        jnp.zeros((N_LAYERS_C, bp, RWKV_HEADS, RWKV_HEAD, RWKV_HEAD), sdt),
        0, p)
    y_sample, lc_s, lh_s, sc_s, ss_s, rs_s, rw_s = run_trunk(
        x_sample, state_lru_conv, state_lru_h, state_ssm_conv, state_ssm,
        state_rwkv_shift, state_rwkv_wkv, PAST_LEN, p)
    return (y_prompt, y_sample, lc_p, lc_s, lh_p, lh_s, sc_p, sc_s, ss_p, ss_s, rs_p, rs_s, rw_p, rw_s)
```

```python
import numpy as np
import concourse.bass as bass
import concourse.mybir as mybir
from concourse.bass_utils import run_bass_kernel_spmd

F32 = mybir.dt.float32
BF16 = mybir.dt.bfloat16
I32 = mybir.dt.int32
AF = mybir.ActivationFunctionType
ALU = mybir.AluOpType
AX = mybir.AxisListType

ENGS = ("pe", "act", "dve", "pool", "sp")
SAME_ENGINE_SYNC = True
import os as _os_
NOSAME_ENGS = tuple(x for x in _os_.environ.get('NOSAME_ENGS', '').split(',') if x)
N_DMA_SEMS = 12


class _St:
    __slots__ = ("w", "r")

    def __init__(self):
        self.w = None
        self.r = {}


class _Op:
    __slots__ = ("eng", "fn", "waits", "dma", "signal", "tok")


class Prog:
    def __init__(self, nc):
        self.nc = nc
        self.ops = {e: [] for e in ENGS}
        self.state = {}
        self.ndma = {e: 0 for e in ENGS}
        self.dma_toks = {}
        self.pending = {}
        self.nosame = False

    def barrier(self):
        toks = set()
        for e in ENGS:
            for o in reversed(self.ops[e]):
                if not o.dma:
                    toks.add(o.tok)
                    break
            n = self.ndma[e]
            for j in range(max(0, n - N_DMA_SEMS), n):
                toks.add(("dma", e, j))
        self.pending = {e: set(toks) for e in ENGS}

    def _states(self, key, create):
        if isinstance(key, tuple):
            buf, sub = key
        else:
            buf, sub = key, None
        d = self.state.setdefault(buf, {})
        if sub is None:
            if create and "*" not in d:
                d["*"] = _St()
            return list(d.values())
        out = []
        if "*" in d:
            out.append(d["*"])
        if sub not in d:
            if create:
                d[sub] = _St()
                out.append(d[sub])
        else:
            out.append(d[sub])
        return out

    def op(self, eng, fn, reads=(), writes=(), dma=False):
        ps_r = [k for k in reads if isinstance(k, tuple) and k[0] == "PS"]
        if ps_r:
            reads = [k for k in reads if not (isinstance(k, tuple) and k[0] == "PS")]
            writes = list(writes) + ps_r
        o = _Op()
        o.eng, o.fn, o.dma, o.signal = eng, fn, dma, False
        idx = len(self.ops[eng])
        if dma:
            j = self.ndma[eng]
            self.ndma[eng] += 1
            o.tok = ("dma", eng, j)
            self.dma_toks[(eng, j)] = o
        else:
            o.tok = ("eng", eng, idx)
        waits = set()
        for k in reads:
            for st in self._states(k, True):
                if st.w is not None:
                    waits.add(st.w)
        for k in writes:
            for st in self._states(k, True):
                if st.w is not None:
                    waits.add(st.w)
                for t in st.r.values():
                    waits.add(t)
        if self.pending.get(eng):
            waits |= self.pending.pop(eng)
        w2 = set()
        for t in waits:
            if t[0] == "eng" and t[1] == eng:
                if not dma and (eng in ("pe", "sp") or not SAME_ENGINE_SYNC or self.nosame or (NOSAME_ENGS and eng in NOSAME_ENGS)):
                    continue
            w2.add(t)
        o.waits = w2
        for k in reads:
            for st in self._states(k, True):
                st.r[o.tok if dma else o.tok[1]] = o.tok
        for k in writes:
            for st in self._states(k, True):
                st.w = o.tok
                st.r = {}
        self.ops[eng].append(o)
        return o

    def emit(self, final_wait_all=True):
        nc = self.nc
        tokmap = {}
        for e in ENGS:
            for i, o in enumerate(self.ops[e]):
                tokmap[o.tok] = o
        for e in ENGS:
            for o in self.ops[e]:
                for t in o.waits:
                    tokmap[t].signal = True
        import contextlib
        with contextlib.ExitStack() as es:
            EPOCH = 16000
            nsig = {e: sum(1 for o in self.ops[e] if o.signal and not o.dma) for e in ENGS}
            esem = {e: [es.enter_context(nc.semaphore("s_%s%d" % (e, i))) for i in range(nsig[e] // EPOCH + 1)] for e in ENGS if e != "sp"}
            dsem = {e: [es.enter_context(nc.semaphore("d_%s%d" % (e, i))) for i in range(N_DMA_SEMS)]
                    for e in ENGS if self.ndma[e] > 0}
            val = {}
            for e in ENGS:
                c = 0
                for o in self.ops[e]:
                    if o.dma:
                        j = o.tok[2]
                        val[o.tok] = (dsem[e][j % N_DMA_SEMS], 16 * (j // N_DMA_SEMS + 1))
                    elif o.signal:
                        val[o.tok] = (esem[e][c // EPOCH], c % EPOCH + 1)
                        c += 1
            self.maxcount = {}
            block = es.enter_context(nc.Block())

            def run(e, eng):
                seen = {}
                for o in self.ops[e]:
                    ws = []
                    for t in o.waits:
                        ws.append(val[t])
                    if o.dma:
                        j = o.tok[2]
                        if j >= N_DMA_SEMS:
                            ws.append(val[("dma", e, j - N_DMA_SEMS)])
                    for (s, v) in ws:
                        if seen.get(id(s), 0) >= v:
                            continue
                        seen[id(s)] = v
                        eng.wait_ge(s, v)
                    ins = o.fn(eng)
                    if o.dma:
                        s, v = val[o.tok]
                        ins.then_inc(s, 16)
                    elif o.signal:
                        ins.then_inc(val[o.tok][0], 1)
                if final_wait_all:
                    n = self.ndma[e]
                    for j in range(max(0, n - N_DMA_SEMS), n):
                        s, v = val[("dma", e, j)]
                        if seen.get(id(s), 0) < v:
                            eng.wait_ge(s, v)
                            seen[id(s)] = v

            @block.tensor
            def _(eng):
                run("pe", eng)

            @block.scalar
            def _(eng):
                run("act", eng)

            @block.vector
            def _(eng):
                run("dve", eng)

            @block.gpsimd
            def _(eng):
                run("pool", eng)

            @block.sync
            def _(eng):
                run("sp", eng)
import contextlib


D = 1024
NTP = 2048
NS = 16
LS = 8
NT = NTP + NS * LS
TT = [(0, 512), (512, 512), (1024, 512), (1536, 512), (2048, 128)]
UC = 1 + NTP + NS * 9
XBC = 3 + NTP + NS * 11

PROW = {}


def _prow_layout():
    r = 0
    def add(name, n):
        nonlocal r
        PROW[name] = r
        r += n
    add("norm_mix", 4); add("norm_ffn", 4); add("norm_final", 1)
    add("lru_conv_w", 8); add("lru_conv_b", 2); add("lru_b_r", 2); add("lru_b_i", 2); add("lru_lambda", 2)
    add("ssm_norm_w", 2); add("ssm_conv_w", 16); add("ssm_conv_b", 4)
    add("rwkv_mu", 6); add("rwkv_w0", 1); add("rwkv_a0", 1); add("rwkv_k_k", 1); add("rwkv_k_a", 1)
    add("rwkv_lnx_w", 1); add("rwkv_lnx_b", 1); add("rwkv_r_k", 1)
    return r


NPROW = _prow_layout()

IN_SPECS = {
    "xp": [NTP, D], "xs": [NS * LS, D],
    "st_lru_conv": [2, NS, 3, D], "st_lru_h": [2, NS, D], "st_ssm_conv": [1, NS, 3, 4096],
    "st_ssm": [1, NS, 32, 64, 128], "st_rw_shift": [1, NS, D], "st_rw_wkv": [1, NS, 16, 64, 64],
    "norm_mix": [4, D], "norm_ffn": [4, D], "norm_final": [1, D],
    "lru_w_in": [2, D, 2048], "lru_conv_w": [2, 4, D], "lru_conv_b": [2, D], "lru_w_r": [2, 8, 128, 128],
    "lru_b_r": [2, D], "lru_w_i": [2, 8, 128, 128], "lru_b_i": [2, D], "lru_lambda": [2, D], "lru_w_out": [2, D, D],
    "ssm_w_in": [1, D, 6176], "ssm_conv_w": [1, 4, 4096], "ssm_conv_b": [1, 4096], "ssm_dt_bias": [1, 32],
    "ssm_a_log": [1, 32], "ssm_d": [1, 32], "ssm_norm_w": [1, 2048], "ssm_w_out": [1, 2048, D],
    "rwkv_mu": [1, 6, D], "rwkv_w_rkv": [1, 3, D, D], "rwkv_w0": [1, D], "rwkv_w_w1": [1, D, 64], "rwkv_w_w2": [1, 64, D],
    "rwkv_a0": [1, D], "rwkv_w_a1": [1, D, 64], "rwkv_w_a2": [1, 64, D], "rwkv_w_g1": [1, D, 128], "rwkv_w_g2": [1, 128, D],
    "rwkv_k_k": [1, D], "rwkv_k_a": [1, D], "rwkv_r_k": [1, 16, 64], "rwkv_lnx_w": [1, D], "rwkv_lnx_b": [1, D],
    "rwkv_w_out": [1, D, D], "ffn_w1": [4, D, 4096], "ffn_w2": [4, 4096, D],
}
OUT_SPECS = {
    "y_p": [NTP, D], "y_s": [NS * LS, D],
    "lc_p": [2, 3, D], "lc_s": [2, NS, 3, D], "lh_p": [2, D], "lh_s": [2, NS, D],
    "sc_p": [3, 4096], "sc_s": [NS, 3, 4096], "ss_p": [32, 64, 128], "ss_s": [NS, 32, 64, 128],
    "rs_p": [1, D], "rs_s": [NS, D], "rw_p": [16, 64, 64], "rw_s": [NS, 16, 64, 64],
}


class KB:
    def __init__(self, layers=(0, 1, 2, 3), debug=False):
        self.nc = nc = bass.Bass("TRN2", target_bir_lowering=False)
        self.P = Prog(nc)
        self.es = contextlib.ExitStack()
        self.I = {k: nc.dram_tensor(k, v, F32, kind="ExternalInput").ap() for k, v in IN_SPECS.items()}
        self.O = {k: nc.dram_tensor(k, v, F32, kind="ExternalOutput").ap() for k, v in OUT_SPECS.items()}
        self.debug = debug
        if debug:
            self.O["dbg_x"] = nc.dram_tensor("dbg_x", [128, 8, NT], F32, kind="ExternalOutput").ap()
            self.O["dbg_scr"] = nc.dram_tensor("dbg_scr", [8, NT, D], F32, kind="ExternalOutput").ap()
        self.bank_i = 0
        self.layers = layers
        self._n = 0

    def sb(self, name, shape, dt=F32):
        self._n += 1
        return self.es.enter_context(self.nc.sbuf_tensor("%s_%d" % (name, self._n), shape, dt))

    @contextlib.contextmanager
    def scope(self):
        old = self.es
        self.es = contextlib.ExitStack()
        try:
            yield
        finally:
            self.es.close()
            self.es = old
            self.P.barrier()

    def bank(self):
        b = self.bank_i
        self.bank_i = (self.bank_i + 1) % 8
        return self.PS[b], ("PS", b)

    def dma(self, eng, out, in_, reads=(), writes=()):
        self.P.op(eng, lambda e: e.dma_start(out=out, in_=in_), reads, writes, dma=True)

    def act(self, out, in_, func, reads, writes, **kw):
        self.P.op("act", lambda e: e.activation(out=out, in_=in_, func=func, **kw), reads, writes)

    def mm(self, out, lhsT, rhs, start, stop, reads, writes):
        self.P.op("pe", lambda e: e.matmul(out, lhsT=lhsT, rhs=rhs, start=start, stop=stop), reads, writes)

    def tr(self, out, in_, ident, reads, writes):
        self.P.op("pe", lambda e: e.transpose(out, in_, ident), reads, writes)

    def ts(self, eng, out, in0, s1, s2, op0, op1, reads, writes):
        if op1 is None:
            self.P.op(eng, lambda e: e.tensor_scalar(out=out, in0=in0, scalar1=s1, scalar2=None, op0=op0), reads, writes)
        else:
            self.P.op(eng, lambda e: e.tensor_scalar(out=out, in0=in0, scalar1=s1, scalar2=s2, op0=op0, op1=op1), reads, writes)

    def stt(self, out, in0, scalar, in1, op0, op1, reads, writes):
        self.P.op("dve", lambda e: e.scalar_tensor_tensor(out=out, in0=in0, scalar=scalar, in1=in1, op0=op0, op1=op1), reads, writes)

    def tt(self, eng, out, in0, in1, op, reads, writes):
        self.P.op(eng, lambda e: e.tensor_tensor(out=out, in0=in0, in1=in1, op=op), reads, writes)

    def cp(self, eng, out, in_, reads, writes):
        if eng == "act":
            self.P.op("act", lambda e: e.activation(out=out, in_=in_, func=AF.Copy), reads, writes)
        else:
            self.P.op(eng, lambda e: e.tensor_copy(out=out, in_=in_), reads, writes)

    def memset(self, eng, ap, v, writes):
        self.P.op(eng, lambda e: e.memset(ap, v), (), writes)

    def scan(self, out, d0, d1, init, reads, writes):
        self.P.op("dve", lambda e: e.tensor_tensor_scan(out=out, data0=d0, data1=d1, initial=init, op0=ALU.mult, op1=ALU.add), reads, writes)

    def xv(self, c, ti):
        c0, w = TT[ti]
        return self.X[:, c, c0:c0 + w]

    def uv(self, c, ti, shift=0):
        if ti < 4:
            s = 1 + 512 * ti + shift
            return self.U[:, c, s:s + 512]
        v = self.U[:, c, 1 + NTP:UC].rearrange("p (s t) -> p s t", t=9)
        return v[:, :, 1 + shift:9 + shift]

    @staticmethod
    def v3(ap, ti):
        if ti < 4:
            return ap
        return ap.rearrange("p (s t) -> p s t", t=8)

    def setup(self):
        nc = self.nc
        self.PS = [self.es.enter_context(nc.psum_tensor("ps%d" % i, [128, 512], F32)) for i in range(8)]
        self.X = self.sb("X", [128, 8, NT])
        self.U = self.sb("U", [128, 8, UC], BF16)
        self.IDF = self.sb("IDF", [128, 128])
        self.IDB = self.sb("IDB", [128, 128], BF16)
        self.ONESB = self.sb("ONESB", [128, 128], BF16)
        self.C1 = self.sb("C1", [128, 4])
        self.PRM = self.sb("PRM", [64, D])
        self.PF = self.sb("PF", [128, 8, 64])
        self.STG = [self.sb("STG%d" % i, [128, D]) for i in range(2)]
        self.SQ = self.sb("SQ", [128, 8, 512], BF16)
        self.RS = self.sb("RS", [128, 512])
        P = self.P
        P.op("pool", lambda e: e.memset(self.IDF[:], 0.0), (), ["IDF"])
        P.op("pool", lambda e: e.affine_select(out=self.IDF[:], in_=self.IDF[:], pattern=[[-1, 128]], compare_op=ALU.not_equal,
                                               fill=1.0, base=0, channel_multiplier=1), ["IDF"], ["IDF"])
        self.cp("dve", self.IDB[:], self.IDF[:], ["IDF"], ["IDB"])
        self.memset("dve", self.ONESB[:], 1.0, ["ONESB"])
        self.memset("dve", self.C1[:, 0:1], 1e-6, [("C1", 0)])
        self.memset("dve", self.C1[:, 1:2], 1.0, [("C1", 1)])
        self.memset("dve", self.C1[:, 2:3], 1e-5, [("C1", 2)])
        self.memset("dve", self.C1[:, 3:4], 64e-5, [("C1", 3)])
        self.memset("dve", self.U[:, :, 0:1], 0.0, [("U", "shiftp")])
        self.memset("pool", self.PRM[:], 0.0, ["PRM"])
        I = self.I
        def row(name, src, n):
            r = PROW[name]
            self.dma("sp", self.PRM[r:r + n, :], src, (), ["PRM"])
        row("norm_mix", I["norm_mix"], 4); row("norm_ffn", I["norm_ffn"], 4); row("norm_final", I["norm_final"], 1)
        row("lru_conv_w", I["lru_conv_w"].rearrange("a k d -> (a k) d"), 8)
        row("lru_conv_b", I["lru_conv_b"], 2); row("lru_b_r", I["lru_b_r"], 2); row("lru_b_i", I["lru_b_i"], 2)
        row("lru_lambda", I["lru_lambda"], 2)
        row("ssm_norm_w", I["ssm_norm_w"].rearrange("a (r d) -> (a r) d", d=D), 2)
        row("ssm_conv_w", I["ssm_conv_w"].rearrange("a k (r d) -> (a k r) d", d=D), 16)
        row("ssm_conv_b", I["ssm_conv_b"].rearrange("a (r d) -> (a r) d", d=D), 4)
        row("rwkv_mu", I["rwkv_mu"].rearrange("a k d -> (a k) d"), 6)
        for nm in ("rwkv_w0", "rwkv_a0", "rwkv_k_k", "rwkv_k_a", "rwkv_lnx_w", "rwkv_lnx_b"):
            row(nm, I[nm], 1)
        row("rwkv_r_k", I["rwkv_r_k"].rearrange("a h n -> a (h n)"), 1)
        for c in range(8):
            ps, pk = self.bank()
            self.tr(ps[:, 0:64], self.PRM[:, c * 128:(c + 1) * 128], self.IDF[0:64, 0:64], ["PRM", "IDF"], [pk])
            self.cp("dve", self.PF[:, c, :], ps[:, 0:64], [pk], [("PF", c)])

    def pf(self, name, k, c):
        r = PROW[name] + k
        return self.PF[:, c, r:r + 1]

    def load_x(self):
        n = 0
        for ti, (c0, w) in enumerate(TT):
            for j in range(w // 128):
                b = n % 2
                n += 1
                src = self.I["xp"][c0 + j * 128:c0 + (j + 1) * 128, :] if ti < 4 else self.I["xs"][:, :]
                self.dma("sp", self.STG[b][:], src, (), [("STG", b)])
                for c in range(8):
                    self.tr(self.PS[c][:, j * 128:(j + 1) * 128], self.STG[b][:, c * 128:(c + 1) * 128], self.IDF[:],
                            [("STG", b), "IDF"], [("PS", c)])
            for c in range(8):
                self.cp("dve" if c % 2 == 0 else "act", self.X[:, c, c0:c0 + w], self.PS[c][:, 0:w], [("PS", c)], [("X", (c, ti))])

    def rows_to_fm(self, src, nrows, dst, dkey):
        self.dma("sp", self.STG[0][0:nrows, :], src, (), [("STG", 0)])
        for c in range(8):
            ps, pk = self.bank()
            self.tr(ps[:, 0:nrows], self.STG[0][0:nrows, c * 128:(c + 1) * 128], self.IDF[0:nrows, 0:nrows], [("STG", 0), "IDF"], [pk])
            self.cp("dve", dst[:, c, 0:nrows], ps[:, 0:nrows], [pk], [dkey])

    def fm_to_rows(self, src, skey, nrows, dst):
        for half in range(2):
            ps, pk = self.bank()
            for cc in range(4):
                c = half * 4 + cc
                self.tr(ps[0:nrows, cc * 128:(cc + 1) * 128], src[:, c, 0:nrows], self.IDF[:], [skey, "IDF"], [pk])
            self.cp("dve", self.STG[1][0:nrows, half * 512:(half + 1) * 512], ps[0:nrows, :], [pk], [("STG", 1)])
        self.dma("sp", dst, self.STG[1][0:nrows, :], [("STG", 1)], ())

    def norm_to_U(self, pname, k, shift_out=None):
        for ti, (c0, w) in enumerate(TT):
            ps, pk = self.bank()
            for c in range(8):
                self.act(self.SQ[:, c, :w], self.X[:, c, c0:c0 + w], AF.Square, [("X", (c, ti))], [("SQ", c)])
                self.mm(ps[:, :w], self.ONESB[:], self.SQ[:, c, :w], c == 0, c == 7, [("SQ", c), "ONESB"], [pk])
            self.act(self.RS[:, :w], ps[:, :w], AF.Ln, [pk, ("C1", 0)], ["RS"], scale=1.0 / D, bias=self.C1[:, 0:1])
            self.act(self.RS[:, :w], self.RS[:, :w], AF.Exp, ["RS"], ["RS"], scale=-0.5)
            for c in range(8):
                self.stt(self.uv(c, ti), self.v3(self.X[:, c, c0:c0 + w], ti), self.pf(pname, k, c), self.v3(self.RS[:, :w], ti),
                         ALU.mult, ALU.mult, [("X", (c, ti)), "RS", ("PF", c)], [("U", (c, ti))])
                if shift_out is not None and ti == 3:
                    self.stt(shift_out[:, c, 0:1], self.X[:, c, NTP - 1:NTP], self.pf(pname, k, c), self.RS[:, 511:512],
                             ALU.mult, ALU.mult, [("X", (c, ti)), "RS", ("PF", c)], ["C_USH"])
                if shift_out is not None and ti == 4:
                    self.stt(shift_out[:, c, 1:], self.X[:, c, NTP:NT].rearrange("p (s t) -> p s t", t=8)[:, :, 7],
                             self.pf(pname, k, c), self.RS[:, 0:128].rearrange("p (s t) -> p s t", t=8)[:, :, 7],
                             ALU.mult, ALU.mult, [("X", (c, ti)), "RS", ("PF", c)], ["C_USH"])

    def u_keys(self, ti):
        return [("U", (c, ti)) for c in range(8)]

    def mlp(self, l):
        self.norm_to_U("norm_ffn", l)
        self.W1S = [self.sb("W1S%d" % i, [128, 8, 512], BF16) for i in range(2)]
        self.W2S = [self.sb("W2S%d" % i, [128, 4, D], BF16) for i in range(2)]
        self.HT = [self.sb("HT%d" % i, [128, 4, 512], BF16) for i in range(2)]
        self.RT = [self.sb("RT%d" % i, [128, 512]) for i in range(2)]
        w1 = self.I["ffn_w1"]
        w2 = self.I["ffn_w2"]
        def load(s):
            b = s % 2
            self.dma("pool", self.W1S[b][:], w1[l, :, s * 512:(s + 1) * 512].rearrange("(kc p) n -> p kc n", p=128), (), [("W1S", b)])
            self.dma("pool", self.W2S[b][:], w2[l, s * 512:(s + 1) * 512, :].rearrange("(fc p) n -> p fc n", p=128), (), [("W2S", b)])
        load(0)
        hb = 0
        rb = 0
        for s in range(8):
            if s + 1 < 8:
                load(s + 1)
            b = s % 2
            for ti, (c0, w) in enumerate(TT):
                H = self.HT[hb]
                hk = "HT%d" % hb
                hb ^= 1
                for fc in range(4):
                    ps, pk = self.bank()
                    for kc in range(8):
                        self.mm(self.v3(ps[:, :w], ti), self.W1S[b][:, kc, fc * 128:(fc + 1) * 128], self.uv(kc, ti), kc == 0, kc == 7,
                                [("W1S", b), ("U", (kc, ti))], [pk])
                    R = self.RT[rb]
                    rk = "RT%d" % rb
                    rb ^= 1
                    self.act(R[:, :w], ps[:, :w], AF.Relu, [pk], [rk])
                    self.act(H[:, fc, :w], R[:, :w], AF.Square, [rk], [(hk, fc)])
                for n in range(8):
                    ps, pk = self.bank()
                    for fc in range(4):
                        self.mm(ps[:, :w], self.W2S[b][:, fc, n * 128:(n + 1) * 128], H[:, fc, :w], fc == 0, fc == 3,
                                [("W2S", b), (hk, fc)], [pk])
                    self.tt("dve", self.X[:, n, c0:c0 + w], self.X[:, n, c0:c0 + w], ps[:, :w], ALU.add,
                            [pk, ("X", (n, ti))], [("X", (n, ti))])

    def final_out(self):
        YT = self.STG
        UF = self.sb("UF", [128, 8, 512])
        n = 0
        for ti, (c0, w) in enumerate(TT):
            ps, pk = self.bank()
            for c in range(8):
                self.act(self.SQ[:, c, :w], self.X[:, c, c0:c0 + w], AF.Square, [("X", (c, ti))], [("SQ", c)])
                self.mm(ps[:, :w], self.ONESB[:], self.SQ[:, c, :w], c == 0, c == 7, [("SQ", c), "ONESB"], [pk])
            self.act(self.RS[:, :w], ps[:, :w], AF.Ln, [pk, ("C1", 0)], ["RS"], scale=1.0 / D, bias=self.C1[:, 0:1])
            self.act(self.RS[:, :w], self.RS[:, :w], AF.Exp, ["RS"], ["RS"], scale=-0.5)
            for c in range(8):
                self.stt(UF[:, c, :w], self.X[:, c, c0:c0 + w], self.pf("norm_final", 0, c), self.RS[:, :w],
                         ALU.mult, ALU.mult, [("X", (c, ti)), "RS", ("PF", c)], [("UF", c)])
            for j in range(w // 128):
                b = n % 2
                n += 1
                for half in range(2):
                    ps2, pk2 = self.bank()
                    for cc in range(4):
                        c = half * 4 + cc
                        self.tr(ps2[:, cc * 128:(cc + 1) * 128], UF[:, c, j * 128:(j + 1) * 128], self.IDF[:], [("UF", c), "IDF"], [pk2])
                    self.cp("act" if half else "dve", YT[b][:, half * 512:(half + 1) * 512], ps2[:, :], [pk2], [("STG", b)])
                dst = self.O["y_p"][c0 + j * 128:c0 + (j + 1) * 128, :] if ti < 4 else self.O["y_s"][:, :]
                self.dma("sp", dst, YT[b][:], [("STG", b)], ())

    def dump_x(self):
        self.dma("sp", self.O["dbg_x"], self.X[:], ["X"], ())


def layer_A(self, l, ia):
    I = self.I
    self.norm_to_U("norm_mix", l)
    XB = self.sb("A_XB", [128, XBC])
    XC = self.sb("A_XC", [128, NT])
    XCb = self.sb("A_XCb", [128, NT], BF16)
    GATE = self.sb("A_GATE", [128, NT], BF16)
    R = self.sb("A_R", [128, NT])
    Iq = self.sb("A_I", [128, NT])
    WIN = [self.sb("A_WIN%d" % i, [128, 8, 256], BF16) for i in range(2)]
    WR = [self.sb("A_WR%d" % i, [128, 128], BF16) for i in range(2)]
    WI = [self.sb("A_WI%d" % i, [128, 128], BF16) for i in range(2)]
    WO = [self.sb("A_WO%d" % i, [128, D], BF16) for i in range(2)]
    CL = self.sb("A_CL", [128, 8])
    H0 = self.sb("A_H0", [128, 8, NS])
    CS0 = self.sb("A_CS0", [128, 8, NS * 3])
    HST = self.sb("A_HST", [128, 8, 1 + NS])
    CST = self.sb("A_CST", [128, 8, 3 + NS * 3])
    XBs = XB[:, 3 + NTP:XBC].rearrange("p (s t) -> p s t", t=11)
    XCs = XC[:, NTP:NT].rearrange("p (s t) -> p s t", t=8)
    rl = PROW["lru_lambda"] + ia
    self.act(CL[:], self.PF[:, :, rl], AF.Exp, ["PF"], ["A_CL"], scale=-1.0)
    self.act(CL[:], CL[:], AF.Ln, ["A_CL", ("C1", 1)], ["A_CL"], bias=self.C1[:, 1:2], scale=1.0)
    self.ts("dve", CL[:], CL[:], -8.0, None, ALU.mult, None, ["A_CL"], ["A_CL"])
    self.rows_to_fm(I["st_lru_h"][ia], NS, H0, "A_H0")
    self.rows_to_fm(I["st_lru_conv"][ia].rearrange("s k d -> (s k) d"), NS * 3, CS0, "A_CS0")
    w_in, w_out = I["lru_w_in"], I["lru_w_out"]

    def load(j):
        b = j % 2
        self.dma("pool", WIN[b][:, :, 0:128], w_in[ia, :, j * 128:(j + 1) * 128].rearrange("(kc p) n -> p kc n", p=128), (), [("A_WIN", b)])
        self.dma("pool", WIN[b][:, :, 128:256], w_in[ia, :, D + j * 128:D + (j + 1) * 128].rearrange("(kc p) n -> p kc n", p=128), (), [("A_WIN", b)])
        self.dma("pool", WR[b][:], I["lru_w_r"][ia, j], (), [("A_WR", b)])
        self.dma("pool", WI[b][:], I["lru_w_i"][ia, j], (), [("A_WI", b)])
        self.dma("pool", WO[b][:], w_out[ia, j * 128:(j + 1) * 128, :], (), [("A_WO", b)])

    load(0)
    for j in range(8):
        if j + 1 < 8:
            load(j + 1)
        b = j % 2
        self.memset("dve", XB[:, 0:3], 0.0, [("A_XB", "st")])
        self.cp("dve", XBs[:, :, 0:3], CS0[:, j, :].rearrange("p (s k) -> p s k", k=3), ["A_CS0"], [("A_XB", "st")])
        for ti, (c0, w) in enumerate(TT):
            for half in range(2):
                ps, pk = self.bank()
                for kc in range(8):
                    self.mm(self.v3(ps[:, :w], ti), WIN[b][:, kc, half * 128:(half + 1) * 128], self.uv(kc, ti), kc == 0, kc == 7,
                            [("A_WIN", b), ("U", (kc, ti))], [pk])
                if half == 0:
                    dst = XB[:, 3 + c0:3 + c0 + w] if ti < 4 else XBs[:, :, 3:11]
                    self.cp("act", dst, self.v3(ps[:, :w], ti), [pk], [("A_XB", ti)])
                else:
                    self.act(GATE[:, c0:c0 + w], ps[:, :w], AF.Gelu_apprx_tanh, [pk], [("A_GATE", ti)])
        self.cp("pool", CST[:, j, 0:3], XB[:, NTP:NTP + 3], ["A_XB"], [("A_CST", j)])
        self.cp("pool", CST[:, j, 3:].rearrange("p (s k) -> p s k", k=3), XBs[:, :, 8:11], ["A_XB"], [("A_CST", j)])
        cw = [self.pf("lru_conv_w", ia * 4 + k, j) for k in range(4)]
        cb = self.pf("lru_conv_b", ia, j)
        for (dst, srcf) in ((XC[:, 0:NTP], lambda k: XB[:, k:k + NTP]), (XCs, lambda k: XBs[:, :, k:k + 8])):
            self.ts("dve", dst, srcf(0), cw[0], cb, ALU.mult, ALU.add, ["A_XB", ("PF", j)], ["A_XC"])
            for k in range(1, 4):
                self.stt(dst, srcf(k), cw[k], dst, ALU.mult, ALU.add, ["A_XB", "A_XC", ("PF", j)], ["A_XC"])
        self.cp("act", XCb[:], XC[:], ["A_XC"], ["A_XCb"])
        for ti, (c0, w) in enumerate(TT):
            for (Wg, dstb, bname, key) in ((WR, R, "lru_b_r", "A_R"), (WI, Iq, "lru_b_i", "A_I")):
                ps, pk = self.bank()
                self.mm(ps[:, :w], Wg[b][:], XCb[:, c0:c0 + w], True, True, ["A_XCb", (key.replace("A_", "A_W"), b)], [pk])
                self.act(dstb[:, c0:c0 + w], ps[:, :w], AF.Sigmoid, [pk, ("PF", j)], [key], bias=self.pf(bname, ia, j), scale=1.0)
        T1 = XB[:, 0:NT]
        self.act(R[:], R[:], AF.Exp, ["A_R", "A_CL"], ["A_R"], scale=CL[:, j:j + 1])
        self.act(T1, R[:], AF.Square, ["A_R", "A_XB"], ["A_XB"])
        self.ts("dve", T1, T1, -1.0, 1.0, ALU.mult, ALU.add, ["A_XB"], ["A_XB"])
        self.ts("dve", T1, T1, 1e-30, None, ALU.max, None, ["A_XB"], ["A_XB"])
        self.act(T1, T1, AF.Sqrt, ["A_XB"], ["A_XB"])
        self.memset("dve", T1[:, 0:1], 1.0, ["A_XB"])
        self.memset("dve", R[:, 0:1], 0.0, ["A_R"])
        self.tt("dve", Iq[:], Iq[:], T1, ALU.mult, ["A_I", "A_XB"], ["A_I"])
        self.tt("dve", Iq[:], Iq[:], XC[:], ALU.mult, ["A_I", "A_XC"], ["A_I"])
        self.scan(XC[:, 0:NTP], R[:, 0:NTP], Iq[:, 0:NTP], 0.0, ["A_R", "A_I"], ["A_XC"])
        for s in range(NS):
            c0 = NTP + s * 8
            self.scan(XC[:, c0:c0 + 8], R[:, c0:c0 + 8], Iq[:, c0:c0 + 8], H0[:, j, s:s + 1], ["A_R", "A_I", "A_H0"], ["A_XC"])
        self.cp("pool", HST[:, j, 0:1], XC[:, NTP - 1:NTP], ["A_XC"], [("A_HST", j)])
        self.cp("pool", HST[:, j, 1:], XCs[:, :, 7], ["A_XC"], [("A_HST", j)])
        self.tt("dve", XCb[:], XC[:], GATE[:], ALU.mult, ["A_XC", "A_GATE"], ["A_XCb"])
        for ti, (c0, w) in enumerate(TT):
            for n in range(8):
                ps, pk = self.bank()
                self.mm(ps[:, :w], WO[b][:, n * 128:(n + 1) * 128], XCb[:, c0:c0 + w], True, True, ["A_XCb", ("A_WO", b)], [pk])
                self.tt("dve", self.X[:, n, c0:c0 + w], self.X[:, n, c0:c0 + w], ps[:, :w], ALU.add, [pk, ("X", (n, ti))], [("X", (n, ti))])
    self.fm_to_rows(HST[:, :, 0:1], "A_HST", 1, self.O["lh_p"][ia:ia + 1, :])
    self.fm_to_rows(HST[:, :, 1:], "A_HST", NS, self.O["lh_s"][ia])
    self.fm_to_rows(CST[:, :, 0:3], "A_CST", 3, self.O["lc_p"][ia])
    self.fm_to_rows(CST[:, :, 3:], "A_CST", NS * 3, self.O["lc_s"][ia].rearrange("s k d -> (s k) d"))


KB.layer_A = layer_A


def layer_B(self, l, ib):
    I = self.I
    self.norm_to_U("norm_mix", l)
    f32 = F32
    TRI = self.sb("B_TRI", [128, 128]); NEGM = self.sb("B_NEGM", [128, 128]); ONESF = self.sb("B_ONESF", [128, 128])
    self.memset("pool", TRI[:], 1.0, ["B_TRI"])
    self.P.op("pool", lambda e: e.affine_select(out=TRI[:], in_=TRI[:], pattern=[[1, 128]], compare_op=ALU.is_ge, fill=0.0, base=0,
                                                channel_multiplier=-1), ["B_TRI"], ["B_TRI"])
    self.memset("pool", NEGM[:], 0.0, ["B_NEGM"])
    self.P.op("pool", lambda e: e.affine_select(out=NEGM[:], in_=NEGM[:], pattern=[[1, 128]], compare_op=ALU.is_ge, fill=-1.0e4, base=0,
                                                channel_multiplier=-1), ["B_NEGM"], ["B_NEGM"])
    self.memset("pool", ONESF[:], 1.0, ["B_ONESF"])
    DTB = self.sb("B_DTB", [128, 32]); AB = self.sb("B_AB", [128, 32]); DB = self.sb("B_DB", [128, 32])
    self.dma("sp", DTB[:], I["ssm_dt_bias"][ib:ib + 1, :].partition_broadcast(128), (), ["B_DTB"])
    self.dma("sp", AB[:], I["ssm_a_log"][ib:ib + 1, :].partition_broadcast(128), (), ["B_AB"])
    self.dma("sp", DB[:], I["ssm_d"][ib:ib + 1, :].partition_broadcast(128), (), ["B_DB"])
    self.act(AB[:], AB[:], AF.Exp, ["B_AB"], ["B_AB"])
    self.ts("dve", AB[:], AB[:], -1.0, None, ALU.mult, None, ["B_AB"], ["B_AB"])
    CS0 = self.sb("B_CS0", [128, 32, NS * 3]); CST = self.sb("B_CST", [128, 32, 3 + NS * 3])
    for r in range(4):
        self.rows_to_fm(I["st_ssm_conv"][ib].rearrange("s k d -> (s k) d")[:, r * D:(r + 1) * D], NS * 3, CS0[:, r * 8:(r + 1) * 8, :], "B_CS0")
    WZD = [self.sb("B_WZD0", [128, 8, 260], BF16)] * 2
    BONES = self.sb("B_BONES", [128, 128]); TRIS = self.sb("B_TRIS", [128, 128]); NEGMS = self.sb("B_NEGMS", [128, 128])
    SEQM = self.sb("B_SEQM", [128, 16]); DAS = self.sb("B_DAS", [128, 64]); CDS = self.sb("B_CDS", [128, 64])
    YO = self.sb("B_YO", [128, 256]); BTM = self.sb("B_BTM", [128, 128], BF16); USC = self.sb("B_USC", [128, 8, 128], BF16)
    def _asel(ap, pattern, base, cm, fill, keys):
        self.P.op("pool", lambda e: e.affine_select(out=ap, in_=ap, pattern=pattern, compare_op=ALU.is_ge, fill=fill, base=base,
                                                    channel_multiplier=cm), keys, keys)
    self.memset("pool", SEQM[:], 1.0, ["B_SEQM"])
    _asel(SEQM[:], [[-8, 16]], 0, 1, 0.0, ["B_SEQM"])
    _asel(SEQM[:], [[8, 16]], 7, -1, 0.0, ["B_SEQM"])
    self.memset("pool", BONES[:], 1.0, ["B_BONES"])
    _asel(BONES[:].rearrange("p (s t) -> p s t", t=8), [[-8, 16], [0, 8]], 0, 1, 0.0, ["B_BONES"])
    _asel(BONES[:].rearrange("p (s t) -> p s t", t=8), [[8, 16], [0, 8]], 7, -1, 0.0, ["B_BONES"])
    self.tt("pool", TRIS[:], TRI[:], BONES[:], ALU.mult, ["B_TRI", "B_BONES"], ["B_TRIS"])
    self.tt("pool", NEGMS[:], NEGM[:], BONES[:], ALU.mult, ["B_NEGM", "B_BONES"], ["B_NEGMS"])
    self.ts("dve", YO[:, 0:128], BONES[:], 1.0e4, -1.0e4, ALU.mult, ALU.add, ["B_BONES"], ["B_YO"])
    self.tt("dve", NEGMS[:], NEGMS[:], YO[:, 0:128], ALU.add, ["B_NEGMS", "B_YO"], ["B_NEGMS"])
    self.cp("dve", USC[:].rearrange("p c (s t) -> p c s t", t=8),
            self.U[:, :, 1 + NTP:UC].rearrange("p c (s t) -> p c s t", t=9)[:, :, :, 1:9], [("U", (c_, 4)) for c_ in range(8)], ["B_USC"])

    WXBC = [self.sb("B_WXBC0", [128, 8, 512], BF16)] * 2
    WOUT = [self.sb("B_WOUT0", [128, 2, D], BF16)] * 2
    XF = self.sb("B_XF", [128, 4, 3 + 512]); XFS = self.sb("B_XFS", [128, 4, NS * 11])
    XCf = self.sb("B_XCf", [128, 4, 512]); BCb = self.sb("B_BCb", [128, 2, 512], BF16)
    YGT = self.sb("B_YGT", [128, 2, 512], BF16)
    ST = self.sb("B_ST", [128, 256]); STb = self.sb("B_STb", [128, 256], BF16)
    SIN = self.sb("B_SIN", [128, 2, 128]); SOUT = self.sb("B_SOUT", [128, 2, 128])
    XT = self.sb("B_XT", [128, 256]); BT = self.sb("B_BT", [128, 128], BF16)
    SM = self.sb("B_SM", [128, 40])
    TRIH = self.sb("B_TRIH", [128, 4, 128]); LT = self.sb("B_LT", [128, 4, 128]); MT = self.sb("B_MT", [128, 4, 128], BF16)
    XDT = self.sb("B_XDT", [128, 256], BF16); XDD = self.sb("B_XDD", [128, 256], BF16)
    Y1 = self.sb("B_Y1", [128, 256]); T2 = self.sb("B_T2", [128, 256]); SZ = self.sb("B_SZ", [128, 256])
    DTV, DT_, DA, NACS, EACS, DEND, CD, MS = (SM[:, 0:4], SM[:, 4:8], SM[:, 8:12], SM[:, 12:16], SM[:, 16:20], SM[:, 20:24],
                                              SM[:, 24:28], SM[:, 28:29])
    w_in, w_out = I["ssm_w_in"], I["ssm_w_out"]
    XFSv = [XFS[:, q, :].rearrange("p (s t) -> p s t", t=11) for q in range(4)]

    def bc(ap, cs):
        return ap.unsqueeze(2).to_broadcast([cs, 4, 64])

    def v4(ap):
        return ap.rearrange("p (h q) -> p h q", q=64)

    def wv(c0, n):
        return w_in[ib, :, c0:c0 + n].rearrange("(kc p) n -> p kc n", p=128)

    def load(g):
        self.dma("pool", WZD[0][:, :, 0:256], wv(g * 256, 256), (), ["B_WZD"])
        self.dma("pool", WZD[0][:, :, 256:260], wv(6144 + 4 * g, 4), (), ["B_WZD"])

    def load_x(g):
        self.dma("pool", WXBC[0][:, :, 0:256], wv(2048 + g * 256, 256), (), ["B_WXBC"])
        self.dma("pool", WXBC[0][:, :, 256:384], wv(4096 + g * 128, 128), (), ["B_WXBC"])
        self.dma("pool", WXBC[0][:, :, 384:512], wv(5120 + g * 128, 128), (), ["B_WXBC"])

    def load_o(g):
        self.dma("pool", WOUT[0][:], w_out[ib, g * 256:(g + 1) * 256, :].rearrange("(h p) n -> p h n", p=128), (), ["B_WOUT"])

    def chunk(g, b, tc, cs, ucol, sample=False):
        hs = slice(4 * g, 4 * g + 4)
        tri, negm = (TRIS, NEGMS) if sample else (TRI, NEGM)
        import os as _os
        STG_ = int(_os.environ.get('BDBG_C', '99'))
        if STG_ <= 0:
            return
        pzd, kzd = self.bank()
        for kc in range(8):
            self.mm(pzd[0:cs, 0:260], (USC[:, kc, :] if sample else self.U[:, kc, ucol:ucol + cs]), WZD[b][:, kc, :], kc == 0, kc == 7, ["B_WZD", "U", "B_USC"], [kzd])
        if STG_ <= 1:
            return
        pt, kt = self.bank()
        for q in range(3):
            self.tr(pt[0:cs, q * 128:(q + 1) * 128], XCf[:, q, tc:tc + cs], self.IDF[:], ["B_XCf", "IDF"], [kt])
        self.cp("act", XT[0:cs, :], pt[0:cs, 0:256], [kt], ["B_XT"])
        self.cp("dve", BT[0:cs, :], pt[0:cs, 256:384], [kt], ["B_BT"])
        if STG_ <= 2:
            return
        self.tt("dve", DTV[0:cs], pzd[0:cs, 256:260], DTB[0:cs, hs], ALU.add, [kzd, "B_DTB"], ["B_SM"])
        self.act(DT_[0:cs], DTV[0:cs], AF.Exp, ["B_SM"], ["B_SM"])
        self.act(DT_[0:cs], DT_[0:cs], AF.Ln, ["B_SM", ("C1", 1)], ["B_SM"], bias=self.C1[0:cs, 1:2], scale=1.0)
        self.tt("dve", DA[0:cs], DT_[0:cs], AB[0:cs, hs], ALU.mult, ["B_SM", "B_AB"], ["B_SM"])
        if STG_ <= 3:
            return
        pa, ka = self.bank()
        self.mm(pa[0:cs, 0:4], tri[0:cs, 0:cs], DA[0:cs], True, True, ["B_TRI", "B_TRIS", "B_SM"], [ka])
        self.mm(pa[:, 4:8], (BONES[:, :] if sample else ONESF[0:cs, :]), DA[0:cs], True, True, ["B_ONESF", "B_BONES", "B_SM"], [ka])
        self.ts("dve", NACS[0:cs], pa[0:cs, 0:4], -1.0, None, ALU.mult, None, [ka], ["B_SM"])
        self.act(EACS[0:cs], pa[0:cs, 0:4], AF.Exp, [ka], ["B_SM"])
        self.tt("dve", DEND[0:cs], pa[0:cs, 4:8], NACS[0:cs], ALU.add, [ka, "B_SM"], ["B_SM"])
        self.act(DEND[0:cs], DEND[0:cs], AF.Exp, ["B_SM"], ["B_SM"])
        if not sample:
            self.act(CD, pa[:, 4:8], AF.Exp, [ka], ["B_SM"])
        else:
            self.tt("dve", DAS[:].rearrange("p (s h) -> p s h", h=4), DA[:].unsqueeze(1).to_broadcast([128, 16, 4]),
                    SEQM[:].unsqueeze(2).to_broadcast([128, 16, 4]), ALU.mult, ["B_SM", "B_SEQM"], ["B_DAS"])
            pcd, kcd = self.bank()
            self.mm(pcd[:, 0:64], ONESF[:, :], DAS[:], True, True, ["B_ONESF", "B_DAS"], [kcd])
            self.act(CDS[:], pcd[:, 0:64], AF.Exp, [kcd], ["B_CDS"])
        if STG_ <= 4:
            return
        pl, kl = self.bank()
        for h in range(4):
            self.ts("dve", TRIH[0:cs, h, 0:cs], tri[0:cs, 0:cs], DA[0:cs, h:h + 1], None, ALU.mult, None, ["B_TRI", "B_TRIS", "B_SM"], [("B_TRIH", h)])
            self.mm(pl[0:cs, h * 128:h * 128 + cs], ONESF[0:cs, 0:cs], TRIH[0:cs, h, 0:cs], True, False, ["B_ONESF", ("B_TRIH", h)], [kl])
            self.mm(pl[0:cs, h * 128:h * 128 + cs], self.IDF[0:cs, 0:cs], negm[0:cs, 0:cs], False, True, ["IDF", "B_NEGM", "B_NEGMS"], [kl])
        for h in range(4):
            self.act(LT[0:cs, h, 0:cs], pl[0:cs, h * 128:h * 128 + cs], AF.Exp, [kl, "B_SM"], [("B_LT", h)], bias=NACS[0:cs, h:h + 1], scale=1.0)
        if STG_ <= 5:
            return
        pc, kc_ = self.bank()
        self.mm(pc[0:cs, 0:cs], BCb[:, 0, tc:tc + cs], BCb[:, 1, tc:tc + cs], True, True, ["B_BCb"], [kc_])
        for h in range(4):
            self.tt("dve", MT[0:cs, h, 0:cs], LT[0:cs, h, 0:cs], pc[0:cs, 0:cs], ALU.mult, [kc_, ("B_LT", h)], [("B_MT", h)])
        if STG_ <= 6:
            return
        self.tt("dve", v4(XDT[0:cs, :]), v4(XT[0:cs, :]), bc(DT_[0:cs], cs), ALU.mult, ["B_XT", "B_SM"], ["B_XDT"])
        self.tt("dve", v4(XDD[0:cs, :]), v4(XDT[0:cs, :]), bc(DEND[0:cs], cs), ALU.mult, ["B_XDT", "B_SM"], ["B_XDD"])
        if STG_ <= 7:
            return
        if sample:
            self.act(SZ[0:cs, :], pzd[0:cs, 0:256], AF.Silu, [kzd], ["B_SZ"])
            self.memset("dve", YO[:], 0.0, ["B_YO"])
            for sq in range(NSQ):
                state_in(g, sq)
                po, ko = self.bank()
                self.mm(po[:, 0:256], BCb[:, 1, 0:128], STb[:], True, True, ["B_BCb", "B_STb"], [ko])
                self.stt(YO[:], po[:, 0:256], SEQM[:, sq:sq + 1], YO[:], ALU.mult, ALU.add, [ko, "B_SEQM", "B_YO"], ["B_YO"])
                self.ts("dve", BTM[:], BT[:], SEQM[:, sq:sq + 1], None, ALU.mult, None, ["B_BT", "B_SEQM"], ["B_BTM"])
                pst, kst = self.bank()
                self.mm(pst[:, 0:256], BTM[:], XDD[:], True, True, ["B_BTM", "B_XDD"], [kst])
                self.tt("dve", v4(ST[:]), v4(ST[:]), bc(CDS[:, 4 * sq:4 * sq + 4], 128), ALU.mult, ["B_ST", "B_CDS"], ["B_ST"])
                self.tt("dve", ST[:], ST[:], pst[:, 0:256], ALU.add, ["B_ST", kst], ["B_ST"])
                state_out(g, self.O["ss_s"][sq, 4 * g:4 * g + 4])
        py, ky = self.bank()
        for h in range(4):
            self.mm(py[0:cs, h * 64:(h + 1) * 64], MT[0:cs, h, 0:cs], XDT[0:cs, h * 64:(h + 1) * 64], True, True, [("B_MT", h), "B_XDT"], [ky])
        if not sample:
            po, ko = self.bank()
            self.mm(po[0:cs, 0:256], BCb[:, 1, tc:tc + cs], STb[:], True, True, ["B_BCb", "B_STb"], [ko])
            self.tt("dve", v4(Y1[0:cs, :]), v4(po[0:cs, 0:256]), bc(EACS[0:cs], cs), ALU.mult, [ko, "B_SM"], ["B_Y1"])
        else:
            self.tt("dve", v4(Y1[:]), v4(YO[:]), bc(EACS[:], 128), ALU.mult, ["B_YO", "B_SM"], ["B_Y1"])
        self.tt("dve", Y1[0:cs, :], Y1[0:cs, :], py[0:cs, 0:256], ALU.add, [ky, "B_Y1"], ["B_Y1"])
        self.tt("pool", v4(T2[0:cs, :]), v4(XT[0:cs, :]), bc(DB[0:cs, hs], cs), ALU.mult, ["B_XT", "B_DB"], ["B_T2"])
        self.tt("dve", Y1[0:cs, :], Y1[0:cs, :], T2[0:cs, :], ALU.add, ["B_T2", "B_Y1"], ["B_Y1"])
        if STG_ <= 8:
            return
        if not sample:
            pst, kst = self.bank()
            self.mm(pst[:, 0:256], BT[0:cs, :], XDD[0:cs, :], True, True, ["B_BT", "B_XDD"], [kst])
            self.tt("dve", v4(ST[:]), v4(ST[:]), bc(CD, 128), ALU.mult, ["B_ST", "B_SM"], ["B_ST"])
            self.tt("dve", ST[:], ST[:], pst[:, 0:256], ALU.add, ["B_ST", kst], ["B_ST"])
            self.cp("act", STb[:], ST[:], ["B_ST"], ["B_STb"])
        if STG_ <= 9:
            return
        if not sample:
            self.act(SZ[0:cs, :], pzd[0:cs, 0:256], AF.Silu, [kzd], ["B_SZ"])
        self.tt("dve", Y1[0:cs, :], Y1[0:cs, :], SZ[0:cs, :], ALU.mult, ["B_SZ", "B_Y1"], ["B_Y1"])
        self.P.op("dve", lambda e: e.scalar_tensor_tensor(out=T2[0:cs, :], in0=Y1[0:cs, :], scalar=1.0, in1=Y1[0:cs, :], op0=ALU.mult,
                                                          op1=ALU.mult, accum_out=MS[0:cs]), ["B_Y1", "B_T2"], ["B_T2", "B_SM"])
        self.act(MS[0:cs], MS[0:cs], AF.Ln, ["B_SM", ("C1", 2)], ["B_SM"], scale=1.0 / 256, bias=self.C1[0:cs, 2:3])
        self.act(MS[0:cs], MS[0:cs], AF.Exp, ["B_SM"], ["B_SM"], scale=-0.5)
        self.ts("dve", Y1[0:cs, :], Y1[0:cs, :], MS[0:cs], None, ALU.mult, None, ["B_SM", "B_Y1"], ["B_Y1"])
        pg, kg = self.bank()
        for hf in range(2):
            self.tr(pg[:, hf * 128:hf * 128 + cs], Y1[0:cs, hf * 128:(hf + 1) * 128], self.IDF[0:cs, 0:cs], ["B_Y1", "IDF"], [kg])
        for hf in range(2):
            ch = 2 * g + hf
            self.ts("dve", YGT[:, hf, tc:tc + cs], pg[:, hf * 128:hf * 128 + cs], self.pf("ssm_norm_w", ch // 8, ch % 8), None, ALU.mult, None,
                    [kg, "PF"], ["B_YGT"])

    def state_in(g, s):
        self.dma("sp", SIN[:], I["st_ssm"][ib, s, 4 * g:4 * g + 4].rearrange("(a h) p n -> (h p) a n", a=2), (), ["B_SIN"])
        ps, pk = self.bank()
        for a in range(2):
            self.tr(ps[:, a * 128:(a + 1) * 128], SIN[:, a, :], self.IDF[:], ["B_SIN", "IDF"], [pk])
        self.cp("dve", ST[:], ps[:, 0:256], [pk], ["B_ST"])
        self.cp("act", STb[:], ps[:, 0:256], [pk], ["B_STb"])

    def state_out(g, dst):
        ps, pk = self.bank()
        for a in range(2):
            self.tr(ps[:, a * 128:(a + 1) * 128], ST[:, a * 128:(a + 1) * 128], self.IDF[:], ["B_ST", "IDF"], [pk])
        self.cp("dve", SOUT[:].rearrange("p a n -> p (a n)"), ps[:, 0:256], [pk], ["B_SOUT"])
        self.dma("sp", dst.rearrange("(a h) p n -> (h p) a n", a=2), SOUT[:], ["B_SOUT"], ())

    import os as _os
    NG = int(_os.environ.get('BDBG_G', '8')); TLIST = [int(c) for c in _os.environ.get('BDBG_T', '01234')]; NSQ = int(_os.environ.get('BDBG_S', '16'))
    load(0)
    load_x(0)
    load_o(0)
    for g in range(NG):
        b = g % 2
        chs = [2 * g, 2 * g + 1, 16 + g, 24 + g]
        for ti, (c0, w) in enumerate(TT):
            if ti not in TLIST:
                continue
            if ti == 0:
                self.memset("dve", XF[:, :, 0:3], 0.0, [("B_XF", "st")])
            elif ti < 4:
                self.cp("dve", XF[:, :, 0:3], XF[:, :, 512:515], ["B_XF"], [("B_XF", "st")])
            else:
                for q in range(4):
                    self.cp("dve", XFSv[q][:, :, 0:3], CS0[:, chs[q], :].rearrange("p (s k) -> p s k", k=3), ["B_CS0"], [("B_XFS", "st")])
            for q in range(4):
                ps, pk = self.bank()
                for kc in range(8):
                    self.mm(self.v3(ps[:, :w], ti), WXBC[b][:, kc, q * 128:(q + 1) * 128], self.uv(kc, ti), kc == 0, kc == 7,
                            ["B_WXBC", ("U", (kc, ti))], [pk])
                if ti < 4:
                    self.cp("act", XF[:, q, 3:515], ps[:, :], [pk], [("B_XF", q)])
                else:
                    self.cp("act", XFSv[q][:, :, 3:11], self.v3(ps[:, :w], ti), [pk], [("B_XFS", q)])
            if ti == TLIST[-1] and g + 1 < NG:
                load_x(g + 1)
            for q in range(4):
                ch = chs[q]
                cw = [self.pf("ssm_conv_w", k * 4 + ch // 8, ch % 8) for k in range(4)]
                cb = self.pf("ssm_conv_b", ch // 8, ch % 8)
                if ti < 4:
                    dst = XCf[:, q, :]
                    srcf = lambda k, q=q: XF[:, q, k:k + 512]
                    rk = "B_XF"
                else:
                    dst = XCf[:, q, 0:128].rearrange("p (s t) -> p s t", t=8)
                    srcf = lambda k, q=q: XFSv[q][:, :, k:k + 8]
                    rk = "B_XFS"
                self.ts("dve", dst, srcf(0), cw[0], cb, ALU.mult, ALU.add, [rk, "PF"], [("B_XCf", q)])
                for k in range(1, 4):
                    self.stt(dst, srcf(k), cw[k], dst, ALU.mult, ALU.add, [rk, ("B_XCf", q), "PF"], [("B_XCf", q)])
                self.act(XCf[:, q, :w], XCf[:, q, :w], AF.Silu, [("B_XCf", q)], [("B_XCf", q)])
                if q >= 2:
                    self.cp("pool", BCb[:, q - 2, :w], XCf[:, q, :w], [("B_XCf", q)], ["B_BCb"])
            if ti == 3:
                for q in range(4):
                    self.cp("pool", CST[:, chs[q], 0:3], XF[:, q, 512:515], ["B_XF"], ["B_CST"])
            if ti == 4:
                for q in range(4):
                    self.cp("pool", CST[:, chs[q], 3:].rearrange("p (s k) -> p s k", k=3), XFSv[q][:, :, 8:11], ["B_XFS"], ["B_CST"])
            if ti < 4:
                if ti == 0:
                    self.memset("dve", ST[:], 0.0, ["B_ST"])
                    self.memset("pool", STb[:], 0.0, ["B_STb"])
                for ck in range(4):
                    chunk(g, b, ck * 128, 128, 1 + c0 + ck * 128)
                if ti == 3:
                    state_out(g, self.O["ss_p"][4 * g:4 * g + 4])
            else:
                chunk(g, b, 0, 128, 0, sample=True)
            for n in range(8):
                ps, pk = self.bank()
                for hf in range(2):
                    self.mm(ps[:, :w], WOUT[b][:, hf, n * 128:(n + 1) * 128], YGT[:, hf, :w], hf == 0, hf == 1, ["B_WOUT", "B_YGT"], [pk])
                self.tt("dve", self.X[:, n, c0:c0 + w], self.X[:, n, c0:c0 + w], ps[:, :w], ALU.add, [pk, ("X", (n, ti))], [("X", (n, ti))])
        if g + 1 < NG:
            load_o(g + 1)
            load(g + 1)
    pass
    for r in range(4):
        self.fm_to_rows(CST[:, r * 8:(r + 1) * 8, 0:3], "B_CST", 3, self.O["sc_p"][:, r * D:(r + 1) * D])
        self.fm_to_rows(CST[:, r * 8:(r + 1) * 8, 3:], "B_CST", NS * 3, self.O["sc_s"].rearrange("s k d -> (s k) d")[:, r * D:(r + 1) * D])


KB.layer_B = layer_B


def layer_C(self, l, ic):
    I, nc = self.I, self.nc
    E05 = float(np.exp(-0.5))
    SH0 = self.sb("C_SH0", [128, 8, NS]); USH = self.sb("C_USH", [128, 8, 1 + NS])
    self.rows_to_fm(I["st_rw_shift"][ic], NS, SH0, "C_SH0")
    Us = self.U[:, :, 1 + NTP:UC].rearrange("p c (s t) -> p c s t", t=9)
    self.cp("dve", Us[:, :, :, 0], SH0[:], ["C_SH0"], [("U", "shifts")])
    self.norm_to_U("norm_mix", l, shift_out=USH)
    XD = nc.dram_tensor("c_xspill", [128, 8, NT], F32).ap()
    HM = ("kk", "w", "nq", "k", "r")
    SCR = {k: nc.dram_tensor("c_scr_" + k, [NT, D], F32).ap() for k in ("v", "g", "o")}
    SCRH = {k: nc.dram_tensor("c_scrh_" + k, [16, NT, 64], F32).ap() for k in HM}

    def scr_tm(nm, rows):
        if nm in SCRH:
            return SCRH[nm][:, rows, :].rearrange("h t k -> t h k")
        return SCR[nm][rows, :].rearrange("t (h k) -> t h k", k=64)

    self.P.barrier()
    self.dma("sp", XD, self.X[:], ["X"], ["XD"])
    self.P.barrier()

    def xa(i):
        return self.X[:, i // 2, (i % 2) * D:(i % 2 + 1) * D], ("X", "a%d" % i)

    def h16(ap):
        return ap.rearrange("p (h k) -> p h k", k=64)

    def b16(ap):
        return ap.unsqueeze(2).to_broadcast([128, 16, 64])

    def red(out, in_, reads, writes):
        self.P.op("dve", lambda e: e.tensor_reduce(out=out, in_=in_, axis=AX.X, op=ALU.add), reads, writes)

    def pbc(i, src):
        a, k = xa(i)
        self.dma("sp", a, src.partition_broadcast(128), (), [k])

    with self.scope():
        pbc(0, I["rwkv_w0"][ic:ic + 1, :]); pbc(1, I["rwkv_a0"][ic:ic + 1, :]); pbc(2, I["rwkv_k_k"][ic:ic + 1, :]); pbc(3, I["rwkv_k_a"][ic:ic + 1, :])
        WS = [self.sb("C_WS%d" % i, [128, 8, D], BF16) for i in range(3)]
        for s_ in range(3):
            self.dma("pool", WS[s_][:], I["rwkv_w_rkv"][ic, s_].rearrange("(kc p) n -> p kc n", p=128), (), [("C_WS", s_)])
        W1 = self.sb("C_W1", [128, 8, 256], BF16)
        W2w = self.sb("C_W2w", [64, D], BF16); W2a = self.sb("C_W2a", [64, D], BF16); W2g = self.sb("C_W2g", [128, D], BF16)
        for (c0, n, nm) in ((0, 64, "rwkv_w_w1"), (64, 64, "rwkv_w_a1"), (128, 128, "rwkv_w_g1")):
            self.dma("pool", W1[:, :, c0:c0 + n], I[nm][ic].rearrange("(kc p) n -> p kc n", p=128), (), ["C_W1"])
        self.dma("pool", W2w[:], I["rwkv_w_w2"][ic], (), ["C_W2w"]); self.dma("pool", W2a[:], I["rwkv_w_a2"][ic], (), ["C_W2a"])
        self.dma("pool", W2g[:], I["rwkv_w_g2"][ic], (), ["C_W2g"])
        Dd = self.sb("C_D", [128, 8, 128]); XM = [self.sb("C_XM%d" % i, [128, 8, 128], BF16) for i in range(2)]
        TW = self.sb("C_TW", [64, 128], BF16); TA = self.sb("C_TA", [64, 128], BF16); TG = self.sb("C_TG", [128, 128], BF16)
        SS = self.sb("C_SS", [128, 16])
        (Pw0, kw0), (Pa0, ka0), (Pkk, kkk), (Pka, kka) = xa(0), xa(1), xa(2), xa(3)
        (Rt, kR), (Kt, kK), (Vt, kV), (Wt, kW), (At, kA), (KKt, kKK), (Gt, kG), (T1, kT1), (T2, kT2) = [xa(i) for i in range(7, 16)]
        wsn = 0
        for ti in range(17):
            tok0 = 128 * ti
            if ti < 16:
                cur = self.U[:, :, 1 + tok0:1 + tok0 + 128]; prev = self.U[:, :, tok0:tok0 + 128]
                dv = lambda a: a
                ukeys = [("U", (c, ti // 4)) for c in range(8)]
            else:
                cur = Us[:, :, :, 1:9]; prev = Us[:, :, :, 0:8]
                dv = lambda a: a.rearrange("p c (s t) -> p c s t", t=8) if len(a.shape) == 3 else a.rearrange("p (s t) -> p s t", t=8)
                ukeys = [("U", (c, 4)) for c in range(8)] + [("U", "shifts")]
            if ti % 4 == 0 and ti > 0 and ti < 16:
                ukeys = ukeys + [("U", (c, ti // 4 - 1)) for c in range(8)]
            if ti == 0:
                ukeys = ukeys + [("U", "shiftp")]
            self.tt("dve", dv(Dd[:]), prev, cur, ALU.subtract, ukeys, ["C_D"])

            def mix(s):
                xm = XM[s % 2]; key = "C_XM%d" % (s % 2)
                for kc in range(8):
                    self.stt(dv(xm[:, kc, :]), dv(Dd[:, kc, :]), self.pf("rwkv_mu", s, kc), cur[:, kc], ALU.mult, ALU.add, ["C_D", "PF"] + ukeys, [key])
                return xm, key

            def proj_tm(s, dst, dkey):
                b = s
                xm, key = mix(s)
                for nb in range(2):
                    ps, pk = self.bank()
                    for kc in range(8):
                        self.mm(ps[:, :], xm[:, kc, :], WS[b][:, kc, nb * 512:(nb + 1) * 512], kc == 0, kc == 7, [key, ("C_WS", b)], [pk])
                    self.cp("act", dst[:, nb * 512:(nb + 1) * 512], ps[:, :], [pk], [dkey])

            proj_tm(0, Rt, kR); proj_tm(1, Kt, kK); proj_tm(2, Vt, kV)
            for (s, c0, n, fn, dst, dk) in ((3, 0, 64, AF.Tanh, TW, "C_TW"), (4, 64, 64, AF.Copy, TA, "C_TA"), (5, 128, 128, AF.Sigmoid, TG, "C_TG")):
                xm, key = mix(s)
                ps, pk = self.bank()
                for kc in range(8):
                    self.mm(ps[0:n, 0:128], W1[:, kc, c0:c0 + n], xm[:, kc, :], kc == 0, kc == 7, [key, "C_W1"], [pk])
                self.act(dst[0:n, :], ps[0:n, 0:128], fn, [pk], [dk])
            for nb in range(2):
                cs_ = slice(nb * 512, (nb + 1) * 512)
                ps, pk = self.bank()
                self.mm(ps[:, :], TW[0:64, :], W2w[0:64, cs_], True, True, ["C_TW", "C_W2w"], [pk])
                self.tt("dve", T1[:, cs_], ps[:, :], Pw0[:, cs_], ALU.add, [pk, kw0], [kT1])
                ps, pk = self.bank()
                self.mm(ps[:, :], TA[0:64, :], W2a[0:64, cs_], True, True, ["C_TA", "C_W2a"], [pk])
                self.tt("dve", At[:, cs_], ps[:, :], Pa0[:, cs_], ALU.add, [pk, ka0], [kA])
                ps, pk = self.bank()
                self.mm(ps[:, :], TG[:, :], W2g[:, cs_], True, True, ["C_TG", "C_W2g"], [pk])
                self.cp("act", Gt[:, cs_], ps[:, :], [pk], [kG])
            self.act(T1, T1, AF.Sigmoid, [kT1], [kT1])
            self.act(Wt, T1, AF.Exp, [kT1], [kW], scale=-E05)
            self.act(At, At, AF.Sigmoid, [kA], [kA])
            self.tt("dve", KKt, Kt, Pkk, ALU.mult, [kK, kkk], [kKK])
            self.tt("dve", T1, KKt, KKt, ALU.mult, [kKK], [kT1])
            red(SS[:], h16(T1), [kT1], ["C_SS"])
            self.act(SS[:], SS[:], AF.Sqrt, ["C_SS"], ["C_SS"])
            self.ts("dve", SS[:], SS[:], 1e-12, None, ALU.max, None, ["C_SS"], ["C_SS"])
            self.P.op("dve", lambda e, SS=SS: e.reciprocal(out=SS[:], in_=SS[:]), ["C_SS"], ["C_SS"])
            self.tt("dve", h16(KKt), h16(KKt), b16(SS[:]), ALU.mult, [kKK, "C_SS"], [kKK])
            self.stt(T1, At, -1.0, Pka, ALU.add, ALU.mult, [kA, kka], [kT1])
            self.ts("dve", T1, T1, 1.0, None, ALU.add, None, [kT1], [kT1])
            self.tt("dve", Kt, Kt, T1, ALU.mult, [kK, kT1], [kK])
            self.stt(T2, KKt, -1.0, At, ALU.mult, ALU.mult, [kKK, kA], [kT2])
            rows = slice(tok0, tok0 + 128)
            for (nm, a, k) in (("kk", KKt, kKK), ("w", Wt, kW), ("nq", T2, kT2), ("k", Kt, kK), ("r", Rt, kR), ("v", Vt, kV), ("g", Gt, kG)):
                self.dma("sp", scr_tm(nm, rows), h16(a), [k], [("SCR", nm)])

    if self.debug:
        for i_, nm_ in enumerate(("kk", "w", "nq", "k", "r", "v", "g")):
            self.dma("sp", self.O["dbg_scr"][i_].rearrange("t (h k) -> t h k", k=64), scr_tm(nm_, slice(0, NT)), [("SCR", nm_)], ())
    with self.scope():
        S = self.sb("C_S", [128, 8, 64]); Tm = self.sb("C_Tm", [128, 8, 64]); KV = [self.sb("C_KV%d" % i, [128, 8, 64]) for i in range(2)]
        SA = self.sb("C_SA", [128, 8])
        names = ("kk", "w", "nq", "k", "r")
        ST0 = {nm: self.X[:, i, 0:2048].rearrange("p (t k) -> p t k", k=64) for i, nm in enumerate(names)}
        ST1 = {nm: self.sb("C_ST1" + nm, [128, 32, 64]) for nm in names}
        VS = [self.sb("C_VS%d" % i, [128, 32, 8]) for i in range(2)]
        OS = [self.sb("C_OS%d" % i, [128, 32, 8]) for i in range(2)]

        def bk(ap):
            return ap.unsqueeze(1).to_broadcast([128, 8, 64])

        def bv(ap):
            return ap.unsqueeze(2).to_broadcast([128, 8, 64])

        nsub = 0
        import os as _os
        NOSAME = bool(int(_os.environ.get("C_NOSAME", "1")))
        SP = self.PS[7][:, :].rearrange("p (a k) -> p a k", k=64)
        SK = ("PS", 7)

        def run_segment(row0, nsteps):
            nonlocal nsub
            for t0 in range(0, nsteps, 32):
                n = min(32, nsteps - t0)
                sb_ = nsub % 2; nsub += 1
                stg = ST0 if sb_ == 0 else ST1
                skey = "C_STG%d" % sb_
                r0 = row0 + t0
                for nm in names:
                    src = SCRH[nm][:, r0:r0 + n, :]
                    for vb in range(8):
                        self.dma("sp", stg[nm][vb * 16:(vb + 1) * 16, 0:n, :], src, [("SCR", nm)], [(skey, nm)] + ([("X", "stg")] if sb_ == 0 else []))
                srcv = SCR["v"][r0:r0 + n, :].rearrange("t (h v) -> h t v", v=64)
                for vb in range(8):
                    self.dma("sp", VS[sb_][vb * 16:(vb + 1) * 16, 0:n, :], srcv[:, :, vb * 8:(vb + 1) * 8], [("SCR", "v")], [(skey, "v")])
                xk = [("X", "stg")] if sb_ == 0 else []
                for i in range(n):
                    kvb = i % 2
                    self.tt("pool", KV[kvb][:], bk(stg["k"][:, i, :]), bv(VS[sb_][:, i, :]), ALU.mult, [(skey, "k"), (skey, "v")] + xk, ["C_KV%d" % kvb])
                    self.P.nosame = NOSAME
                    self.tt("dve", Tm[:], SP[:], bk(stg["kk"][:, i, :]), ALU.mult, [SK, (skey, "kk")] + xk, ["C_Tm"])
                    red(SA[:], Tm[:], ["C_Tm"], ["C_SA"])
                    self.tt("dve", SP[:], SP[:], bk(stg["w"][:, i, :]), ALU.mult, [SK, (skey, "w")] + xk, [SK])
                    self.tt("dve", Tm[:], bk(stg["nq"][:, i, :]), bv(SA[:]), ALU.mult, ["C_SA", (skey, "nq")] + xk, ["C_Tm"])
                    self.tt("dve", SP[:], SP[:], Tm[:], ALU.add, [SK, "C_Tm"], [SK])
                    self.tt("dve", SP[:], SP[:], KV[kvb][:], ALU.add, [SK, "C_KV%d" % kvb], [SK])
                    self.tt("dve", Tm[:], SP[:], bk(stg["r"][:, i, :]), ALU.mult, [SK, (skey, "r")] + xk, ["C_Tm"])
                    red(OS[sb_][:, i, :], Tm[:], ["C_Tm"], ["C_OS%d" % sb_])
                    self.P.nosame = False
                dsto = SCR["o"][r0:r0 + n, :].rearrange("t (h v) -> h t v", v=64)
                for vb in range(8):
                    self.dma("act", dsto[:, :, vb * 8:(vb + 1) * 8], OS[sb_][vb * 16:(vb + 1) * 16, 0:n, :], ["C_OS%d" % sb_], [("SCR", "o")])

        def state_io(dram, load):
            for vb in range(8):
                d = dram[:, vb * 8:(vb + 1) * 8, :]
                if load:
                    self.dma("sp", S[vb * 16:(vb + 1) * 16, :, :], d, (), ["C_S"])
                else:
                    self.dma("act", d, S[vb * 16:(vb + 1) * 16, :, :], ["C_S"], ())

        self.memset("dve", S[:], 0.0, ["C_S"])
        self.cp("dve", SP[:], S[:], ["C_S"], [SK])
        run_segment(0, NTP)
        self.cp("dve", S[:], SP[:], [SK], ["C_S"])
        state_io(self.O["rw_p"], False)
        for s in range(NS):
            state_io(I["st_rw_wkv"][ic, s], True)
            self.cp("dve", SP[:], S[:], ["C_S"], [SK])
            run_segment(NTP + 8 * s, 8)
            self.cp("dve", S[:], SP[:], [SK], ["C_S"])
            state_io(self.O["rw_s"][s], False)

    if self.debug:
        self.dma("sp", self.O["dbg_scr"][7], SCR["o"], [("SCR", "o")], ())
    with self.scope():
        pbc(0, I["rwkv_lnx_w"][ic:ic + 1, :]); pbc(1, I["rwkv_lnx_b"][ic:ic + 1, :]); pbc(2, I["rwkv_r_k"].rearrange("a h n -> a (h n)")[ic:ic + 1, :])
        (Plw, klw), (Plb, klb), (Prk, krk) = xa(0), xa(1), xa(2)
        (Ot, kO), (Rt, kR), (Kt, kK), (Vt, kV), (Gt, kG), (T1, kT1) = [xa(i) for i in range(7, 13)]
        WO = self.sb("C_WO", [128, 8, D], BF16)
        self.dma("pool", WO[:], I["rwkv_w_out"][ic].rearrange("(kc p) n -> p kc n", p=128), (), ["C_WO"])
        YT = self.sb("C_YT", [128, 8, 128], BF16); XT_ = self.sb("C_XTL", [128, 8, 128])
        M1 = self.sb("C_M1", [128, 16]); M2 = self.sb("C_M2", [128, 16])
        for ti in range(17):
            rows = slice(128 * ti, 128 * ti + 128)
            for (nm, a, k) in (("o", Ot, kO), ("r", Rt, kR), ("k", Kt, kK), ("v", Vt, kV), ("g", Gt, kG)):
                self.dma("sp", h16(a), scr_tm(nm, rows), [("SCR", nm)], [k])
            self.dma("sp", XT_[:], XD[:, :, rows], ["XD"], ["C_XTL"])
            red(M1[:], h16(Ot), [kO], ["C_M1"])
            self.ts("dve", M1[:], M1[:], -1.0 / 64, None, ALU.mult, None, ["C_M1"], ["C_M1"])
            self.tt("dve", h16(Ot), h16(Ot), b16(M1[:]), ALU.add, [kO, "C_M1"], [kO])
            self.tt("dve", T1, Ot, Ot, ALU.mult, [kO], [kT1])
            red(M2[:], h16(T1), [kT1], ["C_M2"])
            self.act(M2[:], M2[:], AF.Ln, ["C_M2", ("C1", 3)], ["C_M2"], scale=1.0 / 64, bias=self.C1[:, 3:4])
            self.act(M2[:], M2[:], AF.Exp, ["C_M2"], ["C_M2"], scale=-0.5)
            self.tt("dve", h16(Ot), h16(Ot), b16(M2[:]), ALU.mult, [kO, "C_M2"], [kO])
            self.tt("dve", Ot, Ot, Plw, ALU.mult, [kO, klw], [kO])
            self.tt("dve", Ot, Ot, Plb, ALU.add, [kO, klb], [kO])
            self.tt("dve", T1, Rt, Kt, ALU.mult, [kR, kK], [kT1])
            self.tt("dve", T1, T1, Prk, ALU.mult, [kT1, krk], [kT1])
            red(M1[:], h16(T1), [kT1], ["C_M1"])
            self.tt("dve", h16(T1), h16(Vt), b16(M1[:]), ALU.mult, [kV, "C_M1"], [kT1])
            self.tt("dve", Ot, Ot, T1, ALU.add, [kO, kT1], [kO])
            self.tt("dve", Ot, Ot, Gt, ALU.mult, [kO, kG], [kO])
            for half in range(2):
                ps, pk = self.bank()
                for cc in range(4):
                    c = half * 4 + cc
                    self.tr(ps[:, cc * 128:(cc + 1) * 128], Ot[:, c * 128:(c + 1) * 128], self.IDF[:], [kO, "IDF"], [pk])
                self.cp("act", YT[:, half * 4:(half + 1) * 4, :].rearrange("p c t -> p (c t)"), ps[:, :], [pk], ["C_YT"])
            for n in range(8):
                ps, pk = self.bank()
                for c in range(8):
                    self.mm(ps[:, 0:128], WO[:, c, n * 128:(n + 1) * 128], YT[:, c, :], c == 0, c == 7, ["C_WO", "C_YT"], [pk])
                self.tt("dve", XT_[:, n, :], XT_[:, n, :], ps[:, 0:128], ALU.add, [pk, "C_XTL"], ["C_XTL"])
            self.dma("sp", XD[:, :, rows], XT_[:], ["C_XTL"], ["XD"])
    self.dma("sp", self.X[:], XD, ["XD"], ["X"])
    self.fm_to_rows(USH[:, :, 0:1], "C_USH", 1, self.O["rs_p"])
    self.fm_to_rows(USH[:, :, 1:], "C_USH", NS, self.O["rs_s"])


KB.layer_C = layer_C


def build(layers=(0, 1, 2, 3), debug=False, mlp=True):
    kb = KB(layers, debug)
    with kb.es:
        kb.setup()
        kb.load_x()
        for l in layers:
            with kb.scope():
                kind = l % 3
                if kind == 0:
                    kb.layer_A(l, l // 3)
                elif kind == 1:
                    kb.layer_B(l, 0)
                else:
                    kb.layer_C(l, 0)
            if mlp:
                with kb.scope():
                    kb.mlp(l)
        if not getattr(kb, '_skip_final', False):
            with kb.scope():
                kb.final_out()
        if debug:
            kb.dump_x()
        kb.P.emit()
    return kb.nc


def make_in_maps(inputs, cores):
    g = {k: np.ascontiguousarray(np.asarray(v, dtype=np.float32)) for k, v in inputs.items()}
    maps = []
    for i in cores:
        sl = slice(NS * i, NS * (i + 1))
        m = {
            "xp": g["x_prompt"][i], "xs": g["x_sample"][sl].reshape(NS * LS, D),
            "st_lru_conv": g["state_lru_conv"][:, sl], "st_lru_h": g["state_lru_h"][:, sl],
            "st_ssm_conv": g["state_ssm_conv"][:, sl], "st_ssm": g["state_ssm"][:, sl],
            "st_rw_shift": g["state_rwkv_shift"][:, sl], "st_rw_wkv": g["state_rwkv_wkv"][:, sl],
        }
        for k in IN_SPECS:
            if k in m:
                continue
            v = g[k]
            if k == "norm_final":
                v = v.reshape(1, D)
            m[k] = v
        maps.append({k: np.ascontiguousarray(v) for k, v in m.items()})
    return maps


def kernel(**inputs):
    nc = build()
    cores = list(range(8))
    res = run_bass_kernel_spmd(nc, make_in_maps(inputs, cores), core_ids=cores)
    r = res.results
    def cat(name, axis=0):
        return np.concatenate([x[name] for x in r], axis=axis)
    y_p = np.stack([x["y_p"] for x in r])
    y_s = cat("y_s").reshape(128, LS, D)
    lc_p = np.stack([x["lc_p"] for x in r], axis=1)
    lc_s = cat("lc_s", 1)
    lh_p = np.stack([x["lh_p"] for x in r], axis=1)
    lh_s = cat("lh_s", 1)
    sc_p = np.stack([x["sc_p"] for x in r])[None]
    sc_s = cat("sc_s")[None]
    ss_p = np.stack([x["ss_p"] for x in r])[None]
    ss_s = cat("ss_s")[None]
    rs_p = np.stack([x["rs_p"][0] for x in r])[None]
    rs_s = cat("rs_s")[None]
    rw_p = np.stack([x["rw_p"] for x in r])[None]
    rw_s = cat("rw_s")[None]
    outs = (y_p, y_s, lc_p, lc_s, lh_p, lh_s, sc_p, sc_s, ss_p, ss_s, rs_p, rs_s, rw_p, rw_s)
    return tuple(np.ascontiguousarray(o, dtype=np.float32) for o in outs)
```

```python
import numpy as np
import concourse.bass as bass
import concourse.mybir as mybir
from concourse.bass_utils import run_bass_kernel_spmd

F32 = mybir.dt.float32
BF16 = mybir.dt.bfloat16
I32 = mybir.dt.int32
AF = mybir.ActivationFunctionType
ALU = mybir.AluOpType
AX = mybir.AxisListType

ENGS = ("pe", "act", "dve", "pool", "sp")
SAME_ENGINE_SYNC = True
import os as _os_
NOSAME_ENGS = tuple(x for x in _os_.environ.get('NOSAME_ENGS', '').split(',') if x)
N_DMA_SEMS = 12


class _St:
    __slots__ = ("w", "r")

    def __init__(self):
        self.w = None
        self.r = {}


class _Op:
    __slots__ = ("eng", "fn", "waits", "dma", "signal", "tok")


class Prog:
    def __init__(self, nc):
        self.nc = nc
        self.ops = {e: [] for e in ENGS}
        self.state = {}
        self.ndma = {e: 0 for e in ENGS}
        self.dma_toks = {}
        self.pending = {}
        self.nosame = False

    def barrier(self):
        toks = set()
        for e in ENGS:
            for o in reversed(self.ops[e]):
                if not o.dma:
                    toks.add(o.tok)
                    break
            n = self.ndma[e]
            for j in range(max(0, n - N_DMA_SEMS), n):
                toks.add(("dma", e, j))
        self.pending = {e: set(toks) for e in ENGS}

    def _states(self, key, create):
        if isinstance(key, tuple):
            buf, sub = key
        else:
            buf, sub = key, None
        d = self.state.setdefault(buf, {})
        if sub is None:
            if create and "*" not in d:
                d["*"] = _St()
            return list(d.values())
        out = []
        if "*" in d:
            out.append(d["*"])
        if sub not in d:
            if create:
                d[sub] = _St()
                out.append(d[sub])
        else:
            out.append(d[sub])
        return out

    def op(self, eng, fn, reads=(), writes=(), dma=False):
        ps_r = [k for k in reads if isinstance(k, tuple) and k[0] == "PS"]
        if ps_r:
            reads = [k for k in reads if not (isinstance(k, tuple) and k[0] == "PS")]
            writes = list(writes) + ps_r
        o = _Op()
        o.eng, o.fn, o.dma, o.signal = eng, fn, dma, False
        idx = len(self.ops[eng])
        if dma:
            j = self.ndma[eng]
            self.ndma[eng] += 1
            o.tok = ("dma", eng, j)
            self.dma_toks[(eng, j)] = o
        else:
            o.tok = ("eng", eng, idx)
        waits = set()
        for k in reads:
            for st in self._states(k, True):
                if st.w is not None:
                    waits.add(st.w)
        for k in writes:
            for st in self._states(k, True):
                if st.w is not None:
                    waits.add(st.w)
                for t in st.r.values():
                    waits.add(t)
        if self.pending.get(eng):
            waits |= self.pending.pop(eng)
        w2 = set()
        for t in waits:
            if t[0] == "eng" and t[1] == eng:
                if not dma and (eng in ("pe", "sp") or not SAME_ENGINE_SYNC or self.nosame or (NOSAME_ENGS and eng in NOSAME_ENGS)):
                    continue
            w2.add(t)
        o.waits = w2
        for k in reads:
            for st in self._states(k, True):
                st.r[o.tok if dma else o.tok[1]] = o.tok
        for k in writes:
            for st in self._states(k, True):
                st.w = o.tok
                st.r = {}
        self.ops[eng].append(o)
        return o

    def emit(self, final_wait_all=True):
        nc = self.nc
        tokmap = {}
        for e in ENGS:
            for i, o in enumerate(self.ops[e]):
                tokmap[o.tok] = o
        for e in ENGS:
            for o in self.ops[e]:
                for t in o.waits:
                    tokmap[t].signal = True
        import contextlib
        with contextlib.ExitStack() as es:
            EPOCH = 16000
            nsig = {e: sum(1 for o in self.ops[e] if o.signal and not o.dma) for e in ENGS}
            esem = {e: [es.enter_context(nc.semaphore("s_%s%d" % (e, i))) for i in range(nsig[e] // EPOCH + 1)] for e in ENGS if e != "sp"}
            dsem = {e: [es.enter_context(nc.semaphore("d_%s%d" % (e, i))) for i in range(N_DMA_SEMS)]
                    for e in ENGS if self.ndma[e] > 0}
            val = {}
            for e in ENGS:
                c = 0
                for o in self.ops[e]:
                    if o.dma:
                        j = o.tok[2]
                        val[o.tok] = (dsem[e][j % N_DMA_SEMS], 16 * (j // N_DMA_SEMS + 1))
                    elif o.signal:
                        val[o.tok] = (esem[e][c // EPOCH], c % EPOCH + 1)
                        c += 1
            self.maxcount = {}
            block = es.enter_context(nc.Block())

            def run(e, eng):
                seen = {}
                for o in self.ops[e]:
                    ws = []
                    for t in o.waits:
                        ws.append(val[t])
                    if o.dma:
                        j = o.tok[2]
                        if j >= N_DMA_SEMS:
                            ws.append(val[("dma", e, j - N_DMA_SEMS)])
                    for (s, v) in ws:
                        if seen.get(id(s), 0) >= v:
                            continue
                        seen[id(s)] = v
                        eng.wait_ge(s, v)
                    ins = o.fn(eng)
                    if o.dma:
                        s, v = val[o.tok]
                        ins.then_inc(s, 16)
                    elif o.signal:
                        ins.then_inc(val[o.tok][0], 1)
                if final_wait_all:
                    n = self.ndma[e]
                    for j in range(max(0, n - N_DMA_SEMS), n):
                        s, v = val[("dma", e, j)]
                        if seen.get(id(s), 0) < v:
                            eng.wait_ge(s, v)
                            seen[id(s)] = v

            @block.tensor
            def _(eng):
                run("pe", eng)

            @block.scalar
            def _(eng):
                run("act", eng)

            @block.vector
            def _(eng):
                run("dve", eng)

            @block.gpsimd
            def _(eng):
                run("pool", eng)

            @block.sync
            def _(eng):
                run("sp", eng)
import contextlib


D = 1024
NTP = 2048
NS = 16
LS = 8
NT = NTP + NS * LS
TT = [(0, 512), (512, 512), (1024, 512), (1536, 512), (2048, 128)]
UC = 1 + NTP + NS * 9
XBC = 3 + NTP + NS * 11

PROW = {}


def _prow_layout():
    r = 0
    def add(name, n):
        nonlocal r
        PROW[name] = r
        r += n
    add("norm_mix", 4); add("norm_ffn", 4); add("norm_final", 1)
    add("lru_conv_w", 8); add("lru_conv_b", 2); add("lru_b_r", 2); add("lru_b_i", 2); add("lru_lambda", 2)
    add("ssm_norm_w", 2); add("ssm_conv_w", 16); add("ssm_conv_b", 4)
    add("rwkv_mu", 6); add("rwkv_w0", 1); add("rwkv_a0", 1); add("rwkv_k_k", 1); add("rwkv_k_a", 1)
    add("rwkv_lnx_w", 1); add("rwkv_lnx_b", 1); add("rwkv_r_k", 1)
    return r


NPROW = _prow_layout()

IN_SPECS = {
    "xp": [NTP, D], "xs": [NS * LS, D],
    "st_lru_conv": [2, NS, 3, D], "st_lru_h": [2, NS, D], "st_ssm_conv": [1, NS, 3, 4096],
    "st_ssm": [1, NS, 32, 64, 128], "st_rw_shift": [1, NS, D], "st_rw_wkv": [1, NS, 16, 64, 64],
    "norm_mix": [4, D], "norm_ffn": [4, D], "norm_final": [1, D],
    "lru_w_in": [2, D, 2048], "lru_conv_w": [2, 4, D], "lru_conv_b": [2, D], "lru_w_r": [2, 8, 128, 128],
    "lru_b_r": [2, D], "lru_w_i": [2, 8, 128, 128], "lru_b_i": [2, D], "lru_lambda": [2, D], "lru_w_out": [2, D, D],
    "ssm_w_in": [1, D, 6176], "ssm_conv_w": [1, 4, 4096], "ssm_conv_b": [1, 4096], "ssm_dt_bias": [1, 32],
    "ssm_a_log": [1, 32], "ssm_d": [1, 32], "ssm_norm_w": [1, 2048], "ssm_w_out": [1, 2048, D],
    "rwkv_mu": [1, 6, D], "rwkv_w_rkv": [1, 3, D, D], "rwkv_w0": [1, D], "rwkv_w_w1": [1, D, 64], "rwkv_w_w2": [1, 64, D],
    "rwkv_a0": [1, D], "rwkv_w_a1": [1, D, 64], "rwkv_w_a2": [1, 64, D], "rwkv_w_g1": [1, D, 128], "rwkv_w_g2": [1, 128, D],
    "rwkv_k_k": [1, D], "rwkv_k_a": [1, D], "rwkv_r_k": [1, 16, 64], "rwkv_lnx_w": [1, D], "rwkv_lnx_b": [1, D],
    "rwkv_w_out": [1, D, D], "ffn_w1": [4, D, 4096], "ffn_w2": [4, 4096, D],
}
OUT_SPECS = {
    "y_p": [NTP, D], "y_s": [NS * LS, D],
    "lc_p": [2, 3, D], "lc_s": [2, NS, 3, D], "lh_p": [2, D], "lh_s": [2, NS, D],
    "sc_p": [3, 4096], "sc_s": [NS, 3, 4096], "ss_p": [32, 64, 128], "ss_s": [NS, 32, 64, 128],
    "rs_p": [1, D], "rs_s": [NS, D], "rw_p": [16, 64, 64], "rw_s": [NS, 16, 64, 64],
}


class KB:
    def __init__(self, layers=(0, 1, 2, 3), debug=False):
        self.nc = nc = bass.Bass("TRN2", target_bir_lowering=False)
        self.P = Prog(nc)
        self.es = contextlib.ExitStack()
        self.I = {k: nc.dram_tensor(k, v, F32, kind="ExternalInput").ap() for k, v in IN_SPECS.items()}
        self.O = {k: nc.dram_tensor(k, v, F32, kind="ExternalOutput").ap() for k, v in OUT_SPECS.items()}
        self.debug = debug
        if debug:
            self.O["dbg_x"] = nc.dram_tensor("dbg_x", [128, 8, NT], F32, kind="ExternalOutput").ap()
            self.O["dbg_scr"] = nc.dram_tensor("dbg_scr", [8, NT, D], F32, kind="ExternalOutput").ap()
        self.bank_i = 0
        self.layers = layers
        self._n = 0

    def sb(self, name, shape, dt=F32):
        self._n += 1
        return self.es.enter_context(self.nc.sbuf_tensor("%s_%d" % (name, self._n), shape, dt))

    @contextlib.contextmanager
    def scope(self):
        old = self.es
        self.es = contextlib.ExitStack()
        try:
            yield
        finally:
            self.es.close()
            self.es = old
            self.P.barrier()

    def bank(self):
        b = self.bank_i
        self.bank_i = (self.bank_i + 1) % 8
        return self.PS[b], ("PS", b)

    def dma(self, eng, out, in_, reads=(), writes=()):
        self.P.op(eng, lambda e: e.dma_start(out=out, in_=in_), reads, writes, dma=True)

    def act(self, out, in_, func, reads, writes, **kw):
        self.P.op("act", lambda e: e.activation(out=out, in_=in_, func=func, **kw), reads, writes)

    def mm(self, out, lhsT, rhs, start, stop, reads, writes):
        self.P.op("pe", lambda e: e.matmul(out, lhsT=lhsT, rhs=rhs, start=start, stop=stop), reads, writes)

    def tr(self, out, in_, ident, reads, writes):
        self.P.op("pe", lambda e: e.transpose(out, in_, ident), reads, writes)

    def ts(self, eng, out, in0, s1, s2, op0, op1, reads, writes):
        if op1 is None:
            self.P.op(eng, lambda e: e.tensor_scalar(out=out, in0=in0, scalar1=s1, scalar2=None, op0=op0), reads, writes)
        else:
            self.P.op(eng, lambda e: e.tensor_scalar(out=out, in0=in0, scalar1=s1, scalar2=s2, op0=op0, op1=op1), reads, writes)

    def stt(self, out, in0, scalar, in1, op0, op1, reads, writes):
        self.P.op("dve", lambda e: e.scalar_tensor_tensor(out=out, in0=in0, scalar=scalar, in1=in1, op0=op0, op1=op1), reads, writes)

    def tt(self, eng, out, in0, in1, op, reads, writes):
        self.P.op(eng, lambda e: e.tensor_tensor(out=out, in0=in0, in1=in1, op=op), reads, writes)

    def cp(self, eng, out, in_, reads, writes):
        if eng == "act":
            self.P.op("act", lambda e: e.activation(out=out, in_=in_, func=AF.Copy), reads, writes)
        else:
            self.P.op(eng, lambda e: e.tensor_copy(out=out, in_=in_), reads, writes)

    def memset(self, eng, ap, v, writes):
        self.P.op(eng, lambda e: e.memset(ap, v), (), writes)

    def scan(self, out, d0, d1, init, reads, writes):
        self.P.op("dve", lambda e: e.tensor_tensor_scan(out=out, data0=d0, data1=d1, initial=init, op0=ALU.mult, op1=ALU.add), reads, writes)

    def xv(self, c, ti):
        c0, w = TT[ti]
        return self.X[:, c, c0:c0 + w]

    def uv(self, c, ti, shift=0):
        if ti < 4:
            s = 1 + 512 * ti + shift
            return self.U[:, c, s:s + 512]
        v = self.U[:, c, 1 + NTP:UC].rearrange("p (s t) -> p s t", t=9)
        return v[:, :, 1 + shift:9 + shift]

    @staticmethod
    def v3(ap, ti):
        if ti < 4:
            return ap
        return ap.rearrange("p (s t) -> p s t", t=8)

    def setup(self):
        nc = self.nc
        self.PS = [self.es.enter_context(nc.psum_tensor("ps%d" % i, [128, 512], F32)) for i in range(8)]
        self.X = self.sb("X", [128, 8, NT])
        self.U = self.sb("U", [128, 8, UC], BF16)
        self.IDF = self.sb("IDF", [128, 128])
        self.IDB = self.sb("IDB", [128, 128], BF16)
        self.ONESB = self.sb("ONESB", [128, 128], BF16)
        self.C1 = self.sb("C1", [128, 4])
        self.PRM = self.sb("PRM", [64, D])
        self.PF = self.sb("PF", [128, 8, 64])
        self.STG = [self.sb("STG%d" % i, [128, D]) for i in range(2)]
        self.SQ = self.sb("SQ", [128, 8, 512], BF16)
        self.RS = self.sb("RS", [128, 512])
        P = self.P
        P.op("pool", lambda e: e.memset(self.IDF[:], 0.0), (), ["IDF"])
        P.op("pool", lambda e: e.affine_select(out=self.IDF[:], in_=self.IDF[:], pattern=[[-1, 128]], compare_op=ALU.not_equal,
                                               fill=1.0, base=0, channel_multiplier=1), ["IDF"], ["IDF"])
        self.cp("dve", self.IDB[:], self.IDF[:], ["IDF"], ["IDB"])
        self.memset("dve", self.ONESB[:], 1.0, ["ONESB"])
        self.memset("dve", self.C1[:, 0:1], 1e-6, [("C1", 0)])
        self.memset("dve", self.C1[:, 1:2], 1.0, [("C1", 1)])
        self.memset("dve", self.C1[:, 2:3], 1e-5, [("C1", 2)])
        self.memset("dve", self.C1[:, 3:4], 64e-5, [("C1", 3)])
        self.memset("dve", self.U[:, :, 0:1], 0.0, [("U", "shiftp")])
        self.memset("pool", self.PRM[:], 0.0, ["PRM"])
        I = self.I
        def row(name, src, n):
            r = PROW[name]
            self.dma("sp", self.PRM[r:r + n, :], src, (), ["PRM"])
        row("norm_mix", I["norm_mix"], 4); row("norm_ffn", I["norm_ffn"], 4); row("norm_final", I["norm_final"], 1)
        row("lru_conv_w", I["lru_conv_w"].rearrange("a k d -> (a k) d"), 8)
        row("lru_conv_b", I["lru_conv_b"], 2); row("lru_b_r", I["lru_b_r"], 2); row("lru_b_i", I["lru_b_i"], 2)
        row("lru_lambda", I["lru_lambda"], 2)
        row("ssm_norm_w", I["ssm_norm_w"].rearrange("a (r d) -> (a r) d", d=D), 2)
        row("ssm_conv_w", I["ssm_conv_w"].rearrange("a k (r d) -> (a k r) d", d=D), 16)
        row("ssm_conv_b", I["ssm_conv_b"].rearrange("a (r d) -> (a r) d", d=D), 4)
        row("rwkv_mu", I["rwkv_mu"].rearrange("a k d -> (a k) d"), 6)
        for nm in ("rwkv_w0", "rwkv_a0", "rwkv_k_k", "rwkv_k_a", "rwkv_lnx_w", "rwkv_lnx_b"):
            row(nm, I[nm], 1)
        row("rwkv_r_k", I["rwkv_r_k"].rearrange("a h n -> a (h n)"), 1)
        for c in range(8):
            ps, pk = self.bank()
            self.tr(ps[:, 0:64], self.PRM[:, c * 128:(c + 1) * 128], self.IDF[0:64, 0:64], ["PRM", "IDF"], [pk])
            self.cp("dve", self.PF[:, c, :], ps[:, 0:64], [pk], [("PF", c)])

    def pf(self, name, k, c):
        r = PROW[name] + k
        return self.PF[:, c, r:r + 1]

    def load_x(self):
        n = 0
        for ti, (c0, w) in enumerate(TT):
            for j in range(w // 128):
                b = n % 2
                n += 1
                src = self.I["xp"][c0 + j * 128:c0 + (j + 1) * 128, :] if ti < 4 else self.I["xs"][:, :]
                self.dma("sp", self.STG[b][:], src, (), [("STG", b)])
                for c in range(8):
                    self.tr(self.PS[c][:, j * 128:(j + 1) * 128], self.STG[b][:, c * 128:(c + 1) * 128], self.IDF[:],
                            [("STG", b), "IDF"], [("PS", c)])
            for c in range(8):
                self.cp("dve" if c % 2 == 0 else "act", self.X[:, c, c0:c0 + w], self.PS[c][:, 0:w], [("PS", c)], [("X", (c, ti))])

    def rows_to_fm(self, src, nrows, dst, dkey):
        self.dma("sp", self.STG[0][0:nrows, :], src, (), [("STG", 0)])
        for c in range(8):
            ps, pk = self.bank()
            self.tr(ps[:, 0:nrows], self.STG[0][0:nrows, c * 128:(c + 1) * 128], self.IDF[0:nrows, 0:nrows], [("STG", 0), "IDF"], [pk])
            self.cp("dve", dst[:, c, 0:nrows], ps[:, 0:nrows], [pk], [dkey])

    def fm_to_rows(self, src, skey, nrows, dst):
        for half in range(2):
            ps, pk = self.bank()
            for cc in range(4):
                c = half * 4 + cc
                self.tr(ps[0:nrows, cc * 128:(cc + 1) * 128], src[:, c, 0:nrows], self.IDF[:], [skey, "IDF"], [pk])
            self.cp("dve", self.STG[1][0:nrows, half * 512:(half + 1) * 512], ps[0:nrows, :], [pk], [("STG", 1)])
        self.dma("sp", dst, self.STG[1][0:nrows, :], [("STG", 1)], ())

    def norm_to_U(self, pname, k, shift_out=None):
        for ti, (c0, w) in enumerate(TT):
            ps, pk = self.bank()
            for c in range(8):
                self.act(self.SQ[:, c, :w], self.X[:, c, c0:c0 + w], AF.Square, [("X", (c, ti))], [("SQ", c)])
                self.mm(ps[:, :w], self.ONESB[:], self.SQ[:, c, :w], c == 0, c == 7, [("SQ", c), "ONESB"], [pk])
            self.act(self.RS[:, :w], ps[:, :w], AF.Ln, [pk, ("C1", 0)], ["RS"], scale=1.0 / D, bias=self.C1[:, 0:1])
            self.act(self.RS[:, :w], self.RS[:, :w], AF.Exp, ["RS"], ["RS"], scale=-0.5)
            for c in range(8):
                self.stt(self.uv(c, ti), self.v3(self.X[:, c, c0:c0 + w], ti), self.pf(pname, k, c), self.v3(self.RS[:, :w], ti),
                         ALU.mult, ALU.mult, [("X", (c, ti)), "RS", ("PF", c)], [("U", (c, ti))])
                if shift_out is not None and ti == 3:
                    self.stt(shift_out[:, c, 0:1], self.X[:, c, NTP - 1:NTP], self.pf(pname, k, c), self.RS[:, 511:512],
                             ALU.mult, ALU.mult, [("X", (c, ti)), "RS", ("PF", c)], ["C_USH"])
                if shift_out is not None and ti == 4:
                    self.stt(shift_out[:, c, 1:], self.X[:, c, NTP:NT].rearrange("p (s t) -> p s t", t=8)[:, :, 7],
                             self.pf(pname, k, c), self.RS[:, 0:128].rearrange("p (s t) -> p s t", t=8)[:, :, 7],
                             ALU.mult, ALU.mult, [("X", (c, ti)), "RS", ("PF", c)], ["C_USH"])

    def u_keys(self, ti):
        return [("U", (c, ti)) for c in range(8)]

    def mlp(self, l):
        self.norm_to_U("norm_ffn", l)
        self.W1S = [self.sb("W1S%d" % i, [128, 8, 512], BF16) for i in range(2)]
        self.W2S = [self.sb("W2S%d" % i, [128, 4, D], BF16) for i in range(2)]
        self.HT = [self.sb("HT%d" % i, [128, 4, 512], BF16) for i in range(2)]
        self.RT = [self.sb("RT%d" % i, [128, 512]) for i in range(2)]
        w1 = self.I["ffn_w1"]
        w2 = self.I["ffn_w2"]
        def load(s):
            b = s % 2
            self.dma("pool", self.W1S[b][:], w1[l, :, s * 512:(s + 1) * 512].rearrange("(kc p) n -> p kc n", p=128), (), [("W1S", b)])
            self.dma("pool", self.W2S[b][:], w2[l, s * 512:(s + 1) * 512, :].rearrange("(fc p) n -> p fc n", p=128), (), [("W2S", b)])
        load(0)
        hb = 0
        rb = 0
        for s in range(8):
            if s + 1 < 8:
                load(s + 1)
            b = s % 2
            for ti, (c0, w) in enumerate(TT):
                H = self.HT[hb]
                hk = "HT%d" % hb
                hb ^= 1
                for fc in range(4):
                    ps, pk = self.bank()
                    for kc in range(8):
                        self.mm(self.v3(ps[:, :w], ti), self.W1S[b][:, kc, fc * 128:(fc + 1) * 128], self.uv(kc, ti), kc == 0, kc == 7,
                                [("W1S", b), ("U", (kc, ti))], [pk])
                    R = self.RT[rb]
                    rk = "RT%d" % rb
                    rb ^= 1
                    self.act(R[:, :w], ps[:, :w], AF.Relu, [pk], [rk])
                    self.act(H[:, fc, :w], R[:, :w], AF.Square, [rk], [(hk, fc)])
                for n in range(8):
                    ps, pk = self.bank()
                    for fc in range(4):
                        self.mm(ps[:, :w], self.W2S[b][:, fc, n * 128:(n + 1) * 128], H[:, fc, :w], fc == 0, fc == 3,
                                [("W2S", b), (hk, fc)], [pk])
                    self.tt("dve", self.X[:, n, c0:c0 + w], self.X[:, n, c0:c0 + w], ps[:, :w], ALU.add,
                            [pk, ("X", (n, ti))], [("X", (n, ti))])

    def final_out(self):
        YT = self.STG
        UF = self.sb("UF", [128, 8, 512])
        n = 0
        for ti, (c0, w) in enumerate(TT):
            ps, pk = self.bank()
            for c in range(8):
                self.act(self.SQ[:, c, :w], self.X[:, c, c0:c0 + w], AF.Square, [("X", (c, ti))], [("SQ", c)])
                self.mm(ps[:, :w], self.ONESB[:], self.SQ[:, c, :w], c == 0, c == 7, [("SQ", c), "ONESB"], [pk])
            self.act(self.RS[:, :w], ps[:, :w], AF.Ln, [pk, ("C1", 0)], ["RS"], scale=1.0 / D, bias=self.C1[:, 0:1])
            self.act(self.RS[:, :w], self.RS[:, :w], AF.Exp, ["RS"], ["RS"], scale=-0.5)
            for c in range(8):
                self.stt(UF[:, c, :w], self.X[:, c, c0:c0 + w], self.pf("norm_final", 0, c), self.RS[:, :w],
                         ALU.mult, ALU.mult, [("X", (c, ti)), "RS", ("PF", c)], [("UF", c)])
            for j in range(w // 128):
                b = n % 2
                n += 1
                for half in range(2):
                    ps2, pk2 = self.bank()
                    for cc in range(4):
                        c = half * 4 + cc
                        self.tr(ps2[:, cc * 128:(cc + 1) * 128], UF[:, c, j * 128:(j + 1) * 128], self.IDF[:], [("UF", c), "IDF"], [pk2])
                    self.cp("act" if half else "dve", YT[b][:, half * 512:(half + 1) * 512], ps2[:, :], [pk2], [("STG", b)])
                dst = self.O["y_p"][c0 + j * 128:c0 + (j + 1) * 128, :] if ti < 4 else self.O["y_s"][:, :]
                self.dma("sp", dst, YT[b][:], [("STG", b)], ())

    def dump_x(self):
        self.dma("sp", self.O["dbg_x"], self.X[:], ["X"], ())


def layer_A(self, l, ia):
    I = self.I
    self.norm_to_U("norm_mix", l)
    XB = self.sb("A_XB", [128, XBC])
    XC = self.sb("A_XC", [128, NT])
    XCb = self.sb("A_XCb", [128, NT], BF16)
    GATE = self.sb("A_GATE", [128, NT], BF16)
    R = self.sb("A_R", [128, NT])
    Iq = self.sb("A_I", [128, NT])
    WIN = [self.sb("A_WIN%d" % i, [128, 8, 256], BF16) for i in range(2)]
    WR = [self.sb("A_WR%d" % i, [128, 128], BF16) for i in range(2)]
    WI = [self.sb("A_WI%d" % i, [128, 128], BF16) for i in range(2)]
    WO = [self.sb("A_WO%d" % i, [128, D], BF16) for i in range(2)]
    CL = self.sb("A_CL", [128, 8])
    H0 = self.sb("A_H0", [128, 8, NS])
    CS0 = self.sb("A_CS0", [128, 8, NS * 3])
    HST = self.sb("A_HST", [128, 8, 1 + NS])
    CST = self.sb("A_CST", [128, 8, 3 + NS * 3])
    XBs = XB[:, 3 + NTP:XBC].rearrange("p (s t) -> p s t", t=11)
    XCs = XC[:, NTP:NT].rearrange("p (s t) -> p s t", t=8)
    rl = PROW["lru_lambda"] + ia
    self.act(CL[:], self.PF[:, :, rl], AF.Exp, ["PF"], ["A_CL"], scale=-1.0)
    self.act(CL[:], CL[:], AF.Ln, ["A_CL", ("C1", 1)], ["A_CL"], bias=self.C1[:, 1:2], scale=1.0)
    self.ts("dve", CL[:], CL[:], -8.0, None, ALU.mult, None, ["A_CL"], ["A_CL"])
    self.rows_to_fm(I["st_lru_h"][ia], NS, H0, "A_H0")
    self.rows_to_fm(I["st_lru_conv"][ia].rearrange("s k d -> (s k) d"), NS * 3, CS0, "A_CS0")
    w_in, w_out = I["lru_w_in"], I["lru_w_out"]

    def load(j):
        b = j % 2
        self.dma("pool", WIN[b][:, :, 0:128], w_in[ia, :, j * 128:(j + 1) * 128].rearrange("(kc p) n -> p kc n", p=128), (), [("A_WIN", b)])
        self.dma("pool", WIN[b][:, :, 128:256], w_in[ia, :, D + j * 128:D + (j + 1) * 128].rearrange("(kc p) n -> p kc n", p=128), (), [("A_WIN", b)])
        self.dma("pool", WR[b][:], I["lru_w_r"][ia, j], (), [("A_WR", b)])
        self.dma("pool", WI[b][:], I["lru_w_i"][ia, j], (), [("A_WI", b)])
        self.dma("pool", WO[b][:], w_out[ia, j * 128:(j + 1) * 128, :], (), [("A_WO", b)])

    load(0)
    for j in range(8):
        if j + 1 < 8:
            load(j + 1)
        b = j % 2
        self.memset("dve", XB[:, 0:3], 0.0, [("A_XB", "st")])
        self.cp("dve", XBs[:, :, 0:3], CS0[:, j, :].rearrange("p (s k) -> p s k", k=3), ["A_CS0"], [("A_XB", "st")])
        for ti, (c0, w) in enumerate(TT):
            for half in range(2):
                ps, pk = self.bank()
                for kc in range(8):
                    self.mm(self.v3(ps[:, :w], ti), WIN[b][:, kc, half * 128:(half + 1) * 128], self.uv(kc, ti), kc == 0, kc == 7,
                            [("A_WIN", b), ("U", (kc, ti))], [pk])
                if half == 0:
                    dst = XB[:, 3 + c0:3 + c0 + w] if ti < 4 else XBs[:, :, 3:11]
                    self.cp("act", dst, self.v3(ps[:, :w], ti), [pk], [("A_XB", ti)])
                else:
                    self.act(GATE[:, c0:c0 + w], ps[:, :w], AF.Gelu_apprx_tanh, [pk], [("A_GATE", ti)])
        self.cp("pool", CST[:, j, 0:3], XB[:, NTP:NTP + 3], ["A_XB"], [("A_CST", j)])
        self.cp("pool", CST[:, j, 3:].rearrange("p (s k) -> p s k", k=3), XBs[:, :, 8:11], ["A_XB"], [("A_CST", j)])
        cw = [self.pf("lru_conv_w", ia * 4 + k, j) for k in range(4)]
        cb = self.pf("lru_conv_b", ia, j)
        for (dst, srcf) in ((XC[:, 0:NTP], lambda k: XB[:, k:k + NTP]), (XCs, lambda k: XBs[:, :, k:k + 8])):
            self.ts("dve", dst, srcf(0), cw[0], cb, ALU.mult, ALU.add, ["A_XB", ("PF", j)], ["A_XC"])
            for k in range(1, 4):
                self.stt(dst, srcf(k), cw[k], dst, ALU.mult, ALU.add, ["A_XB", "A_XC", ("PF", j)], ["A_XC"])
        self.cp("act", XCb[:], XC[:], ["A_XC"], ["A_XCb"])
        for ti, (c0, w) in enumerate(TT):
            for (Wg, dstb, bname, key) in ((WR, R, "lru_b_r", "A_R"), (WI, Iq, "lru_b_i", "A_I")):
                ps, pk = self.bank()
                self.mm(ps[:, :w], Wg[b][:], XCb[:, c0:c0 + w], True, True, ["A_XCb", (key.replace("A_", "A_W"), b)], [pk])
                self.act(dstb[:, c0:c0 + w], ps[:, :w], AF.Sigmoid, [pk, ("PF", j)], [key], bias=self.pf(bname, ia, j), scale=1.0)
        T1 = XB[:, 0:NT]
        self.act(R[:], R[:], AF.Exp, ["A_R", "A_CL"], ["A_R"], scale=CL[:, j:j + 1])
        self.act(T1, R[:], AF.Square, ["A_R", "A_XB"], ["A_XB"])
        self.ts("dve", T1, T1, -1.0, 1.0, ALU.mult, ALU.add, ["A_XB"], ["A_XB"])
        self.ts("dve", T1, T1, 1e-30, None, ALU.max, None, ["A_XB"], ["A_XB"])
        self.act(T1, T1, AF.Sqrt, ["A_XB"], ["A_XB"])
        self.memset("dve", T1[:, 0:1], 1.0, ["A_XB"])
        self.memset("dve", R[:, 0:1], 0.0, ["A_R"])
        self.tt("dve", Iq[:], Iq[:], T1, ALU.mult, ["A_I", "A_XB"], ["A_I"])
        self.tt("dve", Iq[:], Iq[:], XC[:], ALU.mult, ["A_I", "A_XC"], ["A_I"])
        self.scan(XC[:, 0:NTP], R[:, 0:NTP], Iq[:, 0:NTP], 0.0, ["A_R", "A_I"], ["A_XC"])
        for s in range(NS):
            c0 = NTP + s * 8
            self.scan(XC[:, c0:c0 + 8], R[:, c0:c0 + 8], Iq[:, c0:c0 + 8], H0[:, j, s:s + 1], ["A_R", "A_I", "A_H0"], ["A_XC"])
        self.cp("pool", HST[:, j, 0:1], XC[:, NTP - 1:NTP], ["A_XC"], [("A_HST", j)])
        self.cp("pool", HST[:, j, 1:], XCs[:, :, 7], ["A_XC"], [("A_HST", j)])
        self.tt("dve", XCb[:], XC[:], GATE[:], ALU.mult, ["A_XC", "A_GATE"], ["A_XCb"])
        for ti, (c0, w) in enumerate(TT):
            for n in range(8):
                ps, pk = self.bank()
                self.mm(ps[:, :w], WO[b][:, n * 128:(n + 1) * 128], XCb[:, c0:c0 + w], True, True, ["A_XCb", ("A_WO", b)], [pk])
                self.tt("dve", self.X[:, n, c0:c0 + w], self.X[:, n, c0:c0 + w], ps[:, :w], ALU.add, [pk, ("X", (n, ti))], [("X", (n, ti))])
    self.fm_to_rows(HST[:, :, 0:1], "A_HST", 1, self.O["lh_p"][ia:ia + 1, :])
    self.fm_to_rows(HST[:, :, 1:], "A_HST", NS, self.O["lh_s"][ia])
    self.fm_to_rows(CST[:, :, 0:3], "A_CST", 3, self.O["lc_p"][ia])
    self.fm_to_rows(CST[:, :, 3:], "A_CST", NS * 3, self.O["lc_s"][ia].rearrange("s k d -> (s k) d"))


KB.layer_A = layer_A


def layer_B(self, l, ib):
    I = self.I
    self.norm_to_U("norm_mix", l)
    f32 = F32
    TRI = self.sb("B_TRI", [128, 128]); NEGM = self.sb("B_NEGM", [128, 128]); ONESF = self.sb("B_ONESF", [128, 128])
    self.memset("pool", TRI[:], 1.0, ["B_TRI"])
    self.P.op("pool", lambda e: e.affine_select(out=TRI[:], in_=TRI[:], pattern=[[1, 128]], compare_op=ALU.is_ge, fill=0.0, base=0,
                                                channel_multiplier=-1), ["B_TRI"], ["B_TRI"])
    self.memset("pool", NEGM[:], 0.0, ["B_NEGM"])
    self.P.op("pool", lambda e: e.affine_select(out=NEGM[:], in_=NEGM[:], pattern=[[1, 128]], compare_op=ALU.is_ge, fill=-1.0e4, base=0,
                                                channel_multiplier=-1), ["B_NEGM"], ["B_NEGM"])
    self.memset("pool", ONESF[:], 1.0, ["B_ONESF"])
    DTB = self.sb("B_DTB", [128, 32]); AB = self.sb("B_AB", [128, 32]); DB = self.sb("B_DB", [128, 32])
    self.dma("sp", DTB[:], I["ssm_dt_bias"][ib:ib + 1, :].partition_broadcast(128), (), ["B_DTB"])
    self.dma("sp", AB[:], I["ssm_a_log"][ib:ib + 1, :].partition_broadcast(128), (), ["B_AB"])
    self.dma("sp", DB[:], I["ssm_d"][ib:ib + 1, :].partition_broadcast(128), (), ["B_DB"])
    self.act(AB[:], AB[:], AF.Exp, ["B_AB"], ["B_AB"])
    self.ts("dve", AB[:], AB[:], -1.0, None, ALU.mult, None, ["B_AB"], ["B_AB"])
    CS0 = self.sb("B_CS0", [128, 32, NS * 3]); CST = self.sb("B_CST", [128, 32, 3 + NS * 3])
    for r in range(4):
        self.rows_to_fm(I["st_ssm_conv"][ib].rearrange("s k d -> (s k) d")[:, r * D:(r + 1) * D], NS * 3, CS0[:, r * 8:(r + 1) * 8, :], "B_CS0")
    WZD = [self.sb("B_WZD0", [128, 8, 260], BF16)] * 2
    BONES = self.sb("B_BONES", [128, 128]); TRIS = self.sb("B_TRIS", [128, 128]); NEGMS = self.sb("B_NEGMS", [128, 128])
    SEQM = self.sb("B_SEQM", [128, 16]); DAS = self.sb("B_DAS", [128, 64]); CDS = self.sb("B_CDS", [128, 64])
    YO = self.sb("B_YO", [128, 256]); BTM = self.sb("B_BTM", [128, 128], BF16); USC = self.sb("B_USC", [128, 8, 128], BF16)
    def _asel(ap, pattern, base, cm, fill, keys):
        self.P.op("pool", lambda e: e.affine_select(out=ap, in_=ap, pattern=pattern, compare_op=ALU.is_ge, fill=fill, base=base,
                                                    channel_multiplier=cm), keys, keys)
    self.memset("pool", SEQM[:], 1.0, ["B_SEQM"])
    _asel(SEQM[:], [[-8, 16]], 0, 1, 0.0, ["B_SEQM"])
    _asel(SEQM[:], [[8, 16]], 7, -1, 0.0, ["B_SEQM"])
    self.memset("pool", BONES[:], 1.0, ["B_BONES"])
    _asel(BONES[:].rearrange("p (s t) -> p s t", t=8), [[-8, 16], [0, 8]], 0, 1, 0.0, ["B_BONES"])
    _asel(BONES[:].rearrange("p (s t) -> p s t", t=8), [[8, 16], [0, 8]], 7, -1, 0.0, ["B_BONES"])
    self.tt("pool", TRIS[:], TRI[:], BONES[:], ALU.mult, ["B_TRI", "B_BONES"], ["B_TRIS"])
    self.tt("pool", NEGMS[:], NEGM[:], BONES[:], ALU.mult, ["B_NEGM", "B_BONES"], ["B_NEGMS"])
    self.ts("dve", YO[:, 0:128], BONES[:], 1.0e4, -1.0e4, ALU.mult, ALU.add, ["B_BONES"], ["B_YO"])
    self.tt("dve", NEGMS[:], NEGMS[:], YO[:, 0:128], ALU.add, ["B_NEGMS", "B_YO"], ["B_NEGMS"])
    self.cp("dve", USC[:].rearrange("p c (s t) -> p c s t", t=8),
            self.U[:, :, 1 + NTP:UC].rearrange("p c (s t) -> p c s t", t=9)[:, :, :, 1:9], [("U", (c_, 4)) for c_ in range(8)], ["B_USC"])

    WXBC = [self.sb("B_WXBC0", [128, 8, 512], BF16)] * 2
    WOUT = [self.sb("B_WOUT0", [128, 2, D], BF16)] * 2
    XF = self.sb("B_XF", [128, 4, 3 + 512]); XFS = self.sb("B_XFS", [128, 4, NS * 11])
    XCf = self.sb("B_XCf", [128, 4, 512]); BCb = self.sb("B_BCb", [128, 2, 512], BF16)
    YGT = self.sb("B_YGT", [128, 2, 512], BF16)
    ST = self.sb("B_ST", [128, 256]); STb = self.sb("B_STb", [128, 256], BF16)
    SIN = self.sb("B_SIN", [128, 2, 128]); SOUT = self.sb("B_SOUT", [128, 2, 128])
    XT = self.sb("B_XT", [128, 256]); BT = self.sb("B_BT", [128, 128], BF16)
    SM = self.sb("B_SM", [128, 40])
    TRIH = self.sb("B_TRIH", [128, 4, 128]); LT = self.sb("B_LT", [128, 4, 128]); MT = self.sb("B_MT", [128, 4, 128], BF16)
    XDT = self.sb("B_XDT", [128, 256], BF16); XDD = self.sb("B_XDD", [128, 256], BF16)
    Y1 = self.sb("B_Y1", [128, 256]); T2 = self.sb("B_T2", [128, 256]); SZ = self.sb("B_SZ", [128, 256])
    DTV, DT_, DA, NACS, EACS, DEND, CD, MS = (SM[:, 0:4], SM[:, 4:8], SM[:, 8:12], SM[:, 12:16], SM[:, 16:20], SM[:, 20:24],
                                              SM[:, 24:28], SM[:, 28:29])
    w_in, w_out = I["ssm_w_in"], I["ssm_w_out"]
    XFSv = [XFS[:, q, :].rearrange("p (s t) -> p s t", t=11) for q in range(4)]

    def bc(ap, cs):
        return ap.unsqueeze(2).to_broadcast([cs, 4, 64])

    def v4(ap):
        return ap.rearrange("p (h q) -> p h q", q=64)

    def wv(c0, n):
        return w_in[ib, :, c0:c0 + n].rearrange("(kc p) n -> p kc n", p=128)

    def load(g):
        self.dma("pool", WZD[0][:, :, 0:256], wv(g * 256, 256), (), ["B_WZD"])
        self.dma("pool", WZD[0][:, :, 256:260], wv(6144 + 4 * g, 4), (), ["B_WZD"])

    def load_x(g):
        self.dma("pool", WXBC[0][:, :, 0:256], wv(2048 + g * 256, 256), (), ["B_WXBC"])
        self.dma("pool", WXBC[0][:, :, 256:384], wv(4096 + g * 128, 128), (), ["B_WXBC"])
        self.dma("pool", WXBC[0][:, :, 384:512], wv(5120 + g * 128, 128), (), ["B_WXBC"])

    def load_o(g):
        self.dma("pool", WOUT[0][:], w_out[ib, g * 256:(g + 1) * 256, :].rearrange("(h p) n -> p h n", p=128), (), ["B_WOUT"])

    def chunk(g, b, tc, cs, ucol, sample=False):
        hs = slice(4 * g, 4 * g + 4)
        tri, negm = (TRIS, NEGMS) if sample else (TRI, NEGM)
        import os as _os
        STG_ = int(_os.environ.get('BDBG_C', '99'))
        if STG_ <= 0:
            return
        pzd, kzd = self.bank()
        for kc in range(8):
            self.mm(pzd[0:cs, 0:260], (USC[:, kc, :] if sample else self.U[:, kc, ucol:ucol + cs]), WZD[b][:, kc, :], kc == 0, kc == 7, ["B_WZD", "U", "B_USC"], [kzd])
        if STG_ <= 1:
            return
        pt, kt = self.bank()
        for q in range(3):
            self.tr(pt[0:cs, q * 128:(q + 1) * 128], XCf[:, q, tc:tc + cs], self.IDF[:], ["B_XCf", "IDF"], [kt])
        self.cp("act", XT[0:cs, :], pt[0:cs, 0:256], [kt], ["B_XT"])
        self.cp("dve", BT[0:cs, :], pt[0:cs, 256:384], [kt], ["B_BT"])
        if STG_ <= 2:
            return
        self.tt("dve", DTV[0:cs], pzd[0:cs, 256:260], DTB[0:cs, hs], ALU.add, [kzd, "B_DTB"], ["B_SM"])
        self.act(DT_[0:cs], DTV[0:cs], AF.Exp, ["B_SM"], ["B_SM"])
        self.act(DT_[0:cs], DT_[0:cs], AF.Ln, ["B_SM", ("C1", 1)], ["B_SM"], bias=self.C1[0:cs, 1:2], scale=1.0)
        self.tt("dve", DA[0:cs], DT_[0:cs], AB[0:cs, hs], ALU.mult, ["B_SM", "B_AB"], ["B_SM"])
        if STG_ <= 3:
            return
        pa, ka = self.bank()
        self.mm(pa[0:cs, 0:4], tri[0:cs, 0:cs], DA[0:cs], True, True, ["B_TRI", "B_TRIS", "B_SM"], [ka])
        self.mm(pa[:, 4:8], (BONES[:, :] if sample else ONESF[0:cs, :]), DA[0:cs], True, True, ["B_ONESF", "B_BONES", "B_SM"], [ka])
        self.ts("dve", NACS[0:cs], pa[0:cs, 0:4], -1.0, None, ALU.mult, None, [ka], ["B_SM"])
        self.act(EACS[0:cs], pa[0:cs, 0:4], AF.Exp, [ka], ["B_SM"])
        self.tt("dve", DEND[0:cs], pa[0:cs, 4:8], NACS[0:cs], ALU.add, [ka, "B_SM"], ["B_SM"])
        self.act(DEND[0:cs], DEND[0:cs], AF.Exp, ["B_SM"], ["B_SM"])
        if not sample:
            self.act(CD, pa[:, 4:8], AF.Exp, [ka], ["B_SM"])
        else:
            self.tt("dve", DAS[:].rearrange("p (s h) -> p s h", h=4), DA[:].unsqueeze(1).to_broadcast([128, 16, 4]),
                    SEQM[:].unsqueeze(2).to_broadcast([128, 16, 4]), ALU.mult, ["B_SM", "B_SEQM"], ["B_DAS"])
            pcd, kcd = self.bank()
            self.mm(pcd[:, 0:64], ONESF[:, :], DAS[:], True, True, ["B_ONESF", "B_DAS"], [kcd])
            self.act(CDS[:], pcd[:, 0:64], AF.Exp, [kcd], ["B_CDS"])
        if STG_ <= 4:
            return
        pl, kl = self.bank()
        for h in range(4):
            self.ts("dve", TRIH[0:cs, h, 0:cs], tri[0:cs, 0:cs], DA[0:cs, h:h + 1], None, ALU.mult, None, ["B_TRI", "B_TRIS", "B_SM"], [("B_TRIH", h)])
            self.mm(pl[0:cs, h * 128:h * 128 + cs], ONESF[0:cs, 0:cs], TRIH[0:cs, h, 0:cs], True, False, ["B_ONESF", ("B_TRIH", h)], [kl])
            self.mm(pl[0:cs, h * 128:h * 128 + cs], self.IDF[0:cs, 0:cs], negm[0:cs, 0:cs], False, True, ["IDF", "B_NEGM", "B_NEGMS"], [kl])
        for h in range(4):
            self.act(LT[0:cs, h, 0:cs], pl[0:cs, h * 128:h * 128 + cs], AF.Exp, [kl, "B_SM"], [("B_LT", h)], bias=NACS[0:cs, h:h + 1], scale=1.0)
        if STG_ <= 5:
            return
        pc, kc_ = self.bank()
        self.mm(pc[0:cs, 0:cs], BCb[:, 0, tc:tc + cs], BCb[:, 1, tc:tc + cs], True, True, ["B_BCb"], [kc_])
        for h in range(4):
            self.tt("dve", MT[0:cs, h, 0:cs], LT[0:cs, h, 0:cs], pc[0:cs, 0:cs], ALU.mult, [kc_, ("B_LT", h)], [("B_MT", h)])
        if STG_ <= 6:
            return
        self.tt("dve", v4(XDT[0:cs, :]), v4(XT[0:cs, :]), bc(DT_[0:cs], cs), ALU.mult, ["B_XT", "B_SM"], ["B_XDT"])
        self.tt("dve", v4(XDD[0:cs, :]), v4(XDT[0:cs, :]), bc(DEND[0:cs], cs), ALU.mult, ["B_XDT", "B_SM"], ["B_XDD"])
        if STG_ <= 7:
            return
        if sample:
            self.act(SZ[0:cs, :], pzd[0:cs, 0:256], AF.Silu, [kzd], ["B_SZ"])
            self.memset("dve", YO[:], 0.0, ["B_YO"])
            for sq in range(NSQ):
                state_in(g, sq)
                po, ko = self.bank()
                self.mm(po[:, 0:256], BCb[:, 1, 0:128], STb[:], True, True, ["B_BCb", "B_STb"], [ko])
                self.stt(YO[:], po[:, 0:256], SEQM[:, sq:sq + 1], YO[:], ALU.mult, ALU.add, [ko, "B_SEQM", "B_YO"], ["B_YO"])
                self.ts("dve", BTM[:], BT[:], SEQM[:, sq:sq + 1], None, ALU.mult, None, ["B_BT", "B_SEQM"], ["B_BTM"])
                pst, kst = self.bank()
                self.mm(pst[:, 0:256], BTM[:], XDD[:], True, True, ["B_BTM", "B_XDD"], [kst])
                self.tt("dve", v4(ST[:]), v4(ST[:]), bc(CDS[:, 4 * sq:4 * sq + 4], 128), ALU.mult, ["B_ST", "B_CDS"], ["B_ST"])
                self.tt("dve", ST[:], ST[:], pst[:, 0:256], ALU.add, ["B_ST", kst], ["B_ST"])
                state_out(g, self.O["ss_s"][sq, 4 * g:4 * g + 4])
        py, ky = self.bank()
        for h in range(4):
            self.mm(py[0:cs, h * 64:(h + 1) * 64], MT[0:cs, h, 0:cs], XDT[0:cs, h * 64:(h + 1) * 64], True, True, [("B_MT", h), "B_XDT"], [ky])
        if not sample:
            po, ko = self.bank()
            self.mm(po[0:cs, 0:256], BCb[:, 1, tc:tc + cs], STb[:], True, True, ["B_BCb", "B_STb"], [ko])
            self.tt("dve", v4(Y1[0:cs, :]), v4(po[0:cs, 0:256]), bc(EACS[0:cs], cs), ALU.mult, [ko, "B_SM"], ["B_Y1"])
        else:
            self.tt("dve", v4(Y1[:]), v4(YO[:]), bc(EACS[:], 128), ALU.mult, ["B_YO", "B_SM"], ["B_Y1"])
        self.tt("dve", Y1[0:cs, :], Y1[0:cs, :], py[0:cs, 0:256], ALU.add, [ky, "B_Y1"], ["B_Y1"])
        self.tt("pool", v4(T2[0:cs, :]), v4(XT[0:cs, :]), bc(DB[0:cs, hs], cs), ALU.mult, ["B_XT", "B_DB"], ["B_T2"])
        self.tt("dve", Y1[0:cs, :], Y1[0:cs, :], T2[0:cs, :], ALU.add, ["B_T2", "B_Y1"], ["B_Y1"])
        if STG_ <= 8:
            return
        if not sample:
            pst, kst = self.bank()
            self.mm(pst[:, 0:256], BT[0:cs, :], XDD[0:cs, :], True, True, ["B_BT", "B_XDD"], [kst])
            self.tt("dve", v4(ST[:]), v4(ST[:]), bc(CD, 128), ALU.mult, ["B_ST", "B_SM"], ["B_ST"])
            self.tt("dve", ST[:], ST[:], pst[:, 0:256], ALU.add, ["B_ST", kst], ["B_ST"])
            self.cp("act", STb[:], ST[:], ["B_ST"], ["B_STb"])
        if STG_ <= 9:
            return
        if not sample:
            self.act(SZ[0:cs, :], pzd[0:cs, 0:256], AF.Silu, [kzd], ["B_SZ"])
        self.tt("dve", Y1[0:cs, :], Y1[0:cs, :], SZ[0:cs, :], ALU.mult, ["B_SZ", "B_Y1"], ["B_Y1"])
        self.P.op("dve", lambda e: e.scalar_tensor_tensor(out=T2[0:cs, :], in0=Y1[0:cs, :], scalar=1.0, in1=Y1[0:cs, :], op0=ALU.mult,
                                                          op1=ALU.mult, accum_out=MS[0:cs]), ["B_Y1", "B_T2"], ["B_T2", "B_SM"])
        self.act(MS[0:cs], MS[0:cs], AF.Ln, ["B_SM", ("C1", 2)], ["B_SM"], scale=1.0 / 256, bias=self.C1[0:cs, 2:3])
        self.act(MS[0:cs], MS[0:cs], AF.Exp, ["B_SM"], ["B_SM"], scale=-0.5)
        self.ts("dve", Y1[0:cs, :], Y1[0:cs, :], MS[0:cs], None, ALU.mult, None, ["B_SM", "B_Y1"], ["B_Y1"])
        pg, kg = self.bank()
        for hf in range(2):
            self.tr(pg[:, hf * 128:hf * 128 + cs], Y1[0:cs, hf * 128:(hf + 1) * 128], self.IDF[0:cs, 0:cs], ["B_Y1", "IDF"], [kg])
        for hf in range(2):
            ch = 2 * g + hf
            self.ts("dve", YGT[:, hf, tc:tc + cs], pg[:, hf * 128:hf * 128 + cs], self.pf("ssm_norm_w", ch // 8, ch % 8), None, ALU.mult, None,
                    [kg, "PF"], ["B_YGT"])

    def state_in(g, s):
        self.dma("sp", SIN[:], I["st_ssm"][ib, s, 4 * g:4 * g + 4].rearrange("(a h) p n -> (h p) a n", a=2), (), ["B_SIN"])
        ps, pk = self.bank()
        for a in range(2):
            self.tr(ps[:, a * 128:(a + 1) * 128], SIN[:, a, :], self.IDF[:], ["B_SIN", "IDF"], [pk])
        self.cp("dve", ST[:], ps[:, 0:256], [pk], ["B_ST"])
        self.cp("act", STb[:], ps[:, 0:256], [pk], ["B_STb"])

    def state_out(g, dst):
        ps, pk = self.bank()
        for a in range(2):
            self.tr(ps[:, a * 128:(a + 1) * 128], ST[:, a * 128:(a + 1) * 128], self.IDF[:], ["B_ST", "IDF"], [pk])
        self.cp("dve", SOUT[:].rearrange("p a n -> p (a n)"), ps[:, 0:256], [pk], ["B_SOUT"])
        self.dma("sp", dst.rearrange("(a h) p n -> (h p) a n", a=2), SOUT[:], ["B_SOUT"], ())

    import os as _os
    NG = int(_os.environ.get('BDBG_G', '8')); TLIST = [int(c) for c in _os.environ.get('BDBG_T', '01234')]; NSQ = int(_os.environ.get('BDBG_S', '16'))
    load(0)
    load_x(0)
    load_o(0)
    for g in range(NG):
        b = g % 2
        chs = [2 * g, 2 * g + 1, 16 + g, 24 + g]
        for ti, (c0, w) in enumerate(TT):
            if ti not in TLIST:
                continue
            if ti == 0:
                self.memset("dve", XF[:, :, 0:3], 0.0, [("B_XF", "st")])
            elif ti < 4:
                self.cp("dve", XF[:, :, 0:3], XF[:, :, 512:515], ["B_XF"], [("B_XF", "st")])
            else:
                for q in range(4):
                    self.cp("dve", XFSv[q][:, :, 0:3], CS0[:, chs[q], :].rearrange("p (s k) -> p s k", k=3), ["B_CS0"], [("B_XFS", "st")])
            for q in range(4):
                ps, pk = self.bank()
                for kc in range(8):
                    self.mm(self.v3(ps[:, :w], ti), WXBC[b][:, kc, q * 128:(q + 1) * 128], self.uv(kc, ti), kc == 0, kc == 7,
                            ["B_WXBC", ("U", (kc, ti))], [pk])
                if ti < 4:
                    self.cp("act", XF[:, q, 3:515], ps[:, :], [pk], [("B_XF", q)])
                else:
                    self.cp("act", XFSv[q][:, :, 3:11], self.v3(ps[:, :w], ti), [pk], [("B_XFS", q)])
            if ti == TLIST[-1] and g + 1 < NG:
                load_x(g + 1)
            for q in range(4):
                ch = chs[q]
                cw = [self.pf("ssm_conv_w", k * 4 + ch // 8, ch % 8) for k in range(4)]
                cb = self.pf("ssm_conv_b", ch // 8, ch % 8)
                if ti < 4:
                    dst = XCf[:, q, :]
                    srcf = lambda k, q=q: XF[:, q, k:k + 512]
                    rk = "B_XF"
                else:
                    dst = XCf[:, q, 0:128].rearrange("p (s t) -> p s t", t=8)
                    srcf = lambda k, q=q: XFSv[q][:, :, k:k + 8]
                    rk = "B_XFS"
                self.ts("dve", dst, srcf(0), cw[0], cb, ALU.mult, ALU.add, [rk, "PF"], [("B_XCf", q)])
                for k in range(1, 4):
                    self.stt(dst, srcf(k), cw[k], dst, ALU.mult, ALU.add, [rk, ("B_XCf", q), "PF"], [("B_XCf", q)])
                self.act(XCf[:, q, :w], XCf[:, q, :w], AF.Silu, [("B_XCf", q)], [("B_XCf", q)])
                if q >= 2:
                    self.cp("pool", BCb[:, q - 2, :w], XCf[:, q, :w], [("B_XCf", q)], ["B_BCb"])
            if ti == 3:
                for q in range(4):
                    self.cp("pool", CST[:, chs[q], 0:3], XF[:, q, 512:515], ["B_XF"], ["B_CST"])
            if ti == 4:
                for q in range(4):
                    self.cp("pool", CST[:, chs[q], 3:].rearrange("p (s k) -> p s k", k=3), XFSv[q][:, :, 8:11], ["B_XFS"], ["B_CST"])
            if ti < 4:
                if ti == 0:
                    self.memset("dve", ST[:], 0.0, ["B_ST"])
                    self.memset("pool", STb[:], 0.0, ["B_STb"])
                for ck in range(4):
                    chunk(g, b, ck * 128, 128, 1 + c0 + ck * 128)
                if ti == 3:
                    state_out(g, self.O["ss_p"][4 * g:4 * g + 4])
            else:
                chunk(g, b, 0, 128, 0, sample=True)
            for n in range(8):
                ps, pk = self.bank()
                for hf in range(2):
                    self.mm(ps[:, :w], WOUT[b][:, hf, n * 128:(n + 1) * 128], YGT[:, hf, :w], hf == 0, hf == 1, ["B_WOUT", "B_YGT"], [pk])
                self.tt("dve", self.X[:, n, c0:c0 + w], self.X[:, n, c0:c0 + w], ps[:, :w], ALU.add, [pk, ("X", (n, ti))], [("X", (n, ti))])
        if g + 1 < NG:
            load_o(g + 1)
            load(g + 1)
    pass
    for r in range(4):
        self.fm_to_rows(CST[:, r * 8:(r + 1) * 8, 0:3], "B_CST", 3, self.O["sc_p"][:, r * D:(r + 1) * D])
        self.fm_to_rows(CST[:, r * 8:(r + 1) * 8, 3:], "B_CST", NS * 3, self.O["sc_s"].rearrange("s k d -> (s k) d")[:, r * D:(r + 1) * D])


KB.layer_B = layer_B


def layer_C(self, l, ic):
    I, nc = self.I, self.nc
    E05 = float(np.exp(-0.5))
    SH0 = self.sb("C_SH0", [128, 8, NS]); USH = self.sb("C_USH", [128, 8, 1 + NS])
    self.rows_to_fm(I["st_rw_shift"][ic], NS, SH0, "C_SH0")
    Us = self.U[:, :, 1 + NTP:UC].rearrange("p c (s t) -> p c s t", t=9)
    self.cp("dve", Us[:, :, :, 0], SH0[:], ["C_SH0"], [("U", "shifts")])
    self.norm_to_U("norm_mix", l, shift_out=USH)
    XD = nc.dram_tensor("c_xspill", [128, 8, NT], F32).ap()
    import os as _os2
    CHUNKED = bool(int(_os2.environ.get("C_CHUNKED", "1")))
    HM = () if CHUNKED else ("kk", "w", "nq", "k", "r")
    SCR = {k: nc.dram_tensor("c_scr_" + k, [NT, D], F32).ap() for k in ("kk", "w", "nq", "k", "r", "v", "g", "o") if k not in HM}
    SCRH = {k: nc.dram_tensor("c_scrh_" + k, [16, NT, 64], F32).ap() for k in HM}

    def scr_tm(nm, rows):
        if nm in SCRH:
            return SCRH[nm][:, rows, :].rearrange("h t k -> t h k")
        return SCR[nm][rows, :].rearrange("t (h k) -> t h k", k=64)

    self.P.barrier()
    self.dma("sp", XD, self.X[:], ["X"], ["XD"])
    self.P.barrier()

    def xa(i):
        return self.X[:, i // 2, (i % 2) * D:(i % 2 + 1) * D], ("X", "a%d" % i)

    def h16(ap):
        return ap.rearrange("p (h k) -> p h k", k=64)

    def b16(ap):
        return ap.unsqueeze(2).to_broadcast([128, 16, 64])

    def red(out, in_, reads, writes):
        self.P.op("dve", lambda e: e.tensor_reduce(out=out, in_=in_, axis=AX.X, op=ALU.add), reads, writes)

    def pbc(i, src):
        a, k = xa(i)
        self.dma("sp", a, src.partition_broadcast(128), (), [k])

    with self.scope():
        pbc(0, I["rwkv_w0"][ic:ic + 1, :]); pbc(1, I["rwkv_a0"][ic:ic + 1, :]); pbc(2, I["rwkv_k_k"][ic:ic + 1, :]); pbc(3, I["rwkv_k_a"][ic:ic + 1, :])
        WS = [self.sb("C_WS%d" % i, [128, 8, D], BF16) for i in range(3)]
        for s_ in range(3):
            self.dma("pool", WS[s_][:], I["rwkv_w_rkv"][ic, s_].rearrange("(kc p) n -> p kc n", p=128), (), [("C_WS", s_)])
        W1 = self.sb("C_W1", [128, 8, 256], BF16)
        W2w = self.sb("C_W2w", [64, D], BF16); W2a = self.sb("C_W2a", [64, D], BF16); W2g = self.sb("C_W2g", [128, D], BF16)
        for (c0, n, nm) in ((0, 64, "rwkv_w_w1"), (64, 64, "rwkv_w_a1"), (128, 128, "rwkv_w_g1")):
            self.dma("pool", W1[:, :, c0:c0 + n], I[nm][ic].rearrange("(kc p) n -> p kc n", p=128), (), ["C_W1"])
        self.dma("pool", W2w[:], I["rwkv_w_w2"][ic], (), ["C_W2w"]); self.dma("pool", W2a[:], I["rwkv_w_a2"][ic], (), ["C_W2a"])
        self.dma("pool", W2g[:], I["rwkv_w_g2"][ic], (), ["C_W2g"])
        Dd = self.sb("C_D", [128, 8, 128]); XM = [self.sb("C_XM%d" % i, [128, 8, 128], BF16) for i in range(2)]
        TW = self.sb("C_TW", [64, 128], BF16); TA = self.sb("C_TA", [64, 128], BF16); TG = self.sb("C_TG", [128, 128], BF16)
        SS = self.sb("C_SS", [128, 16])
        (Pw0, kw0), (Pa0, ka0), (Pkk, kkk), (Pka, kka) = xa(0), xa(1), xa(2), xa(3)
        (Rt, kR), (Kt, kK), (Vt, kV), (Wt, kW), (At, kA), (KKt, kKK), (Gt, kG), (T1, kT1), (T2, kT2) = [xa(i) for i in range(7, 16)]
        wsn = 0
        for ti in range(17):
            tok0 = 128 * ti
            if ti < 16:
                cur = self.U[:, :, 1 + tok0:1 + tok0 + 128]; prev = self.U[:, :, tok0:tok0 + 128]
                dv = lambda a: a
                ukeys = [("U", (c, ti // 4)) for c in range(8)]
            else:
                cur = Us[:, :, :, 1:9]; prev = Us[:, :, :, 0:8]
                dv = lambda a: a.rearrange("p c (s t) -> p c s t", t=8) if len(a.shape) == 3 else a.rearrange("p (s t) -> p s t", t=8)
                ukeys = [("U", (c, 4)) for c in range(8)] + [("U", "shifts")]
            if ti % 4 == 0 and ti > 0 and ti < 16:
                ukeys = ukeys + [("U", (c, ti // 4 - 1)) for c in range(8)]
            if ti == 0:
                ukeys = ukeys + [("U", "shiftp")]
            self.tt("dve", dv(Dd[:]), prev, cur, ALU.subtract, ukeys, ["C_D"])

            def mix(s):
                xm = XM[s % 2]; key = "C_XM%d" % (s % 2)
                for kc in range(8):
                    self.stt(dv(xm[:, kc, :]), dv(Dd[:, kc, :]), self.pf("rwkv_mu", s, kc), cur[:, kc], ALU.mult, ALU.add, ["C_D", "PF"] + ukeys, [key])
                return xm, key

            def proj_tm(s, dst, dkey):
                b = s
                xm, key = mix(s)
                for nb in range(2):
                    ps, pk = self.bank()
                    for kc in range(8):
                        self.mm(ps[:, :], xm[:, kc, :], WS[b][:, kc, nb * 512:(nb + 1) * 512], kc == 0, kc == 7, [key, ("C_WS", b)], [pk])
                    self.cp("act", dst[:, nb * 512:(nb + 1) * 512], ps[:, :], [pk], [dkey])

            proj_tm(0, Rt, kR); proj_tm(1, Kt, kK); proj_tm(2, Vt, kV)
            for (s, c0, n, fn, dst, dk) in ((3, 0, 64, AF.Tanh, TW, "C_TW"), (4, 64, 64, AF.Copy, TA, "C_TA"), (5, 128, 128, AF.Sigmoid, TG, "C_TG")):
                xm, key = mix(s)
                ps, pk = self.bank()
                for kc in range(8):
                    self.mm(ps[0:n, 0:128], W1[:, kc, c0:c0 + n], xm[:, kc, :], kc == 0, kc == 7, [key, "C_W1"], [pk])
                self.act(dst[0:n, :], ps[0:n, 0:128], fn, [pk], [dk])
            for nb in range(2):
                cs_ = slice(nb * 512, (nb + 1) * 512)
                ps, pk = self.bank()
                self.mm(ps[:, :], TW[0:64, :], W2w[0:64, cs_], True, True, ["C_TW", "C_W2w"], [pk])
                self.tt("dve", T1[:, cs_], ps[:, :], Pw0[:, cs_], ALU.add, [pk, kw0], [kT1])
                ps, pk = self.bank()
                self.mm(ps[:, :], TA[0:64, :], W2a[0:64, cs_], True, True, ["C_TA", "C_W2a"], [pk])
                self.tt("dve", At[:, cs_], ps[:, :], Pa0[:, cs_], ALU.add, [pk, ka0], [kA])
                ps, pk = self.bank()
                self.mm(ps[:, :], TG[:, :], W2g[:, cs_], True, True, ["C_TG", "C_W2g"], [pk])
                self.cp("act", Gt[:, cs_], ps[:, :], [pk], [kG])
            self.act(T1, T1, AF.Sigmoid, [kT1], [kT1])
            self.act(Wt, T1, AF.Exp, [kT1], [kW], scale=-E05)
            self.act(At, At, AF.Sigmoid, [kA], [kA])
            self.tt("dve", KKt, Kt, Pkk, ALU.mult, [kK, kkk], [kKK])
            self.tt("dve", T1, KKt, KKt, ALU.mult, [kKK], [kT1])
            red(SS[:], h16(T1), [kT1], ["C_SS"])
            self.act(SS[:], SS[:], AF.Sqrt, ["C_SS"], ["C_SS"])
            self.ts("dve", SS[:], SS[:], 1e-12, None, ALU.max, None, ["C_SS"], ["C_SS"])
            self.P.op("dve", lambda e, SS=SS: e.reciprocal(out=SS[:], in_=SS[:]), ["C_SS"], ["C_SS"])
            self.tt("dve", h16(KKt), h16(KKt), b16(SS[:]), ALU.mult, [kKK, "C_SS"], [kKK])
            self.stt(T1, At, -1.0, Pka, ALU.add, ALU.mult, [kA, kka], [kT1])
            self.ts("dve", T1, T1, 1.0, None, ALU.add, None, [kT1], [kT1])
            self.tt("dve", Kt, Kt, T1, ALU.mult, [kK, kT1], [kK])
            self.stt(T2, KKt, -1.0, At, ALU.mult, ALU.mult, [kKK, kA], [kT2])
            rows = slice(tok0, tok0 + 128)
            for (nm, a, k) in (("kk", KKt, kKK), ("w", Wt, kW), ("nq", T2, kT2), ("k", Kt, kK), ("r", Rt, kR), ("v", Vt, kV), ("g", Gt, kG)):
                self.dma("sp", scr_tm(nm, rows), h16(a), [k], [("SCR", nm)])

    if self.debug:
        for i_, nm_ in enumerate(("kk", "w", "nq", "k", "r", "v", "g")):
            self.dma("sp", self.O["dbg_scr"][i_].rearrange("t (h k) -> t h k", k=64), scr_tm(nm_, slice(0, NT)), [("SCR", nm_)], ())

    def phase2_chunked():
        LNE = float(np.log(1.0))
        f32 = F32
        def blockmask(ap3, blk):
            self.P.op("pool", lambda e: e.affine_select(out=ap3, in_=ap3, pattern=[[-blk, 128 // blk], [0, blk]], compare_op=ALU.is_ge, fill=0.0,
                                                        base=0, channel_multiplier=1), ["C_MK"], ["C_MK"])
            self.P.op("pool", lambda e: e.affine_select(out=ap3, in_=ap3, pattern=[[blk, 128 // blk], [0, blk]], compare_op=ALU.is_ge, fill=0.0,
                                                        base=blk - 1, channel_multiplier=-1), ["C_MK"], ["C_MK"])
        MK = {}
        for blk in (32, 8):
            TRIc = self.sb("C_TRIc%d" % blk, [128, 128]); LOWs = self.sb("C_LOWs%d" % blk, [128, 128]); M3 = self.sb("C_M3_%d" % blk, [128, 3, 128])
            MSs = self.sb("C_MSs%d" % blk, [128, 128])
            self.memset("pool", TRIc[:], 1.0, ["C_MK"])
            self.P.op("pool", lambda e, T=TRIc: e.affine_select(out=T[:], in_=T[:], pattern=[[1, 128]], compare_op=ALU.is_ge, fill=0.0, base=0,
                                                                channel_multiplier=-1), ["C_MK"], ["C_MK"])
            blockmask(TRIc[:].rearrange("p (b t) -> p b t", t=blk), blk)
            self.memset("pool", LOWs[:], 1.0, ["C_MK"])
            self.P.op("pool", lambda e, T=LOWs: e.affine_select(out=T[:], in_=T[:], pattern=[[-1, 128]], compare_op=ALU.is_ge, fill=0.0, base=-1,
                                                                channel_multiplier=1), ["C_MK"], ["C_MK"])
            blockmask(LOWs[:].rearrange("p (b t) -> p b t", t=blk), blk)
            self.tt("pool", MSs[:], TRIc[:], self.IDF[:], ALU.subtract, ["C_MK", "IDF"], ["C_MK"])
            self.cp("pool", M3[:, 0, :], TRIc[:], ["C_MK"], ["C_MK"])
            self.cp("pool", M3[:, 1, :], MSs[:], ["C_MK"], ["C_MK"])
            self.cp("pool", M3[:, 2, :], TRIc[:], ["C_MK"], ["C_MK"])
            MK[blk] = (TRIc, LOWs, MSs, M3)
        SEQM = self.sb("C_SEQM", [128, 16])
        self.memset("pool", SEQM[:], 1.0, ["C_MK"])
        self.P.op("pool", lambda e: e.affine_select(out=SEQM[:], in_=SEQM[:], pattern=[[-8, 16]], compare_op=ALU.is_ge, fill=0.0, base=0,
                                                    channel_multiplier=1), ["C_MK"], ["C_MK"])
        self.P.op("pool", lambda e: e.affine_select(out=SEQM[:], in_=SEQM[:], pattern=[[8, 16]], compare_op=ALU.is_ge, fill=0.0, base=7,
                                                    channel_multiplier=-1), ["C_MK"], ["C_MK"])
        CHM = self.sb("C_CHM", [128, 4])
        self.memset("pool", CHM[:], 1.0, ["C_MK"])
        self.P.op("pool", lambda e: e.affine_select(out=CHM[:], in_=CHM[:], pattern=[[-32, 4]], compare_op=ALU.is_ge, fill=0.0, base=0,
                                                    channel_multiplier=1), ["C_MK"], ["C_MK"])
        self.P.op("pool", lambda e: e.affine_select(out=CHM[:], in_=CHM[:], pattern=[[32, 4]], compare_op=ALU.is_ge, fill=0.0, base=31,
                                                    channel_multiplier=-1), ["C_MK"], ["C_MK"])
        PRT = self.sb("C_PRT", [128, 8, 2, 128], BF16); QT = self.sb("C_QT", [128, 8, 128], BF16); KT = self.sb("C_KT", [128, 8, 128], BF16)
        QZ = self.sb("C_QZ", [128, 16, 128], BF16); KZ = self.sb("C_KZ", [128, 16, 128], BF16)
        Vb = self.sb("C_Vb", [128, D], BF16); Vm = self.sb("C_Vm", [128, D], BF16)
        ECLE = self.sb("C_ECLE", [128, 8, 16])
        XN = self.sb("C_XN", [128, 8, 128]); ZN = self.sb("C_ZN", [128, 8, 128]); TTf = self.sb("C_TTf", [128, 8, 128])
        MB = self.sb("C_MB", [128, 16, 3, 128], BF16); TTb = self.sb("C_TTb", [128, 16, 128], BF16)
        Wsb = self.sb("C_Wsb", [128, D]); O0 = self.sb("C_O0", [128, D])
        GWb = self.sb("C_GWb", [128, D], BF16); Ub = self.sb("C_Ub", [128, D], BF16)
        S = self.sb("C_S2", [128, 8, 64]); Sb = self.sb("C_Sb", [128, 16, 64], BF16)
        SbV = Sb[:].rearrange("p (j e) v -> p j e v", e=2)

        def write_sb(src3, keys):
            self.cp("act", SbV[0:64, :, 0, :], src3[0:64], keys, ["C_Sb"])
            self.cp("act", SbV[64:128, :, 1, :], src3[64:128], keys, ["C_Sb"])

        NAT = self.sb("C_NAT", [64, 16, 64])
        self.memset("pool", QZ[:], 0.0, ["C_QZ"]); self.memset("pool", KZ[:], 0.0, ["C_KZ"])
        (KKt, kKK), (NQt, kNQ), (Kt, kK), (Rt, kR), (Vt, kV), (LW, kLW), (CL, kCL), (E1, kE1), (Ot, kO) = [xa(i) for i in range(0, 9)]

        def state_load(dram):
            self.dma("sp", NAT[:], dram.rearrange("h v k -> v h k"), (), ["C_NAT"])
            ps, pk = self.bank()
            for j in range(8):
                self.tr(ps[:, j * 64:(j + 1) * 64], NAT[:, 2 * j:2 * j + 2, :].rearrange("v h k -> v (h k)"), self.IDF[0:64, 0:64], ["C_NAT", "IDF"], [pk])
            self.cp("dve", S[:].rearrange("p j v -> p (j v)"), ps[:, :], [pk], ["C_S2"])
            write_sb(ps[:, :].rearrange("p (j v) -> p j v", v=64), [pk])

        def state_store(dram):
            for half in range(2):
                ps, pk = self.bank()
                for jj in range(4):
                    j = half * 4 + jj
                    self.tr(ps[0:64, jj * 128:(jj + 1) * 128], S[:, j, :], self.IDF[:], ["C_S2", "IDF"], [pk])
                self.cp("dve", NAT[:, half * 8:(half + 1) * 8, :].rearrange("v h k -> v (h k)"), ps[0:64, :], [pk], ["C_NAT"])
            self.dma("act", dram.rearrange("h v k -> v h k"), NAT[:], ["C_NAT"], ())

        def tile(ti, blk, seqs):
            TRIc, LOWs, MSs, M3 = MK[blk]
            STOP = int(_os2.environ.get('C2_STOP', '99'))
            nch = 128 // blk
            rows = slice(128 * ti, 128 * ti + 128)
            for (nm, a, k) in (("kk", KKt, kKK), ("nq", NQt, kNQ), ("k", Kt, kK), ("r", Rt, kR), ("v", Vt, kV), ("w", LW, kLW)):
                self.dma("sp", h16(a), scr_tm(nm, rows), [("SCR", nm)], [k])
            self.act(LW, LW, AF.Ln, [kLW], [kLW])
            for nb in range(2):
                ps, pk = self.bank()
                self.mm(ps[:, :], TRIc[:], LW[:, nb * 512:(nb + 1) * 512], True, True, ["C_MK", kLW], [pk])
                self.cp("act", CL[:, nb * 512:(nb + 1) * 512], ps[:, :], [pk], [kCL])
            self.tt("dve", E1, CL, LW, ALU.subtract, [kCL, kLW], [kE1])
            self.act(E1, E1, AF.Exp, [kE1], [kE1])
            self.tt("dve", KKt, KKt, E1, ALU.mult, [kKK, kE1], [kKK])
            self.act(E1, CL, AF.Exp, [kCL], [kE1], scale=-1.0)
            self.tt("dve", NQt, NQt, E1, ALU.mult, [kNQ, kE1], [kNQ])
            self.tt("dve", Kt, Kt, E1, ALU.mult, [kK, kE1], [kK])
            self.act(E1, CL, AF.Exp, [kCL], [kE1])
            self.tt("dve", Rt, Rt, E1, ALU.mult, [kR, kE1], [kR])
            self.cp("pool", Vb[:], Vt, [kV], ["C_Vb"])
            for (Zb, src, k, zk) in ((QZ, NQt, kNQ, "C_QZ"), (KZ, Kt, kK, "C_KZ")):
                zv = Zb[:].rearrange("p (j e) f -> p j (e f)", e=2)
                sv = src.rearrange("p (j e d) -> p j e d", e=2, d=64)
                self.cp("pool", zv[:, :, 0:64], sv[:, :, 0, :], [k], [zk])
                self.cp("pool", zv[:, :, 192:256], sv[:, :, 1, :], [k], [zk])
            if STOP <= 1:
                return
            for (src, k, dstf, dk) in ((KKt, kKK, lambda j: PRT[:, j, 0, :], "C_PRT"), (Rt, kR, lambda j: PRT[:, j, 1, :], "C_PRT"),
                                       (NQt, kNQ, lambda j: QT[:, j, :], "C_QT"), (Kt, kK, lambda j: KT[:, j, :], "C_KT")):
                for half in range(2):
                    ps, pk = self.bank()
                    for jj in range(4):
                        j = half * 4 + jj
                        self.tr(ps[:, jj * 128:(jj + 1) * 128], src[:, j * 128:(j + 1) * 128], self.IDF[:], [k, "IDF"], [pk])
                    for jj in range(4):
                        j = half * 4 + jj
                        self.cp("act" if jj % 2 else "dve", dstf(j), ps[:, jj * 128:(jj + 1) * 128], [pk], [(dk, j)])
            for half in range(2):
                ps, pk = self.bank()
                for jj in range(4):
                    j = half * 4 + jj
                    self.tr(ps[:, jj * 128:(jj + 1) * 128], E1[:, j * 128:(j + 1) * 128], self.IDF[:], [kE1, "IDF"], [pk])
                self.cp("dve", ECLE[:, half * 4:(half + 1) * 4, 0:nch], ps[:, :].rearrange("p (a c b) -> p a c b", a=4, b=blk)[:, :, :, blk - 1], [pk], ["C_ECLE"])
            if STOP <= 2:
                return
            for hg in range(2):
                for hh in range(8):
                    h = hg * 8 + hh
                    j, e = h // 2, h % 2
                    pr = slice(e * 64, (e + 1) * 64)
                    ps, pk = self.bank()
                    rhsPR = PRT[pr, j, :, :].rearrange("p a t -> p (a t)")
                    self.mm(ps[:, 0:256], QT[pr, j, :], rhsPR, True, True, [("C_QT", j), ("C_PRT", j)], [pk])
                    self.mm(ps[:, 256:512], KT[pr, j, :], rhsPR, True, True, [("C_KT", j), ("C_PRT", j)], [pk])
                    ps2, pk2 = self.bank()
                    self.mm(ps2[:, 0:128], PRT[pr, j, 0, :], QT[pr, j, :], True, True, [("C_QT", j), ("C_PRT", j)], [pk2])
                    self.tt("dve", ZN[:, hh, :], ps[:, 0:128], MSs[:], ALU.mult, [pk, "C_MK"], [("C_ZN", hh)])
                    self.tt("dve", MB[:, h, :, :], ps[:, 128:512].rearrange("p (a t) -> p a t", a=3), M3[:], ALU.mult, [pk, "C_MK"], [("C_MB", h)])
                    self.tt("dve", XN[:, hh, :], ps2[:, 0:128], LOWs[:], ALU.mult, [pk2, "C_MK"], [("C_XN", hh)])
                    self.tt("pool", TTf[:, hh, :], ZN[:, hh, :], self.IDF[:], ALU.add, [("C_ZN", hh), "IDF"], [("C_TT", hh)])
                for n in range(4 if STOP > 3 else 0):
                    for hh in range(8):
                        ps, pk = self.bank()
                        self.mm(ps[:, 0:128], ZN[:, hh, :], XN[:, hh, :], True, True, [("C_ZN", hh), ("C_XN", hh)], [pk])
                        if n < 3:
                            self.mm(ps[:, 128:256], XN[:, hh, :], ZN[:, hh, :], True, True, [("C_ZN", hh), ("C_XN", hh)], [pk])
                        self.cp("act", XN[:, hh, :], ps[:, 0:128], [pk], [("C_XN", hh)])
                        if n < 3:
                            self.cp("act", ZN[:, hh, :], ps[:, 128:256], [pk], [("C_ZN", hh)])
                        ps2, pk2 = self.bank()
                        self.mm(ps2[:, 0:128], XN[:, hh, :], TTf[:, hh, :], True, True, [("C_XN", hh), ("C_TT", hh)], [pk2])
                        self.tt("dve", TTf[:, hh, :], TTf[:, hh, :], ps2[:, 0:128], ALU.add, [pk2, ("C_TT", hh)], [("C_TT", hh)])
                for hh in range(8):
                    self.cp("act", TTb[:, hg * 8 + hh, :], TTf[:, hh, :], [("C_TT", hh)], [("C_TTb", hg * 8 + hh)])
            if STOP <= 4:
                return
            for (ai, dst, dk) in ((1, Wsb, "C_Wsb"), (2, O0, "C_O0")):
                for half in range(2):
                    ps, pk = self.bank()
                    for hh in range(8):
                        h = half * 8 + hh
                        self.mm(ps[:, hh * 64:(hh + 1) * 64], MB[:, h, ai, :], Vb[:, h * 64:(h + 1) * 64], True, True, [("C_MB", h), "C_Vb"], [pk])
                    self.cp("act" if half else "dve", dst[:, half * 512:(half + 1) * 512], ps[:, :], [pk], [dk])
            self.cp("pool", Ot, O0[:], ["C_O0"], [kO])
            if STOP <= 5:
                return
            nrounds = nch if seqs is None else len(seqs)
            for c_ in range(nrounds):
                if seqs is None:
                    ecol = c_; mcol = CHM[:, c_:c_ + 1]
                else:
                    s_ = seqs[c_]
                    ecol = s_; mcol = SEQM[:, s_:s_ + 1]
                    state_load(I["st_rw_wkv"][ic, s_])
                self.ts("pool", Vm[:], Vb[:], mcol, None, ALU.mult, None, ["C_Vb", "C_MK"], ["C_Vm"])
                pG = [self.bank() for _ in range(2)]; pR = [self.bank() for _ in range(2)]
                for h in range(16):
                    j = h // 2
                    bq, cq = h // 8, (h % 8) * 64
                    self.mm(pG[bq][0][:, cq:cq + 64], PRT[:, j, 0, :], Sb[:, h, :], True, True, [("C_PRT", j), "C_Sb"], [pG[bq][1]])
                    self.mm(pR[bq][0][:, cq:cq + 64], PRT[:, j, 1, :], Sb[:, h, :], True, True, [("C_PRT", j), "C_Sb"], [pR[bq][1]])
                for bq in range(2):
                    cs_ = slice(bq * 512, (bq + 1) * 512)
                    self.tt("dve", GWb[:, cs_], pG[bq][0][:, :], Wsb[:, cs_], ALU.add, [pG[bq][1], "C_Wsb"], [("C_GWb", bq)])
                pU = [self.bank() for _ in range(2)]
                for h in range(16):
                    bq, cq = h // 8, (h % 8) * 64
                    self.mm(pU[bq][0][:, cq:cq + 64], TTb[:, h, :], GWb[:, h * 64:(h + 1) * 64], True, True, [("C_TTb", h), ("C_GWb", bq)], [pU[bq][1]])
                for bq in range(2):
                    cs_ = slice(bq * 512, (bq + 1) * 512)
                    self.act(Ub[:, cs_], pU[bq][0][:, :], AF.Copy, [pU[bq][1], "C_MK"], [("C_Ub", bq)], scale=mcol)
                    self.stt(Ot[:, cs_], pR[bq][0][:, :], mcol, Ot[:, cs_], ALU.mult, ALU.add, [pR[bq][1], kO, "C_MK"], [kO])
                pB = [self.bank() for _ in range(2)]
                for h in range(16):
                    bq, cq = h // 8, (h % 8) * 64
                    self.mm(pB[bq][0][:, cq:cq + 64], MB[:, h, 0, :], Ub[:, h * 64:(h + 1) * 64], True, True, [("C_MB", h), ("C_Ub", bq)], [pB[bq][1]])
                for bq in range(2):
                    cs_ = slice(bq * 512, (bq + 1) * 512)
                    self.stt(Ot[:, cs_], pB[bq][0][:, :], mcol, Ot[:, cs_], ALU.mult, ALU.add, [pB[bq][1], kO, "C_MK"], [kO])
                pS, kS = self.bank()
                for j in range(8):
                    for e in range(2):
                        h = 2 * j + e
                        self.mm(pS[:, j * 64:(j + 1) * 64], QZ[:, h, :], Ub[:, h * 64:(h + 1) * 64], e == 0, False, ["C_QZ", ("C_Ub", h // 8)], [kS])
                        self.mm(pS[:, j * 64:(j + 1) * 64], KZ[:, h, :], Vm[:, h * 64:(h + 1) * 64], False, e == 1, ["C_KZ", "C_Vm"], [kS])
                Sf = S[:].rearrange("p j v -> p (j v)")
                self.tt("dve", Sf, Sf, pS[:, :], ALU.add, [kS, "C_S2"], ["C_S2"])
                self.tt("dve", S[:], S[:], ECLE[:, :, ecol:ecol + 1].to_broadcast([128, 8, 64]), ALU.mult, ["C_S2", "C_ECLE"], ["C_S2"])
                write_sb(S, ["C_S2"])
                if seqs is not None:
                    state_store(self.O["rw_s"][s_])
            self.dma("act", scr_tm("o", rows), h16(Ot), [kO], [("SCR", "o")])

        self.memset("dve", S[:], 0.0, ["C_S2"]); self.memset("pool", Sb[:], 0.0, ["C_Sb"])
        for ti in range(16):
            tile(ti, 32, None)
        state_store(self.O["rw_p"])
        tile(16, 8, list(range(NS)))

    if CHUNKED:
        with self.scope():
            phase2_chunked()
    else:
        with self.scope():
            S = self.sb("C_S", [128, 8, 64]); Tm = self.sb("C_Tm", [128, 8, 64]); KV = [self.sb("C_KV%d" % i, [128, 8, 64]) for i in range(2)]
            SA = self.sb("C_SA", [128, 8])
            names = ("kk", "w", "nq", "k", "r")
            ST0 = {nm: self.X[:, i, 0:2048].rearrange("p (t k) -> p t k", k=64) for i, nm in enumerate(names)}
            ST1 = {nm: self.sb("C_ST1" + nm, [128, 32, 64]) for nm in names}
            VS = [self.sb("C_VS%d" % i, [128, 32, 8]) for i in range(2)]
            OS = [self.sb("C_OS%d" % i, [128, 32, 8]) for i in range(2)]

            def bk(ap):
                return ap.unsqueeze(1).to_broadcast([128, 8, 64])

            def bv(ap):
                return ap.unsqueeze(2).to_broadcast([128, 8, 64])

            nsub = 0
            import os as _os
            NOSAME = bool(int(_os.environ.get("C_NOSAME", "1")))
            SP = self.PS[7][:, :].rearrange("p (a k) -> p a k", k=64)
            SK = ("PS", 7)

            def run_segment(row0, nsteps):
                nonlocal nsub
                for t0 in range(0, nsteps, 32):
                    n = min(32, nsteps - t0)
                    sb_ = nsub % 2; nsub += 1
                    stg = ST0 if sb_ == 0 else ST1
                    skey = "C_STG%d" % sb_
                    r0 = row0 + t0
                    for nm in names:
                        src = SCRH[nm][:, r0:r0 + n, :]
                        for vb in range(8):
                            self.dma("sp", stg[nm][vb * 16:(vb + 1) * 16, 0:n, :], src, [("SCR", nm)], [(skey, nm)] + ([("X", "stg")] if sb_ == 0 else []))
                    srcv = SCR["v"][r0:r0 + n, :].rearrange("t (h v) -> h t v", v=64)
                    for vb in range(8):
                        self.dma("sp", VS[sb_][vb * 16:(vb + 1) * 16, 0:n, :], srcv[:, :, vb * 8:(vb + 1) * 8], [("SCR", "v")], [(skey, "v")])
                    xk = [("X", "stg")] if sb_ == 0 else []
                    for i in range(n):
                        kvb = i % 2
                        self.tt("pool", KV[kvb][:], bk(stg["k"][:, i, :]), bv(VS[sb_][:, i, :]), ALU.mult, [(skey, "k"), (skey, "v")] + xk, ["C_KV%d" % kvb])
                        self.P.nosame = NOSAME
                        self.tt("dve", Tm[:], SP[:], bk(stg["kk"][:, i, :]), ALU.mult, [SK, (skey, "kk")] + xk, ["C_Tm"])
                        red(SA[:], Tm[:], ["C_Tm"], ["C_SA"])
                        self.tt("dve", SP[:], SP[:], bk(stg["w"][:, i, :]), ALU.mult, [SK, (skey, "w")] + xk, [SK])
                        self.tt("dve", Tm[:], bk(stg["nq"][:, i, :]), bv(SA[:]), ALU.mult, ["C_SA", (skey, "nq")] + xk, ["C_Tm"])
                        self.tt("dve", SP[:], SP[:], Tm[:], ALU.add, [SK, "C_Tm"], [SK])
                        self.tt("dve", SP[:], SP[:], KV[kvb][:], ALU.add, [SK, "C_KV%d" % kvb], [SK])
                        self.tt("dve", Tm[:], SP[:], bk(stg["r"][:, i, :]), ALU.mult, [SK, (skey, "r")] + xk, ["C_Tm"])
                        red(OS[sb_][:, i, :], Tm[:], ["C_Tm"], ["C_OS%d" % sb_])
                        self.P.nosame = False
                    dsto = SCR["o"][r0:r0 + n, :].rearrange("t (h v) -> h t v", v=64)
                    for vb in range(8):
                        self.dma("act", dsto[:, :, vb * 8:(vb + 1) * 8], OS[sb_][vb * 16:(vb + 1) * 16, 0:n, :], ["C_OS%d" % sb_], [("SCR", "o")])

            def state_io(dram, load):
                for vb in range(8):
                    d = dram[:, vb * 8:(vb + 1) * 8, :]
                    if load:
                        self.dma("sp", S[vb * 16:(vb + 1) * 16, :, :], d, (), ["C_S"])
                    else:
                        self.dma("act", d, S[vb * 16:(vb + 1) * 16, :, :], ["C_S"], ())

            self.memset("dve", S[:], 0.0, ["C_S"])
            self.cp("dve", SP[:], S[:], ["C_S"], [SK])
            run_segment(0, NTP)
            self.cp("dve", S[:], SP[:], [SK], ["C_S"])
            state_io(self.O["rw_p"], False)
            for s in range(NS):
                state_io(I["st_rw_wkv"][ic, s], True)
                self.cp("dve", SP[:], S[:], ["C_S"], [SK])
                run_segment(NTP + 8 * s, 8)
                self.cp("dve", S[:], SP[:], [SK], ["C_S"])
                state_io(self.O["rw_s"][s], False)

    if self.debug:
        self.dma("sp", self.O["dbg_scr"][7], SCR["o"], [("SCR", "o")], ())
    with self.scope():
        pbc(0, I["rwkv_lnx_w"][ic:ic + 1, :]); pbc(1, I["rwkv_lnx_b"][ic:ic + 1, :]); pbc(2, I["rwkv_r_k"].rearrange("a h n -> a (h n)")[ic:ic + 1, :])
        (Plw, klw), (Plb, klb), (Prk, krk) = xa(0), xa(1), xa(2)
        (Ot, kO), (Rt, kR), (Kt, kK), (Vt, kV), (Gt, kG), (T1, kT1) = [xa(i) for i in range(7, 13)]
        WO = self.sb("C_WO", [128, 8, D], BF16)
        self.dma("pool", WO[:], I["rwkv_w_out"][ic].rearrange("(kc p) n -> p kc n", p=128), (), ["C_WO"])
        YT = self.sb("C_YT", [128, 8, 128], BF16); XT_ = self.sb("C_XTL", [128, 8, 128])
        M1 = self.sb("C_M1", [128, 16]); M2 = self.sb("C_M2", [128, 16])
        for ti in range(17):
            rows = slice(128 * ti, 128 * ti + 128)
            for (nm, a, k) in (("o", Ot, kO), ("r", Rt, kR), ("k", Kt, kK), ("v", Vt, kV), ("g", Gt, kG)):
                self.dma("sp", h16(a), scr_tm(nm, rows), [("SCR", nm)], [k])
            self.dma("sp", XT_[:], XD[:, :, rows], ["XD"], ["C_XTL"])
            red(M1[:], h16(Ot), [kO], ["C_M1"])
            self.ts("dve", M1[:], M1[:], -1.0 / 64, None, ALU.mult, None, ["C_M1"], ["C_M1"])
            self.tt("dve", h16(Ot), h16(Ot), b16(M1[:]), ALU.add, [kO, "C_M1"], [kO])
            self.tt("dve", T1, Ot, Ot, ALU.mult, [kO], [kT1])
            red(M2[:], h16(T1), [kT1], ["C_M2"])
            self.act(M2[:], M2[:], AF.Ln, ["C_M2", ("C1", 3)], ["C_M2"], scale=1.0 / 64, bias=self.C1[:, 3:4])
            self.act(M2[:], M2[:], AF.Exp, ["C_M2"], ["C_M2"], scale=-0.5)
            self.tt("dve", h16(Ot), h16(Ot), b16(M2[:]), ALU.mult, [kO, "C_M2"], [kO])
            self.tt("dve", Ot, Ot, Plw, ALU.mult, [kO, klw], [kO])
            self.tt("dve", Ot, Ot, Plb, ALU.add, [kO, klb], [kO])
            self.tt("dve", T1, Rt, Kt, ALU.mult, [kR, kK], [kT1])
            self.tt("dve", T1, T1, Prk, ALU.mult, [kT1, krk], [kT1])
            red(M1[:], h16(T1), [kT1], ["C_M1"])
            self.tt("dve", h16(T1), h16(Vt), b16(M1[:]), ALU.mult, [kV, "C_M1"], [kT1])
            self.tt("dve", Ot, Ot, T1, ALU.add, [kO, kT1], [kO])
            self.tt("dve", Ot, Ot, Gt, ALU.mult, [kO, kG], [kO])
            for half in range(2):
                ps, pk = self.bank()
                for cc in range(4):
                    c = half * 4 + cc
                    self.tr(ps[:, cc * 128:(cc + 1) * 128], Ot[:, c * 128:(c + 1) * 128], self.IDF[:], [kO, "IDF"], [pk])
                self.cp("act", YT[:, half * 4:(half + 1) * 4, :].rearrange("p c t -> p (c t)"), ps[:, :], [pk], ["C_YT"])
            for n in range(8):
                ps, pk = self.bank()
                for c in range(8):
                    self.mm(ps[:, 0:128], WO[:, c, n * 128:(n + 1) * 128], YT[:, c, :], c == 0, c == 7, ["C_WO", "C_YT"], [pk])
                self.tt("dve", XT_[:, n, :], XT_[:, n, :], ps[:, 0:128], ALU.add, [pk, "C_XTL"], ["C_XTL"])
            self.dma("sp", XD[:, :, rows], XT_[:], ["C_XTL"], ["XD"])
    self.dma("sp", self.X[:], XD, ["XD"], ["X"])
    self.fm_to_rows(USH[:, :, 0:1], "C_USH", 1, self.O["rs_p"])
    self.fm_to_rows(USH[:, :, 1:], "C_USH", NS, self.O["rs_s"])


KB.layer_C = layer_C


def build(layers=(0, 1, 2, 3), debug=False, mlp=True):
    kb = KB(layers, debug)
    with kb.es:
        kb.setup()
        kb.load_x()
        for l in layers:
            with kb.scope():
                kind = l % 3
                if kind == 0:
                    kb.layer_A(l, l // 3)
                elif kind == 1:
                    kb.layer_B(l, 0)
                else:
                    kb.layer_C(l, 0)
            if mlp:
                with kb.scope():
                    kb.mlp(l)
        if not getattr(kb, '_skip_final', False):
            with kb.scope():
                kb.final_out()
        if debug:
            kb.dump_x()
        kb.P.emit()
    return kb.nc


def make_in_maps(inputs, cores):
    g = {k: np.ascontiguousarray(np.asarray(v, dtype=np.float32)) for k, v in inputs.items()}
    maps = []
    for i in cores:
        sl = slice(NS * i, NS * (i + 1))
        m = {
            "xp": g["x_prompt"][i], "xs": g["x_sample"][sl].reshape(NS * LS, D),
            "st_lru_conv": g["state_lru_conv"][:, sl], "st_lru_h": g["state_lru_h"][:, sl],
            "st_ssm_conv": g["state_ssm_conv"][:, sl], "st_ssm": g["state_ssm"][:, sl],
            "st_rw_shift": g["state_rwkv_shift"][:, sl], "st_rw_wkv": g["state_rwkv_wkv"][:, sl],
        }
        for k in IN_SPECS:
            if k in m:
                continue
            v = g[k]
            if k == "norm_final":
                v = v.reshape(1, D)
            m[k] = v
        maps.append({k: np.ascontiguousarray(v) for k, v in m.items()})
    return maps


def kernel(**inputs):
    nc = build()
    cores = list(range(8))
    res = run_bass_kernel_spmd(nc, make_in_maps(inputs, cores), core_ids=cores)
    r = res.results
    def cat(name, axis=0):
        return np.concatenate([x[name] for x in r], axis=axis)
    y_p = np.stack([x["y_p"] for x in r])
    y_s = cat("y_s").reshape(128, LS, D)
    lc_p = np.stack([x["lc_p"] for x in r], axis=1)
    lc_s = cat("lc_s", 1)
    lh_p = np.stack([x["lh_p"] for x in r], axis=1)
    lh_s = cat("lh_s", 1)
    sc_p = np.stack([x["sc_p"] for x in r])[None]
    sc_s = cat("sc_s")[None]
    ss_p = np.stack([x["ss_p"] for x in r])[None]
    ss_s = cat("ss_s")[None]
    rs_p = np.stack([x["rs_p"][0] for x in r])[None]
    rs_s = cat("rs_s")[None]
    rw_p = np.stack([x["rw_p"] for x in r])[None]
    rw_s = cat("rw_s")[None]
    outs = (y_p, y_s, lc_p, lc_s, lh_p, lh_s, sc_p, sc_s, ss_p, ss_s, rs_p, rs_s, rw_p, rw_s)
    return tuple(np.ascontiguousarray(o, dtype=np.float32) for o in outs)
```

```python
import numpy as np
import concourse.bass as bass
import concourse.mybir as mybir
from concourse.bass_utils import run_bass_kernel_spmd

F32 = mybir.dt.float32
BF16 = mybir.dt.bfloat16
I32 = mybir.dt.int32
AF = mybir.ActivationFunctionType
ALU = mybir.AluOpType
AX = mybir.AxisListType

ENGS = ("pe", "act", "dve", "pool", "sp")
SAME_ENGINE_SYNC = True
import os as _os_
FUSE_WAIT = bool(int(_os_.environ.get('FUSE_WAIT', '1')))
NOSAME_ENGS = tuple(x for x in _os_.environ.get('NOSAME_ENGS', '').split(',') if x)
N_DMA_SEMS = 12


class _St:
    __slots__ = ("w", "r")

    def __init__(self):
        self.w = None
        self.r = {}


class _Op:
    __slots__ = ("eng", "fn", "waits", "dma", "signal", "tok")


class Prog:
    def __init__(self, nc):
        self.nc = nc
        self.ops = {e: [] for e in ENGS}
        self.state = {}
        self.ndma = {e: 0 for e in ENGS}
        self.dma_toks = {}
        self.pending = {}
        self.nosame = False

    def barrier(self):
        toks = set()
        for e in ENGS:
            for o in reversed(self.ops[e]):
                if not o.dma:
                    toks.add(o.tok)
                    break
            n = self.ndma[e]
            for j in range(max(0, n - N_DMA_SEMS), n):
                toks.add(("dma", e, j))
        self.pending = {e: set(toks) for e in ENGS}

    def _states(self, key, create):
        if isinstance(key, tuple):
            buf, sub = key
        else:
            buf, sub = key, None
        d = self.state.setdefault(buf, {})
        if sub is None:
            if create and "*" not in d:
                d["*"] = _St()
            return list(d.values())
        out = []
        if "*" in d:
            out.append(d["*"])
        if sub not in d:
            if create:
                d[sub] = _St()
                out.append(d[sub])
        else:
            out.append(d[sub])
        return out

    def op(self, eng, fn, reads=(), writes=(), dma=False):
        ps_r = [k for k in reads if isinstance(k, tuple) and k[0] == "PS"]
        if ps_r:
            reads = [k for k in reads if not (isinstance(k, tuple) and k[0] == "PS")]
            writes = list(writes) + ps_r
        o = _Op()
        o.eng, o.fn, o.dma, o.signal = eng, fn, dma, False
        idx = len(self.ops[eng])
        if dma:
            j = self.ndma[eng]
            self.ndma[eng] += 1
            o.tok = ("dma", eng, j)
            self.dma_toks[(eng, j)] = o
        else:
            o.tok = ("eng", eng, idx)
        waits = set()
        for k in reads:
            for st in self._states(k, True):
                if st.w is not None:
                    waits.add(st.w)
        for k in writes:
            for st in self._states(k, True):
                if st.w is not None:
                    waits.add(st.w)
                for t in st.r.values():
                    waits.add(t)
        if self.pending.get(eng):
            waits |= self.pending.pop(eng)
        w2 = set()
        for t in waits:
            if t[0] == "eng" and t[1] == eng:
                if not dma and (eng in ("pe", "sp") or not SAME_ENGINE_SYNC or self.nosame or (NOSAME_ENGS and eng in NOSAME_ENGS)):
                    continue
            w2.add(t)
        o.waits = w2
        for k in reads:
            for st in self._states(k, True):
                st.r[o.tok if dma else o.tok[1]] = o.tok
        for k in writes:
            for st in self._states(k, True):
                st.w = o.tok
                st.r = {}
        self.ops[eng].append(o)
        return o

    def emit(self, final_wait_all=True):
        nc = self.nc
        tokmap = {}
        for e in ENGS:
            for i, o in enumerate(self.ops[e]):
                tokmap[o.tok] = o
        for e in ENGS:
            for o in self.ops[e]:
                for t in o.waits:
                    tokmap[t].signal = True
        import contextlib
        with contextlib.ExitStack() as es:
            EPOCH = 16000
            nsig = {e: sum(1 for o in self.ops[e] if o.signal and not o.dma) for e in ENGS}
            esem = {e: [es.enter_context(nc.semaphore("s_%s%d" % (e, i))) for i in range(nsig[e] // EPOCH + 1)] for e in ENGS if e != "sp"}
            dsem = {e: [es.enter_context(nc.semaphore("d_%s%d" % (e, i))) for i in range(N_DMA_SEMS)]
                    for e in ENGS if self.ndma[e] > 0}
            val = {}
            for e in ENGS:
                c = 0
                for o in self.ops[e]:
                    if o.dma:
                        j = o.tok[2]
                        val[o.tok] = (dsem[e][j % N_DMA_SEMS], 16 * (j // N_DMA_SEMS + 1))
                    elif o.signal:
                        val[o.tok] = (esem[e][c // EPOCH], c % EPOCH + 1)
                        c += 1
            self.maxcount = {}
            block = es.enter_context(nc.Block())

            def run(e, eng):
                seen = {}
                for o in self.ops[e]:
                    ws = []
                    for t in o.waits:
                        ws.append(val[t])
                    if o.dma:
                        j = o.tok[2]
                        if j >= N_DMA_SEMS:
                            ws.append(val[("dma", e, j - N_DMA_SEMS)])
                    need = []
                    for (s, v) in ws:
                        if seen.get(id(s), 0) >= v:
                            continue
                        seen[id(s)] = v
                        need = [(s2, v2) for (s2, v2) in need if s2 is not s] + [(s, v)]
                    fuse = FUSE_WAIT and need and not o.dma
                    for (s, v) in (need[:-1] if fuse else need):
                        eng.wait_ge(s, v)
                    ins = o.fn(eng)
                    if fuse:
                        ins._wait_ge(need[-1][0], need[-1][1])
                    if o.dma:
                        s, v = val[o.tok]
                        ins.then_inc(s, 16)
                    elif o.signal:
                        ins.then_inc(val[o.tok][0], 1)
                if final_wait_all:
                    n = self.ndma[e]
                    for j in range(max(0, n - N_DMA_SEMS), n):
                        s, v = val[("dma", e, j)]
                        if seen.get(id(s), 0) < v:
                            eng.wait_ge(s, v)
                            seen[id(s)] = v

            @block.tensor
            def _(eng):
                run("pe", eng)

            @block.scalar
            def _(eng):
                run("act", eng)

            @block.vector
            def _(eng):
                run("dve", eng)

            @block.gpsimd
            def _(eng):
                run("pool", eng)

            @block.sync
            def _(eng):
                run("sp", eng)
import contextlib


D = 1024
NTP = 2048
NS = 16
LS = 8
NT = NTP + NS * LS
TT = [(0, 512), (512, 512), (1024, 512), (1536, 512), (2048, 128)]
UC = 1 + NTP + NS * 9
XBC = 3 + NTP + NS * 11

PROW = {}


def _prow_layout():
    r = 0
    def add(name, n):
        nonlocal r
        PROW[name] = r
        r += n
    add("norm_mix", 4); add("norm_ffn", 4); add("norm_final", 1)
    add("lru_conv_w", 8); add("lru_conv_b", 2); add("lru_b_r", 2); add("lru_b_i", 2); add("lru_lambda", 2)
    add("ssm_norm_w", 2); add("ssm_conv_w", 16); add("ssm_conv_b", 4)
    add("rwkv_mu", 6); add("rwkv_w0", 1); add("rwkv_a0", 1); add("rwkv_k_k", 1); add("rwkv_k_a", 1)
    add("rwkv_lnx_w", 1); add("rwkv_lnx_b", 1); add("rwkv_r_k", 1)
    return r


NPROW = _prow_layout()

IN_SPECS = {
    "xp": [NTP, D], "xs": [NS * LS, D],
    "st_lru_conv": [2, NS, 3, D], "st_lru_h": [2, NS, D], "st_ssm_conv": [1, NS, 3, 4096],
    "st_ssm": [1, NS, 32, 64, 128], "st_rw_shift": [1, NS, D], "st_rw_wkv": [1, NS, 16, 64, 64],
    "norm_mix": [4, D], "norm_ffn": [4, D], "norm_final": [1, D],
    "lru_w_in": [2, D, 2048], "lru_conv_w": [2, 4, D], "lru_conv_b": [2, D], "lru_w_r": [2, 8, 128, 128],
    "lru_b_r": [2, D], "lru_w_i": [2, 8, 128, 128], "lru_b_i": [2, D], "lru_lambda": [2, D], "lru_w_out": [2, D, D],
    "ssm_w_in": [1, D, 6176], "ssm_conv_w": [1, 4, 4096], "ssm_conv_b": [1, 4096], "ssm_dt_bias": [1, 32],
    "ssm_a_log": [1, 32], "ssm_d": [1, 32], "ssm_norm_w": [1, 2048], "ssm_w_out": [1, 2048, D],
    "rwkv_mu": [1, 6, D], "rwkv_w_rkv": [1, 3, D, D], "rwkv_w0": [1, D], "rwkv_w_w1": [1, D, 64], "rwkv_w_w2": [1, 64, D],
    "rwkv_a0": [1, D], "rwkv_w_a1": [1, D, 64], "rwkv_w_a2": [1, 64, D], "rwkv_w_g1": [1, D, 128], "rwkv_w_g2": [1, 128, D],
    "rwkv_k_k": [1, D], "rwkv_k_a": [1, D], "rwkv_r_k": [1, 16, 64], "rwkv_lnx_w": [1, D], "rwkv_lnx_b": [1, D],
    "rwkv_w_out": [1, D, D], "ffn_w1": [4, D, 4096], "ffn_w2": [4, 4096, D],
}
OUT_SPECS = {
    "y_p": [NTP, D], "y_s": [NS * LS, D],
    "lc_p": [2, 3, D], "lc_s": [2, NS, 3, D], "lh_p": [2, D], "lh_s": [2, NS, D],
    "sc_p": [3, 4096], "sc_s": [NS, 3, 4096], "ss_p": [32, 64, 128], "ss_s": [NS, 32, 64, 128],
    "rs_p": [1, D], "rs_s": [NS, D], "rw_p": [16, 64, 64], "rw_s": [NS, 16, 64, 64],
}


class KB:
    def __init__(self, layers=(0, 1, 2, 3), debug=False):
        self.nc = nc = bass.Bass("TRN2", target_bir_lowering=False)
        self.P = Prog(nc)
        self.es = contextlib.ExitStack()
        self.I = {k: nc.dram_tensor(k, v, F32, kind="ExternalInput").ap() for k, v in IN_SPECS.items()}
        self.O = {k: nc.dram_tensor(k, v, F32, kind="ExternalOutput").ap() for k, v in OUT_SPECS.items()}
        self.debug = debug
        if debug:
            self.O["dbg_x"] = nc.dram_tensor("dbg_x", [128, 8, NT], F32, kind="ExternalOutput").ap()
            self.O["dbg_scr"] = nc.dram_tensor("dbg_scr", [8, NT, D], F32, kind="ExternalOutput").ap()
        self.bank_i = 0
        self.layers = layers
        self._n = 0

    def sb(self, name, shape, dt=F32):
        self._n += 1
        return self.es.enter_context(self.nc.sbuf_tensor("%s_%d" % (name, self._n), shape, dt))

    @contextlib.contextmanager
    def scope(self):
        old = self.es
        self.es = contextlib.ExitStack()
        try:
            yield
        finally:
            self.es.close()
            self.es = old
            self.P.barrier()

    def bank(self):
        b = self.bank_i
        self.bank_i = (self.bank_i + 1) % 8
        return self.PS[b], ("PS", b)

    def dma(self, eng, out, in_, reads=(), writes=()):
        self.P.op(eng, lambda e: e.dma_start(out=out, in_=in_), reads, writes, dma=True)

    def act(self, out, in_, func, reads, writes, **kw):
        self.P.op("act", lambda e: e.activation(out=out, in_=in_, func=func, **kw), reads, writes)

    def mm(self, out, lhsT, rhs, start, stop, reads, writes):
        self.P.op("pe", lambda e: e.matmul(out, lhsT=lhsT, rhs=rhs, start=start, stop=stop), reads, writes)

    def tr(self, out, in_, ident, reads, writes):
        self.P.op("pe", lambda e: e.transpose(out, in_, ident), reads, writes)

    def ts(self, eng, out, in0, s1, s2, op0, op1, reads, writes):
        if op1 is None:
            self.P.op(eng, lambda e: e.tensor_scalar(out=out, in0=in0, scalar1=s1, scalar2=None, op0=op0), reads, writes)
        else:
            self.P.op(eng, lambda e: e.tensor_scalar(out=out, in0=in0, scalar1=s1, scalar2=s2, op0=op0, op1=op1), reads, writes)

    def stt(self, out, in0, scalar, in1, op0, op1, reads, writes):
        self.P.op("dve", lambda e: e.scalar_tensor_tensor(out=out, in0=in0, scalar=scalar, in1=in1, op0=op0, op1=op1), reads, writes)

    def tt(self, eng, out, in0, in1, op, reads, writes):
        self.P.op(eng, lambda e: e.tensor_tensor(out=out, in0=in0, in1=in1, op=op), reads, writes)

    def cp(self, eng, out, in_, reads, writes):
        if eng == "act":
            self.P.op("act", lambda e: e.activation(out=out, in_=in_, func=AF.Copy), reads, writes)
        else:
            self.P.op(eng, lambda e: e.tensor_copy(out=out, in_=in_), reads, writes)

    def memset(self, eng, ap, v, writes):
        self.P.op(eng, lambda e: e.memset(ap, v), (), writes)

    def scan(self, out, d0, d1, init, reads, writes):
        self.P.op("dve", lambda e: e.tensor_tensor_scan(out=out, data0=d0, data1=d1, initial=init, op0=ALU.mult, op1=ALU.add), reads, writes)

    def xv(self, c, ti):
        c0, w = TT[ti]
        return self.X[:, c, c0:c0 + w]

    def uv(self, c, ti, shift=0):
        if ti < 4:
            s = 1 + 512 * ti + shift
            return self.U[:, c, s:s + 512]
        v = self.U[:, c, 1 + NTP:UC].rearrange("p (s t) -> p s t", t=9)
        return v[:, :, 1 + shift:9 + shift]

    @staticmethod
    def v3(ap, ti):
        if ti < 4:
            return ap
        return ap.rearrange("p (s t) -> p s t", t=8)

    def setup(self):
        nc = self.nc
        self.PS = [self.es.enter_context(nc.psum_tensor("ps%d" % i, [128, 512], F32)) for i in range(8)]
        self.X = self.sb("X", [128, 8, NT])
        self.U = self.sb("U", [128, 8, UC], BF16)
        self.IDF = self.sb("IDF", [128, 128])
        self.IDB = self.sb("IDB", [128, 128], BF16)
        self.ONESB = self.sb("ONESB", [128, 128], BF16)
        self.C1 = self.sb("C1", [128, 4])
        self.PRM = self.sb("PRM", [64, D])
        self.PF = self.sb("PF", [128, 8, 64])
        self.STG = [self.sb("STG%d" % i, [128, D]) for i in range(2)]
        self.SQ = self.sb("SQ", [128, 8, 512], BF16)
        self.RS = self.sb("RS", [128, 512])
        P = self.P
        P.op("pool", lambda e: e.memset(self.IDF[:], 0.0), (), ["IDF"])
        P.op("pool", lambda e: e.affine_select(out=self.IDF[:], in_=self.IDF[:], pattern=[[-1, 128]], compare_op=ALU.not_equal,
                                               fill=1.0, base=0, channel_multiplier=1), ["IDF"], ["IDF"])
        self.cp("dve", self.IDB[:], self.IDF[:], ["IDF"], ["IDB"])
        self.memset("dve", self.ONESB[:], 1.0, ["ONESB"])
        self.memset("dve", self.C1[:, 0:1], 1e-6, [("C1", 0)])
        self.memset("dve", self.C1[:, 1:2], 1.0, [("C1", 1)])
        self.memset("dve", self.C1[:, 2:3], 1e-5, [("C1", 2)])
        self.memset("dve", self.C1[:, 3:4], 64e-5, [("C1", 3)])
        self.memset("dve", self.U[:, :, 0:1], 0.0, [("U", "shiftp")])
        self.memset("pool", self.PRM[:], 0.0, ["PRM"])
        I = self.I
        def row(name, src, n):
            r = PROW[name]
            self.dma("sp", self.PRM[r:r + n, :], src, (), ["PRM"])
        row("norm_mix", I["norm_mix"], 4); row("norm_ffn", I["norm_ffn"], 4); row("norm_final", I["norm_final"], 1)
        row("lru_conv_w", I["lru_conv_w"].rearrange("a k d -> (a k) d"), 8)
        row("lru_conv_b", I["lru_conv_b"], 2); row("lru_b_r", I["lru_b_r"], 2); row("lru_b_i", I["lru_b_i"], 2)
        row("lru_lambda", I["lru_lambda"], 2)
        row("ssm_norm_w", I["ssm_norm_w"].rearrange("a (r d) -> (a r) d", d=D), 2)
        row("ssm_conv_w", I["ssm_conv_w"].rearrange("a k (r d) -> (a k r) d", d=D), 16)
        row("ssm_conv_b", I["ssm_conv_b"].rearrange("a (r d) -> (a r) d", d=D), 4)
        row("rwkv_mu", I["rwkv_mu"].rearrange("a k d -> (a k) d"), 6)
        for nm in ("rwkv_w0", "rwkv_a0", "rwkv_k_k", "rwkv_k_a", "rwkv_lnx_w", "rwkv_lnx_b"):
            row(nm, I[nm], 1)
        row("rwkv_r_k", I["rwkv_r_k"].rearrange("a h n -> a (h n)"), 1)
        for c in range(8):
            ps, pk = self.bank()
            self.tr(ps[:, 0:64], self.PRM[:, c * 128:(c + 1) * 128], self.IDF[0:64, 0:64], ["PRM", "IDF"], [pk])
            self.cp("dve", self.PF[:, c, :], ps[:, 0:64], [pk], [("PF", c)])

    def pf(self, name, k, c):
        r = PROW[name] + k
        return self.PF[:, c, r:r + 1]

    def load_x(self):
        n = 0
        for ti, (c0, w) in enumerate(TT):
            for j in range(w // 128):
                b = n % 2
                n += 1
                src = self.I["xp"][c0 + j * 128:c0 + (j + 1) * 128, :] if ti < 4 else self.I["xs"][:, :]
                self.dma("sp", self.STG[b][:], src, (), [("STG", b)])
                for c in range(8):
                    self.tr(self.PS[c][:, j * 128:(j + 1) * 128], self.STG[b][:, c * 128:(c + 1) * 128], self.IDF[:],
                            [("STG", b), "IDF"], [("PS", c)])
            for c in range(8):
                self.cp("dve" if c % 2 == 0 else "act", self.X[:, c, c0:c0 + w], self.PS[c][:, 0:w], [("PS", c)], [("X", (c, ti))])

    def rows_to_fm(self, src, nrows, dst, dkey):
        self.dma("sp", self.STG[0][0:nrows, :], src, (), [("STG", 0)])
        for c in range(8):
            ps, pk = self.bank()
            self.tr(ps[:, 0:nrows], self.STG[0][0:nrows, c * 128:(c + 1) * 128], self.IDF[0:nrows, 0:nrows], [("STG", 0), "IDF"], [pk])
            self.cp("dve", dst[:, c, 0:nrows], ps[:, 0:nrows], [pk], [dkey])

    def fm_to_rows(self, src, skey, nrows, dst):
        for half in range(2):
            ps, pk = self.bank()
            for cc in range(4):
                c = half * 4 + cc
                self.tr(ps[0:nrows, cc * 128:(cc + 1) * 128], src[:, c, 0:nrows], self.IDF[:], [skey, "IDF"], [pk])
            self.cp("dve", self.STG[1][0:nrows, half * 512:(half + 1) * 512], ps[0:nrows, :], [pk], [("STG", 1)])
        self.dma("sp", dst, self.STG[1][0:nrows, :], [("STG", 1)], ())

    def norm_to_U(self, pname, k, shift_out=None):
        for ti, (c0, w) in enumerate(TT):
            ps, pk = self.bank()
            for c in range(8):
                self.act(self.SQ[:, c, :w], self.X[:, c, c0:c0 + w], AF.Square, [("X", (c, ti))], [("SQ", c)])
                self.mm(ps[:, :w], self.ONESB[:], self.SQ[:, c, :w], c == 0, c == 7, [("SQ", c), "ONESB"], [pk])
            self.act(self.RS[:, :w], ps[:, :w], AF.Ln, [pk, ("C1", 0)], ["RS"], scale=1.0 / D, bias=self.C1[:, 0:1])
            self.act(self.RS[:, :w], self.RS[:, :w], AF.Exp, ["RS"], ["RS"], scale=-0.5)
            for c in range(8):
                self.stt(self.uv(c, ti), self.v3(self.X[:, c, c0:c0 + w], ti), self.pf(pname, k, c), self.v3(self.RS[:, :w], ti),
                         ALU.mult, ALU.mult, [("X", (c, ti)), "RS", ("PF", c)], [("U", (c, ti))])
                if shift_out is not None and ti == 3:
                    self.stt(shift_out[:, c, 0:1], self.X[:, c, NTP - 1:NTP], self.pf(pname, k, c), self.RS[:, 511:512],
                             ALU.mult, ALU.mult, [("X", (c, ti)), "RS", ("PF", c)], ["C_USH"])
                if shift_out is not None and ti == 4:
                    self.stt(shift_out[:, c, 1:], self.X[:, c, NTP:NT].rearrange("p (s t) -> p s t", t=8)[:, :, 7],
                             self.pf(pname, k, c), self.RS[:, 0:128].rearrange("p (s t) -> p s t", t=8)[:, :, 7],
                             ALU.mult, ALU.mult, [("X", (c, ti)), "RS", ("PF", c)], ["C_USH"])

    def u_keys(self, ti):
        return [("U", (c, ti)) for c in range(8)]

    def mlp(self, l):
        self.norm_to_U("norm_ffn", l)
        self.W1S = [self.sb("W1S%d" % i, [128, 8, 512], BF16) for i in range(2)]
        self.W2S = [self.sb("W2S%d" % i, [128, 4, D], BF16) for i in range(2)]
        self.HT = [self.sb("HT%d" % i, [128, 4, 512], BF16) for i in range(2)]
        self.RT = [self.sb("RT%d" % i, [128, 512]) for i in range(2)]
        w1 = self.I["ffn_w1"]
        w2 = self.I["ffn_w2"]
        def load(s):
            b = s % 2
            self.dma("pool", self.W1S[b][:], w1[l, :, s * 512:(s + 1) * 512].rearrange("(kc p) n -> p kc n", p=128), (), [("W1S", b)])
            self.dma("pool", self.W2S[b][:], w2[l, s * 512:(s + 1) * 512, :].rearrange("(fc p) n -> p fc n", p=128), (), [("W2S", b)])
        load(0)
        hb = 0
        rb = 0
        for s in range(8):
            if s + 1 < 8:
                load(s + 1)
            b = s % 2
            for ti, (c0, w) in enumerate(TT):
                H = self.HT[hb]
                hk = "HT%d" % hb
                hb ^= 1
                for fc in range(4):
                    ps, pk = self.bank()
                    for kc in range(8):
                        self.mm(self.v3(ps[:, :w], ti), self.W1S[b][:, kc, fc * 128:(fc + 1) * 128], self.uv(kc, ti), kc == 0, kc == 7,
                                [("W1S", b), ("U", (kc, ti))], [pk])
                    R = self.RT[rb]
                    rk = "RT%d" % rb
                    rb ^= 1
                    self.act(R[:, :w], ps[:, :w], AF.Relu, [pk], [rk])
                    self.act(H[:, fc, :w], R[:, :w], AF.Square, [rk], [(hk, fc)])
                for n in range(8):
                    ps, pk = self.bank()
                    for fc in range(4):
                        self.mm(ps[:, :w], self.W2S[b][:, fc, n * 128:(n + 1) * 128], H[:, fc, :w], fc == 0, fc == 3,
                                [("W2S", b), (hk, fc)], [pk])
                    self.tt("dve", self.X[:, n, c0:c0 + w], self.X[:, n, c0:c0 + w], ps[:, :w], ALU.add,
                            [pk, ("X", (n, ti))], [("X", (n, ti))])

    def final_out(self):
        YT = self.STG
        UF = self.sb("UF", [128, 8, 512])
        n = 0
        for ti, (c0, w) in enumerate(TT):
            ps, pk = self.bank()
            for c in range(8):
                self.act(self.SQ[:, c, :w], self.X[:, c, c0:c0 + w], AF.Square, [("X", (c, ti))], [("SQ", c)])
                self.mm(ps[:, :w], self.ONESB[:], self.SQ[:, c, :w], c == 0, c == 7, [("SQ", c), "ONESB"], [pk])
            self.act(self.RS[:, :w], ps[:, :w], AF.Ln, [pk, ("C1", 0)], ["RS"], scale=1.0 / D, bias=self.C1[:, 0:1])
            self.act(self.RS[:, :w], self.RS[:, :w], AF.Exp, ["RS"], ["RS"], scale=-0.5)
            for c in range(8):
                self.stt(UF[:, c, :w], self.X[:, c, c0:c0 + w], self.pf("norm_final", 0, c), self.RS[:, :w],
                         ALU.mult, ALU.mult, [("X", (c, ti)), "RS", ("PF", c)], [("UF", c)])
            for j in range(w // 128):
                b = n % 2
                n += 1
                for half in range(2):
                    ps2, pk2 = self.bank()
                    for cc in range(4):
                        c = half * 4 + cc
                        self.tr(ps2[:, cc * 128:(cc + 1) * 128], UF[:, c, j * 128:(j + 1) * 128], self.IDF[:], [("UF", c), "IDF"], [pk2])
                    self.cp("act" if half else "dve", YT[b][:, half * 512:(half + 1) * 512], ps2[:, :], [pk2], [("STG", b)])
                dst = self.O["y_p"][c0 + j * 128:c0 + (j + 1) * 128, :] if ti < 4 else self.O["y_s"][:, :]
                self.dma("sp", dst, YT[b][:], [("STG", b)], ())

    def dump_x(self):
        self.dma("sp", self.O["dbg_x"], self.X[:], ["X"], ())


def layer_A(self, l, ia):
    I = self.I
    self.norm_to_U("norm_mix", l)
    XB = self.sb("A_XB", [128, XBC])
    XC = self.sb("A_XC", [128, NT])
    XCb = self.sb("A_XCb", [128, NT], BF16)
    GATE = self.sb("A_GATE", [128, NT], BF16)
    R = self.sb("A_R", [128, NT])
    Iq = self.sb("A_I", [128, NT])
    WIN = [self.sb("A_WIN%d" % i, [128, 8, 256], BF16) for i in range(2)]
    WR = [self.sb("A_WR%d" % i, [128, 128], BF16) for i in range(2)]
    WI = [self.sb("A_WI%d" % i, [128, 128], BF16) for i in range(2)]
    WO = [self.sb("A_WO%d" % i, [128, D], BF16) for i in range(2)]
    CL = self.sb("A_CL", [128, 8])
    H0 = self.sb("A_H0", [128, 8, NS])
    CS0 = self.sb("A_CS0", [128, 8, NS * 3])
    HST = self.sb("A_HST", [128, 8, 1 + NS])
    CST = self.sb("A_CST", [128, 8, 3 + NS * 3])
    XBs = XB[:, 3 + NTP:XBC].rearrange("p (s t) -> p s t", t=11)
    XCs = XC[:, NTP:NT].rearrange("p (s t) -> p s t", t=8)
    rl = PROW["lru_lambda"] + ia
    self.act(CL[:], self.PF[:, :, rl], AF.Exp, ["PF"], ["A_CL"], scale=-1.0)
    self.act(CL[:], CL[:], AF.Ln, ["A_CL", ("C1", 1)], ["A_CL"], bias=self.C1[:, 1:2], scale=1.0)
    self.ts("dve", CL[:], CL[:], -8.0, None, ALU.mult, None, ["A_CL"], ["A_CL"])
    self.rows_to_fm(I["st_lru_h"][ia], NS, H0, "A_H0")
    self.rows_to_fm(I["st_lru_conv"][ia].rearrange("s k d -> (s k) d"), NS * 3, CS0, "A_CS0")
    w_in, w_out = I["lru_w_in"], I["lru_w_out"]

    def load(j):
        b = j % 2
        self.dma("pool", WIN[b][:, :, 0:128], w_in[ia, :, j * 128:(j + 1) * 128].rearrange("(kc p) n -> p kc n", p=128), (), [("A_WIN", b)])
        self.dma("pool", WIN[b][:, :, 128:256], w_in[ia, :, D + j * 128:D + (j + 1) * 128].rearrange("(kc p) n -> p kc n", p=128), (), [("A_WIN", b)])
        self.dma("pool", WR[b][:], I["lru_w_r"][ia, j], (), [("A_WR", b)])
        self.dma("pool", WI[b][:], I["lru_w_i"][ia, j], (), [("A_WI", b)])
        self.dma("pool", WO[b][:], w_out[ia, j * 128:(j + 1) * 128, :], (), [("A_WO", b)])

    load(0)
    for j in range(8):
        if j + 1 < 8:
            load(j + 1)
        b = j % 2
        self.memset("dve", XB[:, 0:3], 0.0, [("A_XB", "st")])
        self.cp("dve", XBs[:, :, 0:3], CS0[:, j, :].rearrange("p (s k) -> p s k", k=3), ["A_CS0"], [("A_XB", "st")])
        for ti, (c0, w) in enumerate(TT):
            for half in range(2):
                ps, pk = self.bank()
                for kc in range(8):
                    self.mm(self.v3(ps[:, :w], ti), WIN[b][:, kc, half * 128:(half + 1) * 128], self.uv(kc, ti), kc == 0, kc == 7,
                            [("A_WIN", b), ("U", (kc, ti))], [pk])
                if half == 0:
                    dst = XB[:, 3 + c0:3 + c0 + w] if ti < 4 else XBs[:, :, 3:11]
                    self.cp("act", dst, self.v3(ps[:, :w], ti), [pk], [("A_XB", ti)])
                else:
                    self.act(GATE[:, c0:c0 + w], ps[:, :w], AF.Gelu_apprx_tanh, [pk], [("A_GATE", ti)])
        self.cp("pool", CST[:, j, 0:3], XB[:, NTP:NTP + 3], ["A_XB"], [("A_CST", j)])
        self.cp("pool", CST[:, j, 3:].rearrange("p (s k) -> p s k", k=3), XBs[:, :, 8:11], ["A_XB"], [("A_CST", j)])
        cw = [self.pf("lru_conv_w", ia * 4 + k, j) for k in range(4)]
        cb = self.pf("lru_conv_b", ia, j)
        for (dst, srcf) in ((XC[:, 0:NTP], lambda k: XB[:, k:k + NTP]), (XCs, lambda k: XBs[:, :, k:k + 8])):
            self.ts("dve", dst, srcf(0), cw[0], cb, ALU.mult, ALU.add, ["A_XB", ("PF", j)], ["A_XC"])
            for k in range(1, 4):
                self.stt(dst, srcf(k), cw[k], dst, ALU.mult, ALU.add, ["A_XB", "A_XC", ("PF", j)], ["A_XC"])
        self.cp("act", XCb[:], XC[:], ["A_XC"], ["A_XCb"])
        for ti, (c0, w) in enumerate(TT):
            for (Wg, dstb, bname, key) in ((WR, R, "lru_b_r", "A_R"), (WI, Iq, "lru_b_i", "A_I")):
                ps, pk = self.bank()
                self.mm(ps[:, :w], Wg[b][:], XCb[:, c0:c0 + w], True, True, ["A_XCb", (key.replace("A_", "A_W"), b)], [pk])
                self.act(dstb[:, c0:c0 + w], ps[:, :w], AF.Sigmoid, [pk, ("PF", j)], [key], bias=self.pf(bname, ia, j), scale=1.0)
        T1 = XB[:, 0:NT]
        self.act(R[:], R[:], AF.Exp, ["A_R", "A_CL"], ["A_R"], scale=CL[:, j:j + 1])
        self.act(T1, R[:], AF.Square, ["A_R", "A_XB"], ["A_XB"])
        self.ts("dve", T1, T1, -1.0, 1.0, ALU.mult, ALU.add, ["A_XB"], ["A_XB"])
        self.ts("dve", T1, T1, 1e-30, None, ALU.max, None, ["A_XB"], ["A_XB"])
        self.act(T1, T1, AF.Sqrt, ["A_XB"], ["A_XB"])
        self.memset("dve", T1[:, 0:1], 1.0, ["A_XB"])
        self.memset("dve", R[:, 0:1], 0.0, ["A_R"])
        self.tt("dve", Iq[:], Iq[:], T1, ALU.mult, ["A_I", "A_XB"], ["A_I"])
        self.tt("dve", Iq[:], Iq[:], XC[:], ALU.mult, ["A_I", "A_XC"], ["A_I"])
        self.scan(XC[:, 0:NTP], R[:, 0:NTP], Iq[:, 0:NTP], 0.0, ["A_R", "A_I"], ["A_XC"])
        for s in range(NS):
            c0 = NTP + s * 8
            self.scan(XC[:, c0:c0 + 8], R[:, c0:c0 + 8], Iq[:, c0:c0 + 8], H0[:, j, s:s + 1], ["A_R", "A_I", "A_H0"], ["A_XC"])
        self.cp("pool", HST[:, j, 0:1], XC[:, NTP - 1:NTP], ["A_XC"], [("A_HST", j)])
        self.cp("pool", HST[:, j, 1:], XCs[:, :, 7], ["A_XC"], [("A_HST", j)])
        self.tt("dve", XCb[:], XC[:], GATE[:], ALU.mult, ["A_XC", "A_GATE"], ["A_XCb"])
        for ti, (c0, w) in enumerate(TT):
            for n in range(8):
                ps, pk = self.bank()
                self.mm(ps[:, :w], WO[b][:, n * 128:(n + 1) * 128], XCb[:, c0:c0 + w], True, True, ["A_XCb", ("A_WO", b)], [pk])
                self.tt("dve", self.X[:, n, c0:c0 + w], self.X[:, n, c0:c0 + w], ps[:, :w], ALU.add, [pk, ("X", (n, ti))], [("X", (n, ti))])
    self.fm_to_rows(HST[:, :, 0:1], "A_HST", 1, self.O["lh_p"][ia:ia + 1, :])
    self.fm_to_rows(HST[:, :, 1:], "A_HST", NS, self.O["lh_s"][ia])
    self.fm_to_rows(CST[:, :, 0:3], "A_CST", 3, self.O["lc_p"][ia])
    self.fm_to_rows(CST[:, :, 3:], "A_CST", NS * 3, self.O["lc_s"][ia].rearrange("s k d -> (s k) d"))


KB.layer_A = layer_A


def layer_B(self, l, ib):
    I = self.I
    self.norm_to_U("norm_mix", l)
    f32 = F32
    TRI = self.sb("B_TRI", [128, 128]); NEGM = self.sb("B_NEGM", [128, 128]); ONESF = self.sb("B_ONESF", [128, 128])
    self.memset("pool", TRI[:], 1.0, ["B_TRI"])
    self.P.op("pool", lambda e: e.affine_select(out=TRI[:], in_=TRI[:], pattern=[[1, 128]], compare_op=ALU.is_ge, fill=0.0, base=0,
                                                channel_multiplier=-1), ["B_TRI"], ["B_TRI"])
    self.memset("pool", NEGM[:], 0.0, ["B_NEGM"])
    self.P.op("pool", lambda e: e.affine_select(out=NEGM[:], in_=NEGM[:], pattern=[[1, 128]], compare_op=ALU.is_ge, fill=-1.0e4, base=0,
                                                channel_multiplier=-1), ["B_NEGM"], ["B_NEGM"])
    self.memset("pool", ONESF[:], 1.0, ["B_ONESF"])
    DTB = self.sb("B_DTB", [128, 32]); AB = self.sb("B_AB", [128, 32]); DB = self.sb("B_DB", [128, 32])
    self.dma("sp", DTB[:], I["ssm_dt_bias"][ib:ib + 1, :].partition_broadcast(128), (), ["B_DTB"])
    self.dma("sp", AB[:], I["ssm_a_log"][ib:ib + 1, :].partition_broadcast(128), (), ["B_AB"])
    self.dma("sp", DB[:], I["ssm_d"][ib:ib + 1, :].partition_broadcast(128), (), ["B_DB"])
    self.act(AB[:], AB[:], AF.Exp, ["B_AB"], ["B_AB"])
    self.ts("dve", AB[:], AB[:], -1.0, None, ALU.mult, None, ["B_AB"], ["B_AB"])
    CS0 = self.sb("B_CS0", [128, 32, NS * 3]); CST = self.sb("B_CST", [128, 32, 3 + NS * 3])
    for r in range(4):
        self.rows_to_fm(I["st_ssm_conv"][ib].rearrange("s k d -> (s k) d")[:, r * D:(r + 1) * D], NS * 3, CS0[:, r * 8:(r + 1) * 8, :], "B_CS0")
    WZD = [self.sb("B_WZD0", [128, 8, 260], BF16)] * 2
    BONES = self.sb("B_BONES", [128, 128]); TRIS = self.sb("B_TRIS", [128, 128]); NEGMS = self.sb("B_NEGMS", [128, 128])
    SEQM = self.sb("B_SEQM", [128, 16]); DAS = self.sb("B_DAS", [128, 64]); CDS = self.sb("B_CDS", [128, 64])
    YO = self.sb("B_YO", [128, 256]); BTM = self.sb("B_BTM", [128, 128], BF16); USC = self.sb("B_USC", [128, 8, 128], BF16)
    def _asel(ap, pattern, base, cm, fill, keys):
        self.P.op("pool", lambda e: e.affine_select(out=ap, in_=ap, pattern=pattern, compare_op=ALU.is_ge, fill=fill, base=base,
                                                    channel_multiplier=cm), keys, keys)
    self.memset("pool", SEQM[:], 1.0, ["B_SEQM"])
    _asel(SEQM[:], [[-8, 16]], 0, 1, 0.0, ["B_SEQM"])
    _asel(SEQM[:], [[8, 16]], 7, -1, 0.0, ["B_SEQM"])
    self.memset("pool", BONES[:], 1.0, ["B_BONES"])
    _asel(BONES[:].rearrange("p (s t) -> p s t", t=8), [[-8, 16], [0, 8]], 0, 1, 0.0, ["B_BONES"])
    _asel(BONES[:].rearrange("p (s t) -> p s t", t=8), [[8, 16], [0, 8]], 7, -1, 0.0, ["B_BONES"])
    self.tt("pool", TRIS[:], TRI[:], BONES[:], ALU.mult, ["B_TRI", "B_BONES"], ["B_TRIS"])
    self.tt("pool", NEGMS[:], NEGM[:], BONES[:], ALU.mult, ["B_NEGM", "B_BONES"], ["B_NEGMS"])
    self.ts("dve", YO[:, 0:128], BONES[:], 1.0e4, -1.0e4, ALU.mult, ALU.add, ["B_BONES"], ["B_YO"])
    self.tt("dve", NEGMS[:], NEGMS[:], YO[:, 0:128], ALU.add, ["B_NEGMS", "B_YO"], ["B_NEGMS"])
    self.cp("dve", USC[:].rearrange("p c (s t) -> p c s t", t=8),
            self.U[:, :, 1 + NTP:UC].rearrange("p c (s t) -> p c s t", t=9)[:, :, :, 1:9], [("U", (c_, 4)) for c_ in range(8)], ["B_USC"])

    WXBC = [self.sb("B_WXBC0", [128, 8, 512], BF16)] * 2
    WOUT = [self.sb("B_WOUT0", [128, 2, D], BF16)] * 2
    XF = self.sb("B_XF", [128, 4, 3 + 512]); XFS = self.sb("B_XFS", [128, 4, NS * 11])
    XCf = self.sb("B_XCf", [128, 4, 512]); BCb = self.sb("B_BCb", [128, 2, 512], BF16)
    YGT = self.sb("B_YGT", [128, 2, 512], BF16)
    ST = self.sb("B_ST", [128, 256]); STb = self.sb("B_STb", [128, 256], BF16)
    SIN = self.sb("B_SIN", [128, 2, 128]); SOUT = self.sb("B_SOUT", [128, 2, 128])
    XT = self.sb("B_XT", [128, 256]); BT = self.sb("B_BT", [128, 128], BF16)
    SM = self.sb("B_SM", [128, 40])
    TRIH = self.sb("B_TRIH", [128, 4, 128]); LT = self.sb("B_LT", [128, 4, 128]); MT = self.sb("B_MT", [128, 4, 128], BF16)
    XDT = self.sb("B_XDT", [128, 256], BF16); XDD = self.sb("B_XDD", [128, 256], BF16)
    Y1 = self.sb("B_Y1", [128, 256]); T2 = self.sb("B_T2", [128, 256]); SZ = self.sb("B_SZ", [128, 256])
    DTV, DT_, DA, NACS, EACS, DEND, CD, MS = (SM[:, 0:4], SM[:, 4:8], SM[:, 8:12], SM[:, 12:16], SM[:, 16:20], SM[:, 20:24],
                                              SM[:, 24:28], SM[:, 28:29])
    w_in, w_out = I["ssm_w_in"], I["ssm_w_out"]
    XFSv = [XFS[:, q, :].rearrange("p (s t) -> p s t", t=11) for q in range(4)]

    def bc(ap, cs):
        return ap.unsqueeze(2).to_broadcast([cs, 4, 64])

    def v4(ap):
        return ap.rearrange("p (h q) -> p h q", q=64)

    def wv(c0, n):
        return w_in[ib, :, c0:c0 + n].rearrange("(kc p) n -> p kc n", p=128)

    def load(g):
        self.dma("pool", WZD[0][:, :, 0:256], wv(g * 256, 256), (), ["B_WZD"])
        self.dma("pool", WZD[0][:, :, 256:260], wv(6144 + 4 * g, 4), (), ["B_WZD"])

    def load_x(g):
        self.dma("pool", WXBC[0][:, :, 0:256], wv(2048 + g * 256, 256), (), ["B_WXBC"])
        self.dma("pool", WXBC[0][:, :, 256:384], wv(4096 + g * 128, 128), (), ["B_WXBC"])
        self.dma("pool", WXBC[0][:, :, 384:512], wv(5120 + g * 128, 128), (), ["B_WXBC"])

    def load_o(g):
        self.dma("pool", WOUT[0][:], w_out[ib, g * 256:(g + 1) * 256, :].rearrange("(h p) n -> p h n", p=128), (), ["B_WOUT"])

    def chunk(g, b, tc, cs, ucol, sample=False):
        hs = slice(4 * g, 4 * g + 4)
        tri, negm = (TRIS, NEGMS) if sample else (TRI, NEGM)
        import os as _os
        STG_ = int(_os.environ.get('BDBG_C', '99'))
        if STG_ <= 0:
            return
        pzd, kzd = self.bank()
        for kc in range(8):
            self.mm(pzd[0:cs, 0:260], (USC[:, kc, :] if sample else self.U[:, kc, ucol:ucol + cs]), WZD[b][:, kc, :], kc == 0, kc == 7, ["B_WZD", "U", "B_USC"], [kzd])
        if STG_ <= 1:
            return
        pt, kt = self.bank()
        for q in range(3):
            self.tr(pt[0:cs, q * 128:(q + 1) * 128], XCf[:, q, tc:tc + cs], self.IDF[:], ["B_XCf", "IDF"], [kt])
        self.cp("act", XT[0:cs, :], pt[0:cs, 0:256], [kt], ["B_XT"])
        self.cp("dve", BT[0:cs, :], pt[0:cs, 256:384], [kt], ["B_BT"])
        if STG_ <= 2:
            return
        self.tt("dve", DTV[0:cs], pzd[0:cs, 256:260], DTB[0:cs, hs], ALU.add, [kzd, "B_DTB"], ["B_SM"])
        self.act(DT_[0:cs], DTV[0:cs], AF.Exp, ["B_SM"], ["B_SM"])
        self.act(DT_[0:cs], DT_[0:cs], AF.Ln, ["B_SM", ("C1", 1)], ["B_SM"], bias=self.C1[0:cs, 1:2], scale=1.0)
        self.tt("dve", DA[0:cs], DT_[0:cs], AB[0:cs, hs], ALU.mult, ["B_SM", "B_AB"], ["B_SM"])
        if STG_ <= 3:
            return
        pa, ka = self.bank()
        self.mm(pa[0:cs, 0:4], tri[0:cs, 0:cs], DA[0:cs], True, True, ["B_TRI", "B_TRIS", "B_SM"], [ka])
        self.mm(pa[:, 4:8], (BONES[:, :] if sample else ONESF[0:cs, :]), DA[0:cs], True, True, ["B_ONESF", "B_BONES", "B_SM"], [ka])
        self.ts("dve", NACS[0:cs], pa[0:cs, 0:4], -1.0, None, ALU.mult, None, [ka], ["B_SM"])
        self.act(EACS[0:cs], pa[0:cs, 0:4], AF.Exp, [ka], ["B_SM"])
        self.tt("dve", DEND[0:cs], pa[0:cs, 4:8], NACS[0:cs], ALU.add, [ka, "B_SM"], ["B_SM"])
        self.act(DEND[0:cs], DEND[0:cs], AF.Exp, ["B_SM"], ["B_SM"])
        if not sample:
            self.act(CD, pa[:, 4:8], AF.Exp, [ka], ["B_SM"])
        else:
            self.tt("dve", DAS[:].rearrange("p (s h) -> p s h", h=4), DA[:].unsqueeze(1).to_broadcast([128, 16, 4]),
                    SEQM[:].unsqueeze(2).to_broadcast([128, 16, 4]), ALU.mult, ["B_SM", "B_SEQM"], ["B_DAS"])
            pcd, kcd = self.bank()
            self.mm(pcd[:, 0:64], ONESF[:, :], DAS[:], True, True, ["B_ONESF", "B_DAS"], [kcd])
            self.act(CDS[:], pcd[:, 0:64], AF.Exp, [kcd], ["B_CDS"])
        if STG_ <= 4:
            return
        pl, kl = self.bank()
        for h in range(4):
            self.ts("dve", TRIH[0:cs, h, 0:cs], tri[0:cs, 0:cs], DA[0:cs, h:h + 1], None, ALU.mult, None, ["B_TRI", "B_TRIS", "B_SM"], [("B_TRIH", h)])
            self.mm(pl[0:cs, h * 128:h * 128 + cs], ONESF[0:cs, 0:cs], TRIH[0:cs, h, 0:cs], True, False, ["B_ONESF", ("B_TRIH", h)], [kl])
            self.mm(pl[0:cs, h * 128:h * 128 + cs], self.IDF[0:cs, 0:cs], negm[0:cs, 0:cs], False, True, ["IDF", "B_NEGM", "B_NEGMS"], [kl])
        for h in range(4):
            self.act(LT[0:cs, h, 0:cs], pl[0:cs, h * 128:h * 128 + cs], AF.Exp, [kl, "B_SM"], [("B_LT", h)], bias=NACS[0:cs, h:h + 1], scale=1.0)
        if STG_ <= 5:
            return
        pc, kc_ = self.bank()
        self.mm(pc[0:cs, 0:cs], BCb[:, 0, tc:tc + cs], BCb[:, 1, tc:tc + cs], True, True, ["B_BCb"], [kc_])
        for h in range(4):
            self.tt("dve", MT[0:cs, h, 0:cs], LT[0:cs, h, 0:cs], pc[0:cs, 0:cs], ALU.mult, [kc_, ("B_LT", h)], [("B_MT", h)])
        if STG_ <= 6:
            return
        self.tt("dve", v4(XDT[0:cs, :]), v4(XT[0:cs, :]), bc(DT_[0:cs], cs), ALU.mult, ["B_XT", "B_SM"], ["B_XDT"])
        self.tt("dve", v4(XDD[0:cs, :]), v4(XDT[0:cs, :]), bc(DEND[0:cs], cs), ALU.mult, ["B_XDT", "B_SM"], ["B_XDD"])
        if STG_ <= 7:
            return
        if sample:
            self.act(SZ[0:cs, :], pzd[0:cs, 0:256], AF.Silu, [kzd], ["B_SZ"])
            self.memset("dve", YO[:], 0.0, ["B_YO"])
            for sq in range(NSQ):
                state_in(g, sq)
                po, ko = self.bank()
                self.mm(po[:, 0:256], BCb[:, 1, 0:128], STb[:], True, True, ["B_BCb", "B_STb"], [ko])
                self.stt(YO[:], po[:, 0:256], SEQM[:, sq:sq + 1], YO[:], ALU.mult, ALU.add, [ko, "B_SEQM", "B_YO"], ["B_YO"])
                self.ts("dve", BTM[:], BT[:], SEQM[:, sq:sq + 1], None, ALU.mult, None, ["B_BT", "B_SEQM"], ["B_BTM"])
                pst, kst = self.bank()
                self.mm(pst[:, 0:256], BTM[:], XDD[:], True, True, ["B_BTM", "B_XDD"], [kst])
                self.tt("dve", v4(ST[:]), v4(ST[:]), bc(CDS[:, 4 * sq:4 * sq + 4], 128), ALU.mult, ["B_ST", "B_CDS"], ["B_ST"])
                self.tt("dve", ST[:], ST[:], pst[:, 0:256], ALU.add, ["B_ST", kst], ["B_ST"])
                state_out(g, self.O["ss_s"][sq, 4 * g:4 * g + 4])
        py, ky = self.bank()
        for h in range(4):
            self.mm(py[0:cs, h * 64:(h + 1) * 64], MT[0:cs, h, 0:cs], XDT[0:cs, h * 64:(h + 1) * 64], True, True, [("B_MT", h), "B_XDT"], [ky])
        if not sample:
            po, ko = self.bank()
            self.mm(po[0:cs, 0:256], BCb[:, 1, tc:tc + cs], STb[:], True, True, ["B_BCb", "B_STb"], [ko])
            self.tt("dve", v4(Y1[0:cs, :]), v4(po[0:cs, 0:256]), bc(EACS[0:cs], cs), ALU.mult, [ko, "B_SM"], ["B_Y1"])
        else:
            self.tt("dve", v4(Y1[:]), v4(YO[:]), bc(EACS[:], 128), ALU.mult, ["B_YO", "B_SM"], ["B_Y1"])
        self.tt("dve", Y1[0:cs, :], Y1[0:cs, :], py[0:cs, 0:256], ALU.add, [ky, "B_Y1"], ["B_Y1"])
        self.tt("pool", v4(T2[0:cs, :]), v4(XT[0:cs, :]), bc(DB[0:cs, hs], cs), ALU.mult, ["B_XT", "B_DB"], ["B_T2"])
        self.tt("dve", Y1[0:cs, :], Y1[0:cs, :], T2[0:cs, :], ALU.add, ["B_T2", "B_Y1"], ["B_Y1"])
        if STG_ <= 8:
            return
        if not sample:
            pst, kst = self.bank()
            self.mm(pst[:, 0:256], BT[0:cs, :], XDD[0:cs, :], True, True, ["B_BT", "B_XDD"], [kst])
            self.tt("dve", v4(ST[:]), v4(ST[:]), bc(CD, 128), ALU.mult, ["B_ST", "B_SM"], ["B_ST"])
            self.tt("dve", ST[:], ST[:], pst[:, 0:256], ALU.add, ["B_ST", kst], ["B_ST"])
            self.cp("act", STb[:], ST[:], ["B_ST"], ["B_STb"])
        if STG_ <= 9:
            return
        if not sample:
            self.act(SZ[0:cs, :], pzd[0:cs, 0:256], AF.Silu, [kzd], ["B_SZ"])
        self.tt("dve", Y1[0:cs, :], Y1[0:cs, :], SZ[0:cs, :], ALU.mult, ["B_SZ", "B_Y1"], ["B_Y1"])
        self.P.op("dve", lambda e: e.scalar_tensor_tensor(out=T2[0:cs, :], in0=Y1[0:cs, :], scalar=1.0, in1=Y1[0:cs, :], op0=ALU.mult,
                                                          op1=ALU.mult, accum_out=MS[0:cs]), ["B_Y1", "B_T2"], ["B_T2", "B_SM"])
        self.act(MS[0:cs], MS[0:cs], AF.Ln, ["B_SM", ("C1", 2)], ["B_SM"], scale=1.0 / 256, bias=self.C1[0:cs, 2:3])
        self.act(MS[0:cs], MS[0:cs], AF.Exp, ["B_SM"], ["B_SM"], scale=-0.5)
        self.ts("dve", Y1[0:cs, :], Y1[0:cs, :], MS[0:cs], None, ALU.mult, None, ["B_SM", "B_Y1"], ["B_Y1"])
        pg, kg = self.bank()
        for hf in range(2):
            self.tr(pg[:, hf * 128:hf * 128 + cs], Y1[0:cs, hf * 128:(hf + 1) * 128], self.IDF[0:cs, 0:cs], ["B_Y1", "IDF"], [kg])
        for hf in range(2):
            ch = 2 * g + hf
            self.ts("dve", YGT[:, hf, tc:tc + cs], pg[:, hf * 128:hf * 128 + cs], self.pf("ssm_norm_w", ch // 8, ch % 8), None, ALU.mult, None,
                    [kg, "PF"], ["B_YGT"])

    def state_in(g, s):
        self.dma("sp", SIN[:], I["st_ssm"][ib, s, 4 * g:4 * g + 4].rearrange("(a h) p n -> (h p) a n", a=2), (), ["B_SIN"])
        ps, pk = self.bank()
        for a in range(2):
            self.tr(ps[:, a * 128:(a + 1) * 128], SIN[:, a, :], self.IDF[:], ["B_SIN", "IDF"], [pk])
        self.cp("dve", ST[:], ps[:, 0:256], [pk], ["B_ST"])
        self.cp("act", STb[:], ps[:, 0:256], [pk], ["B_STb"])

    def state_out(g, dst):
        ps, pk = self.bank()
        for a in range(2):
            self.tr(ps[:, a * 128:(a + 1) * 128], ST[:, a * 128:(a + 1) * 128], self.IDF[:], ["B_ST", "IDF"], [pk])
        self.cp("dve", SOUT[:].rearrange("p a n -> p (a n)"), ps[:, 0:256], [pk], ["B_SOUT"])
        self.dma("sp", dst.rearrange("(a h) p n -> (h p) a n", a=2), SOUT[:], ["B_SOUT"], ())

    import os as _os
    NG = int(_os.environ.get('BDBG_G', '8')); TLIST = [int(c) for c in _os.environ.get('BDBG_T', '01234')]; NSQ = int(_os.environ.get('BDBG_S', '16'))
    load(0)
    load_x(0)
    load_o(0)
    for g in range(NG):
        b = g % 2
        chs = [2 * g, 2 * g + 1, 16 + g, 24 + g]
        for ti, (c0, w) in enumerate(TT):
            if ti not in TLIST:
                continue
            if ti == 0:
                self.memset("dve", XF[:, :, 0:3], 0.0, [("B_XF", "st")])
            elif ti < 4:
                self.cp("dve", XF[:, :, 0:3], XF[:, :, 512:515], ["B_XF"], [("B_XF", "st")])
            else:
                for q in range(4):
                    self.cp("dve", XFSv[q][:, :, 0:3], CS0[:, chs[q], :].rearrange("p (s k) -> p s k", k=3), ["B_CS0"], [("B_XFS", "st")])
            for q in range(4):
                ps, pk = self.bank()
                for kc in range(8):
                    self.mm(self.v3(ps[:, :w], ti), WXBC[b][:, kc, q * 128:(q + 1) * 128], self.uv(kc, ti), kc == 0, kc == 7,
                            ["B_WXBC", ("U", (kc, ti))], [pk])
                if ti < 4:
                    self.cp("act", XF[:, q, 3:515], ps[:, :], [pk], [("B_XF", q)])
                else:
                    self.cp("act", XFSv[q][:, :, 3:11], self.v3(ps[:, :w], ti), [pk], [("B_XFS", q)])
            if ti == TLIST[-1] and g + 1 < NG:
                load_x(g + 1)
            for q in range(4):
                ch = chs[q]
                cw = [self.pf("ssm_conv_w", k * 4 + ch // 8, ch % 8) for k in range(4)]
                cb = self.pf("ssm_conv_b", ch // 8, ch % 8)
                if ti < 4:
                    dst = XCf[:, q, :]
                    srcf = lambda k, q=q: XF[:, q, k:k + 512]
                    rk = "B_XF"
                else:
                    dst = XCf[:, q, 0:128].rearrange("p (s t) -> p s t", t=8)
                    srcf = lambda k, q=q: XFSv[q][:, :, k:k + 8]
                    rk = "B_XFS"
                self.ts("dve", dst, srcf(0), cw[0], cb, ALU.mult, ALU.add, [rk, "PF"], [("B_XCf", q)])
                for k in range(1, 4):
                    self.stt(dst, srcf(k), cw[k], dst, ALU.mult, ALU.add, [rk, ("B_XCf", q), "PF"], [("B_XCf", q)])
                self.act(XCf[:, q, :w], XCf[:, q, :w], AF.Silu, [("B_XCf", q)], [("B_XCf", q)])
                if q >= 2:
                    self.cp("pool", BCb[:, q - 2, :w], XCf[:, q, :w], [("B_XCf", q)], ["B_BCb"])
            if ti == 3:
                for q in range(4):
                    self.cp("pool", CST[:, chs[q], 0:3], XF[:, q, 512:515], ["B_XF"], ["B_CST"])
            if ti == 4:
                for q in range(4):
                    self.cp("pool", CST[:, chs[q], 3:].rearrange("p (s k) -> p s k", k=3), XFSv[q][:, :, 8:11], ["B_XFS"], ["B_CST"])
            if ti < 4:
                if ti == 0:
                    self.memset("dve", ST[:], 0.0, ["B_ST"])
                    self.memset("pool", STb[:], 0.0, ["B_STb"])
                for ck in range(4):
                    chunk(g, b, ck * 128, 128, 1 + c0 + ck * 128)
                if ti == 3:
                    state_out(g, self.O["ss_p"][4 * g:4 * g + 4])
            else:
                chunk(g, b, 0, 128, 0, sample=True)
            for n in range(8):
                ps, pk = self.bank()
                for hf in range(2):
                    self.mm(ps[:, :w], WOUT[b][:, hf, n * 128:(n + 1) * 128], YGT[:, hf, :w], hf == 0, hf == 1, ["B_WOUT", "B_YGT"], [pk])
                self.tt("dve", self.X[:, n, c0:c0 + w], self.X[:, n, c0:c0 + w], ps[:, :w], ALU.add, [pk, ("X", (n, ti))], [("X", (n, ti))])
        if g + 1 < NG:
            load_o(g + 1)
            load(g + 1)
    pass
    for r in range(4):
        self.fm_to_rows(CST[:, r * 8:(r + 1) * 8, 0:3], "B_CST", 3, self.O["sc_p"][:, r * D:(r + 1) * D])
        self.fm_to_rows(CST[:, r * 8:(r + 1) * 8, 3:], "B_CST", NS * 3, self.O["sc_s"].rearrange("s k d -> (s k) d")[:, r * D:(r + 1) * D])


KB.layer_B = layer_B


def layer_C(self, l, ic):
    I, nc = self.I, self.nc
    E05 = float(np.exp(-0.5))
    SH0 = self.sb("C_SH0", [128, 8, NS]); USH = self.sb("C_USH", [128, 8, 1 + NS])
    self.rows_to_fm(I["st_rw_shift"][ic], NS, SH0, "C_SH0")
    Us = self.U[:, :, 1 + NTP:UC].rearrange("p c (s t) -> p c s t", t=9)
    self.cp("dve", Us[:, :, :, 0], SH0[:], ["C_SH0"], [("U", "shifts")])
    self.norm_to_U("norm_mix", l, shift_out=USH)
    XD = nc.dram_tensor("c_xspill", [128, 8, NT], F32).ap()
    import os as _os2
    CHUNKED = bool(int(_os2.environ.get("C_CHUNKED", "1")))
    HM = () if CHUNKED else ("kk", "w", "nq", "k", "r")
    SCR = {k: nc.dram_tensor("c_scr_" + k, [NT, D], F32).ap() for k in ("kk", "w", "nq", "k", "r", "v", "g", "o") if k not in HM}
    SCRH = {k: nc.dram_tensor("c_scrh_" + k, [16, NT, 64], F32).ap() for k in HM}

    def scr_tm(nm, rows):
        if nm in SCRH:
            return SCRH[nm][:, rows, :].rearrange("h t k -> t h k")
        return SCR[nm][rows, :].rearrange("t (h k) -> t h k", k=64)

    self.P.barrier()
    self.dma("sp", XD, self.X[:], ["X"], ["XD"])
    self.P.barrier()

    def xa(i):
        return self.X[:, i // 2, (i % 2) * D:(i % 2 + 1) * D], ("X", "a%d" % i)

    def h16(ap):
        return ap.rearrange("p (h k) -> p h k", k=64)

    def b16(ap):
        return ap.unsqueeze(2).to_broadcast([128, 16, 64])

    def red(out, in_, reads, writes):
        self.P.op("dve", lambda e: e.tensor_reduce(out=out, in_=in_, axis=AX.X, op=ALU.add), reads, writes)

    def pbc(i, src):
        a, k = xa(i)
        self.dma("sp", a, src.partition_broadcast(128), (), [k])

    with self.scope():
        pbc(0, I["rwkv_w0"][ic:ic + 1, :]); pbc(1, I["rwkv_a0"][ic:ic + 1, :]); pbc(2, I["rwkv_k_k"][ic:ic + 1, :]); pbc(3, I["rwkv_k_a"][ic:ic + 1, :])
        WS = [self.sb("C_WS%d" % i, [128, 8, D], BF16) for i in range(3)]
        for s_ in range(3):
            self.dma("pool", WS[s_][:], I["rwkv_w_rkv"][ic, s_].rearrange("(kc p) n -> p kc n", p=128), (), [("C_WS", s_)])
        W1 = self.sb("C_W1", [128, 8, 256], BF16)
        W2w = self.sb("C_W2w", [64, D], BF16); W2a = self.sb("C_W2a", [64, D], BF16); W2g = self.sb("C_W2g", [128, D], BF16)
        for (c0, n, nm) in ((0, 64, "rwkv_w_w1"), (64, 64, "rwkv_w_a1"), (128, 128, "rwkv_w_g1")):
            self.dma("pool", W1[:, :, c0:c0 + n], I[nm][ic].rearrange("(kc p) n -> p kc n", p=128), (), ["C_W1"])
        self.dma("pool", W2w[:], I["rwkv_w_w2"][ic], (), ["C_W2w"]); self.dma("pool", W2a[:], I["rwkv_w_a2"][ic], (), ["C_W2a"])
        self.dma("pool", W2g[:], I["rwkv_w_g2"][ic], (), ["C_W2g"])
        Dd = self.sb("C_D", [128, 8, 128]); XM = [self.sb("C_XM%d" % i, [128, 8, 128], BF16) for i in range(2)]
        TW = self.sb("C_TW", [64, 128], BF16); TA = self.sb("C_TA", [64, 128], BF16); TG = self.sb("C_TG", [128, 128], BF16)
        SS = self.sb("C_SS", [128, 16])
        (Pw0, kw0), (Pa0, ka0), (Pkk, kkk), (Pka, kka) = xa(0), xa(1), xa(2), xa(3)
        (Rt, kR), (Kt, kK), (Vt, kV), (Wt, kW), (At, kA), (KKt, kKK), (Gt, kG), (T1, kT1), (T2, kT2) = [xa(i) for i in range(7, 16)]
        wsn = 0
        for ti in range(17):
            tok0 = 128 * ti
            if ti < 16:
                cur = self.U[:, :, 1 + tok0:1 + tok0 + 128]; prev = self.U[:, :, tok0:tok0 + 128]
                dv = lambda a: a
                ukeys = [("U", (c, ti // 4)) for c in range(8)]
            else:
                cur = Us[:, :, :, 1:9]; prev = Us[:, :, :, 0:8]
                dv = lambda a: a.rearrange("p c (s t) -> p c s t", t=8) if len(a.shape) == 3 else a.rearrange("p (s t) -> p s t", t=8)
                ukeys = [("U", (c, 4)) for c in range(8)] + [("U", "shifts")]
            if ti % 4 == 0 and ti > 0 and ti < 16:
                ukeys = ukeys + [("U", (c, ti // 4 - 1)) for c in range(8)]
            if ti == 0:
                ukeys = ukeys + [("U", "shiftp")]
            self.tt("dve", dv(Dd[:]), prev, cur, ALU.subtract, ukeys, ["C_D"])

            def mix(s):
                xm = XM[s % 2]; key = "C_XM%d" % (s % 2)
                for kc in range(8):
                    self.stt(dv(xm[:, kc, :]), dv(Dd[:, kc, :]), self.pf("rwkv_mu", s, kc), cur[:, kc], ALU.mult, ALU.add, ["C_D", "PF"] + ukeys, [key])
                return xm, key

            def proj_tm(s, dst, dkey):
                b = s
                xm, key = mix(s)
                for nb in range(2):
                    ps, pk = self.bank()
                    for kc in range(8):
                        self.mm(ps[:, :], xm[:, kc, :], WS[b][:, kc, nb * 512:(nb + 1) * 512], kc == 0, kc == 7, [key, ("C_WS", b)], [pk])
                    self.cp("act", dst[:, nb * 512:(nb + 1) * 512], ps[:, :], [pk], [dkey])

            proj_tm(0, Rt, kR); proj_tm(1, Kt, kK); proj_tm(2, Vt, kV)
            for (s, c0, n, fn, dst, dk) in ((3, 0, 64, AF.Tanh, TW, "C_TW"), (4, 64, 64, AF.Copy, TA, "C_TA"), (5, 128, 128, AF.Sigmoid, TG, "C_TG")):
                xm, key = mix(s)
                ps, pk = self.bank()
                for kc in range(8):
                    self.mm(ps[0:n, 0:128], W1[:, kc, c0:c0 + n], xm[:, kc, :], kc == 0, kc == 7, [key, "C_W1"], [pk])
                self.act(dst[0:n, :], ps[0:n, 0:128], fn, [pk], [dk])
            for nb in range(2):
                cs_ = slice(nb * 512, (nb + 1) * 512)
                ps, pk = self.bank()
                self.mm(ps[:, :], TW[0:64, :], W2w[0:64, cs_], True, True, ["C_TW", "C_W2w"], [pk])
                self.tt("dve", T1[:, cs_], ps[:, :], Pw0[:, cs_], ALU.add, [pk, kw0], [kT1])
                ps, pk = self.bank()
                self.mm(ps[:, :], TA[0:64, :], W2a[0:64, cs_], True, True, ["C_TA", "C_W2a"], [pk])
                self.tt("dve", At[:, cs_], ps[:, :], Pa0[:, cs_], ALU.add, [pk, ka0], [kA])
                ps, pk = self.bank()
                self.mm(ps[:, :], TG[:, :], W2g[:, cs_], True, True, ["C_TG", "C_W2g"], [pk])
                self.cp("act", Gt[:, cs_], ps[:, :], [pk], [kG])
            self.act(T1, T1, AF.Sigmoid, [kT1], [kT1])
            self.act(Wt, T1, AF.Exp, [kT1], [kW], scale=-E05)
            self.act(At, At, AF.Sigmoid, [kA], [kA])
            self.tt("dve", KKt, Kt, Pkk, ALU.mult, [kK, kkk], [kKK])
            self.tt("dve", T1, KKt, KKt, ALU.mult, [kKK], [kT1])
            red(SS[:], h16(T1), [kT1], ["C_SS"])
            self.act(SS[:], SS[:], AF.Sqrt, ["C_SS"], ["C_SS"])
            self.ts("dve", SS[:], SS[:], 1e-12, None, ALU.max, None, ["C_SS"], ["C_SS"])
            self.P.op("dve", lambda e, SS=SS: e.reciprocal(out=SS[:], in_=SS[:]), ["C_SS"], ["C_SS"])
            self.tt("dve", h16(KKt), h16(KKt), b16(SS[:]), ALU.mult, [kKK, "C_SS"], [kKK])
            self.stt(T1, At, -1.0, Pka, ALU.add, ALU.mult, [kA, kka], [kT1])
            self.ts("dve", T1, T1, 1.0, None, ALU.add, None, [kT1], [kT1])
            self.tt("dve", Kt, Kt, T1, ALU.mult, [kK, kT1], [kK])
            self.stt(T2, KKt, -1.0, At, ALU.mult, ALU.mult, [kKK, kA], [kT2])
            rows = slice(tok0, tok0 + 128)
            for (nm, a, k) in (("kk", KKt, kKK), ("w", Wt, kW), ("nq", T2, kT2), ("k", Kt, kK), ("r", Rt, kR), ("v", Vt, kV), ("g", Gt, kG)):
                self.dma("sp", scr_tm(nm, rows), h16(a), [k], [("SCR", nm)])

    if self.debug:
        for i_, nm_ in enumerate(("kk", "w", "nq", "k", "r", "v", "g")):
            self.dma("sp", self.O["dbg_scr"][i_].rearrange("t (h k) -> t h k", k=64), scr_tm(nm_, slice(0, NT)), [("SCR", nm_)], ())

    def phase2_chunked():
        LNE = float(np.log(1.0))
        f32 = F32
        def blockmask(ap3, blk):
            self.P.op("pool", lambda e: e.affine_select(out=ap3, in_=ap3, pattern=[[-blk, 128 // blk], [0, blk]], compare_op=ALU.is_ge, fill=0.0,
                                                        base=0, channel_multiplier=1), ["C_MK"], ["C_MK"])
            self.P.op("pool", lambda e: e.affine_select(out=ap3, in_=ap3, pattern=[[blk, 128 // blk], [0, blk]], compare_op=ALU.is_ge, fill=0.0,
                                                        base=blk - 1, channel_multiplier=-1), ["C_MK"], ["C_MK"])
        MK = {}
        for blk in (32, 8):
            TRIc = self.sb("C_TRIc%d" % blk, [128, 128]); LOWs = self.sb("C_LOWs%d" % blk, [128, 128]); M3 = self.sb("C_M3_%d" % blk, [128, 3, 128])
            MSs = self.sb("C_MSs%d" % blk, [128, 128])
            self.memset("pool", TRIc[:], 1.0, ["C_MK"])
            self.P.op("pool", lambda e, T=TRIc: e.affine_select(out=T[:], in_=T[:], pattern=[[1, 128]], compare_op=ALU.is_ge, fill=0.0, base=0,
                                                                channel_multiplier=-1), ["C_MK"], ["C_MK"])
            blockmask(TRIc[:].rearrange("p (b t) -> p b t", t=blk), blk)
            self.memset("pool", LOWs[:], 1.0, ["C_MK"])
            self.P.op("pool", lambda e, T=LOWs: e.affine_select(out=T[:], in_=T[:], pattern=[[-1, 128]], compare_op=ALU.is_ge, fill=0.0, base=-1,
                                                                channel_multiplier=1), ["C_MK"], ["C_MK"])
            blockmask(LOWs[:].rearrange("p (b t) -> p b t", t=blk), blk)
            self.tt("pool", MSs[:], TRIc[:], self.IDF[:], ALU.subtract, ["C_MK", "IDF"], ["C_MK"])
            self.cp("pool", M3[:, 0, :], TRIc[:], ["C_MK"], ["C_MK"])
            self.cp("pool", M3[:, 1, :], MSs[:], ["C_MK"], ["C_MK"])
            self.cp("pool", M3[:, 2, :], TRIc[:], ["C_MK"], ["C_MK"])
            MK[blk] = (TRIc, LOWs, MSs, M3)
        SEQM = self.sb("C_SEQM", [128, 16])
        self.memset("pool", SEQM[:], 1.0, ["C_MK"])
        self.P.op("pool", lambda e: e.affine_select(out=SEQM[:], in_=SEQM[:], pattern=[[-8, 16]], compare_op=ALU.is_ge, fill=0.0, base=0,
                                                    channel_multiplier=1), ["C_MK"], ["C_MK"])
        self.P.op("pool", lambda e: e.affine_select(out=SEQM[:], in_=SEQM[:], pattern=[[8, 16]], compare_op=ALU.is_ge, fill=0.0, base=7,
                                                    channel_multiplier=-1), ["C_MK"], ["C_MK"])
        CHM = self.sb("C_CHM", [128, 4])
        self.memset("pool", CHM[:], 1.0, ["C_MK"])
        self.P.op("pool", lambda e: e.affine_select(out=CHM[:], in_=CHM[:], pattern=[[-32, 4]], compare_op=ALU.is_ge, fill=0.0, base=0,
                                                    channel_multiplier=1), ["C_MK"], ["C_MK"])
        self.P.op("pool", lambda e: e.affine_select(out=CHM[:], in_=CHM[:], pattern=[[32, 4]], compare_op=ALU.is_ge, fill=0.0, base=31,
                                                    channel_multiplier=-1), ["C_MK"], ["C_MK"])
        PRT = self.sb("C_PRT", [128, 8, 2, 128], BF16); QT = self.sb("C_QT", [128, 8, 128], BF16); KT = self.sb("C_KT", [128, 8, 128], BF16)
        QZ = self.sb("C_QZ", [128, 16, 128], BF16); KZ = self.sb("C_KZ", [128, 16, 128], BF16)
        Vb = self.sb("C_Vb", [128, D], BF16); Vm = self.sb("C_Vm", [128, D], BF16)
        ECLE = self.sb("C_ECLE", [128, 8, 16])
        XN = self.sb("C_XN", [128, 8, 128]); ZN = self.sb("C_ZN", [128, 8, 128]); TTf = self.sb("C_TTf", [128, 8, 128])
        MB = self.sb("C_MB", [128, 16, 3, 128], BF16); TTb = self.sb("C_TTb", [128, 16, 128], BF16)
        Wsb = self.sb("C_Wsb", [128, D]); O0 = self.sb("C_O0", [128, D])
        GWb = self.sb("C_GWb", [128, D], BF16); Ub = self.sb("C_Ub", [128, D], BF16)
        S = self.sb("C_S2", [128, 8, 64]); Sb = self.sb("C_Sb", [128, 16, 64], BF16)
        SbV = Sb[:].rearrange("p (j e) v -> p j e v", e=2)

        def write_sb(src3, keys):
            self.cp("act", SbV[0:64, :, 0, :], src3[0:64], keys, ["C_Sb"])
            self.cp("act", SbV[64:128, :, 1, :], src3[64:128], keys, ["C_Sb"])

        NAT = self.sb("C_NAT", [64, 16, 64])
        self.memset("pool", QZ[:], 0.0, ["C_QZ"]); self.memset("pool", KZ[:], 0.0, ["C_KZ"])
        (KKt, kKK), (NQt, kNQ), (Kt, kK), (Rt, kR), (Vt, kV), (LW, kLW), (CL, kCL), (E1, kE1), (Ot, kO) = [xa(i) for i in range(0, 9)]

        def state_load(dram):
            self.dma("sp", NAT[:], dram.rearrange("h v k -> v h k"), (), ["C_NAT"])
            ps, pk = self.bank()
            for j in range(8):
                self.tr(ps[:, j * 64:(j + 1) * 64], NAT[:, 2 * j:2 * j + 2, :].rearrange("v h k -> v (h k)"), self.IDF[0:64, 0:64], ["C_NAT", "IDF"], [pk])
            self.cp("dve", S[:].rearrange("p j v -> p (j v)"), ps[:, :], [pk], ["C_S2"])
            write_sb(ps[:, :].rearrange("p (j v) -> p j v", v=64), [pk])

        def state_store(dram):
            for half in range(2):
                ps, pk = self.bank()
                for jj in range(4):
                    j = half * 4 + jj
                    self.tr(ps[0:64, jj * 128:(jj + 1) * 128], S[:, j, :], self.IDF[:], ["C_S2", "IDF"], [pk])
                self.cp("dve", NAT[:, half * 8:(half + 1) * 8, :].rearrange("v h k -> v (h k)"), ps[0:64, :], [pk], ["C_NAT"])
            self.dma("act", dram.rearrange("h v k -> v h k"), NAT[:], ["C_NAT"], ())

        def tile(ti, blk, seqs):
            TRIc, LOWs, MSs, M3 = MK[blk]
            STOP = int(_os2.environ.get('C2_STOP', '99'))
            nch = 128 // blk
            rows = slice(128 * ti, 128 * ti + 128)
            for (nm, a, k) in (("kk", KKt, kKK), ("nq", NQt, kNQ), ("k", Kt, kK), ("r", Rt, kR), ("v", Vt, kV), ("w", LW, kLW)):
                self.dma("sp", h16(a), scr_tm(nm, rows), [("SCR", nm)], [k])
            self.act(LW, LW, AF.Ln, [kLW], [kLW])
            for nb in range(2):
                ps, pk = self.bank()
                self.mm(ps[:, :], TRIc[:], LW[:, nb * 512:(nb + 1) * 512], True, True, ["C_MK", kLW], [pk])
                self.cp("act", CL[:, nb * 512:(nb + 1) * 512], ps[:, :], [pk], [kCL])
            self.tt("dve", E1, CL, LW, ALU.subtract, [kCL, kLW], [kE1])
            self.act(E1, E1, AF.Exp, [kE1], [kE1])
            self.tt("dve", KKt, KKt, E1, ALU.mult, [kKK, kE1], [kKK])
            self.act(E1, CL, AF.Exp, [kCL], [kE1], scale=-1.0)
            self.tt("dve", NQt, NQt, E1, ALU.mult, [kNQ, kE1], [kNQ])
            self.tt("dve", Kt, Kt, E1, ALU.mult, [kK, kE1], [kK])
            self.act(E1, CL, AF.Exp, [kCL], [kE1])
            self.tt("dve", Rt, Rt, E1, ALU.mult, [kR, kE1], [kR])
            self.cp("pool", Vb[:], Vt, [kV], ["C_Vb"])
            for (Zb, src, k, zk) in ((QZ, NQt, kNQ, "C_QZ"), (KZ, Kt, kK, "C_KZ")):
                zv = Zb[:].rearrange("p (j e) f -> p j (e f)", e=2)
                sv = src.rearrange("p (j e d) -> p j e d", e=2, d=64)
                self.cp("pool", zv[:, :, 0:64], sv[:, :, 0, :], [k], [zk])
                self.cp("pool", zv[:, :, 192:256], sv[:, :, 1, :], [k], [zk])
            if STOP <= 1:
                return
            for (src, k, dstf, dk) in ((KKt, kKK, lambda j: PRT[:, j, 0, :], "C_PRT"), (Rt, kR, lambda j: PRT[:, j, 1, :], "C_PRT"),
                                       (NQt, kNQ, lambda j: QT[:, j, :], "C_QT"), (Kt, kK, lambda j: KT[:, j, :], "C_KT")):
                for half in range(2):
                    ps, pk = self.bank()
                    for jj in range(4):
                        j = half * 4 + jj
                        self.tr(ps[:, jj * 128:(jj + 1) * 128], src[:, j * 128:(j + 1) * 128], self.IDF[:], [k, "IDF"], [pk])
                    for jj in range(4):
                        j = half * 4 + jj
                        self.cp("act" if jj % 2 else "dve", dstf(j), ps[:, jj * 128:(jj + 1) * 128], [pk], [(dk, j)])
            for half in range(2):
                ps, pk = self.bank()
                for jj in range(4):
                    j = half * 4 + jj
                    self.tr(ps[:, jj * 128:(jj + 1) * 128], E1[:, j * 128:(j + 1) * 128], self.IDF[:], [kE1, "IDF"], [pk])
                self.cp("dve", ECLE[:, half * 4:(half + 1) * 4, 0:nch], ps[:, :].rearrange("p (a c b) -> p a c b", a=4, b=blk)[:, :, :, blk - 1], [pk], ["C_ECLE"])
            if STOP <= 2:
                return
            for hg in range(2):
                for hh in range(8):
                    h = hg * 8 + hh
                    j, e = h // 2, h % 2
                    pr = slice(e * 64, (e + 1) * 64)
                    ps, pk = self.bank()
                    rhsPR = PRT[pr, j, :, :].rearrange("p a t -> p (a t)")
                    self.mm(ps[:, 0:256], QT[pr, j, :], rhsPR, True, True, [("C_QT", j), ("C_PRT", j)], [pk])
                    self.mm(ps[:, 256:512], KT[pr, j, :], rhsPR, True, True, [("C_KT", j), ("C_PRT", j)], [pk])
                    ps2, pk2 = self.bank()
                    self.mm(ps2[:, 0:128], PRT[pr, j, 0, :], QT[pr, j, :], True, True, [("C_QT", j), ("C_PRT", j)], [pk2])
                    self.tt("dve", ZN[:, hh, :], ps[:, 0:128], MSs[:], ALU.mult, [pk, "C_MK"], [("C_ZN", hh)])
                    self.tt("dve", MB[:, h, :, :], ps[:, 128:512].rearrange("p (a t) -> p a t", a=3), M3[:], ALU.mult, [pk, "C_MK"], [("C_MB", h)])
                    self.tt("dve", XN[:, hh, :], ps2[:, 0:128], LOWs[:], ALU.mult, [pk2, "C_MK"], [("C_XN", hh)])
                    self.tt("pool", TTf[:, hh, :], ZN[:, hh, :], self.IDF[:], ALU.add, [("C_ZN", hh), "IDF"], [("C_TT", hh)])
                for n in range(4 if STOP > 3 else 0):
                    for hh in range(8):
                        ps, pk = self.bank()
                        self.mm(ps[:, 0:128], ZN[:, hh, :], XN[:, hh, :], True, True, [("C_ZN", hh), ("C_XN", hh)], [pk])
                        if n < 3:
                            self.mm(ps[:, 128:256], XN[:, hh, :], ZN[:, hh, :], True, True, [("C_ZN", hh), ("C_XN", hh)], [pk])
                        self.cp("act", XN[:, hh, :], ps[:, 0:128], [pk], [("C_XN", hh)])
                        if n < 3:
                            self.cp("act", ZN[:, hh, :], ps[:, 128:256], [pk], [("C_ZN", hh)])
                        ps2, pk2 = self.bank()
                        self.mm(ps2[:, 0:128], XN[:, hh, :], TTf[:, hh, :], True, True, [("C_XN", hh), ("C_TT", hh)], [pk2])
                        self.tt("dve", TTf[:, hh, :], TTf[:, hh, :], ps2[:, 0:128], ALU.add, [pk2, ("C_TT", hh)], [("C_TT", hh)])
                for hh in range(8):
                    self.cp("act", TTb[:, hg * 8 + hh, :], TTf[:, hh, :], [("C_TT", hh)], [("C_TTb", hg * 8 + hh)])
            if STOP <= 4:
                return
            for (ai, dst, dk) in ((1, Wsb, "C_Wsb"), (2, O0, "C_O0")):
                for half in range(2):
                    ps, pk = self.bank()
                    for hh in range(8):
                        h = half * 8 + hh
                        self.mm(ps[:, hh * 64:(hh + 1) * 64], MB[:, h, ai, :], Vb[:, h * 64:(h + 1) * 64], True, True, [("C_MB", h), "C_Vb"], [pk])
                    self.cp("act" if half else "dve", dst[:, half * 512:(half + 1) * 512], ps[:, :], [pk], [dk])
            self.cp("pool", Ot, O0[:], ["C_O0"], [kO])
            if STOP <= 5:
                return
            nrounds = nch if seqs is None else len(seqs)
            for c_ in range(nrounds):
                if seqs is None:
                    ecol = c_; mcol = CHM[:, c_:c_ + 1]
                else:
                    s_ = seqs[c_]
                    ecol = s_; mcol = SEQM[:, s_:s_ + 1]
                    state_load(I["st_rw_wkv"][ic, s_])
                self.ts("pool", Vm[:], Vb[:], mcol, None, ALU.mult, None, ["C_Vb", "C_MK"], ["C_Vm"])
                pG = [self.bank() for _ in range(2)]; pR = [self.bank() for _ in range(2)]
                for h in range(16):
                    j = h // 2
                    bq, cq = h // 8, (h % 8) * 64
                    self.mm(pG[bq][0][:, cq:cq + 64], PRT[:, j, 0, :], Sb[:, h, :], True, True, [("C_PRT", j), "C_Sb"], [pG[bq][1]])
                    self.mm(pR[bq][0][:, cq:cq + 64], PRT[:, j, 1, :], Sb[:, h, :], True, True, [("C_PRT", j), "C_Sb"], [pR[bq][1]])
                for bq in range(2):
                    cs_ = slice(bq * 512, (bq + 1) * 512)
                    self.tt("dve", GWb[:, cs_], pG[bq][0][:, :], Wsb[:, cs_], ALU.add, [pG[bq][1], "C_Wsb"], [("C_GWb", bq)])
                pU = [self.bank() for _ in range(2)]
                for h in range(16):
                    bq, cq = h // 8, (h % 8) * 64
                    self.mm(pU[bq][0][:, cq:cq + 64], TTb[:, h, :], GWb[:, h * 64:(h + 1) * 64], True, True, [("C_TTb", h), ("C_GWb", bq)], [pU[bq][1]])
                for bq in range(2):
                    cs_ = slice(bq * 512, (bq + 1) * 512)
                    self.act(Ub[:, cs_], pU[bq][0][:, :], AF.Copy, [pU[bq][1], "C_MK"], [("C_Ub", bq)], scale=mcol)
                    self.stt(Ot[:, cs_], pR[bq][0][:, :], mcol, Ot[:, cs_], ALU.mult, ALU.add, [pR[bq][1], kO, "C_MK"], [kO])
                pB = [self.bank() for _ in range(2)]
                for h in range(16):
                    bq, cq = h // 8, (h % 8) * 64
                    self.mm(pB[bq][0][:, cq:cq + 64], MB[:, h, 0, :], Ub[:, h * 64:(h + 1) * 64], True, True, [("C_MB", h), ("C_Ub", bq)], [pB[bq][1]])
                for bq in range(2):
                    cs_ = slice(bq * 512, (bq + 1) * 512)
                    self.stt(Ot[:, cs_], pB[bq][0][:, :], mcol, Ot[:, cs_], ALU.mult, ALU.add, [pB[bq][1], kO, "C_MK"], [kO])
                pS, kS = self.bank()
                for j in range(8):
                    for e in range(2):
                        h = 2 * j + e
                        self.mm(pS[:, j * 64:(j + 1) * 64], QZ[:, h, :], Ub[:, h * 64:(h + 1) * 64], e == 0, False, ["C_QZ", ("C_Ub", h // 8)], [kS])
                        self.mm(pS[:, j * 64:(j + 1) * 64], KZ[:, h, :], Vm[:, h * 64:(h + 1) * 64], False, e == 1, ["C_KZ", "C_Vm"], [kS])
                Sf = S[:].rearrange("p j v -> p (j v)")
                self.tt("dve", Sf, Sf, pS[:, :], ALU.add, [kS, "C_S2"], ["C_S2"])
                self.tt("dve", S[:], S[:], ECLE[:, :, ecol:ecol + 1].to_broadcast([128, 8, 64]), ALU.mult, ["C_S2", "C_ECLE"], ["C_S2"])
                write_sb(S, ["C_S2"])
                if seqs is not None:
                    state_store(self.O["rw_s"][s_])
            self.dma("act", scr_tm("o", rows), h16(Ot), [kO], [("SCR", "o")])

        self.memset("dve", S[:], 0.0, ["C_S2"]); self.memset("pool", Sb[:], 0.0, ["C_Sb"])
        for ti in range(16):
            tile(ti, 32, None)
        state_store(self.O["rw_p"])
        tile(16, 8, list(range(NS)))

    if CHUNKED:
        with self.scope():
            phase2_chunked()
    else:
        with self.scope():
            S = self.sb("C_S", [128, 8, 64]); Tm = self.sb("C_Tm", [128, 8, 64]); KV = [self.sb("C_KV%d" % i, [128, 8, 64]) for i in range(2)]
            SA = self.sb("C_SA", [128, 8])
            names = ("kk", "w", "nq", "k", "r")
            ST0 = {nm: self.X[:, i, 0:2048].rearrange("p (t k) -> p t k", k=64) for i, nm in enumerate(names)}
            ST1 = {nm: self.sb("C_ST1" + nm, [128, 32, 64]) for nm in names}
            VS = [self.sb("C_VS%d" % i, [128, 32, 8]) for i in range(2)]
            OS = [self.sb("C_OS%d" % i, [128, 32, 8]) for i in range(2)]

            def bk(ap):
                return ap.unsqueeze(1).to_broadcast([128, 8, 64])

            def bv(ap):
                return ap.unsqueeze(2).to_broadcast([128, 8, 64])

            nsub = 0
            import os as _os
            NOSAME = bool(int(_os.environ.get("C_NOSAME", "1")))
            SP = self.PS[7][:, :].rearrange("p (a k) -> p a k", k=64)
            SK = ("PS", 7)

            def run_segment(row0, nsteps):
                nonlocal nsub
                for t0 in range(0, nsteps, 32):
                    n = min(32, nsteps - t0)
                    sb_ = nsub % 2; nsub += 1
                    stg = ST0 if sb_ == 0 else ST1
                    skey = "C_STG%d" % sb_
                    r0 = row0 + t0
                    for nm in names:
                        src = SCRH[nm][:, r0:r0 + n, :]
                        for vb in range(8):
                            self.dma("sp", stg[nm][vb * 16:(vb + 1) * 16, 0:n, :], src, [("SCR", nm)], [(skey, nm)] + ([("X", "stg")] if sb_ == 0 else []))
                    srcv = SCR["v"][r0:r0 + n, :].rearrange("t (h v) -> h t v", v=64)
                    for vb in range(8):
                        self.dma("sp", VS[sb_][vb * 16:(vb + 1) * 16, 0:n, :], srcv[:, :, vb * 8:(vb + 1) * 8], [("SCR", "v")], [(skey, "v")])
                    xk = [("X", "stg")] if sb_ == 0 else []
                    for i in range(n):
                        kvb = i % 2
                        self.tt("pool", KV[kvb][:], bk(stg["k"][:, i, :]), bv(VS[sb_][:, i, :]), ALU.mult, [(skey, "k"), (skey, "v")] + xk, ["C_KV%d" % kvb])
                        self.P.nosame = NOSAME
                        self.tt("dve", Tm[:], SP[:], bk(stg["kk"][:, i, :]), ALU.mult, [SK, (skey, "kk")] + xk, ["C_Tm"])
                        red(SA[:], Tm[:], ["C_Tm"], ["C_SA"])
                        self.tt("dve", SP[:], SP[:], bk(stg["w"][:, i, :]), ALU.mult, [SK, (skey, "w")] + xk, [SK])
                        self.tt("dve", Tm[:], bk(stg["nq"][:, i, :]), bv(SA[:]), ALU.mult, ["C_SA", (skey, "nq")] + xk, ["C_Tm"])
                        self.tt("dve", SP[:], SP[:], Tm[:], ALU.add, [SK, "C_Tm"], [SK])
                        self.tt("dve", SP[:], SP[:], KV[kvb][:], ALU.add, [SK, "C_KV%d" % kvb], [SK])
                        self.tt("dve", Tm[:], SP[:], bk(stg["r"][:, i, :]), ALU.mult, [SK, (skey, "r")] + xk, ["C_Tm"])
                        red(OS[sb_][:, i, :], Tm[:], ["C_Tm"], ["C_OS%d" % sb_])
                        self.P.nosame = False
                    dsto = SCR["o"][r0:r0 + n, :].rearrange("t (h v) -> h t v", v=64)
                    for vb in range(8):
                        self.dma("act", dsto[:, :, vb * 8:(vb + 1) * 8], OS[sb_][vb * 16:(vb + 1) * 16, 0:n, :], ["C_OS%d" % sb_], [("SCR", "o")])

            def state_io(dram, load):
                for vb in range(8):
                    d = dram[:, vb * 8:(vb + 1) * 8, :]
                    if load:
                        self.dma("sp", S[vb * 16:(vb + 1) * 16, :, :], d, (), ["C_S"])
                    else:
                        self.dma("act", d, S[vb * 16:(vb + 1) * 16, :, :], ["C_S"], ())

            self.memset("dve", S[:], 0.0, ["C_S"])
            self.cp("dve", SP[:], S[:], ["C_S"], [SK])
            run_segment(0, NTP)
            self.cp("dve", S[:], SP[:], [SK], ["C_S"])
            state_io(self.O["rw_p"], False)
            for s in range(NS):
                state_io(I["st_rw_wkv"][ic, s], True)
                self.cp("dve", SP[:], S[:], ["C_S"], [SK])
                run_segment(NTP + 8 * s, 8)
                self.cp("dve", S[:], SP[:], [SK], ["C_S"])
                state_io(self.O["rw_s"][s], False)

    if self.debug:
        self.dma("sp", self.O["dbg_scr"][7], SCR["o"], [("SCR", "o")], ())
    with self.scope():
        pbc(0, I["rwkv_lnx_w"][ic:ic + 1, :]); pbc(1, I["rwkv_lnx_b"][ic:ic + 1, :]); pbc(2, I["rwkv_r_k"].rearrange("a h n -> a (h n)")[ic:ic + 1, :])
        (Plw, klw), (Plb, klb), (Prk, krk) = xa(0), xa(1), xa(2)
        (Ot, kO), (Rt, kR), (Kt, kK), (Vt, kV), (Gt, kG), (T1, kT1) = [xa(i) for i in range(7, 13)]
        WO = self.sb("C_WO", [128, 8, D], BF16)
        self.dma("pool", WO[:], I["rwkv_w_out"][ic].rearrange("(kc p) n -> p kc n", p=128), (), ["C_WO"])
        YT = self.sb("C_YT", [128, 8, 128], BF16); XT_ = self.sb("C_XTL", [128, 8, 128])
        M1 = self.sb("C_M1", [128, 16]); M2 = self.sb("C_M2", [128, 16])
        for ti in range(17):
            rows = slice(128 * ti, 128 * ti + 128)
            for (nm, a, k) in (("o", Ot, kO), ("r", Rt, kR), ("k", Kt, kK), ("v", Vt, kV), ("g", Gt, kG)):
                self.dma("sp", h16(a), scr_tm(nm, rows), [("SCR", nm)], [k])
            self.dma("sp", XT_[:], XD[:, :, rows], ["XD"], ["C_XTL"])
            red(M1[:], h16(Ot), [kO], ["C_M1"])
            self.ts("dve", M1[:], M1[:], -1.0 / 64, None, ALU.mult, None, ["C_M1"], ["C_M1"])
            self.tt("dve", h16(Ot), h16(Ot), b16(M1[:]), ALU.add, [kO, "C_M1"], [kO])
            self.tt("dve", T1, Ot, Ot, ALU.mult, [kO], [kT1])
            red(M2[:], h16(T1), [kT1], ["C_M2"])
            self.act(M2[:], M2[:], AF.Ln, ["C_M2", ("C1", 3)], ["C_M2"], scale=1.0 / 64, bias=self.C1[:, 3:4])
            self.act(M2[:], M2[:], AF.Exp, ["C_M2"], ["C_M2"], scale=-0.5)
            self.tt("dve", h16(Ot), h16(Ot), b16(M2[:]), ALU.mult, [kO, "C_M2"], [kO])
            self.tt("dve", Ot, Ot, Plw, ALU.mult, [kO, klw], [kO])
            self.tt("dve", Ot, Ot, Plb, ALU.add, [kO, klb], [kO])
            self.tt("dve", T1, Rt, Kt, ALU.mult, [kR, kK], [kT1])
            self.tt("dve", T1, T1, Prk, ALU.mult, [kT1, krk], [kT1])
            red(M1[:], h16(T1), [kT1], ["C_M1"])
            self.tt("dve", h16(T1), h16(Vt), b16(M1[:]), ALU.mult, [kV, "C_M1"], [kT1])
            self.tt("dve", Ot, Ot, T1, ALU.add, [kO, kT1], [kO])
            self.tt("dve", Ot, Ot, Gt, ALU.mult, [kO, kG], [kO])
            for half in range(2):
                ps, pk = self.bank()
                for cc in range(4):
                    c = half * 4 + cc
                    self.tr(ps[:, cc * 128:(cc + 1) * 128], Ot[:, c * 128:(c + 1) * 128], self.IDF[:], [kO, "IDF"], [pk])
                self.cp("act", YT[:, half * 4:(half + 1) * 4, :].rearrange("p c t -> p (c t)"), ps[:, :], [pk], ["C_YT"])
            for n in range(8):
                ps, pk = self.bank()
                for c in range(8):
                    self.mm(ps[:, 0:128], WO[:, c, n * 128:(n + 1) * 128], YT[:, c, :], c == 0, c == 7, ["C_WO", "C_YT"], [pk])
                self.tt("dve", XT_[:, n, :], XT_[:, n, :], ps[:, 0:128], ALU.add, [pk, "C_XTL"], ["C_XTL"])
            self.dma("sp", XD[:, :, rows], XT_[:], ["C_XTL"], ["XD"])
    self.dma("sp", self.X[:], XD, ["XD"], ["X"])
    self.fm_to_rows(USH[:, :, 0:1], "C_USH", 1, self.O["rs_p"])
    self.fm_to_rows(USH[:, :, 1:], "C_USH", NS, self.O["rs_s"])


KB.layer_C = layer_C


def build(layers=(0, 1, 2, 3), debug=False, mlp=True):
    kb = KB(layers, debug)
    with kb.es:
        kb.setup()
        kb.load_x()
        for l in layers:
            with kb.scope():
                kind = l % 3
                if kind == 0:
                    kb.layer_A(l, l // 3)
                elif kind == 1:
                    kb.layer_B(l, 0)
                else:
                    kb.layer_C(l, 0)
            if mlp:
                with kb.scope():
                    kb.mlp(l)
        if not getattr(kb, '_skip_final', False):
            with kb.scope():
                kb.final_out()
        if debug:
            kb.dump_x()
        kb.P.emit()
    return kb.nc


def make_in_maps(inputs, cores):
    g = {k: np.ascontiguousarray(np.asarray(v, dtype=np.float32)) for k, v in inputs.items()}
    maps = []
    for i in cores:
        sl = slice(NS * i, NS * (i + 1))
        m = {
            "xp": g["x_prompt"][i], "xs": g["x_sample"][sl].reshape(NS * LS, D),
            "st_lru_conv": g["state_lru_conv"][:, sl], "st_lru_h": g["state_lru_h"][:, sl],
            "st_ssm_conv": g["state_ssm_conv"][:, sl], "st_ssm": g["state_ssm"][:, sl],
            "st_rw_shift": g["state_rwkv_shift"][:, sl], "st_rw_wkv": g["state_rwkv_wkv"][:, sl],
        }
        for k in IN_SPECS:
            if k in m:
                continue
            v = g[k]
            if k == "norm_final":
                v = v.reshape(1, D)
            m[k] = v
        maps.append({k: np.ascontiguousarray(v) for k, v in m.items()})
    return maps


def kernel(**inputs):
    nc = build()
    cores = list(range(8))
    res = run_bass_kernel_spmd(nc, make_in_maps(inputs, cores), core_ids=cores)
    r = res.results
    def cat(name, axis=0):
        return np.concatenate([x[name] for x in r], axis=axis)
    y_p = np.stack([x["y_p"] for x in r])
    y_s = cat("y_s").reshape(128, LS, D)
    lc_p = np.stack([x["lc_p"] for x in r], axis=1)
    lc_s = cat("lc_s", 1)
    lh_p = np.stack([x["lh_p"] for x in r], axis=1)
    lh_s = cat("lh_s", 1)
    sc_p = np.stack([x["sc_p"] for x in r])[None]
    sc_s = cat("sc_s")[None]
    ss_p = np.stack([x["ss_p"] for x in r])[None]
    ss_s = cat("ss_s")[None]
    rs_p = np.stack([x["rs_p"][0] for x in r])[None]
    rs_s = cat("rs_s")[None]
    rw_p = np.stack([x["rw_p"] for x in r])[None]
    rw_s = cat("rw_s")[None]
    outs = (y_p, y_s, lc_p, lc_s, lh_p, lh_s, sc_p, sc_s, ss_p, ss_s, rs_p, rs_s, rw_p, rw_s)
    return tuple(np.ascontiguousarray(o, dtype=np.float32) for o in outs)
```

```python
import numpy as np
import concourse.bass as bass
import concourse.mybir as mybir
from concourse.bass_utils import run_bass_kernel_spmd

F32 = mybir.dt.float32
BF16 = mybir.dt.bfloat16
I32 = mybir.dt.int32
AF = mybir.ActivationFunctionType
ALU = mybir.AluOpType
AX = mybir.AxisListType

ENGS = ("pe", "act", "dve", "pool", "sp")
SAME_ENGINE_SYNC = True
import os as _os_
NFUSE = int(_os_.environ.get('NFUSE', '1'))
FUSE_WAIT = bool(int(_os_.environ.get('FUSE_WAIT', '1')))
NOSAME_ENGS = tuple(x for x in _os_.environ.get('NOSAME_ENGS', '').split(',') if x)
N_DMA_SEMS = 12


class _St:
    __slots__ = ("w", "r")

    def __init__(self):
        self.w = None
        self.r = {}


class _Op:
    __slots__ = ("eng", "fn", "waits", "dma", "signal", "tok")


class Prog:
    def __init__(self, nc):
        self.nc = nc
        self.ops = {e: [] for e in ENGS}
        self.state = {}
        self.ndma = {e: 0 for e in ENGS}
        self.dma_toks = {}
        self.pending = {}
        self.nosame = False

    def barrier(self):
        toks = set()
        for e in ENGS:
            for o in reversed(self.ops[e]):
                if not o.dma:
                    toks.add(o.tok)
                    break
            n = self.ndma[e]
            for j in range(max(0, n - N_DMA_SEMS), n):
                toks.add(("dma", e, j))
        self.pending = {e: set(toks) for e in ENGS}

    def _states(self, key, create):
        if isinstance(key, tuple):
            buf, sub = key
        else:
            buf, sub = key, None
        d = self.state.setdefault(buf, {})
        if sub is None:
            if create and "*" not in d:
                d["*"] = _St()
            return list(d.values())
        out = []
        if "*" in d:
            out.append(d["*"])
        if sub not in d:
            if create:
                d[sub] = _St()
                out.append(d[sub])
        else:
            out.append(d[sub])
        return out

    def op(self, eng, fn, reads=(), writes=(), dma=False):
        ps_r = [k for k in reads if isinstance(k, tuple) and k[0] == "PS"]
        if ps_r:
            reads = [k for k in reads if not (isinstance(k, tuple) and k[0] == "PS")]
            writes = list(writes) + ps_r
        o = _Op()
        o.eng, o.fn, o.dma, o.signal = eng, fn, dma, False
        idx = len(self.ops[eng])
        if dma:
            j = self.ndma[eng]
            self.ndma[eng] += 1
            o.tok = ("dma", eng, j)
            self.dma_toks[(eng, j)] = o
        else:
            o.tok = ("eng", eng, idx)
        waits = set()
        for k in reads:
            for st in self._states(k, True):
                if st.w is not None:
                    waits.add(st.w)
        for k in writes:
            for st in self._states(k, True):
                if st.w is not None:
                    waits.add(st.w)
                for t in st.r.values():
                    waits.add(t)
        if self.pending.get(eng):
            waits |= self.pending.pop(eng)
        w2 = set()
        for t in waits:
            if t[0] == "eng" and t[1] == eng:
                if not dma and (eng in ("pe", "sp") or not SAME_ENGINE_SYNC or self.nosame or (NOSAME_ENGS and eng in NOSAME_ENGS)):
                    continue
            w2.add(t)
        o.waits = w2
        for k in reads:
            for st in self._states(k, True):
                st.r[o.tok if dma else o.tok[1]] = o.tok
        for k in writes:
            for st in self._states(k, True):
                st.w = o.tok
                st.r = {}
        self.ops[eng].append(o)
        return o

    def emit(self, final_wait_all=True):
        nc = self.nc
        tokmap = {}
        for e in ENGS:
            for i, o in enumerate(self.ops[e]):
                tokmap[o.tok] = o
        for e in ENGS:
            for o in self.ops[e]:
                for t in o.waits:
                    tokmap[t].signal = True
        import contextlib
        with contextlib.ExitStack() as es:
            EPOCH = 16000
            nsig = {e: sum(1 for o in self.ops[e] if o.signal and not o.dma) for e in ENGS}
            esem = {e: [es.enter_context(nc.semaphore("s_%s%d" % (e, i))) for i in range(nsig[e] // EPOCH + 1)] for e in ENGS if e != "sp"}
            dsem = {e: [es.enter_context(nc.semaphore("d_%s%d" % (e, i))) for i in range(N_DMA_SEMS)]
                    for e in ENGS if self.ndma[e] > 0}
            val = {}
            for e in ENGS:
                c = 0
                for o in self.ops[e]:
                    if o.dma:
                        j = o.tok[2]
                        val[o.tok] = (dsem[e][j % N_DMA_SEMS], 16 * (j // N_DMA_SEMS + 1))
                    elif o.signal:
                        val[o.tok] = (esem[e][c // EPOCH], c % EPOCH + 1)
                        c += 1
            self.maxcount = {}
            block = es.enter_context(nc.Block())

            def run(e, eng):
                seen = {}
                for o in self.ops[e]:
                    ws = []
                    for t in o.waits:
                        ws.append(val[t])
                    if o.dma:
                        j = o.tok[2]
                        if j >= N_DMA_SEMS:
                            ws.append(val[("dma", e, j - N_DMA_SEMS)])
                    need = []
                    for (s, v) in ws:
                        if seen.get(id(s), 0) >= v:
                            continue
                        seen[id(s)] = v
                        need = [(s2, v2) for (s2, v2) in need if s2 is not s] + [(s, v)]
                    fuse = FUSE_WAIT and need and not o.dma
                    self.stats = getattr(self, "stats", {})
                    self.stats[(e, len(need))] = self.stats.get((e, len(need)), 0) + 1
                    nf = min(len(need), NFUSE) if fuse else 0
                    for (s, v) in need[:len(need) - nf]:
                        eng.wait_ge(s, v)
                    ins = o.fn(eng)
                    for (s, v) in need[len(need) - nf:]:
                        ins._wait_ge(s, v)
                    if o.dma:
                        s, v = val[o.tok]
                        ins.then_inc(s, 16)
                    elif o.signal:
                        ins.then_inc(val[o.tok][0], 1)
                if final_wait_all:
                    n = self.ndma[e]
                    for j in range(max(0, n - N_DMA_SEMS), n):
                        s, v = val[("dma", e, j)]
                        if seen.get(id(s), 0) < v:
                            eng.wait_ge(s, v)
                            seen[id(s)] = v

            @block.tensor
            def _(eng):
                run("pe", eng)

            @block.scalar
            def _(eng):
                run("act", eng)

            @block.vector
            def _(eng):
                run("dve", eng)

            @block.gpsimd
            def _(eng):
                run("pool", eng)

            @block.sync
            def _(eng):
                run("sp", eng)
import contextlib


D = 1024
NTP = 2048
NS = 16
LS = 8
NT = NTP + NS * LS
TT = [(0, 512), (512, 512), (1024, 512), (1536, 512), (2048, 128)]
UC = 1 + NTP + NS * 9
XBC = 3 + NTP + NS * 11

PROW = {}


def _prow_layout():
    r = 0
    def add(name, n):
        nonlocal r
        PROW[name] = r
        r += n
    add("norm_mix", 4); add("norm_ffn", 4); add("norm_final", 1)
    add("lru_conv_w", 8); add("lru_conv_b", 2); add("lru_b_r", 2); add("lru_b_i", 2); add("lru_lambda", 2)
    add("ssm_norm_w", 2); add("ssm_conv_w", 16); add("ssm_conv_b", 4)
    add("rwkv_mu", 6); add("rwkv_w0", 1); add("rwkv_a0", 1); add("rwkv_k_k", 1); add("rwkv_k_a", 1)
    add("rwkv_lnx_w", 1); add("rwkv_lnx_b", 1); add("rwkv_r_k", 1)
    return r


NPROW = _prow_layout()

IN_SPECS = {
    "xp": [NTP, D], "xs": [NS * LS, D],
    "st_lru_conv": [2, NS, 3, D], "st_lru_h": [2, NS, D], "st_ssm_conv": [1, NS, 3, 4096],
    "st_ssm": [1, NS, 32, 64, 128], "st_rw_shift": [1, NS, D], "st_rw_wkv": [1, NS, 16, 64, 64],
    "norm_mix": [4, D], "norm_ffn": [4, D], "norm_final": [1, D],
    "lru_w_in": [2, D, 2048], "lru_conv_w": [2, 4, D], "lru_conv_b": [2, D], "lru_w_r": [2, 8, 128, 128],
    "lru_b_r": [2, D], "lru_w_i": [2, 8, 128, 128], "lru_b_i": [2, D], "lru_lambda": [2, D], "lru_w_out": [2, D, D],
    "ssm_w_in": [1, D, 6176], "ssm_conv_w": [1, 4, 4096], "ssm_conv_b": [1, 4096], "ssm_dt_bias": [1, 32],
    "ssm_a_log": [1, 32], "ssm_d": [1, 32], "ssm_norm_w": [1, 2048], "ssm_w_out": [1, 2048, D],
    "rwkv_mu": [1, 6, D], "rwkv_w_rkv": [1, 3, D, D], "rwkv_w0": [1, D], "rwkv_w_w1": [1, D, 64], "rwkv_w_w2": [1, 64, D],
    "rwkv_a0": [1, D], "rwkv_w_a1": [1, D, 64], "rwkv_w_a2": [1, 64, D], "rwkv_w_g1": [1, D, 128], "rwkv_w_g2": [1, 128, D],
    "rwkv_k_k": [1, D], "rwkv_k_a": [1, D], "rwkv_r_k": [1, 16, 64], "rwkv_lnx_w": [1, D], "rwkv_lnx_b": [1, D],
    "rwkv_w_out": [1, D, D], "ffn_w1": [4, D, 4096], "ffn_w2": [4, 4096, D],
}
OUT_SPECS = {
    "y_p": [NTP, D], "y_s": [NS * LS, D],
    "lc_p": [2, 3, D], "lc_s": [2, NS, 3, D], "lh_p": [2, D], "lh_s": [2, NS, D],
    "sc_p": [3, 4096], "sc_s": [NS, 3, 4096], "ss_p": [32, 64, 128], "ss_s": [NS, 32, 64, 128],
    "rs_p": [1, D], "rs_s": [NS, D], "rw_p": [16, 64, 64], "rw_s": [NS, 16, 64, 64],
}


class KB:
    def __init__(self, layers=(0, 1, 2, 3), debug=False):
        self.nc = nc = bass.Bass("TRN2", target_bir_lowering=False)
        self.P = Prog(nc)
        self.es = contextlib.ExitStack()
        self.I = {k: nc.dram_tensor(k, v, F32, kind="ExternalInput").ap() for k, v in IN_SPECS.items()}
        self.O = {k: nc.dram_tensor(k, v, F32, kind="ExternalOutput").ap() for k, v in OUT_SPECS.items()}
        self.debug = debug
        if debug:
            self.O["dbg_x"] = nc.dram_tensor("dbg_x", [128, 8, NT], F32, kind="ExternalOutput").ap()
            self.O["dbg_scr"] = nc.dram_tensor("dbg_scr", [8, NT, D], F32, kind="ExternalOutput").ap()
        self.bank_i = 0
        self.layers = layers
        self._n = 0

    def sb(self, name, shape, dt=F32):
        self._n += 1
        return self.es.enter_context(self.nc.sbuf_tensor("%s_%d" % (name, self._n), shape, dt))

    @contextlib.contextmanager
    def scope(self):
        old = self.es
        self.es = contextlib.ExitStack()
        try:
            yield
        finally:
            self.es.close()
            self.es = old
            self.P.barrier()

    def bank(self):
        b = self.bank_i
        self.bank_i = (self.bank_i + 1) % 8
        return self.PS[b], ("PS", b)

    def dma(self, eng, out, in_, reads=(), writes=()):
        self.P.op(eng, lambda e: e.dma_start(out=out, in_=in_), reads, writes, dma=True)

    def act(self, out, in_, func, reads, writes, **kw):
        self.P.op("act", lambda e: e.activation(out=out, in_=in_, func=func, **kw), reads, writes)

    def mm(self, out, lhsT, rhs, start, stop, reads, writes):
        self.P.op("pe", lambda e: e.matmul(out, lhsT=lhsT, rhs=rhs, start=start, stop=stop), reads, writes)

    def tr(self, out, in_, ident, reads, writes):
        self.P.op("pe", lambda e: e.transpose(out, in_, ident), reads, writes)

    def ts(self, eng, out, in0, s1, s2, op0, op1, reads, writes):
        if op1 is None:
            self.P.op(eng, lambda e: e.tensor_scalar(out=out, in0=in0, scalar1=s1, scalar2=None, op0=op0), reads, writes)
        else:
            self.P.op(eng, lambda e: e.tensor_scalar(out=out, in0=in0, scalar1=s1, scalar2=s2, op0=op0, op1=op1), reads, writes)

    def stt(self, out, in0, scalar, in1, op0, op1, reads, writes):
        self.P.op("dve", lambda e: e.scalar_tensor_tensor(out=out, in0=in0, scalar=scalar, in1=in1, op0=op0, op1=op1), reads, writes)

    def tt(self, eng, out, in0, in1, op, reads, writes):
        self.P.op(eng, lambda e: e.tensor_tensor(out=out, in0=in0, in1=in1, op=op), reads, writes)

    def cp(self, eng, out, in_, reads, writes):
        if eng == "act":
            self.P.op("act", lambda e: e.activation(out=out, in_=in_, func=AF.Copy), reads, writes)
        else:
            self.P.op(eng, lambda e: e.tensor_copy(out=out, in_=in_), reads, writes)

    def memset(self, eng, ap, v, writes):
        self.P.op(eng, lambda e: e.memset(ap, v), (), writes)

    def scan(self, out, d0, d1, init, reads, writes):
        self.P.op("dve", lambda e: e.tensor_tensor_scan(out=out, data0=d0, data1=d1, initial=init, op0=ALU.mult, op1=ALU.add), reads, writes)

    def xv(self, c, ti):
        c0, w = TT[ti]
        return self.X[:, c, c0:c0 + w]

    def uv(self, c, ti, shift=0):
        if ti < 4:
            s = 1 + 512 * ti + shift
            return self.U[:, c, s:s + 512]
        v = self.U[:, c, 1 + NTP:UC].rearrange("p (s t) -> p s t", t=9)
        return v[:, :, 1 + shift:9 + shift]

    @staticmethod
    def v3(ap, ti):
        if ti < 4:
            return ap
        return ap.rearrange("p (s t) -> p s t", t=8)

    def setup(self):
        nc = self.nc
        self.PS = [self.es.enter_context(nc.psum_tensor("ps%d" % i, [128, 512], F32)) for i in range(8)]
        self.X = self.sb("X", [128, 8, NT])
        self.U = self.sb("U", [128, 8, UC], BF16)
        self.IDF = self.sb("IDF", [128, 128])
        self.IDB = self.sb("IDB", [128, 128], BF16)
        self.ONESB = self.sb("ONESB", [128, 128], BF16)
        self.C1 = self.sb("C1", [128, 4])
        self.PRM = self.sb("PRM", [64, D])
        self.PF = self.sb("PF", [128, 8, 64])
        self.STG = [self.sb("STG%d" % i, [128, D]) for i in range(2)]
        self.SQ = self.sb("SQ", [128, 8, 512], BF16)
        self.RS = self.sb("RS", [128, 512])
        P = self.P
        P.op("pool", lambda e: e.memset(self.IDF[:], 0.0), (), ["IDF"])
        P.op("pool", lambda e: e.affine_select(out=self.IDF[:], in_=self.IDF[:], pattern=[[-1, 128]], compare_op=ALU.not_equal,
                                               fill=1.0, base=0, channel_multiplier=1), ["IDF"], ["IDF"])
        self.cp("dve", self.IDB[:], self.IDF[:], ["IDF"], ["IDB"])
        self.memset("dve", self.ONESB[:], 1.0, ["ONESB"])
        self.memset("dve", self.C1[:, 0:1], 1e-6, [("C1", 0)])
        self.memset("dve", self.C1[:, 1:2], 1.0, [("C1", 1)])
        self.memset("dve", self.C1[:, 2:3], 1e-5, [("C1", 2)])
        self.memset("dve", self.C1[:, 3:4], 64e-5, [("C1", 3)])
        self.memset("dve", self.U[:, :, 0:1], 0.0, [("U", "shiftp")])
        self.memset("pool", self.PRM[:], 0.0, ["PRM"])
        I = self.I
        def row(name, src, n):
            r = PROW[name]
            self.dma("sp", self.PRM[r:r + n, :], src, (), ["PRM"])
        row("norm_mix", I["norm_mix"], 4); row("norm_ffn", I["norm_ffn"], 4); row("norm_final", I["norm_final"], 1)
        row("lru_conv_w", I["lru_conv_w"].rearrange("a k d -> (a k) d"), 8)
        row("lru_conv_b", I["lru_conv_b"], 2); row("lru_b_r", I["lru_b_r"], 2); row("lru_b_i", I["lru_b_i"], 2)
        row("lru_lambda", I["lru_lambda"], 2)
        row("ssm_norm_w", I["ssm_norm_w"].rearrange("a (r d) -> (a r) d", d=D), 2)
        row("ssm_conv_w", I["ssm_conv_w"].rearrange("a k (r d) -> (a k r) d", d=D), 16)
        row("ssm_conv_b", I["ssm_conv_b"].rearrange("a (r d) -> (a r) d", d=D), 4)
        row("rwkv_mu", I["rwkv_mu"].rearrange("a k d -> (a k) d"), 6)
        for nm in ("rwkv_w0", "rwkv_a0", "rwkv_k_k", "rwkv_k_a", "rwkv_lnx_w", "rwkv_lnx_b"):
            row(nm, I[nm], 1)
        row("rwkv_r_k", I["rwkv_r_k"].rearrange("a h n -> a (h n)"), 1)
        for c in range(8):
            ps, pk = self.bank()
            self.tr(ps[:, 0:64], self.PRM[:, c * 128:(c + 1) * 128], self.IDF[0:64, 0:64], ["PRM", "IDF"], [pk])
            self.cp("dve", self.PF[:, c, :], ps[:, 0:64], [pk], [("PF", c)])

    def pf(self, name, k, c):
        r = PROW[name] + k
        return self.PF[:, c, r:r + 1]

    def load_x(self):
        n = 0
        for ti, (c0, w) in enumerate(TT):
            for j in range(w // 128):
                b = n % 2
                n += 1
                src = self.I["xp"][c0 + j * 128:c0 + (j + 1) * 128, :] if ti < 4 else self.I["xs"][:, :]
                self.dma("sp", self.STG[b][:], src, (), [("STG", b)])
                for c in range(8):
                    self.tr(self.PS[c][:, j * 128:(j + 1) * 128], self.STG[b][:, c * 128:(c + 1) * 128], self.IDF[:],
                            [("STG", b), "IDF"], [("PS", c)])
            for c in range(8):
                self.cp("dve" if c % 2 == 0 else "act", self.X[:, c, c0:c0 + w], self.PS[c][:, 0:w], [("PS", c)], [("X", (c, ti))])

    def rows_to_fm(self, src, nrows, dst, dkey):
        self.dma("sp", self.STG[0][0:nrows, :], src, (), [("STG", 0)])
        for c in range(8):
            ps, pk = self.bank()
            self.tr(ps[:, 0:nrows], self.STG[0][0:nrows, c * 128:(c + 1) * 128], self.IDF[0:nrows, 0:nrows], [("STG", 0), "IDF"], [pk])
            self.cp("dve", dst[:, c, 0:nrows], ps[:, 0:nrows], [pk], [dkey])

    def fm_to_rows(self, src, skey, nrows, dst):
        for half in range(2):
            ps, pk = self.bank()
            for cc in range(4):
                c = half * 4 + cc
                self.tr(ps[0:nrows, cc * 128:(cc + 1) * 128], src[:, c, 0:nrows], self.IDF[:], [skey, "IDF"], [pk])
            self.cp("dve", self.STG[1][0:nrows, half * 512:(half + 1) * 512], ps[0:nrows, :], [pk], [("STG", 1)])
        self.dma("sp", dst, self.STG[1][0:nrows, :], [("STG", 1)], ())

    def norm_to_U(self, pname, k, shift_out=None):
        for ti, (c0, w) in enumerate(TT):
            ps, pk = self.bank()
            for c in range(8):
                self.act(self.SQ[:, c, :w], self.X[:, c, c0:c0 + w], AF.Square, [("X", (c, ti))], [("SQ", c)])
                self.mm(ps[:, :w], self.ONESB[:], self.SQ[:, c, :w], c == 0, c == 7, [("SQ", c), "ONESB"], [pk])
            self.act(self.RS[:, :w], ps[:, :w], AF.Ln, [pk, ("C1", 0)], ["RS"], scale=1.0 / D, bias=self.C1[:, 0:1])
            self.act(self.RS[:, :w], self.RS[:, :w], AF.Exp, ["RS"], ["RS"], scale=-0.5)
            for c in range(8):
                self.stt(self.uv(c, ti), self.v3(self.X[:, c, c0:c0 + w], ti), self.pf(pname, k, c), self.v3(self.RS[:, :w], ti),
                         ALU.mult, ALU.mult, [("X", (c, ti)), "RS", ("PF", c)], [("U", (c, ti))])
                if shift_out is not None and ti == 3:
                    self.stt(shift_out[:, c, 0:1], self.X[:, c, NTP - 1:NTP], self.pf(pname, k, c), self.RS[:, 511:512],
                             ALU.mult, ALU.mult, [("X", (c, ti)), "RS", ("PF", c)], ["C_USH"])
                if shift_out is not None and ti == 4:
                    self.stt(shift_out[:, c, 1:], self.X[:, c, NTP:NT].rearrange("p (s t) -> p s t", t=8)[:, :, 7],
                             self.pf(pname, k, c), self.RS[:, 0:128].rearrange("p (s t) -> p s t", t=8)[:, :, 7],
                             ALU.mult, ALU.mult, [("X", (c, ti)), "RS", ("PF", c)], ["C_USH"])

    def u_keys(self, ti):
        return [("U", (c, ti)) for c in range(8)]

    def mlp(self, l):
        self.norm_to_U("norm_ffn", l)
        self.W1S = [self.sb("W1S%d" % i, [128, 8, 512], BF16) for i in range(2)]
        self.W2S = [self.sb("W2S%d" % i, [128, 4, D], BF16) for i in range(2)]
        self.HT = [self.sb("HT%d" % i, [128, 4, 512], BF16) for i in range(2)]
        self.RT = [self.sb("RT%d" % i, [128, 512]) for i in range(2)]
        w1 = self.I["ffn_w1"]
        w2 = self.I["ffn_w2"]
        def load(s):
            b = s % 2
            self.dma("pool", self.W1S[b][:], w1[l, :, s * 512:(s + 1) * 512].rearrange("(kc p) n -> p kc n", p=128), (), [("W1S", b)])
            self.dma("pool", self.W2S[b][:], w2[l, s * 512:(s + 1) * 512, :].rearrange("(fc p) n -> p fc n", p=128), (), [("W2S", b)])
        load(0)
        hb = 0
        rb = 0
        for s in range(8):
            if s + 1 < 8:
                load(s + 1)
            b = s % 2
            for ti, (c0, w) in enumerate(TT):
                H = self.HT[hb]
                hk = "HT%d" % hb
                hb ^= 1
                for fc in range(4):
                    ps, pk = self.bank()
                    for kc in range(8):
                        self.mm(self.v3(ps[:, :w], ti), self.W1S[b][:, kc, fc * 128:(fc + 1) * 128], self.uv(kc, ti), kc == 0, kc == 7,
                                [("W1S", b), ("U", (kc, ti))], [pk])
                    R = self.RT[rb]
                    rk = "RT%d" % rb
                    rb ^= 1
                    self.act(R[:, :w], ps[:, :w], AF.Relu, [pk], [rk])
                    self.act(H[:, fc, :w], R[:, :w], AF.Square, [rk], [(hk, fc)])
                for n in range(8):
                    ps, pk = self.bank()
                    for fc in range(4):
                        self.mm(ps[:, :w], self.W2S[b][:, fc, n * 128:(n + 1) * 128], H[:, fc, :w], fc == 0, fc == 3,
                                [("W2S", b), (hk, fc)], [pk])
                    self.tt("dve", self.X[:, n, c0:c0 + w], self.X[:, n, c0:c0 + w], ps[:, :w], ALU.add,
                            [pk, ("X", (n, ti))], [("X", (n, ti))])

    def final_out(self):
        YT = self.STG
        UF = self.sb("UF", [128, 8, 512])
        n = 0
        for ti, (c0, w) in enumerate(TT):
            ps, pk = self.bank()
            for c in range(8):
                self.act(self.SQ[:, c, :w], self.X[:, c, c0:c0 + w], AF.Square, [("X", (c, ti))], [("SQ", c)])
                self.mm(ps[:, :w], self.ONESB[:], self.SQ[:, c, :w], c == 0, c == 7, [("SQ", c), "ONESB"], [pk])
            self.act(self.RS[:, :w], ps[:, :w], AF.Ln, [pk, ("C1", 0)], ["RS"], scale=1.0 / D, bias=self.C1[:, 0:1])
            self.act(self.RS[:, :w], self.RS[:, :w], AF.Exp, ["RS"], ["RS"], scale=-0.5)
            for c in range(8):
                self.stt(UF[:, c, :w], self.X[:, c, c0:c0 + w], self.pf("norm_final", 0, c), self.RS[:, :w],
                         ALU.mult, ALU.mult, [("X", (c, ti)), "RS", ("PF", c)], [("UF", c)])
            for j in range(w // 128):
                b = n % 2
                n += 1
                for half in range(2):
                    ps2, pk2 = self.bank()
                    for cc in range(4):
                        c = half * 4 + cc
                        self.tr(ps2[:, cc * 128:(cc + 1) * 128], UF[:, c, j * 128:(j + 1) * 128], self.IDF[:], [("UF", c), "IDF"], [pk2])
                    self.cp("act" if half else "dve", YT[b][:, half * 512:(half + 1) * 512], ps2[:, :], [pk2], [("STG", b)])
                dst = self.O["y_p"][c0 + j * 128:c0 + (j + 1) * 128, :] if ti < 4 else self.O["y_s"][:, :]
                self.dma("sp", dst, YT[b][:], [("STG", b)], ())

    def dump_x(self):
        self.dma("sp", self.O["dbg_x"], self.X[:], ["X"], ())


def layer_A(self, l, ia):
    I = self.I
    self.norm_to_U("norm_mix", l)
    XB = self.sb("A_XB", [128, XBC])
    XC = self.sb("A_XC", [128, NT])
    XCb = self.sb("A_XCb", [128, NT], BF16)
    GATE = self.sb("A_GATE", [128, NT], BF16)
    R = self.sb("A_R", [128, NT])
    Iq = self.sb("A_I", [128, NT])
    WIN = [self.sb("A_WIN%d" % i, [128, 8, 256], BF16) for i in range(2)]
    WR = [self.sb("A_WR%d" % i, [128, 128], BF16) for i in range(2)]
    WI = [self.sb("A_WI%d" % i, [128, 128], BF16) for i in range(2)]
    WO = [self.sb("A_WO%d" % i, [128, D], BF16) for i in range(2)]
    CL = self.sb("A_CL", [128, 8])
    H0 = self.sb("A_H0", [128, 8, NS])
    CS0 = self.sb("A_CS0", [128, 8, NS * 3])
    HST = self.sb("A_HST", [128, 8, 1 + NS])
    CST = self.sb("A_CST", [128, 8, 3 + NS * 3])
    XBs = XB[:, 3 + NTP:XBC].rearrange("p (s t) -> p s t", t=11)
    XCs = XC[:, NTP:NT].rearrange("p (s t) -> p s t", t=8)
    rl = PROW["lru_lambda"] + ia
    self.act(CL[:], self.PF[:, :, rl], AF.Exp, ["PF"], ["A_CL"], scale=-1.0)
    self.act(CL[:], CL[:], AF.Ln, ["A_CL", ("C1", 1)], ["A_CL"], bias=self.C1[:, 1:2], scale=1.0)
    self.ts("dve", CL[:], CL[:], -8.0, None, ALU.mult, None, ["A_CL"], ["A_CL"])
    self.rows_to_fm(I["st_lru_h"][ia], NS, H0, "A_H0")
    self.rows_to_fm(I["st_lru_conv"][ia].rearrange("s k d -> (s k) d"), NS * 3, CS0, "A_CS0")
    w_in, w_out = I["lru_w_in"], I["lru_w_out"]

    def load(j):
        b = j % 2
        self.dma("pool", WIN[b][:, :, 0:128], w_in[ia, :, j * 128:(j + 1) * 128].rearrange("(kc p) n -> p kc n", p=128), (), [("A_WIN", b)])
        self.dma("pool", WIN[b][:, :, 128:256], w_in[ia, :, D + j * 128:D + (j + 1) * 128].rearrange("(kc p) n -> p kc n", p=128), (), [("A_WIN", b)])
        self.dma("pool", WR[b][:], I["lru_w_r"][ia, j], (), [("A_WR", b)])
        self.dma("pool", WI[b][:], I["lru_w_i"][ia, j], (), [("A_WI", b)])
        self.dma("pool", WO[b][:], w_out[ia, j * 128:(j + 1) * 128, :], (), [("A_WO", b)])

    load(0)
    for j in range(8):
        if j + 1 < 8:
            load(j + 1)
        b = j % 2
        self.memset("dve", XB[:, 0:3], 0.0, [("A_XB", "st")])
        self.cp("dve", XBs[:, :, 0:3], CS0[:, j, :].rearrange("p (s k) -> p s k", k=3), ["A_CS0"], [("A_XB", "st")])
        for ti, (c0, w) in enumerate(TT):
            for half in range(2):
                ps, pk = self.bank()
                for kc in range(8):
                    self.mm(self.v3(ps[:, :w], ti), WIN[b][:, kc, half * 128:(half + 1) * 128], self.uv(kc, ti), kc == 0, kc == 7,
                            [("A_WIN", b), ("U", (kc, ti))], [pk])
                if half == 0:
                    dst = XB[:, 3 + c0:3 + c0 + w] if ti < 4 else XBs[:, :, 3:11]
                    self.cp("act", dst, self.v3(ps[:, :w], ti), [pk], [("A_XB", ti)])
                else:
                    self.act(GATE[:, c0:c0 + w], ps[:, :w], AF.Gelu_apprx_tanh, [pk], [("A_GATE", ti)])
        self.cp("pool", CST[:, j, 0:3], XB[:, NTP:NTP + 3], ["A_XB"], [("A_CST", j)])
        self.cp("pool", CST[:, j, 3:].rearrange("p (s k) -> p s k", k=3), XBs[:, :, 8:11], ["A_XB"], [("A_CST", j)])
        cw = [self.pf("lru_conv_w", ia * 4 + k, j) for k in range(4)]
        cb = self.pf("lru_conv_b", ia, j)
        for (dst, srcf) in ((XC[:, 0:NTP], lambda k: XB[:, k:k + NTP]), (XCs, lambda k: XBs[:, :, k:k + 8])):
            self.ts("dve", dst, srcf(0), cw[0], cb, ALU.mult, ALU.add, ["A_XB", ("PF", j)], ["A_XC"])
            for k in range(1, 4):
                self.stt(dst, srcf(k), cw[k], dst, ALU.mult, ALU.add, ["A_XB", "A_XC", ("PF", j)], ["A_XC"])
        self.cp("act", XCb[:], XC[:], ["A_XC"], ["A_XCb"])
        for ti, (c0, w) in enumerate(TT):
            for (Wg, dstb, bname, key) in ((WR, R, "lru_b_r", "A_R"), (WI, Iq, "lru_b_i", "A_I")):
                ps, pk = self.bank()
                self.mm(ps[:, :w], Wg[b][:], XCb[:, c0:c0 + w], True, True, ["A_XCb", (key.replace("A_", "A_W"), b)], [pk])
                self.act(dstb[:, c0:c0 + w], ps[:, :w], AF.Sigmoid, [pk, ("PF", j)], [key], bias=self.pf(bname, ia, j), scale=1.0)
        T1 = XB[:, 0:NT]
        self.act(R[:], R[:], AF.Exp, ["A_R", "A_CL"], ["A_R"], scale=CL[:, j:j + 1])
        self.act(T1, R[:], AF.Square, ["A_R", "A_XB"], ["A_XB"])
        self.ts("dve", T1, T1, -1.0, 1.0, ALU.mult, ALU.add, ["A_XB"], ["A_XB"])
        self.ts("dve", T1, T1, 1e-30, None, ALU.max, None, ["A_XB"], ["A_XB"])
        self.act(T1, T1, AF.Sqrt, ["A_XB"], ["A_XB"])
        self.memset("dve", T1[:, 0:1], 1.0, ["A_XB"])
        self.memset("dve", R[:, 0:1], 0.0, ["A_R"])
        self.tt("dve", Iq[:], Iq[:], T1, ALU.mult, ["A_I", "A_XB"], ["A_I"])
        self.tt("dve", Iq[:], Iq[:], XC[:], ALU.mult, ["A_I", "A_XC"], ["A_I"])
        self.scan(XC[:, 0:NTP], R[:, 0:NTP], Iq[:, 0:NTP], 0.0, ["A_R", "A_I"], ["A_XC"])
        for s in range(NS):
            c0 = NTP + s * 8
            self.scan(XC[:, c0:c0 + 8], R[:, c0:c0 + 8], Iq[:, c0:c0 + 8], H0[:, j, s:s + 1], ["A_R", "A_I", "A_H0"], ["A_XC"])
        self.cp("pool", HST[:, j, 0:1], XC[:, NTP - 1:NTP], ["A_XC"], [("A_HST", j)])
        self.cp("pool", HST[:, j, 1:], XCs[:, :, 7], ["A_XC"], [("A_HST", j)])
        self.tt("dve", XCb[:], XC[:], GATE[:], ALU.mult, ["A_XC", "A_GATE"], ["A_XCb"])
        for ti, (c0, w) in enumerate(TT):
            for n in range(8):
                ps, pk = self.bank()
                self.mm(ps[:, :w], WO[b][:, n * 128:(n + 1) * 128], XCb[:, c0:c0 + w], True, True, ["A_XCb", ("A_WO", b)], [pk])
                self.tt("dve", self.X[:, n, c0:c0 + w], self.X[:, n, c0:c0 + w], ps[:, :w], ALU.add, [pk, ("X", (n, ti))], [("X", (n, ti))])
    self.fm_to_rows(HST[:, :, 0:1], "A_HST", 1, self.O["lh_p"][ia:ia + 1, :])
    self.fm_to_rows(HST[:, :, 1:], "A_HST", NS, self.O["lh_s"][ia])
    self.fm_to_rows(CST[:, :, 0:3], "A_CST", 3, self.O["lc_p"][ia])
    self.fm_to_rows(CST[:, :, 3:], "A_CST", NS * 3, self.O["lc_s"][ia].rearrange("s k d -> (s k) d"))


KB.layer_A = layer_A


def layer_B(self, l, ib):
    I = self.I
    self.norm_to_U("norm_mix", l)
    f32 = F32
    TRI = self.sb("B_TRI", [128, 128]); NEGM = self.sb("B_NEGM", [128, 128]); ONESF = self.sb("B_ONESF", [128, 128])
    self.memset("pool", TRI[:], 1.0, ["B_TRI"])
    self.P.op("pool", lambda e: e.affine_select(out=TRI[:], in_=TRI[:], pattern=[[1, 128]], compare_op=ALU.is_ge, fill=0.0, base=0,
                                                channel_multiplier=-1), ["B_TRI"], ["B_TRI"])
    self.memset("pool", NEGM[:], 0.0, ["B_NEGM"])
    self.P.op("pool", lambda e: e.affine_select(out=NEGM[:], in_=NEGM[:], pattern=[[1, 128]], compare_op=ALU.is_ge, fill=-1.0e4, base=0,
                                                channel_multiplier=-1), ["B_NEGM"], ["B_NEGM"])
    self.memset("pool", ONESF[:], 1.0, ["B_ONESF"])
    DTB = self.sb("B_DTB", [128, 32]); AB = self.sb("B_AB", [128, 32]); DB = self.sb("B_DB", [128, 32])
    self.dma("sp", DTB[:], I["ssm_dt_bias"][ib:ib + 1, :].partition_broadcast(128), (), ["B_DTB"])
    self.dma("sp", AB[:], I["ssm_a_log"][ib:ib + 1, :].partition_broadcast(128), (), ["B_AB"])
    self.dma("sp", DB[:], I["ssm_d"][ib:ib + 1, :].partition_broadcast(128), (), ["B_DB"])
    self.act(AB[:], AB[:], AF.Exp, ["B_AB"], ["B_AB"])
    self.ts("dve", AB[:], AB[:], -1.0, None, ALU.mult, None, ["B_AB"], ["B_AB"])
    CS0 = self.sb("B_CS0", [128, 32, NS * 3]); CST = self.sb("B_CST", [128, 32, 3 + NS * 3])
    for r in range(4):
        self.rows_to_fm(I["st_ssm_conv"][ib].rearrange("s k d -> (s k) d")[:, r * D:(r + 1) * D], NS * 3, CS0[:, r * 8:(r + 1) * 8, :], "B_CS0")
    WZD = [self.sb("B_WZD0", [128, 8, 260], BF16)] * 2
    BONES = self.sb("B_BONES", [128, 128]); TRIS = self.sb("B_TRIS", [128, 128]); NEGMS = self.sb("B_NEGMS", [128, 128])
    SEQM = self.sb("B_SEQM", [128, 16]); DAS = self.sb("B_DAS", [128, 64]); CDS = self.sb("B_CDS", [128, 64])
    YO = self.sb("B_YO", [128, 256]); BTM = self.sb("B_BTM", [128, 128], BF16); USC = self.sb("B_USC", [128, 8, 128], BF16)
    def _asel(ap, pattern, base, cm, fill, keys):
        self.P.op("pool", lambda e: e.affine_select(out=ap, in_=ap, pattern=pattern, compare_op=ALU.is_ge, fill=fill, base=base,
                                                    channel_multiplier=cm), keys, keys)
    self.memset("pool", SEQM[:], 1.0, ["B_SEQM"])
    _asel(SEQM[:], [[-8, 16]], 0, 1, 0.0, ["B_SEQM"])
    _asel(SEQM[:], [[8, 16]], 7, -1, 0.0, ["B_SEQM"])
    self.memset("pool", BONES[:], 1.0, ["B_BONES"])
    _asel(BONES[:].rearrange("p (s t) -> p s t", t=8), [[-8, 16], [0, 8]], 0, 1, 0.0, ["B_BONES"])
    _asel(BONES[:].rearrange("p (s t) -> p s t", t=8), [[8, 16], [0, 8]], 7, -1, 0.0, ["B_BONES"])
    self.tt("pool", TRIS[:], TRI[:], BONES[:], ALU.mult, ["B_TRI", "B_BONES"], ["B_TRIS"])
    self.tt("pool", NEGMS[:], NEGM[:], BONES[:], ALU.mult, ["B_NEGM", "B_BONES"], ["B_NEGMS"])
    self.ts("dve", YO[:, 0:128], BONES[:], 1.0e4, -1.0e4, ALU.mult, ALU.add, ["B_BONES"], ["B_YO"])
    self.tt("dve", NEGMS[:], NEGMS[:], YO[:, 0:128], ALU.add, ["B_NEGMS", "B_YO"], ["B_NEGMS"])
    self.cp("dve", USC[:].rearrange("p c (s t) -> p c s t", t=8),
            self.U[:, :, 1 + NTP:UC].rearrange("p c (s t) -> p c s t", t=9)[:, :, :, 1:9], [("U", (c_, 4)) for c_ in range(8)], ["B_USC"])

    WXBC = [self.sb("B_WXBC0", [128, 8, 512], BF16)] * 2
    WOUT = [self.sb("B_WOUT0", [128, 2, D], BF16)] * 2
    XF = self.sb("B_XF", [128, 4, 3 + 512]); XFS = self.sb("B_XFS", [128, 4, NS * 11])
    XCf = self.sb("B_XCf", [128, 4, 512]); BCb = self.sb("B_BCb", [128, 2, 512], BF16)
    YGT = self.sb("B_YGT", [128, 2, 512], BF16)
    ST = self.sb("B_ST", [128, 256]); STb = self.sb("B_STb", [128, 256], BF16)
    SIN = self.sb("B_SIN", [128, 2, 128]); SOUT = self.sb("B_SOUT", [128, 2, 128])
    XTs = [self.sb("B_XT%d" % i, [128, 256]) for i in range(2)]; BTs = [self.sb("B_BT%d" % i, [128, 128], BF16) for i in range(2)]
    SMs = [self.sb("B_SM%d" % i, [128, 40]) for i in range(2)]
    LT = self.sb("B_LT", [128, 4, 128]); MTs = [self.sb("B_MT%d" % i, [128, 4, 128], BF16) for i in range(2)]
    XDTs = [self.sb("B_XDT%d" % i, [128, 256], BF16) for i in range(2)]; XDDs = [self.sb("B_XDD%d" % i, [128, 256], BF16) for i in range(2)]
    Y1 = self.sb("B_Y1", [128, 256]); T2 = self.sb("B_T2", [128, 256]); SZ = self.sb("B_SZ", [128, 256])
    w_in, w_out = I["ssm_w_in"], I["ssm_w_out"]
    XFSv = [XFS[:, q, :].rearrange("p (s t) -> p s t", t=11) for q in range(4)]

    def bc(ap, cs):
        return ap.unsqueeze(2).to_broadcast([cs, 4, 64])

    def v4(ap):
        return ap.rearrange("p (h q) -> p h q", q=64)

    def wv(c0, n):
        return w_in[ib, :, c0:c0 + n].rearrange("(kc p) n -> p kc n", p=128)

    def load(g):
        self.dma("pool", WZD[0][:, :, 0:256], wv(g * 256, 256), (), ["B_WZD"])
        self.dma("pool", WZD[0][:, :, 256:260], wv(6144 + 4 * g, 4), (), ["B_WZD"])

    def load_x(g):
        self.dma("pool", WXBC[0][:, :, 0:256], wv(2048 + g * 256, 256), (), ["B_WXBC"])
        self.dma("pool", WXBC[0][:, :, 256:384], wv(4096 + g * 128, 128), (), ["B_WXBC"])
        self.dma("pool", WXBC[0][:, :, 384:512], wv(5120 + g * 128, 128), (), ["B_WXBC"])

    def load_o(g):
        self.dma("pool", WOUT[0][:], w_out[ib, g * 256:(g + 1) * 256, :].rearrange("(h p) n -> p h n", p=128), (), ["B_WOUT"])

    def chunk_front(g, b, tc, cs, ucol, st, sample=False):
        hs = slice(4 * g, 4 * g + 4)
        tri, negm = (TRIS, NEGMS) if sample else (TRI, NEGM)
        XT, BT, SM, MT, XDT, XDD = XTs[st], BTs[st], SMs[st], MTs[st], XDTs[st], XDDs[st]
        DTV, DT_, DA, NACS, EACS, DEND, CD = (SM[:, 0:4], SM[:, 4:8], SM[:, 8:12], SM[:, 12:16], SM[:, 16:20], SM[:, 20:24], SM[:, 24:28])
        K = lambda nm: ("B_" + nm, st)
        pzd, kzd = self.PS[st], ("PS", st)
        pt, kt = self.PS[2], ("PS", 2)
        pa, ka = self.PS[3], ("PS", 3)
        pl, kl = self.PS[4], ("PS", 4)
        pc, kc_ = self.PS[5], ("PS", 5)
        for kc in range(8):
            self.mm(pzd[0:cs, 0:260], (USC[:, kc, :] if sample else self.U[:, kc, ucol:ucol + cs]), WZD[b][:, kc, :], kc == 0, kc == 7, ["B_WZD", "U", "B_USC"], [kzd])
        for q in range(3):
            self.tr(pt[0:cs, q * 128:(q + 1) * 128], XCf[:, q, tc:tc + cs], self.IDF[:], ["B_XCf", "IDF"], [kt])
        self.cp("act", XT[0:cs, :], pt[0:cs, 0:256], [kt], [K("XT")])
        self.cp("dve", BT[0:cs, :], pt[0:cs, 256:384], [kt], [K("BT")])
        self.mm(pc[0:cs, 0:cs], BCb[:, 0, tc:tc + cs], BCb[:, 1, tc:tc + cs], True, True, ["B_BCb"], [kc_])
        self.tt("dve", DTV[0:cs], pzd[0:cs, 256:260], DTB[0:cs, hs], ALU.add, [kzd, "B_DTB"], [K("DTV")])
        self.act(DT_[0:cs], DTV[0:cs], AF.Exp, [K("DTV")], [K("DT")])
        self.act(DT_[0:cs], DT_[0:cs], AF.Ln, [K("DT"), ("C1", 1)], [K("DT")], bias=self.C1[0:cs, 1:2], scale=1.0)
        self.tt("dve", DA[0:cs], DT_[0:cs], AB[0:cs, hs], ALU.mult, [K("DT"), "B_AB"], [K("DA")])
        self.mm(pa[0:cs, 0:4], tri[0:cs, 0:cs], DA[0:cs], True, True, ["B_TRI", "B_TRIS", K("DA")], [ka])
        self.mm(pa[:, 4:8], (BONES[:, :] if sample else ONESF[0:cs, :]), DA[0:cs], True, True, ["B_ONESF", "B_BONES", K("DA")], [ka])
        self.ts("dve", NACS[0:cs], pa[0:cs, 0:4], -1.0, None, ALU.mult, None, [ka], [K("NACS")])
        self.act(EACS[0:cs], pa[0:cs, 0:4], AF.Exp, [ka], [K("EACS")])
        self.tt("dve", DEND[0:cs], pa[0:cs, 4:8], NACS[0:cs], ALU.add, [ka, K("NACS")], [K("DEND")])
        self.act(DEND[0:cs], DEND[0:cs], AF.Exp, [K("DEND")], [K("DEND")])
        if not sample:
            self.act(CD, pa[:, 4:8], AF.Exp, [ka], [K("CD")])
        else:
            self.tt("dve", DAS[:].rearrange("p (s h) -> p s h", h=4), DA[:].unsqueeze(1).to_broadcast([128, 16, 4]),
                    SEQM[:].unsqueeze(2).to_broadcast([128, 16, 4]), ALU.mult, [K("DA"), "B_SEQM"], ["B_DAS"])
            pcd, kcd = self.PS[6], ("PS", 6)
            self.mm(pcd[:, 0:64], ONESF[:, :], DAS[:], True, True, ["B_ONESF", "B_DAS"], [kcd])
            self.act(CDS[:], pcd[:, 0:64], AF.Exp, [kcd], ["B_CDS"])
        for h in range(4):
            self.mm(pl[0:cs, h * 128:h * 128 + cs], DA[0:cs, h:h + 1].to_broadcast([cs, cs]), tri[0:cs, 0:cs], True, False, [K("DA"), "B_TRI", "B_TRIS"], [kl])
            self.mm(pl[0:cs, h * 128:h * 128 + cs], self.IDF[0:cs, 0:cs], negm[0:cs, 0:cs], False, True, ["IDF", "B_NEGM", "B_NEGMS"], [kl])
        for h in range(4):
            self.act(LT[0:cs, h, 0:cs], pl[0:cs, h * 128:h * 128 + cs], AF.Exp, [kl, K("NACS")], [("B_LT", h)], bias=NACS[0:cs, h:h + 1], scale=1.0)
        for h in range(4):
            self.tt("dve", MT[0:cs, h, 0:cs], LT[0:cs, h, 0:cs], pc[0:cs, 0:cs], ALU.mult, [kc_, ("B_LT", h)], [("B_MT%d" % st, h)])
        self.tt("dve", v4(XDT[0:cs, :]), v4(XT[0:cs, :]), bc(DT_[0:cs], cs), ALU.mult, [K("XT"), K("DT")], [K("XDT")])
        self.tt("dve", v4(XDD[0:cs, :]), v4(XDT[0:cs, :]), bc(DEND[0:cs], cs), ALU.mult, [K("XDT"), K("DEND")], [K("XDD")])

    def chunk_back(g, b, tc, cs, ucol, st, sample=False):
        hs = slice(4 * g, 4 * g + 4)
        XT, BT, SM, MT, XDT, XDD = XTs[st], BTs[st], SMs[st], MTs[st], XDTs[st], XDDs[st]
        EACS, CD, MS = SM[:, 16:20], SM[:, 24:28], SM[:, 28:29]
        K = lambda nm: ("B_" + nm, st)
        pzd, kzd = self.PS[st], ("PS", st)
        py, ky = self.PS[6], ("PS", 6)
        p7, k7 = self.PS[7], ("PS", 7)
        self.act(SZ[0:cs, :], pzd[0:cs, 0:256], AF.Exp, [kzd], ["B_SZ"], scale=-1.0)
        self.act(SZ[0:cs, :], SZ[0:cs, :], AF.Ln, ["B_SZ", ("C1", 1)], ["B_SZ"], bias=self.C1[0:cs, 1:2], scale=1.0)
        self.act(SZ[0:cs, :], SZ[0:cs, :], AF.Exp, ["B_SZ"], ["B_SZ"], scale=-1.0)
        self.tt("dve", SZ[0:cs, :], SZ[0:cs, :], pzd[0:cs, 0:256], ALU.mult, [kzd, "B_SZ"], ["B_SZ"])
        if sample:
            self.memset("dve", YO[:], 0.0, ["B_YO"])
            for sq in range(NSQ):
                state_in(g, sq)
                po, ko = self.bank()
                self.mm(po[:, 0:256], BCb[:, 1, 0:128], STb[:], True, True, ["B_BCb", "B_STb"], [ko])
                self.stt(YO[:], po[:, 0:256], SEQM[:, sq:sq + 1], YO[:], ALU.mult, ALU.add, [ko, "B_SEQM", "B_YO"], ["B_YO"])
                self.ts("dve", BTM[:], BT[:], SEQM[:, sq:sq + 1], None, ALU.mult, None, [K("BT"), "B_SEQM"], ["B_BTM"])
                pst, kst = self.bank()
                self.mm(pst[:, 0:256], BTM[:], XDD[:], True, True, ["B_BTM", K("XDD")], [kst])
                self.tt("dve", v4(ST[:]), v4(ST[:]), bc(CDS[:, 4 * sq:4 * sq + 4], 128), ALU.mult, ["B_ST", "B_CDS"], ["B_ST"])
                self.tt("dve", ST[:], ST[:], pst[:, 0:256], ALU.add, ["B_ST", kst], ["B_ST"])
                state_out(g, self.O["ss_s"][sq, 4 * g:4 * g + 4])
        for h in range(4):
            self.mm(py[0:cs, h * 64:(h + 1) * 64], MT[0:cs, h, 0:cs], XDT[0:cs, h * 64:(h + 1) * 64], True, True, [("B_MT%d" % st, h), K("XDT")], [ky])
        if not sample:
            self.mm(p7[0:cs, 0:256], BCb[:, 1, tc:tc + cs], STb[:], True, True, ["B_BCb", "B_STb"], [k7])
            self.tt("dve", v4(Y1[0:cs, :]), v4(p7[0:cs, 0:256]), bc(EACS[0:cs], cs), ALU.mult, [k7, K("EACS")], ["B_Y1"])
        else:
            self.tt("dve", v4(Y1[:]), v4(YO[:]), bc(EACS[:], 128), ALU.mult, ["B_YO", K("EACS")], ["B_Y1"])
        self.tt("dve", Y1[0:cs, :], Y1[0:cs, :], py[0:cs, 0:256], ALU.add, [ky, "B_Y1"], ["B_Y1"])
        self.tt("pool", v4(T2[0:cs, :]), v4(XT[0:cs, :]), bc(DB[0:cs, hs], cs), ALU.mult, [K("XT"), "B_DB"], ["B_T2"])
        self.tt("dve", Y1[0:cs, :], Y1[0:cs, :], T2[0:cs, :], ALU.add, ["B_T2", "B_Y1"], ["B_Y1"])
        if not sample:
            self.mm(p7[:, 256:512], BT[0:cs, :], XDD[0:cs, :], True, True, [K("BT"), K("XDD")], [k7])
            self.tt("dve", v4(ST[:]), v4(ST[:]), bc(CD, 128), ALU.mult, ["B_ST", K("CD")], ["B_ST"])
            self.tt("dve", ST[:], ST[:], p7[:, 256:512], ALU.add, ["B_ST", k7], ["B_ST"])
            self.cp("pool", STb[:], ST[:], ["B_ST"], ["B_STb"])
        self.tt("dve", Y1[0:cs, :], Y1[0:cs, :], SZ[0:cs, :], ALU.mult, ["B_SZ", "B_Y1"], ["B_Y1"])
        self.P.op("dve", lambda e: e.scalar_tensor_tensor(out=T2[0:cs, :], in0=Y1[0:cs, :], scalar=1.0, in1=Y1[0:cs, :], op0=ALU.mult,
                                                          op1=ALU.mult, accum_out=MS[0:cs]), ["B_Y1", "B_T2"], ["B_T2", K("MS")])
        self.act(MS[0:cs], MS[0:cs], AF.Ln, [K("MS"), ("C1", 2)], [K("MS")], scale=1.0 / 256, bias=self.C1[0:cs, 2:3])
        self.act(MS[0:cs], MS[0:cs], AF.Exp, [K("MS")], [K("MS")], scale=-0.5)
        self.ts("dve", Y1[0:cs, :], Y1[0:cs, :], MS[0:cs], None, ALU.mult, None, [K("MS"), "B_Y1"], ["B_Y1"])
        pg, kg = self.PS[7], ("PS", 7)
        for hf in range(2):
            self.tr(pg[:, hf * 128:hf * 128 + cs], Y1[0:cs, hf * 128:(hf + 1) * 128], self.IDF[0:cs, 0:cs], ["B_Y1", "IDF"], [kg])
        for hf in range(2):
            ch = 2 * g + hf
            self.ts("dve", YGT[:, hf, tc:tc + cs], pg[:, hf * 128:hf * 128 + cs], self.pf("ssm_norm_w", ch // 8, ch % 8), None, ALU.mult, None,
                    [kg, "PF"], ["B_YGT"])

    def state_in(g, s):
        self.dma("sp", SIN[:], I["st_ssm"][ib, s, 4 * g:4 * g + 4].rearrange("(a h) p n -> (h p) a n", a=2), (), ["B_SIN"])
        ps, pk = self.bank()
        for a in range(2):
            self.tr(ps[:, a * 128:(a + 1) * 128], SIN[:, a, :], self.IDF[:], ["B_SIN", "IDF"], [pk])
        self.cp("dve", ST[:], ps[:, 0:256], [pk], ["B_ST"])
        self.cp("act", STb[:], ps[:, 0:256], [pk], ["B_STb"])

    def state_out(g, dst):
        ps, pk = self.bank()
        for a in range(2):
            self.tr(ps[:, a * 128:(a + 1) * 128], ST[:, a * 128:(a + 1) * 128], self.IDF[:], ["B_ST", "IDF"], [pk])
        self.cp("dve", SOUT[:].rearrange("p a n -> p (a n)"), ps[:, 0:256], [pk], ["B_SOUT"])
        self.dma("sp", dst.rearrange("(a h) p n -> (h p) a n", a=2), SOUT[:], ["B_SOUT"], ())

    import os as _os
    NG = int(_os.environ.get('BDBG_G', '8')); TLIST = [int(c) for c in _os.environ.get('BDBG_T', '01234')]; NSQ = int(_os.environ.get('BDBG_S', '16'))
    load(0)
    load_x(0)
    load_o(0)
    for g in range(NG):
        b = g % 2
        chs = [2 * g, 2 * g + 1, 16 + g, 24 + g]
        for ti, (c0, w) in enumerate(TT):
            if ti not in TLIST:
                continue
            if ti == 0:
                self.memset("dve", XF[:, :, 0:3], 0.0, [("B_XF", "st")])
            elif ti < 4:
                self.cp("dve", XF[:, :, 0:3], XF[:, :, 512:515], ["B_XF"], [("B_XF", "st")])
            else:
                for q in range(4):
                    self.cp("dve", XFSv[q][:, :, 0:3], CS0[:, chs[q], :].rearrange("p (s k) -> p s k", k=3), ["B_CS0"], [("B_XFS", "st")])
            for q in range(4):
                ps, pk = self.bank()
                for kc in range(8):
                    self.mm(self.v3(ps[:, :w], ti), WXBC[b][:, kc, q * 128:(q + 1) * 128], self.uv(kc, ti), kc == 0, kc == 7,
                            ["B_WXBC", ("U", (kc, ti))], [pk])
                if ti < 4:
                    self.cp("act", XF[:, q, 3:515], ps[:, :], [pk], [("B_XF", q)])
                else:
                    self.cp("act", XFSv[q][:, :, 3:11], self.v3(ps[:, :w], ti), [pk], [("B_XFS", q)])
            if ti == TLIST[-1] and g + 1 < NG:
                load_x(g + 1)
            for q in range(4):
                ch = chs[q]
                cw = [self.pf("ssm_conv_w", k * 4 + ch // 8, ch % 8) for k in range(4)]
                cb = self.pf("ssm_conv_b", ch // 8, ch % 8)
                if ti < 4:
                    dst = XCf[:, q, :]
                    srcf = lambda k, q=q: XF[:, q, k:k + 512]
                    rk = "B_XF"
                else:
                    dst = XCf[:, q, 0:128].rearrange("p (s t) -> p s t", t=8)
                    srcf = lambda k, q=q: XFSv[q][:, :, k:k + 8]
                    rk = "B_XFS"
                self.ts("dve", dst, srcf(0), cw[0], cb, ALU.mult, ALU.add, [rk, "PF"], [("B_XCf", q)])
                for k in range(1, 4):
                    self.stt(dst, srcf(k), cw[k], dst, ALU.mult, ALU.add, [rk, ("B_XCf", q), "PF"], [("B_XCf", q)])
                self.act(XCf[:, q, :w], XCf[:, q, :w], AF.Silu, [("B_XCf", q)], [("B_XCf", q)])
                if q >= 2:
                    self.cp("pool", BCb[:, q - 2, :w], XCf[:, q, :w], [("B_XCf", q)], ["B_BCb"])
            if ti == 3:
                for q in range(4):
                    self.cp("pool", CST[:, chs[q], 0:3], XF[:, q, 512:515], ["B_XF"], ["B_CST"])
            if ti == 4:
                for q in range(4):
                    self.cp("pool", CST[:, chs[q], 3:].rearrange("p (s k) -> p s k", k=3), XFSv[q][:, :, 8:11], ["B_XFS"], ["B_CST"])
            if ti < 4:
                if ti == 0:
                    self.memset("dve", ST[:], 0.0, ["B_ST"])
                    self.memset("pool", STb[:], 0.0, ["B_STb"])
                args = [(g, b, ck * 128, 128, 1 + c0 + ck * 128, ck % 2) for ck in range(4)]
                chunk_front(*args[0])
                for ck in range(4):
                    if ck + 1 < 4:
                        chunk_front(*args[ck + 1])
                    chunk_back(*args[ck])
                if ti == 3:
                    state_out(g, self.O["ss_p"][4 * g:4 * g + 4])
            else:
                chunk_front(g, b, 0, 128, 0, 0, sample=True)
                chunk_back(g, b, 0, 128, 0, 0, sample=True)
            for n in range(8):
                ps, pk = self.bank()
                for hf in range(2):
                    self.mm(ps[:, :w], WOUT[b][:, hf, n * 128:(n + 1) * 128], YGT[:, hf, :w], hf == 0, hf == 1, ["B_WOUT", "B_YGT"], [pk])
                self.tt("dve", self.X[:, n, c0:c0 + w], self.X[:, n, c0:c0 + w], ps[:, :w], ALU.add, [pk, ("X", (n, ti))], [("X", (n, ti))])
        if g + 1 < NG:
            load_o(g + 1)
            load(g + 1)
    pass
    for r in range(4):
        self.fm_to_rows(CST[:, r * 8:(r + 1) * 8, 0:3], "B_CST", 3, self.O["sc_p"][:, r * D:(r + 1) * D])
        self.fm_to_rows(CST[:, r * 8:(r + 1) * 8, 3:], "B_CST", NS * 3, self.O["sc_s"].rearrange("s k d -> (s k) d")[:, r * D:(r + 1) * D])


KB.layer_B = layer_B


def layer_C(self, l, ic):
    I, nc = self.I, self.nc
    E05 = float(np.exp(-0.5))
    SH0 = self.sb("C_SH0", [128, 8, NS]); USH = self.sb("C_USH", [128, 8, 1 + NS])
    self.rows_to_fm(I["st_rw_shift"][ic], NS, SH0, "C_SH0")
    Us = self.U[:, :, 1 + NTP:UC].rearrange("p c (s t) -> p c s t", t=9)
    self.cp("dve", Us[:, :, :, 0], SH0[:], ["C_SH0"], [("U", "shifts")])
    self.norm_to_U("norm_mix", l, shift_out=USH)
    XD = nc.dram_tensor("c_xspill", [128, 8, NT], F32).ap()
    import os as _os2
    CHUNKED = bool(int(_os2.environ.get("C_CHUNKED", "1")))
    HM = () if CHUNKED else ("kk", "w", "nq", "k", "r")
    SCR = {k: nc.dram_tensor("c_scr_" + k, [NT, D], F32).ap() for k in ("kk", "w", "nq", "k", "r", "v", "g", "o") if k not in HM}
    SCRH = {k: nc.dram_tensor("c_scrh_" + k, [16, NT, 64], F32).ap() for k in HM}

    def scr_tm(nm, rows):
        if nm in SCRH:
            return SCRH[nm][:, rows, :].rearrange("h t k -> t h k")
        return SCR[nm][rows, :].rearrange("t (h k) -> t h k", k=64)

    self.P.barrier()
    self.dma("sp", XD, self.X[:], ["X"], ["XD"])
    self.P.barrier()

    def xa(i):
        return self.X[:, i // 2, (i % 2) * D:(i % 2 + 1) * D], ("X", "a%d" % i)

    def h16(ap):
        return ap.rearrange("p (h k) -> p h k", k=64)

    def b16(ap):
        return ap.unsqueeze(2).to_broadcast([128, 16, 64])

    def red(out, in_, reads, writes):
        self.P.op("dve", lambda e: e.tensor_reduce(out=out, in_=in_, axis=AX.X, op=ALU.add), reads, writes)

    def pbc(i, src):
        a, k = xa(i)
        self.dma("sp", a, src.partition_broadcast(128), (), [k])

    with self.scope():
        pbc(0, I["rwkv_w0"][ic:ic + 1, :]); pbc(1, I["rwkv_a0"][ic:ic + 1, :]); pbc(2, I["rwkv_k_k"][ic:ic + 1, :]); pbc(3, I["rwkv_k_a"][ic:ic + 1, :])
        WS = [self.sb("C_WS%d" % i, [128, 8, D], BF16) for i in range(3)]
        for s_ in range(3):
            self.dma("pool", WS[s_][:], I["rwkv_w_rkv"][ic, s_].rearrange("(kc p) n -> p kc n", p=128), (), [("C_WS", s_)])
        W1 = self.sb("C_W1", [128, 8, 256], BF16)
        W2w = self.sb("C_W2w", [64, D], BF16); W2a = self.sb("C_W2a", [64, D], BF16); W2g = self.sb("C_W2g", [128, D], BF16)
        for (c0, n, nm) in ((0, 64, "rwkv_w_w1"), (64, 64, "rwkv_w_a1"), (128, 128, "rwkv_w_g1")):
            self.dma("pool", W1[:, :, c0:c0 + n], I[nm][ic].rearrange("(kc p) n -> p kc n", p=128), (), ["C_W1"])
        self.dma("pool", W2w[:], I["rwkv_w_w2"][ic], (), ["C_W2w"]); self.dma("pool", W2a[:], I["rwkv_w_a2"][ic], (), ["C_W2a"])
        self.dma("pool", W2g[:], I["rwkv_w_g2"][ic], (), ["C_W2g"])
        Dd = self.sb("C_D", [128, 8, 128]); XM = [self.sb("C_XM%d" % i, [128, 8, 128], BF16) for i in range(2)]
        TW = self.sb("C_TW", [64, 128], BF16); TA = self.sb("C_TA", [64, 128], BF16); TG = self.sb("C_TG", [128, 128], BF16)
        SS = self.sb("C_SS", [128, 16])
        (Pw0, kw0), (Pa0, ka0), (Pkk, kkk), (Pka, kka) = xa(0), xa(1), xa(2), xa(3)
        (Rt, kR), (Kt, kK), (Vt, kV), (Wt, kW), (At, kA), (KKt, kKK), (Gt, kG), (T1, kT1), (T2, kT2) = [xa(i) for i in range(7, 16)]
        wsn = 0
        for ti in range(17):
            tok0 = 128 * ti
            if ti < 16:
                cur = self.U[:, :, 1 + tok0:1 + tok0 + 128]; prev = self.U[:, :, tok0:tok0 + 128]
                dv = lambda a: a
                ukeys = [("U", (c, ti // 4)) for c in range(8)]
            else:
                cur = Us[:, :, :, 1:9]; prev = Us[:, :, :, 0:8]
                dv = lambda a: a.rearrange("p c (s t) -> p c s t", t=8) if len(a.shape) == 3 else a.rearrange("p (s t) -> p s t", t=8)
                ukeys = [("U", (c, 4)) for c in range(8)] + [("U", "shifts")]
            if ti % 4 == 0 and ti > 0 and ti < 16:
                ukeys = ukeys + [("U", (c, ti // 4 - 1)) for c in range(8)]
            if ti == 0:
                ukeys = ukeys + [("U", "shiftp")]
            self.tt("dve", dv(Dd[:]), prev, cur, ALU.subtract, ukeys, ["C_D"])

            def mix(s):
                xm = XM[s % 2]; key = "C_XM%d" % (s % 2)
                for kc in range(8):
                    self.stt(dv(xm[:, kc, :]), dv(Dd[:, kc, :]), self.pf("rwkv_mu", s, kc), cur[:, kc], ALU.mult, ALU.add, ["C_D", "PF"] + ukeys, [key])
                return xm, key

            def proj_tm(s, dst, dkey):
                b = s
                xm, key = mix(s)
                for nb in range(2):
                    ps, pk = self.bank()
                    for kc in range(8):
                        self.mm(ps[:, :], xm[:, kc, :], WS[b][:, kc, nb * 512:(nb + 1) * 512], kc == 0, kc == 7, [key, ("C_WS", b)], [pk])
                    self.cp("act", dst[:, nb * 512:(nb + 1) * 512], ps[:, :], [pk], [dkey])

            proj_tm(0, Rt, kR); proj_tm(1, Kt, kK); proj_tm(2, Vt, kV)
            for (s, c0, n, fn, dst, dk) in ((3, 0, 64, AF.Tanh, TW, "C_TW"), (4, 64, 64, AF.Copy, TA, "C_TA"), (5, 128, 128, AF.Sigmoid, TG, "C_TG")):
                xm, key = mix(s)
                ps, pk = self.bank()
                for kc in range(8):
                    self.mm(ps[0:n, 0:128], W1[:, kc, c0:c0 + n], xm[:, kc, :], kc == 0, kc == 7, [key, "C_W1"], [pk])
                self.act(dst[0:n, :], ps[0:n, 0:128], fn, [pk], [dk])
            for nb in range(2):
                cs_ = slice(nb * 512, (nb + 1) * 512)
                ps, pk = self.bank()
                self.mm(ps[:, :], TW[0:64, :], W2w[0:64, cs_], True, True, ["C_TW", "C_W2w"], [pk])
                self.tt("dve", T1[:, cs_], ps[:, :], Pw0[:, cs_], ALU.add, [pk, kw0], [kT1])
                ps, pk = self.bank()
                self.mm(ps[:, :], TA[0:64, :], W2a[0:64, cs_], True, True, ["C_TA", "C_W2a"], [pk])
                self.tt("dve", At[:, cs_], ps[:, :], Pa0[:, cs_], ALU.add, [pk, ka0], [kA])
                ps, pk = self.bank()
                self.mm(ps[:, :], TG[:, :], W2g[:, cs_], True, True, ["C_TG", "C_W2g"], [pk])
                self.cp("act", Gt[:, cs_], ps[:, :], [pk], [kG])
            self.act(T1, T1, AF.Sigmoid, [kT1], [kT1])
            self.act(Wt, T1, AF.Exp, [kT1], [kW], scale=-E05)
            self.act(At, At, AF.Sigmoid, [kA], [kA])
            self.tt("dve", KKt, Kt, Pkk, ALU.mult, [kK, kkk], [kKK])
            self.tt("dve", T1, KKt, KKt, ALU.mult, [kKK], [kT1])
            red(SS[:], h16(T1), [kT1], ["C_SS"])
            self.act(SS[:], SS[:], AF.Sqrt, ["C_SS"], ["C_SS"])
            self.ts("dve", SS[:], SS[:], 1e-12, None, ALU.max, None, ["C_SS"], ["C_SS"])
            self.P.op("dve", lambda e, SS=SS: e.reciprocal(out=SS[:], in_=SS[:]), ["C_SS"], ["C_SS"])
            self.tt("dve", h16(KKt), h16(KKt), b16(SS[:]), ALU.mult, [kKK, "C_SS"], [kKK])
            self.stt(T1, At, -1.0, Pka, ALU.add, ALU.mult, [kA, kka], [kT1])
            self.ts("dve", T1, T1, 1.0, None, ALU.add, None, [kT1], [kT1])
            self.tt("dve", Kt, Kt, T1, ALU.mult, [kK, kT1], [kK])
            self.stt(T2, KKt, -1.0, At, ALU.mult, ALU.mult, [kKK, kA], [kT2])
            rows = slice(tok0, tok0 + 128)
            for (nm, a, k) in (("kk", KKt, kKK), ("w", Wt, kW), ("nq", T2, kT2), ("k", Kt, kK), ("r", Rt, kR), ("v", Vt, kV), ("g", Gt, kG)):
                self.dma("sp", scr_tm(nm, rows), h16(a), [k], [("SCR", nm)])

    if self.debug:
        for i_, nm_ in enumerate(("kk", "w", "nq", "k", "r", "v", "g")):
            self.dma("sp", self.O["dbg_scr"][i_].rearrange("t (h k) -> t h k", k=64), scr_tm(nm_, slice(0, NT)), [("SCR", nm_)], ())

    def phase2_chunked():
        LNE = float(np.log(1.0))
        f32 = F32
        def blockmask(ap3, blk):
            self.P.op("pool", lambda e: e.affine_select(out=ap3, in_=ap3, pattern=[[-blk, 128 // blk], [0, blk]], compare_op=ALU.is_ge, fill=0.0,
                                                        base=0, channel_multiplier=1), ["C_MK"], ["C_MK"])
            self.P.op("pool", lambda e: e.affine_select(out=ap3, in_=ap3, pattern=[[blk, 128 // blk], [0, blk]], compare_op=ALU.is_ge, fill=0.0,
                                                        base=blk - 1, channel_multiplier=-1), ["C_MK"], ["C_MK"])
        MK = {}
        for blk in (32, 8):
            TRIc = self.sb("C_TRIc%d" % blk, [128, 128]); LOWs = self.sb("C_LOWs%d" % blk, [128, 128]); M3 = self.sb("C_M3_%d" % blk, [128, 3, 128])
            MSs = self.sb("C_MSs%d" % blk, [128, 128])
            self.memset("pool", TRIc[:], 1.0, ["C_MK"])
            self.P.op("pool", lambda e, T=TRIc: e.affine_select(out=T[:], in_=T[:], pattern=[[1, 128]], compare_op=ALU.is_ge, fill=0.0, base=0,
                                                                channel_multiplier=-1), ["C_MK"], ["C_MK"])
            blockmask(TRIc[:].rearrange("p (b t) -> p b t", t=blk), blk)
            self.memset("pool", LOWs[:], 1.0, ["C_MK"])
            self.P.op("pool", lambda e, T=LOWs: e.affine_select(out=T[:], in_=T[:], pattern=[[-1, 128]], compare_op=ALU.is_ge, fill=0.0, base=-1,
                                                                channel_multiplier=1), ["C_MK"], ["C_MK"])
            blockmask(LOWs[:].rearrange("p (b t) -> p b t", t=blk), blk)
            self.tt("pool", MSs[:], TRIc[:], self.IDF[:], ALU.subtract, ["C_MK", "IDF"], ["C_MK"])
            self.cp("pool", M3[:, 0, :], TRIc[:], ["C_MK"], ["C_MK"])
            self.cp("pool", M3[:, 1, :], MSs[:], ["C_MK"], ["C_MK"])
            self.cp("pool", M3[:, 2, :], TRIc[:], ["C_MK"], ["C_MK"])
            MK[blk] = (TRIc, LOWs, MSs, M3)
        SEQM = self.sb("C_SEQM", [128, 16])
        self.memset("pool", SEQM[:], 1.0, ["C_MK"])
        self.P.op("pool", lambda e: e.affine_select(out=SEQM[:], in_=SEQM[:], pattern=[[-8, 16]], compare_op=ALU.is_ge, fill=0.0, base=0,
                                                    channel_multiplier=1), ["C_MK"], ["C_MK"])
        self.P.op("pool", lambda e: e.affine_select(out=SEQM[:], in_=SEQM[:], pattern=[[8, 16]], compare_op=ALU.is_ge, fill=0.0, base=7,
                                                    channel_multiplier=-1), ["C_MK"], ["C_MK"])
        CHM = self.sb("C_CHM", [128, 4])
        self.memset("pool", CHM[:], 1.0, ["C_MK"])
        self.P.op("pool", lambda e: e.affine_select(out=CHM[:], in_=CHM[:], pattern=[[-32, 4]], compare_op=ALU.is_ge, fill=0.0, base=0,
                                                    channel_multiplier=1), ["C_MK"], ["C_MK"])
        self.P.op("pool", lambda e: e.affine_select(out=CHM[:], in_=CHM[:], pattern=[[32, 4]], compare_op=ALU.is_ge, fill=0.0, base=31,
                                                    channel_multiplier=-1), ["C_MK"], ["C_MK"])
        PRT = self.sb("C_PRT", [128, 8, 2, 128], BF16); QT = self.sb("C_QT", [128, 8, 128], BF16); KT = self.sb("C_KT", [128, 8, 128], BF16)
        QZ = self.sb("C_QZ", [128, 16, 128], BF16); KZ = self.sb("C_KZ", [128, 16, 128], BF16)
        Vb = self.sb("C_Vb", [128, D], BF16); Vm = self.sb("C_Vm", [128, D], BF16)
        ECLE = self.sb("C_ECLE", [128, 8, 16])
        XN = self.sb("C_XN", [128, 8, 128]); ZN = self.sb("C_ZN", [128, 8, 128]); TTf = self.sb("C_TTf", [128, 8, 128])
        MB = self.sb("C_MB", [128, 16, 3, 128], BF16); TTb = self.sb("C_TTb", [128, 16, 128], BF16)
        Wsb = self.sb("C_Wsb", [128, D]); O0 = self.sb("C_O0", [128, D])
        GWb = self.sb("C_GWb", [128, D], BF16); Ub = self.sb("C_Ub", [128, D], BF16)
        S = self.sb("C_S2", [128, 8, 64]); Sb = self.sb("C_Sb", [128, 16, 64], BF16)
        SbV = Sb[:].rearrange("p (j e) v -> p j e v", e=2)

        def write_sb(src3, keys):
            self.cp("act", SbV[0:64, :, 0, :], src3[0:64], keys, ["C_Sb"])
            self.cp("act", SbV[64:128, :, 1, :], src3[64:128], keys, ["C_Sb"])

        NAT = self.sb("C_NAT", [64, 16, 64])
        self.memset("pool", QZ[:], 0.0, ["C_QZ"]); self.memset("pool", KZ[:], 0.0, ["C_KZ"])
        (KKt, kKK), (NQt, kNQ), (Kt, kK), (Rt, kR), (Vt, kV), (LW, kLW), (CL, kCL), (E1, kE1), (Ot, kO) = [xa(i) for i in range(0, 9)]

        def state_load(dram):
            self.dma("sp", NAT[:], dram.rearrange("h v k -> v h k"), (), ["C_NAT"])
            ps, pk = self.bank()
            for j in range(8):
                self.tr(ps[:, j * 64:(j + 1) * 64], NAT[:, 2 * j:2 * j + 2, :].rearrange("v h k -> v (h k)"), self.IDF[0:64, 0:64], ["C_NAT", "IDF"], [pk])
            self.cp("dve", S[:].rearrange("p j v -> p (j v)"), ps[:, :], [pk], ["C_S2"])
            write_sb(ps[:, :].rearrange("p (j v) -> p j v", v=64), [pk])

        def state_store(dram):
            for half in range(2):
                ps, pk = self.bank()
                for jj in range(4):
                    j = half * 4 + jj
                    self.tr(ps[0:64, jj * 128:(jj + 1) * 128], S[:, j, :], self.IDF[:], ["C_S2", "IDF"], [pk])
                self.cp("dve", NAT[:, half * 8:(half + 1) * 8, :].rearrange("v h k -> v (h k)"), ps[0:64, :], [pk], ["C_NAT"])
            self.dma("act", dram.rearrange("h v k -> v h k"), NAT[:], ["C_NAT"], ())

        def tile(ti, blk, seqs):
            TRIc, LOWs, MSs, M3 = MK[blk]
            STOP = int(_os2.environ.get('C2_STOP', '99'))
            nch = 128 // blk
            rows = slice(128 * ti, 128 * ti + 128)
            for (nm, a, k) in (("kk", KKt, kKK), ("nq", NQt, kNQ), ("k", Kt, kK), ("r", Rt, kR), ("v", Vt, kV), ("w", LW, kLW)):
                self.dma("sp", h16(a), scr_tm(nm, rows), [("SCR", nm)], [k])
            self.act(LW, LW, AF.Ln, [kLW], [kLW])
            for nb in range(2):
                ps, pk = self.bank()
                self.mm(ps[:, :], TRIc[:], LW[:, nb * 512:(nb + 1) * 512], True, True, ["C_MK", kLW], [pk])
                self.cp("act", CL[:, nb * 512:(nb + 1) * 512], ps[:, :], [pk], [kCL])
            self.tt("dve", E1, CL, LW, ALU.subtract, [kCL, kLW], [kE1])
            self.act(E1, E1, AF.Exp, [kE1], [kE1])
            self.tt("dve", KKt, KKt, E1, ALU.mult, [kKK, kE1], [kKK])
            self.act(E1, CL, AF.Exp, [kCL], [kE1], scale=-1.0)
            self.tt("dve", NQt, NQt, E1, ALU.mult, [kNQ, kE1], [kNQ])
            self.tt("dve", Kt, Kt, E1, ALU.mult, [kK, kE1], [kK])
            self.act(E1, CL, AF.Exp, [kCL], [kE1])
            self.tt("dve", Rt, Rt, E1, ALU.mult, [kR, kE1], [kR])
            self.cp("pool", Vb[:], Vt, [kV], ["C_Vb"])
            for (Zb, src, k, zk) in ((QZ, NQt, kNQ, "C_QZ"), (KZ, Kt, kK, "C_KZ")):
                zv = Zb[:].rearrange("p (j e) f -> p j (e f)", e=2)
                sv = src.rearrange("p (j e d) -> p j e d", e=2, d=64)
                self.cp("pool", zv[:, :, 0:64], sv[:, :, 0, :], [k], [zk])
                self.cp("pool", zv[:, :, 192:256], sv[:, :, 1, :], [k], [zk])
            if STOP <= 1:
                return
            for (src, k, dstf, dk) in ((KKt, kKK, lambda j: PRT[:, j, 0, :], "C_PRT"), (Rt, kR, lambda j: PRT[:, j, 1, :], "C_PRT"),
                                       (NQt, kNQ, lambda j: QT[:, j, :], "C_QT"), (Kt, kK, lambda j: KT[:, j, :], "C_KT")):
                for half in range(2):
                    ps, pk = self.bank()
                    for jj in range(4):
                        j = half * 4 + jj
                        self.tr(ps[:, jj * 128:(jj + 1) * 128], src[:, j * 128:(j + 1) * 128], self.IDF[:], [k, "IDF"], [pk])
                    for jj in range(4):
                        j = half * 4 + jj
                        self.cp("act" if jj % 2 else "dve", dstf(j), ps[:, jj * 128:(jj + 1) * 128], [pk], [(dk, j)])
            for half in range(2):
                ps, pk = self.bank()
                for jj in range(4):
                    j = half * 4 + jj
                    self.tr(ps[:, jj * 128:(jj + 1) * 128], E1[:, j * 128:(j + 1) * 128], self.IDF[:], [kE1, "IDF"], [pk])
                self.cp("dve", ECLE[:, half * 4:(half + 1) * 4, 0:nch], ps[:, :].rearrange("p (a c b) -> p a c b", a=4, b=blk)[:, :, :, blk - 1], [pk], ["C_ECLE"])
            if STOP <= 2:
                return
            for hg in range(2):
                for hh in range(8):
                    h = hg * 8 + hh
                    j, e = h // 2, h % 2
                    pr = slice(e * 64, (e + 1) * 64)
                    ps, pk = self.bank()
                    rhsPR = PRT[pr, j, :, :].rearrange("p a t -> p (a t)")
                    self.mm(ps[:, 0:256], QT[pr, j, :], rhsPR, True, True, [("C_QT", j), ("C_PRT", j)], [pk])
                    self.mm(ps[:, 256:512], KT[pr, j, :], rhsPR, True, True, [("C_KT", j), ("C_PRT", j)], [pk])
                    ps2, pk2 = self.bank()
                    self.mm(ps2[:, 0:128], PRT[pr, j, 0, :], QT[pr, j, :], True, True, [("C_QT", j), ("C_PRT", j)], [pk2])
                    self.tt("dve", ZN[:, hh, :], ps[:, 0:128], MSs[:], ALU.mult, [pk, "C_MK"], [("C_ZN", hh)])
                    self.tt("dve", MB[:, h, :, :], ps[:, 128:512].rearrange("p (a t) -> p a t", a=3), M3[:], ALU.mult, [pk, "C_MK"], [("C_MB", h)])
                    self.tt("dve", XN[:, hh, :], ps2[:, 0:128], LOWs[:], ALU.mult, [pk2, "C_MK"], [("C_XN", hh)])
                    self.tt("pool", TTf[:, hh, :], ZN[:, hh, :], self.IDF[:], ALU.add, [("C_ZN", hh), "IDF"], [("C_TT", hh)])
                for n in range(4 if STOP > 3 else 0):
                    for hh in range(8):
                        ps, pk = self.bank()
                        self.mm(ps[:, 0:128], ZN[:, hh, :], XN[:, hh, :], True, True, [("C_ZN", hh), ("C_XN", hh)], [pk])
                        if n < 3:
                            self.mm(ps[:, 128:256], XN[:, hh, :], ZN[:, hh, :], True, True, [("C_ZN", hh), ("C_XN", hh)], [pk])
                        self.cp("act", XN[:, hh, :], ps[:, 0:128], [pk], [("C_XN", hh)])
                        if n < 3:
                            self.cp("act", ZN[:, hh, :], ps[:, 128:256], [pk], [("C_ZN", hh)])
                        ps2, pk2 = self.bank()
                        self.mm(ps2[:, 0:128], XN[:, hh, :], TTf[:, hh, :], True, True, [("C_XN", hh), ("C_TT", hh)], [pk2])
                        self.tt("dve", TTf[:, hh, :], TTf[:, hh, :], ps2[:, 0:128], ALU.add, [pk2, ("C_TT", hh)], [("C_TT", hh)])
                for hh in range(8):
                    self.cp("act", TTb[:, hg * 8 + hh, :], TTf[:, hh, :], [("C_TT", hh)], [("C_TTb", hg * 8 + hh)])
            if STOP <= 4:
                return
            for (ai, dst, dk) in ((1, Wsb, "C_Wsb"), (2, O0, "C_O0")):
                for half in range(2):
                    ps, pk = self.bank()
                    for hh in range(8):
                        h = half * 8 + hh
                        self.mm(ps[:, hh * 64:(hh + 1) * 64], MB[:, h, ai, :], Vb[:, h * 64:(h + 1) * 64], True, True, [("C_MB", h), "C_Vb"], [pk])
                    self.cp("act" if half else "dve", dst[:, half * 512:(half + 1) * 512], ps[:, :], [pk], [dk])
            self.cp("pool", Ot, O0[:], ["C_O0"], [kO])
            if STOP <= 5:
                return
            nrounds = nch if seqs is None else len(seqs)
            for c_ in range(nrounds):
                if seqs is None:
                    ecol = c_; mcol = CHM[:, c_:c_ + 1]
                else:
                    s_ = seqs[c_]
                    ecol = s_; mcol = SEQM[:, s_:s_ + 1]
                    state_load(I["st_rw_wkv"][ic, s_])
                self.ts("pool", Vm[:], Vb[:], mcol, None, ALU.mult, None, ["C_Vb", "C_MK"], ["C_Vm"])
                pG = [self.bank() for _ in range(2)]; pR = [self.bank() for _ in range(2)]
                for h in range(16):
                    j = h // 2
                    bq, cq = h // 8, (h % 8) * 64
                    self.mm(pG[bq][0][:, cq:cq + 64], PRT[:, j, 0, :], Sb[:, h, :], True, True, [("C_PRT", j), "C_Sb"], [pG[bq][1]])
                    self.mm(pR[bq][0][:, cq:cq + 64], PRT[:, j, 1, :], Sb[:, h, :], True, True, [("C_PRT", j), "C_Sb"], [pR[bq][1]])
                for bq in range(2):
                    cs_ = slice(bq * 512, (bq + 1) * 512)
                    self.tt("dve", GWb[:, cs_], pG[bq][0][:, :], Wsb[:, cs_], ALU.add, [pG[bq][1], "C_Wsb"], [("C_GWb", bq)])
                pU = [self.bank() for _ in range(2)]
                for h in range(16):
                    bq, cq = h // 8, (h % 8) * 64
                    self.mm(pU[bq][0][:, cq:cq + 64], TTb[:, h, :], GWb[:, h * 64:(h + 1) * 64], True, True, [("C_TTb", h), ("C_GWb", bq)], [pU[bq][1]])
                for bq in range(2):
                    cs_ = slice(bq * 512, (bq + 1) * 512)
                    self.act(Ub[:, cs_], pU[bq][0][:, :], AF.Copy, [pU[bq][1], "C_MK"], [("C_Ub", bq)], scale=mcol)
                    self.stt(Ot[:, cs_], pR[bq][0][:, :], mcol, Ot[:, cs_], ALU.mult, ALU.add, [pR[bq][1], kO, "C_MK"], [kO])
                pB = [self.bank() for _ in range(2)]
                for h in range(16):
                    bq, cq = h // 8, (h % 8) * 64
                    self.mm(pB[bq][0][:, cq:cq + 64], MB[:, h, 0, :], Ub[:, h * 64:(h + 1) * 64], True, True, [("C_MB", h), ("C_Ub", bq)], [pB[bq][1]])
                for bq in range(2):
                    cs_ = slice(bq * 512, (bq + 1) * 512)
                    self.stt(Ot[:, cs_], pB[bq][0][:, :], mcol, Ot[:, cs_], ALU.mult, ALU.add, [pB[bq][1], kO, "C_MK"], [kO])
                pS, kS = self.bank()
                for j in range(8):
                    for e in range(2):
                        h = 2 * j + e
                        self.mm(pS[:, j * 64:(j + 1) * 64], QZ[:, h, :], Ub[:, h * 64:(h + 1) * 64], e == 0, False, ["C_QZ", ("C_Ub", h // 8)], [kS])
                        self.mm(pS[:, j * 64:(j + 1) * 64], KZ[:, h, :], Vm[:, h * 64:(h + 1) * 64], False, e == 1, ["C_KZ", "C_Vm"], [kS])
                Sf = S[:].rearrange("p j v -> p (j v)")
                self.tt("dve", Sf, Sf, pS[:, :], ALU.add, [kS, "C_S2"], ["C_S2"])
                self.tt("dve", S[:], S[:], ECLE[:, :, ecol:ecol + 1].to_broadcast([128, 8, 64]), ALU.mult, ["C_S2", "C_ECLE"], ["C_S2"])
                write_sb(S, ["C_S2"])
                if seqs is not None:
                    state_store(self.O["rw_s"][s_])
            self.dma("act", scr_tm("o", rows), h16(Ot), [kO], [("SCR", "o")])

        self.memset("dve", S[:], 0.0, ["C_S2"]); self.memset("pool", Sb[:], 0.0, ["C_Sb"])
        for ti in range(16):
            tile(ti, 32, None)
        state_store(self.O["rw_p"])
        tile(16, 8, list(range(NS)))

    if CHUNKED:
        with self.scope():
            phase2_chunked()
    else:
        with self.scope():
            S = self.sb("C_S", [128, 8, 64]); Tm = self.sb("C_Tm", [128, 8, 64]); KV = [self.sb("C_KV%d" % i, [128, 8, 64]) for i in range(2)]
            SA = self.sb("C_SA", [128, 8])
            names = ("kk", "w", "nq", "k", "r")
            ST0 = {nm: self.X[:, i, 0:2048].rearrange("p (t k) -> p t k", k=64) for i, nm in enumerate(names)}
            ST1 = {nm: self.sb("C_ST1" + nm, [128, 32, 64]) for nm in names}
            VS = [self.sb("C_VS%d" % i, [128, 32, 8]) for i in range(2)]
            OS = [self.sb("C_OS%d" % i, [128, 32, 8]) for i in range(2)]

            def bk(ap):
                return ap.unsqueeze(1).to_broadcast([128, 8, 64])

            def bv(ap):
                return ap.unsqueeze(2).to_broadcast([128, 8, 64])

            nsub = 0
            import os as _os
            NOSAME = bool(int(_os.environ.get("C_NOSAME", "1")))
            SP = self.PS[7][:, :].rearrange("p (a k) -> p a k", k=64)
            SK = ("PS", 7)

            def run_segment(row0, nsteps):
                nonlocal nsub
                for t0 in range(0, nsteps, 32):
                    n = min(32, nsteps - t0)
                    sb_ = nsub % 2; nsub += 1
                    stg = ST0 if sb_ == 0 else ST1
                    skey = "C_STG%d" % sb_
                    r0 = row0 + t0
                    for nm in names:
                        src = SCRH[nm][:, r0:r0 + n, :]
                        for vb in range(8):
                            self.dma("sp", stg[nm][vb * 16:(vb + 1) * 16, 0:n, :], src, [("SCR", nm)], [(skey, nm)] + ([("X", "stg")] if sb_ == 0 else []))
                    srcv = SCR["v"][r0:r0 + n, :].rearrange("t (h v) -> h t v", v=64)
                    for vb in range(8):
                        self.dma("sp", VS[sb_][vb * 16:(vb + 1) * 16, 0:n, :], srcv[:, :, vb * 8:(vb + 1) * 8], [("SCR", "v")], [(skey, "v")])
                    xk = [("X", "stg")] if sb_ == 0 else []
                    for i in range(n):
                        kvb = i % 2
                        self.tt("pool", KV[kvb][:], bk(stg["k"][:, i, :]), bv(VS[sb_][:, i, :]), ALU.mult, [(skey, "k"), (skey, "v")] + xk, ["C_KV%d" % kvb])
                        self.P.nosame = NOSAME
                        self.tt("dve", Tm[:], SP[:], bk(stg["kk"][:, i, :]), ALU.mult, [SK, (skey, "kk")] + xk, ["C_Tm"])
                        red(SA[:], Tm[:], ["C_Tm"], ["C_SA"])
                        self.tt("dve", SP[:], SP[:], bk(stg["w"][:, i, :]), ALU.mult, [SK, (skey, "w")] + xk, [SK])
                        self.tt("dve", Tm[:], bk(stg["nq"][:, i, :]), bv(SA[:]), ALU.mult, ["C_SA", (skey, "nq")] + xk, ["C_Tm"])
                        self.tt("dve", SP[:], SP[:], Tm[:], ALU.add, [SK, "C_Tm"], [SK])
                        self.tt("dve", SP[:], SP[:], KV[kvb][:], ALU.add, [SK, "C_KV%d" % kvb], [SK])
                        self.tt("dve", Tm[:], SP[:], bk(stg["r"][:, i, :]), ALU.mult, [SK, (skey, "r")] + xk, ["C_Tm"])
                        red(OS[sb_][:, i, :], Tm[:], ["C_Tm"], ["C_OS%d" % sb_])
                        self.P.nosame = False
                    dsto = SCR["o"][r0:r0 + n, :].rearrange("t (h v) -> h t v", v=64)
                    for vb in range(8):
                        self.dma("act", dsto[:, :, vb * 8:(vb + 1) * 8], OS[sb_][vb * 16:(vb + 1) * 16, 0:n, :], ["C_OS%d" % sb_], [("SCR", "o")])

            def state_io(dram, load):
                for vb in range(8):
                    d = dram[:, vb * 8:(vb + 1) * 8, :]
                    if load:
                        self.dma("sp", S[vb * 16:(vb + 1) * 16, :, :], d, (), ["C_S"])
                    else:
                        self.dma("act", d, S[vb * 16:(vb + 1) * 16, :, :], ["C_S"], ())

            self.memset("dve", S[:], 0.0, ["C_S"])
            self.cp("dve", SP[:], S[:], ["C_S"], [SK])
            run_segment(0, NTP)
            self.cp("dve", S[:], SP[:], [SK], ["C_S"])
            state_io(self.O["rw_p"], False)
            for s in range(NS):
                state_io(I["st_rw_wkv"][ic, s], True)
                self.cp("dve", SP[:], S[:], ["C_S"], [SK])
                run_segment(NTP + 8 * s, 8)
                self.cp("dve", S[:], SP[:], [SK], ["C_S"])
                state_io(self.O["rw_s"][s], False)

    if self.debug:
        self.dma("sp", self.O["dbg_scr"][7], SCR["o"], [("SCR", "o")], ())
    with self.scope():
        pbc(0, I["rwkv_lnx_w"][ic:ic + 1, :]); pbc(1, I["rwkv_lnx_b"][ic:ic + 1, :]); pbc(2, I["rwkv_r_k"].rearrange("a h n -> a (h n)")[ic:ic + 1, :])
        (Plw, klw), (Plb, klb), (Prk, krk) = xa(0), xa(1), xa(2)
        (Ot, kO), (Rt, kR), (Kt, kK), (Vt, kV), (Gt, kG), (T1, kT1) = [xa(i) for i in range(7, 13)]
        WO = self.sb("C_WO", [128, 8, D], BF16)
        self.dma("pool", WO[:], I["rwkv_w_out"][ic].rearrange("(kc p) n -> p kc n", p=128), (), ["C_WO"])
        YT = self.sb("C_YT", [128, 8, 128], BF16); XT_ = self.sb("C_XTL", [128, 8, 128])
        M1 = self.sb("C_M1", [128, 16]); M2 = self.sb("C_M2", [128, 16])
        for ti in range(17):
            rows = slice(128 * ti, 128 * ti + 128)
            for (nm, a, k) in (("o", Ot, kO), ("r", Rt, kR), ("k", Kt, kK), ("v", Vt, kV), ("g", Gt, kG)):
                self.dma("sp", h16(a), scr_tm(nm, rows), [("SCR", nm)], [k])
            self.dma("sp", XT_[:], XD[:, :, rows], ["XD"], ["C_XTL"])
            red(M1[:], h16(Ot), [kO], ["C_M1"])
            self.ts("dve", M1[:], M1[:], -1.0 / 64, None, ALU.mult, None, ["C_M1"], ["C_M1"])
            self.tt("dve", h16(Ot), h16(Ot), b16(M1[:]), ALU.add, [kO, "C_M1"], [kO])
            self.tt("dve", T1, Ot, Ot, ALU.mult, [kO], [kT1])
            red(M2[:], h16(T1), [kT1], ["C_M2"])
            self.act(M2[:], M2[:], AF.Ln, ["C_M2", ("C1", 3)], ["C_M2"], scale=1.0 / 64, bias=self.C1[:, 3:4])
            self.act(M2[:], M2[:], AF.Exp, ["C_M2"], ["C_M2"], scale=-0.5)
            self.tt("dve", h16(Ot), h16(Ot), b16(M2[:]), ALU.mult, [kO, "C_M2"], [kO])
            self.tt("dve", Ot, Ot, Plw, ALU.mult, [kO, klw], [kO])
            self.tt("dve", Ot, Ot, Plb, ALU.add, [kO, klb], [kO])
            self.tt("dve", T1, Rt, Kt, ALU.mult, [kR, kK], [kT1])
            self.tt("dve", T1, T1, Prk, ALU.mult, [kT1, krk], [kT1])
            red(M1[:], h16(T1), [kT1], ["C_M1"])
            self.tt("dve", h16(T1), h16(Vt), b16(M1[:]), ALU.mult, [kV, "C_M1"], [kT1])
            self.tt("dve", Ot, Ot, T1, ALU.add, [kO, kT1], [kO])
            self.tt("dve", Ot, Ot, Gt, ALU.mult, [kO, kG], [kO])
            for half in range(2):
                ps, pk = self.bank()
                for cc in range(4):
                    c = half * 4 + cc
                    self.tr(ps[:, cc * 128:(cc + 1) * 128], Ot[:, c * 128:(c + 1) * 128], self.IDF[:], [kO, "IDF"], [pk])
                self.cp("act", YT[:, half * 4:(half + 1) * 4, :].rearrange("p c t -> p (c t)"), ps[:, :], [pk], ["C_YT"])
            for n in range(8):
                ps, pk = self.bank()
                for c in range(8):
                    self.mm(ps[:, 0:128], WO[:, c, n * 128:(n + 1) * 128], YT[:, c, :], c == 0, c == 7, ["C_WO", "C_YT"], [pk])
                self.tt("dve", XT_[:, n, :], XT_[:, n, :], ps[:, 0:128], ALU.add, [pk, "C_XTL"], ["C_XTL"])
            self.dma("sp", XD[:, :, rows], XT_[:], ["C_XTL"], ["XD"])
    self.dma("sp", self.X[:], XD, ["XD"], ["X"])
    self.fm_to_rows(USH[:, :, 0:1], "C_USH", 1, self.O["rs_p"])
    self.fm_to_rows(USH[:, :, 1:], "C_USH", NS, self.O["rs_s"])


KB.layer_C = layer_C


def build(layers=(0, 1, 2, 3), debug=False, mlp=True):
    kb = KB(layers, debug)
    with kb.es:
        kb.setup()
        kb.load_x()
        for l in layers:
            with kb.scope():
                kind = l % 3
                if kind == 0:
                    kb.layer_A(l, l // 3)
                elif kind == 1:
                    kb.layer_B(l, 0)
                else:
                    kb.layer_C(l, 0)
            if mlp:
                with kb.scope():
                    kb.mlp(l)
        if not getattr(kb, '_skip_final', False):
            with kb.scope():
                kb.final_out()
        if debug:
            kb.dump_x()
        kb.P.emit()
    return kb.nc


def make_in_maps(inputs, cores):
    g = {k: np.ascontiguousarray(np.asarray(v, dtype=np.float32)) for k, v in inputs.items()}
    maps = []
    for i in cores:
        sl = slice(NS * i, NS * (i + 1))
        m = {
            "xp": g["x_prompt"][i], "xs": g["x_sample"][sl].reshape(NS * LS, D),
            "st_lru_conv": g["state_lru_conv"][:, sl], "st_lru_h": g["state_lru_h"][:, sl],
            "st_ssm_conv": g["state_ssm_conv"][:, sl], "st_ssm": g["state_ssm"][:, sl],
            "st_rw_shift": g["state_rwkv_shift"][:, sl], "st_rw_wkv": g["state_rwkv_wkv"][:, sl],
        }
        for k in IN_SPECS:
            if k in m:
                continue
            v = g[k]
            if k == "norm_final":
                v = v.reshape(1, D)
            m[k] = v
        maps.append({k: np.ascontiguousarray(v) for k, v in m.items()})
    return maps


def kernel(**inputs):
    nc = build()
    cores = list(range(8))
    res = run_bass_kernel_spmd(nc, make_in_maps(inputs, cores), core_ids=cores)
    r = res.results
    def cat(name, axis=0):
        return np.concatenate([x[name] for x in r], axis=axis)
    y_p = np.stack([x["y_p"] for x in r])
    y_s = cat("y_s").reshape(128, LS, D)
    lc_p = np.stack([x["lc_p"] for x in r], axis=1)
    lc_s = cat("lc_s", 1)
    lh_p = np.stack([x["lh_p"] for x in r], axis=1)
    lh_s = cat("lh_s", 1)
    sc_p = np.stack([x["sc_p"] for x in r])[None]
    sc_s = cat("sc_s")[None]
    ss_p = np.stack([x["ss_p"] for x in r])[None]
    ss_s = cat("ss_s")[None]
    rs_p = np.stack([x["rs_p"][0] for x in r])[None]
    rs_s = cat("rs_s")[None]
    rw_p = np.stack([x["rw_p"] for x in r])[None]
    rw_s = cat("rw_s")[None]
    outs = (y_p, y_s, lc_p, lc_s, lh_p, lh_s, sc_p, sc_s, ss_p, ss_s, rs_p, rs_s, rw_p, rw_s)
    return tuple(np.ascontiguousarray(o, dtype=np.float32) for o in outs)
```

```python
import numpy as np
import concourse.bass as bass
import concourse.mybir as mybir
from concourse.bass_utils import run_bass_kernel_spmd

F32 = mybir.dt.float32
BF16 = mybir.dt.bfloat16
I32 = mybir.dt.int32
AF = mybir.ActivationFunctionType
ALU = mybir.AluOpType
AX = mybir.AxisListType

ENGS = ("pe", "act", "dve", "pool", "sp")
SAME_ENGINE_SYNC = True
import os as _os_
NFUSE = int(_os_.environ.get('NFUSE', '1'))
FUSE_WAIT = bool(int(_os_.environ.get('FUSE_WAIT', '1')))
NOSAME_ENGS = tuple(x for x in _os_.environ.get('NOSAME_ENGS', '').split(',') if x)
N_DMA_SEMS = 12


class _St:
    __slots__ = ("w", "r")

    def __init__(self):
        self.w = None
        self.r = {}


class _Op:
    __slots__ = ("eng", "fn", "waits", "dma", "signal", "tok")


class Prog:
    def __init__(self, nc):
        self.nc = nc
        self.ops = {e: [] for e in ENGS}
        self.state = {}
        self.ndma = {e: 0 for e in ENGS}
        self.dma_toks = {}
        self.pending = {}
        self.nosame = False

    def barrier(self):
        toks = set()
        for e in ENGS:
            for o in reversed(self.ops[e]):
                if not o.dma:
                    toks.add(o.tok)
                    break
            n = self.ndma[e]
            for j in range(max(0, n - N_DMA_SEMS), n):
                toks.add(("dma", e, j))
        self.pending = {e: set(toks) for e in ENGS}

    def _states(self, key, create):
        if isinstance(key, tuple):
            buf, sub = key
        else:
            buf, sub = key, None
        d = self.state.setdefault(buf, {})
        if sub is None:
            if create and "*" not in d:
                d["*"] = _St()
            return list(d.values())
        out = []
        if "*" in d:
            out.append(d["*"])
        if sub not in d:
            if create:
                d[sub] = _St()
                out.append(d[sub])
        else:
            out.append(d[sub])
        return out

    def op(self, eng, fn, reads=(), writes=(), dma=False):
        ps_r = [k for k in reads if isinstance(k, tuple) and k[0] == "PS"]
        if ps_r:
            reads = [k for k in reads if not (isinstance(k, tuple) and k[0] == "PS")]
            writes = list(writes) + ps_r
        o = _Op()
        o.eng, o.fn, o.dma, o.signal = eng, fn, dma, False
        idx = len(self.ops[eng])
        if dma:
            j = self.ndma[eng]
            self.ndma[eng] += 1
            o.tok = ("dma", eng, j)
            self.dma_toks[(eng, j)] = o
        else:
            o.tok = ("eng", eng, idx)
        waits = set()
        for k in reads:
            for st in self._states(k, True):
                if st.w is not None:
                    waits.add(st.w)
        for k in writes:
            for st in self._states(k, True):
                if st.w is not None:
                    waits.add(st.w)
                for t in st.r.values():
                    waits.add(t)
        if self.pending.get(eng):
            waits |= self.pending.pop(eng)
        w2 = set()
        for t in waits:
            if t[0] == "eng" and t[1] == eng:
                if not dma and (eng in ("pe", "sp") or not SAME_ENGINE_SYNC or self.nosame or (NOSAME_ENGS and eng in NOSAME_ENGS)):
                    continue
            w2.add(t)
        o.waits = w2
        for k in reads:
            for st in self._states(k, True):
                st.r[o.tok if dma else o.tok[1]] = o.tok
        for k in writes:
            for st in self._states(k, True):
                st.w = o.tok
                st.r = {}
        self.ops[eng].append(o)
        return o

    def emit(self, final_wait_all=True):
        nc = self.nc
        tokmap = {}
        for e in ENGS:
            for i, o in enumerate(self.ops[e]):
                tokmap[o.tok] = o
        for e in ENGS:
            for o in self.ops[e]:
                for t in o.waits:
                    tokmap[t].signal = True
        import contextlib
        with contextlib.ExitStack() as es:
            EPOCH = 16000
            nsig = {e: sum(1 for o in self.ops[e] if o.signal and not o.dma) for e in ENGS}
            esem = {e: [es.enter_context(nc.semaphore("s_%s%d" % (e, i))) for i in range(nsig[e] // EPOCH + 1)] for e in ENGS if e != "sp"}
            dsem = {e: [es.enter_context(nc.semaphore("d_%s%d" % (e, i))) for i in range(N_DMA_SEMS)]
                    for e in ENGS if self.ndma[e] > 0}
            val = {}
            for e in ENGS:
                c = 0
                for o in self.ops[e]:
                    if o.dma:
                        j = o.tok[2]
                        val[o.tok] = (dsem[e][j % N_DMA_SEMS], 16 * (j // N_DMA_SEMS + 1))
                    elif o.signal:
                        val[o.tok] = (esem[e][c // EPOCH], c % EPOCH + 1)
                        c += 1
            self.maxcount = {}
            block = es.enter_context(nc.Block())

            def run(e, eng):
                seen = {}
                for o in self.ops[e]:
                    ws = []
                    for t in o.waits:
                        ws.append(val[t])
                    if o.dma:
                        j = o.tok[2]
                        if j >= N_DMA_SEMS:
                            ws.append(val[("dma", e, j - N_DMA_SEMS)])
                    need = []
                    for (s, v) in ws:
                        if seen.get(id(s), 0) >= v:
                            continue
                        seen[id(s)] = v
                        need = [(s2, v2) for (s2, v2) in need if s2 is not s] + [(s, v)]
                    fuse = FUSE_WAIT and need and not o.dma
                    self.stats = getattr(self, "stats", {})
                    self.stats[(e, len(need))] = self.stats.get((e, len(need)), 0) + 1
                    nf = min(len(need), NFUSE) if fuse else 0
                    for (s, v) in need[:len(need) - nf]:
                        eng.wait_ge(s, v)
                    ins = o.fn(eng)
                    for (s, v) in need[len(need) - nf:]:
                        ins._wait_ge(s, v)
                    if o.dma:
                        s, v = val[o.tok]
                        ins.then_inc(s, 16)
                    elif o.signal:
                        ins.then_inc(val[o.tok][0], 1)
                if final_wait_all:
                    n = self.ndma[e]
                    for j in range(max(0, n - N_DMA_SEMS), n):
                        s, v = val[("dma", e, j)]
                        if seen.get(id(s), 0) < v:
                            eng.wait_ge(s, v)
                            seen[id(s)] = v

            @block.tensor
            def _(eng):
                run("pe", eng)

            @block.scalar
            def _(eng):
                run("act", eng)

            @block.vector
            def _(eng):
                run("dve", eng)

            @block.gpsimd
            def _(eng):
                run("pool", eng)

            @block.sync
            def _(eng):
                run("sp", eng)
import contextlib


D = 1024
NTP = 2048
NS = 16
LS = 8
NT = NTP + NS * LS
TT = [(0, 512), (512, 512), (1024, 512), (1536, 512), (2048, 128)]
UC = 1 + NTP + NS * 9
XBC = 3 + NTP + NS * 11

PROW = {}


def _prow_layout():
    r = 0
    def add(name, n):
        nonlocal r
        PROW[name] = r
        r += n
    add("norm_mix", 4); add("norm_ffn", 4); add("norm_final", 1)
    add("lru_conv_w", 8); add("lru_conv_b", 2); add("lru_b_r", 2); add("lru_b_i", 2); add("lru_lambda", 2)
    add("ssm_norm_w", 2); add("ssm_conv_w", 16); add("ssm_conv_b", 4)
    add("rwkv_mu", 6); add("rwkv_w0", 1); add("rwkv_a0", 1); add("rwkv_k_k", 1); add("rwkv_k_a", 1)
    add("rwkv_lnx_w", 1); add("rwkv_lnx_b", 1); add("rwkv_r_k", 1)
    return r


NPROW = _prow_layout()

IN_SPECS = {
    "xp": [NTP, D], "xs": [NS * LS, D],
    "st_lru_conv": [2, NS, 3, D], "st_lru_h": [2, NS, D], "st_ssm_conv": [1, NS, 3, 4096],
    "st_ssm": [1, NS, 32, 64, 128], "st_rw_shift": [1, NS, D], "st_rw_wkv": [1, NS, 16, 64, 64],
    "norm_mix": [4, D], "norm_ffn": [4, D], "norm_final": [1, D],
    "lru_w_in": [2, D, 2048], "lru_conv_w": [2, 4, D], "lru_conv_b": [2, D], "lru_w_r": [2, 8, 128, 128],
    "lru_b_r": [2, D], "lru_w_i": [2, 8, 128, 128], "lru_b_i": [2, D], "lru_lambda": [2, D], "lru_w_out": [2, D, D],
    "ssm_w_in": [1, D, 6176], "ssm_conv_w": [1, 4, 4096], "ssm_conv_b": [1, 4096], "ssm_dt_bias": [1, 32],
    "ssm_a_log": [1, 32], "ssm_d": [1, 32], "ssm_norm_w": [1, 2048], "ssm_w_out": [1, 2048, D],
    "rwkv_mu": [1, 6, D], "rwkv_w_rkv": [1, 3, D, D], "rwkv_w0": [1, D], "rwkv_w_w1": [1, D, 64], "rwkv_w_w2": [1, 64, D],
    "rwkv_a0": [1, D], "rwkv_w_a1": [1, D, 64], "rwkv_w_a2": [1, 64, D], "rwkv_w_g1": [1, D, 128], "rwkv_w_g2": [1, 128, D],
    "rwkv_k_k": [1, D], "rwkv_k_a": [1, D], "rwkv_r_k": [1, 16, 64], "rwkv_lnx_w": [1, D], "rwkv_lnx_b": [1, D],
    "rwkv_w_out": [1, D, D], "ffn_w1": [4, D, 4096], "ffn_w2": [4, 4096, D],
}
OUT_SPECS = {
    "y_p": [NTP, D], "y_s": [NS * LS, D],
    "lc_p": [2, 3, D], "lc_s": [2, NS, 3, D], "lh_p": [2, D], "lh_s": [2, NS, D],
    "sc_p": [3, 4096], "sc_s": [NS, 3, 4096], "ss_p": [32, 64, 128], "ss_s": [NS, 32, 64, 128],
    "rs_p": [1, D], "rs_s": [NS, D], "rw_p": [16, 64, 64], "rw_s": [NS, 16, 64, 64],
}


class KB:
    def __init__(self, layers=(0, 1, 2, 3), debug=False):
        self.nc = nc = bass.Bass("TRN2", target_bir_lowering=False)
        self.P = Prog(nc)
        self.es = contextlib.ExitStack()
        self.I = {k: nc.dram_tensor(k, v, F32, kind="ExternalInput").ap() for k, v in IN_SPECS.items()}
        self.O = {k: nc.dram_tensor(k, v, F32, kind="ExternalOutput").ap() for k, v in OUT_SPECS.items()}
        self.debug = debug
        if debug:
            self.O["dbg_x"] = nc.dram_tensor("dbg_x", [128, 8, NT], F32, kind="ExternalOutput").ap()
            self.O["dbg_scr"] = nc.dram_tensor("dbg_scr", [8, NT, D], F32, kind="ExternalOutput").ap()
        self.bank_i = 0
        self.layers = layers
        self._n = 0

    def sb(self, name, shape, dt=F32):
        self._n += 1
        return self.es.enter_context(self.nc.sbuf_tensor("%s_%d" % (name, self._n), shape, dt))

    @contextlib.contextmanager
    def scope(self):
        old = self.es
        self.es = contextlib.ExitStack()
        try:
            yield
        finally:
            self.es.close()
            self.es = old
            self.P.barrier()

    def bank(self):
        b = self.bank_i
        self.bank_i = (self.bank_i + 1) % 8
        return self.PS[b], ("PS", b)

    def dma(self, eng, out, in_, reads=(), writes=()):
        self.P.op(eng, lambda e: e.dma_start(out=out, in_=in_), reads, writes, dma=True)

    def act(self, out, in_, func, reads, writes, **kw):
        self.P.op("act", lambda e: e.activation(out=out, in_=in_, func=func, **kw), reads, writes)

    def mm(self, out, lhsT, rhs, start, stop, reads, writes):
        self.P.op("pe", lambda e: e.matmul(out, lhsT=lhsT, rhs=rhs, start=start, stop=stop), reads, writes)

    def tr(self, out, in_, ident, reads, writes):
        self.P.op("pe", lambda e: e.transpose(out, in_, ident), reads, writes)

    def ts(self, eng, out, in0, s1, s2, op0, op1, reads, writes):
        if op1 is None:
            self.P.op(eng, lambda e: e.tensor_scalar(out=out, in0=in0, scalar1=s1, scalar2=None, op0=op0), reads, writes)
        else:
            self.P.op(eng, lambda e: e.tensor_scalar(out=out, in0=in0, scalar1=s1, scalar2=s2, op0=op0, op1=op1), reads, writes)

    def stt(self, out, in0, scalar, in1, op0, op1, reads, writes):
        self.P.op("dve", lambda e: e.scalar_tensor_tensor(out=out, in0=in0, scalar=scalar, in1=in1, op0=op0, op1=op1), reads, writes)

    def tt(self, eng, out, in0, in1, op, reads, writes):
        self.P.op(eng, lambda e: e.tensor_tensor(out=out, in0=in0, in1=in1, op=op), reads, writes)

    def cp(self, eng, out, in_, reads, writes):
        if eng == "act":
            self.P.op("act", lambda e: e.activation(out=out, in_=in_, func=AF.Copy), reads, writes)
        else:
            self.P.op(eng, lambda e: e.tensor_copy(out=out, in_=in_), reads, writes)

    def memset(self, eng, ap, v, writes):
        self.P.op(eng, lambda e: e.memset(ap, v), (), writes)

    def scan(self, out, d0, d1, init, reads, writes):
        self.P.op("dve", lambda e: e.tensor_tensor_scan(out=out, data0=d0, data1=d1, initial=init, op0=ALU.mult, op1=ALU.add), reads, writes)

    def xv(self, c, ti):
        c0, w = TT[ti]
        return self.X[:, c, c0:c0 + w]

    def uv(self, c, ti, shift=0):
        if ti < 4:
            s = 1 + 512 * ti + shift
            return self.U[:, c, s:s + 512]
        v = self.U[:, c, 1 + NTP:UC].rearrange("p (s t) -> p s t", t=9)
        return v[:, :, 1 + shift:9 + shift]

    @staticmethod
    def v3(ap, ti):
        if ti < 4:
            return ap
        return ap.rearrange("p (s t) -> p s t", t=8)

    def setup(self):
        nc = self.nc
        self.PS = [self.es.enter_context(nc.psum_tensor("ps%d" % i, [128, 512], F32)) for i in range(8)]
        self.X = self.sb("X", [128, 8, NT])
        self.U = self.sb("U", [128, 8, UC], BF16)
        self.IDF = self.sb("IDF", [128, 128])
        self.IDB = self.sb("IDB", [128, 128], BF16)
        self.ONESB = self.sb("ONESB", [128, 128], BF16)
        self.C1 = self.sb("C1", [128, 4])
        self.PRM = self.sb("PRM", [64, D])
        self.PF = self.sb("PF", [128, 8, 64])
        self.STG = [self.sb("STG%d" % i, [128, D]) for i in range(2)]
        self.SQ = self.sb("SQ", [128, 8, 512], BF16)
        self.RS = self.sb("RS", [128, 512])
        P = self.P
        P.op("pool", lambda e: e.memset(self.IDF[:], 0.0), (), ["IDF"])
        P.op("pool", lambda e: e.affine_select(out=self.IDF[:], in_=self.IDF[:], pattern=[[-1, 128]], compare_op=ALU.not_equal,
                                               fill=1.0, base=0, channel_multiplier=1), ["IDF"], ["IDF"])
        self.cp("dve", self.IDB[:], self.IDF[:], ["IDF"], ["IDB"])
        self.memset("dve", self.ONESB[:], 1.0, ["ONESB"])
        self.memset("dve", self.C1[:, 0:1], 1e-6, [("C1", 0)])
        self.memset("dve", self.C1[:, 1:2], 1.0, [("C1", 1)])
        self.memset("dve", self.C1[:, 2:3], 1e-5, [("C1", 2)])
        self.memset("dve", self.C1[:, 3:4], 64e-5, [("C1", 3)])
        self.memset("dve", self.U[:, :, 0:1], 0.0, [("U", "shiftp")])
        self.memset("pool", self.PRM[:], 0.0, ["PRM"])
        I = self.I
        def row(name, src, n):
            r = PROW[name]
            self.dma("sp", self.PRM[r:r + n, :], src, (), ["PRM"])
        row("norm_mix", I["norm_mix"], 4); row("norm_ffn", I["norm_ffn"], 4); row("norm_final", I["norm_final"], 1)
        row("lru_conv_w", I["lru_conv_w"].rearrange("a k d -> (a k) d"), 8)
        row("lru_conv_b", I["lru_conv_b"], 2); row("lru_b_r", I["lru_b_r"], 2); row("lru_b_i", I["lru_b_i"], 2)
        row("lru_lambda", I["lru_lambda"], 2)
        row("ssm_norm_w", I["ssm_norm_w"].rearrange("a (r d) -> (a r) d", d=D), 2)
        row("ssm_conv_w", I["ssm_conv_w"].rearrange("a k (r d) -> (a k r) d", d=D), 16)
        row("ssm_conv_b", I["ssm_conv_b"].rearrange("a (r d) -> (a r) d", d=D), 4)
        row("rwkv_mu", I["rwkv_mu"].rearrange("a k d -> (a k) d"), 6)
        for nm in ("rwkv_w0", "rwkv_a0", "rwkv_k_k", "rwkv_k_a", "rwkv_lnx_w", "rwkv_lnx_b"):
            row(nm, I[nm], 1)
        row("rwkv_r_k", I["rwkv_r_k"].rearrange("a h n -> a (h n)"), 1)
        for c in range(8):
            ps, pk = self.bank()
            self.tr(ps[:, 0:64], self.PRM[:, c * 128:(c + 1) * 128], self.IDF[0:64, 0:64], ["PRM", "IDF"], [pk])
            self.cp("dve", self.PF[:, c, :], ps[:, 0:64], [pk], [("PF", c)])

    def pf(self, name, k, c):
        r = PROW[name] + k
        return self.PF[:, c, r:r + 1]

    def load_x(self):
        n = 0
        for ti, (c0, w) in enumerate(TT):
            for j in range(w // 128):
                b = n % 2
                n += 1
                src = self.I["xp"][c0 + j * 128:c0 + (j + 1) * 128, :] if ti < 4 else self.I["xs"][:, :]
                self.dma("sp", self.STG[b][:], src, (), [("STG", b)])
                for c in range(8):
                    self.tr(self.PS[c][:, j * 128:(j + 1) * 128], self.STG[b][:, c * 128:(c + 1) * 128], self.IDF[:],
                            [("STG", b), "IDF"], [("PS", c)])
            for c in range(8):
                self.cp("dve" if c % 2 == 0 else "act", self.X[:, c, c0:c0 + w], self.PS[c][:, 0:w], [("PS", c)], [("X", (c, ti))])

    def rows_to_fm(self, src, nrows, dst, dkey):
        self.dma("sp", self.STG[0][0:nrows, :], src, (), [("STG", 0)])
        for c in range(8):
            ps, pk = self.bank()
            self.tr(ps[:, 0:nrows], self.STG[0][0:nrows, c * 128:(c + 1) * 128], self.IDF[0:nrows, 0:nrows], [("STG", 0), "IDF"], [pk])
            self.cp("dve", dst[:, c, 0:nrows], ps[:, 0:nrows], [pk], [dkey])

    def fm_to_rows(self, src, skey, nrows, dst):
        for half in range(2):
            ps, pk = self.bank()
            for cc in range(4):
                c = half * 4 + cc
                self.tr(ps[0:nrows, cc * 128:(cc + 1) * 128], src[:, c, 0:nrows], self.IDF[:], [skey, "IDF"], [pk])
            self.cp("dve", self.STG[1][0:nrows, half * 512:(half + 1) * 512], ps[0:nrows, :], [pk], [("STG", 1)])
        self.dma("sp", dst, self.STG[1][0:nrows, :], [("STG", 1)], ())

    def norm_to_U(self, pname, k, shift_out=None):
        for ti, (c0, w) in enumerate(TT):
            ps, pk = self.bank()
            for c in range(8):
                self.act(self.SQ[:, c, :w], self.X[:, c, c0:c0 + w], AF.Square, [("X", (c, ti))], [("SQ", c)])
                self.mm(ps[:, :w], self.ONESB[:], self.SQ[:, c, :w], c == 0, c == 7, [("SQ", c), "ONESB"], [pk])
            self.act(self.RS[:, :w], ps[:, :w], AF.Ln, [pk, ("C1", 0)], ["RS"], scale=1.0 / D, bias=self.C1[:, 0:1])
            self.act(self.RS[:, :w], self.RS[:, :w], AF.Exp, ["RS"], ["RS"], scale=-0.5)
            for c in range(8):
                self.stt(self.uv(c, ti), self.v3(self.X[:, c, c0:c0 + w], ti), self.pf(pname, k, c), self.v3(self.RS[:, :w], ti),
                         ALU.mult, ALU.mult, [("X", (c, ti)), "RS", ("PF", c)], [("U", (c, ti))])
                if shift_out is not None and ti == 3:
                    self.stt(shift_out[:, c, 0:1], self.X[:, c, NTP - 1:NTP], self.pf(pname, k, c), self.RS[:, 511:512],
                             ALU.mult, ALU.mult, [("X", (c, ti)), "RS", ("PF", c)], ["C_USH"])
                if shift_out is not None and ti == 4:
                    self.stt(shift_out[:, c, 1:], self.X[:, c, NTP:NT].rearrange("p (s t) -> p s t", t=8)[:, :, 7],
                             self.pf(pname, k, c), self.RS[:, 0:128].rearrange("p (s t) -> p s t", t=8)[:, :, 7],
                             ALU.mult, ALU.mult, [("X", (c, ti)), "RS", ("PF", c)], ["C_USH"])

    def u_keys(self, ti):
        return [("U", (c, ti)) for c in range(8)]

    def mlp(self, l):
        self.norm_to_U("norm_ffn", l)
        self.W1S = [self.sb("W1S%d" % i, [128, 8, 512], BF16) for i in range(2)]
        self.W2S = [self.sb("W2S%d" % i, [128, 4, D], BF16) for i in range(2)]
        self.HT = [self.sb("HT%d" % i, [128, 4, 512], BF16) for i in range(2)]
        self.RT = [self.sb("RT%d" % i, [128, 512]) for i in range(2)]
        w1 = self.I["ffn_w1"]
        w2 = self.I["ffn_w2"]
        def load(s):
            b = s % 2
            self.dma("pool", self.W1S[b][:], w1[l, :, s * 512:(s + 1) * 512].rearrange("(kc p) n -> p kc n", p=128), (), [("W1S", b)])
            self.dma("pool", self.W2S[b][:], w2[l, s * 512:(s + 1) * 512, :].rearrange("(fc p) n -> p fc n", p=128), (), [("W2S", b)])
        load(0)
        hb = 0
        rb = 0
        for s in range(8):
            if s + 1 < 8:
                load(s + 1)
            b = s % 2
            for ti, (c0, w) in enumerate(TT):
                H = self.HT[hb]
                hk = "HT%d" % hb
                hb ^= 1
                for fc in range(4):
                    ps, pk = self.bank()
                    for kc in range(8):
                        self.mm(self.v3(ps[:, :w], ti), self.W1S[b][:, kc, fc * 128:(fc + 1) * 128], self.uv(kc, ti), kc == 0, kc == 7,
                                [("W1S", b), ("U", (kc, ti))], [pk])
                    R = self.RT[rb]
                    rk = "RT%d" % rb
                    rb ^= 1
                    self.act(R[:, :w], ps[:, :w], AF.Relu, [pk], [rk])
                    self.act(H[:, fc, :w], R[:, :w], AF.Square, [rk], [(hk, fc)])
                for n in range(8):
                    ps, pk = self.bank()
                    for fc in range(4):
                        self.mm(ps[:, :w], self.W2S[b][:, fc, n * 128:(n + 1) * 128], H[:, fc, :w], fc == 0, fc == 3,
                                [("W2S", b), (hk, fc)], [pk])
                    self.tt("dve", self.X[:, n, c0:c0 + w], self.X[:, n, c0:c0 + w], ps[:, :w], ALU.add,
                            [pk, ("X", (n, ti))], [("X", (n, ti))])

    def final_out(self):
        YT = self.STG
        UF = self.sb("UF", [128, 8, 512])
        n = 0
        for ti, (c0, w) in enumerate(TT):
            ps, pk = self.bank()
            for c in range(8):
                self.act(self.SQ[:, c, :w], self.X[:, c, c0:c0 + w], AF.Square, [("X", (c, ti))], [("SQ", c)])
                self.mm(ps[:, :w], self.ONESB[:], self.SQ[:, c, :w], c == 0, c == 7, [("SQ", c), "ONESB"], [pk])
            self.act(self.RS[:, :w], ps[:, :w], AF.Ln, [pk, ("C1", 0)], ["RS"], scale=1.0 / D, bias=self.C1[:, 0:1])
            self.act(self.RS[:, :w], self.RS[:, :w], AF.Exp, ["RS"], ["RS"], scale=-0.5)
            for c in range(8):
                self.stt(UF[:, c, :w], self.X[:, c, c0:c0 + w], self.pf("norm_final", 0, c), self.RS[:, :w],
                         ALU.mult, ALU.mult, [("X", (c, ti)), "RS", ("PF", c)], [("UF", c)])
            for j in range(w // 128):
                b = n % 2
                n += 1
                for half in range(2):
                    ps2, pk2 = self.bank()
                    for cc in range(4):
                        c = half * 4 + cc
                        self.tr(ps2[:, cc * 128:(cc + 1) * 128], UF[:, c, j * 128:(j + 1) * 128], self.IDF[:], [("UF", c), "IDF"], [pk2])
                    self.cp("act" if half else "dve", YT[b][:, half * 512:(half + 1) * 512], ps2[:, :], [pk2], [("STG", b)])
                dst = self.O["y_p"][c0 + j * 128:c0 + (j + 1) * 128, :] if ti < 4 else self.O["y_s"][:, :]
                self.dma("sp", dst, YT[b][:], [("STG", b)], ())

    def dump_x(self):
        self.dma("sp", self.O["dbg_x"], self.X[:], ["X"], ())


def layer_A(self, l, ia):
    I = self.I
    self.norm_to_U("norm_mix", l)
    XB = self.sb("A_XB", [128, XBC])
    XC = self.sb("A_XC", [128, NT])
    XCb = self.sb("A_XCb", [128, NT], BF16)
    GATE = self.sb("A_GATE", [128, NT], BF16)
    R = self.sb("A_R", [128, NT])
    Iq = self.sb("A_I", [128, NT])
    WIN = [self.sb("A_WIN%d" % i, [128, 8, 256], BF16) for i in range(2)]
    WR = [self.sb("A_WR%d" % i, [128, 128], BF16) for i in range(2)]
    WI = [self.sb("A_WI%d" % i, [128, 128], BF16) for i in range(2)]
    WO = [self.sb("A_WO%d" % i, [128, D], BF16) for i in range(2)]
    CL = self.sb("A_CL", [128, 8])
    H0 = self.sb("A_H0", [128, 8, NS])
    CS0 = self.sb("A_CS0", [128, 8, NS * 3])
    HST = self.sb("A_HST", [128, 8, 1 + NS])
    CST = self.sb("A_CST", [128, 8, 3 + NS * 3])
    XBs = XB[:, 3 + NTP:XBC].rearrange("p (s t) -> p s t", t=11)
    XCs = XC[:, NTP:NT].rearrange("p (s t) -> p s t", t=8)
    rl = PROW["lru_lambda"] + ia
    self.act(CL[:], self.PF[:, :, rl], AF.Exp, ["PF"], ["A_CL"], scale=-1.0)
    self.act(CL[:], CL[:], AF.Ln, ["A_CL", ("C1", 1)], ["A_CL"], bias=self.C1[:, 1:2], scale=1.0)
    self.ts("dve", CL[:], CL[:], -8.0, None, ALU.mult, None, ["A_CL"], ["A_CL"])
    self.rows_to_fm(I["st_lru_h"][ia], NS, H0, "A_H0")
    self.rows_to_fm(I["st_lru_conv"][ia].rearrange("s k d -> (s k) d"), NS * 3, CS0, "A_CS0")
    w_in, w_out = I["lru_w_in"], I["lru_w_out"]

    def load(j):
        b = j % 2
        self.dma("pool", WIN[b][:, :, 0:128], w_in[ia, :, j * 128:(j + 1) * 128].rearrange("(kc p) n -> p kc n", p=128), (), [("A_WIN", b)])
        self.dma("pool", WIN[b][:, :, 128:256], w_in[ia, :, D + j * 128:D + (j + 1) * 128].rearrange("(kc p) n -> p kc n", p=128), (), [("A_WIN", b)])
        self.dma("pool", WR[b][:], I["lru_w_r"][ia, j], (), [("A_WR", b)])
        self.dma("pool", WI[b][:], I["lru_w_i"][ia, j], (), [("A_WI", b)])
        self.dma("pool", WO[b][:], w_out[ia, j * 128:(j + 1) * 128, :], (), [("A_WO", b)])

    load(0)
    for j in range(8):
        if j + 1 < 8:
            load(j + 1)
        b = j % 2
        self.memset("dve", XB[:, 0:3], 0.0, [("A_XB", "st")])
        self.cp("dve", XBs[:, :, 0:3], CS0[:, j, :].rearrange("p (s k) -> p s k", k=3), ["A_CS0"], [("A_XB", "st")])
        for ti, (c0, w) in enumerate(TT):
            for half in range(2):
                ps, pk = self.bank()
                for kc in range(8):
                    self.mm(self.v3(ps[:, :w], ti), WIN[b][:, kc, half * 128:(half + 1) * 128], self.uv(kc, ti), kc == 0, kc == 7,
                            [("A_WIN", b), ("U", (kc, ti))], [pk])
                if half == 0:
                    dst = XB[:, 3 + c0:3 + c0 + w] if ti < 4 else XBs[:, :, 3:11]
                    self.cp("act", dst, self.v3(ps[:, :w], ti), [pk], [("A_XB", ti)])
                else:
                    self.act(GATE[:, c0:c0 + w], ps[:, :w], AF.Gelu_apprx_tanh, [pk], [("A_GATE", ti)])
        self.cp("pool", CST[:, j, 0:3], XB[:, NTP:NTP + 3], ["A_XB"], [("A_CST", j)])
        self.cp("pool", CST[:, j, 3:].rearrange("p (s k) -> p s k", k=3), XBs[:, :, 8:11], ["A_XB"], [("A_CST", j)])
        cw = [self.pf("lru_conv_w", ia * 4 + k, j) for k in range(4)]
        cb = self.pf("lru_conv_b", ia, j)
        for (dst, srcf) in ((XC[:, 0:NTP], lambda k: XB[:, k:k + NTP]), (XCs, lambda k: XBs[:, :, k:k + 8])):
            self.ts("dve", dst, srcf(0), cw[0], cb, ALU.mult, ALU.add, ["A_XB", ("PF", j)], ["A_XC"])
            for k in range(1, 4):
                self.stt(dst, srcf(k), cw[k], dst, ALU.mult, ALU.add, ["A_XB", "A_XC", ("PF", j)], ["A_XC"])
        self.cp("act", XCb[:], XC[:], ["A_XC"], ["A_XCb"])
        for ti, (c0, w) in enumerate(TT):
            for (Wg, dstb, bname, key) in ((WR, R, "lru_b_r", "A_R"), (WI, Iq, "lru_b_i", "A_I")):
                ps, pk = self.bank()
                self.mm(ps[:, :w], Wg[b][:], XCb[:, c0:c0 + w], True, True, ["A_XCb", (key.replace("A_", "A_W"), b)], [pk])
                self.act(dstb[:, c0:c0 + w], ps[:, :w], AF.Sigmoid, [pk, ("PF", j)], [key], bias=self.pf(bname, ia, j), scale=1.0)
        T1 = XB[:, 0:NT]
        self.act(R[:], R[:], AF.Exp, ["A_R", "A_CL"], ["A_R"], scale=CL[:, j:j + 1])
        self.act(T1, R[:], AF.Square, ["A_R", "A_XB"], ["A_XB"])
        self.ts("dve", T1, T1, -1.0, 1.0, ALU.mult, ALU.add, ["A_XB"], ["A_XB"])
        self.ts("dve", T1, T1, 1e-30, None, ALU.max, None, ["A_XB"], ["A_XB"])
        self.act(T1, T1, AF.Sqrt, ["A_XB"], ["A_XB"])
        self.memset("dve", T1[:, 0:1], 1.0, ["A_XB"])
        self.memset("dve", R[:, 0:1], 0.0, ["A_R"])
        self.tt("dve", Iq[:], Iq[:], T1, ALU.mult, ["A_I", "A_XB"], ["A_I"])
        self.tt("dve", Iq[:], Iq[:], XC[:], ALU.mult, ["A_I", "A_XC"], ["A_I"])
        self.scan(XC[:, 0:NTP], R[:, 0:NTP], Iq[:, 0:NTP], 0.0, ["A_R", "A_I"], ["A_XC"])
        for s in range(NS):
            c0 = NTP + s * 8
            self.scan(XC[:, c0:c0 + 8], R[:, c0:c0 + 8], Iq[:, c0:c0 + 8], H0[:, j, s:s + 1], ["A_R", "A_I", "A_H0"], ["A_XC"])
        self.cp("pool", HST[:, j, 0:1], XC[:, NTP - 1:NTP], ["A_XC"], [("A_HST", j)])
        self.cp("pool", HST[:, j, 1:], XCs[:, :, 7], ["A_XC"], [("A_HST", j)])
        self.tt("dve", XCb[:], XC[:], GATE[:], ALU.mult, ["A_XC", "A_GATE"], ["A_XCb"])
        for ti, (c0, w) in enumerate(TT):
            for n in range(8):
                ps, pk = self.bank()
                self.mm(ps[:, :w], WO[b][:, n * 128:(n + 1) * 128], XCb[:, c0:c0 + w], True, True, ["A_XCb", ("A_WO", b)], [pk])
                self.tt("dve", self.X[:, n, c0:c0 + w], self.X[:, n, c0:c0 + w], ps[:, :w], ALU.add, [pk, ("X", (n, ti))], [("X", (n, ti))])
    self.fm_to_rows(HST[:, :, 0:1], "A_HST", 1, self.O["lh_p"][ia:ia + 1, :])
    self.fm_to_rows(HST[:, :, 1:], "A_HST", NS, self.O["lh_s"][ia])
    self.fm_to_rows(CST[:, :, 0:3], "A_CST", 3, self.O["lc_p"][ia])
    self.fm_to_rows(CST[:, :, 3:], "A_CST", NS * 3, self.O["lc_s"][ia].rearrange("s k d -> (s k) d"))


KB.layer_A = layer_A


def layer_B(self, l, ib):
    I = self.I
    self.norm_to_U("norm_mix", l)
    f32 = F32
    TRI = self.sb("B_TRI", [128, 128]); NEGM = self.sb("B_NEGM", [128, 128]); ONESF = self.sb("B_ONESF", [128, 128])
    self.memset("pool", TRI[:], 1.0, ["B_TRI"])
    self.P.op("pool", lambda e: e.affine_select(out=TRI[:], in_=TRI[:], pattern=[[1, 128]], compare_op=ALU.is_ge, fill=0.0, base=0,
                                                channel_multiplier=-1), ["B_TRI"], ["B_TRI"])
    self.memset("pool", NEGM[:], 0.0, ["B_NEGM"])
    self.P.op("pool", lambda e: e.affine_select(out=NEGM[:], in_=NEGM[:], pattern=[[1, 128]], compare_op=ALU.is_ge, fill=-1.0e4, base=0,
                                                channel_multiplier=-1), ["B_NEGM"], ["B_NEGM"])
    self.memset("pool", ONESF[:], 1.0, ["B_ONESF"])
    DTB = self.sb("B_DTB", [128, 32]); AB = self.sb("B_AB", [128, 32]); DB = self.sb("B_DB", [128, 32])
    self.dma("sp", DTB[:], I["ssm_dt_bias"][ib:ib + 1, :].partition_broadcast(128), (), ["B_DTB"])
    self.dma("sp", AB[:], I["ssm_a_log"][ib:ib + 1, :].partition_broadcast(128), (), ["B_AB"])
    self.dma("sp", DB[:], I["ssm_d"][ib:ib + 1, :].partition_broadcast(128), (), ["B_DB"])
    self.act(AB[:], AB[:], AF.Exp, ["B_AB"], ["B_AB"])
    self.ts("dve", AB[:], AB[:], -1.0, None, ALU.mult, None, ["B_AB"], ["B_AB"])
    CS0 = self.sb("B_CS0", [128, 32, NS * 3]); CST = self.sb("B_CST", [128, 32, 3 + NS * 3])
    for r in range(4):
        self.rows_to_fm(I["st_ssm_conv"][ib].rearrange("s k d -> (s k) d")[:, r * D:(r + 1) * D], NS * 3, CS0[:, r * 8:(r + 1) * 8, :], "B_CS0")
    WZD = [self.sb("B_WZD0", [128, 8, 260], BF16)] * 2
    BONES = self.sb("B_BONES", [128, 128]); TRIS = self.sb("B_TRIS", [128, 128]); NEGMS = self.sb("B_NEGMS", [128, 128])
    SEQM = self.sb("B_SEQM", [128, 16]); DAS = self.sb("B_DAS", [128, 64]); CDS = self.sb("B_CDS", [128, 64])
    YO = self.sb("B_YO", [128, 256]); BTM = self.sb("B_BTM", [128, 128], BF16); USC = self.sb("B_USC", [128, 8, 128], BF16)
    def _asel(ap, pattern, base, cm, fill, keys):
        self.P.op("pool", lambda e: e.affine_select(out=ap, in_=ap, pattern=pattern, compare_op=ALU.is_ge, fill=fill, base=base,
                                                    channel_multiplier=cm), keys, keys)
    self.memset("pool", SEQM[:], 1.0, ["B_SEQM"])
    _asel(SEQM[:], [[-8, 16]], 0, 1, 0.0, ["B_SEQM"])
    _asel(SEQM[:], [[8, 16]], 7, -1, 0.0, ["B_SEQM"])
    self.memset("pool", BONES[:], 1.0, ["B_BONES"])
    _asel(BONES[:].rearrange("p (s t) -> p s t", t=8), [[-8, 16], [0, 8]], 0, 1, 0.0, ["B_BONES"])
    _asel(BONES[:].rearrange("p (s t) -> p s t", t=8), [[8, 16], [0, 8]], 7, -1, 0.0, ["B_BONES"])
    self.tt("pool", TRIS[:], TRI[:], BONES[:], ALU.mult, ["B_TRI", "B_BONES"], ["B_TRIS"])
    self.tt("pool", NEGMS[:], NEGM[:], BONES[:], ALU.mult, ["B_NEGM", "B_BONES"], ["B_NEGMS"])
    self.ts("dve", YO[:, 0:128], BONES[:], 1.0e4, -1.0e4, ALU.mult, ALU.add, ["B_BONES"], ["B_YO"])
    self.tt("dve", NEGMS[:], NEGMS[:], YO[:, 0:128], ALU.add, ["B_NEGMS", "B_YO"], ["B_NEGMS"])
    self.cp("dve", USC[:].rearrange("p c (s t) -> p c s t", t=8),
            self.U[:, :, 1 + NTP:UC].rearrange("p c (s t) -> p c s t", t=9)[:, :, :, 1:9], [("U", (c_, 4)) for c_ in range(8)], ["B_USC"])

    WXBC = [self.sb("B_WXBC0", [128, 8, 512], BF16)] * 2
    WOUT = [self.sb("B_WOUT0", [128, 2, D], BF16)] * 2
    XF = self.sb("B_XF", [128, 4, 3 + 512]); XFS = self.sb("B_XFS", [128, 4, NS * 11])
    XCf = self.sb("B_XCf", [128, 4, 512]); BCb = self.sb("B_BCb", [128, 2, 512], BF16)
    YGT = self.sb("B_YGT", [128, 2, 512], BF16)
    ST = self.sb("B_ST", [128, 256]); STb = self.sb("B_STb", [128, 256], BF16)
    SIN = self.sb("B_SIN", [128, 2, 128]); SOUT = self.sb("B_SOUT", [128, 2, 128])
    XTs = [self.sb("B_XT%d" % i, [128, 256]) for i in range(2)]; BTs = [self.sb("B_BT%d" % i, [128, 128], BF16) for i in range(2)]
    SMs = [self.sb("B_SM%d" % i, [128, 40]) for i in range(2)]
    LT = self.sb("B_LT", [128, 4, 128]); MTs = [self.sb("B_MT%d" % i, [128, 4, 128], BF16) for i in range(2)]
    XDTs = [self.sb("B_XDT%d" % i, [128, 256], BF16) for i in range(2)]; XDDs = [self.sb("B_XDD%d" % i, [128, 256], BF16) for i in range(2)]
    Y1 = self.sb("B_Y1", [128, 256]); T2 = self.sb("B_T2", [128, 256]); SZ = self.sb("B_SZ", [128, 256])
    w_in, w_out = I["ssm_w_in"], I["ssm_w_out"]
    XFSv = [XFS[:, q, :].rearrange("p (s t) -> p s t", t=11) for q in range(4)]

    def bc(ap, cs):
        return ap.unsqueeze(2).to_broadcast([cs, 4, 64])

    def v4(ap):
        return ap.rearrange("p (h q) -> p h q", q=64)

    def wv(c0, n):
        return w_in[ib, :, c0:c0 + n].rearrange("(kc p) n -> p kc n", p=128)

    def load(g):
        self.dma("pool", WZD[0][:, :, 0:256], wv(g * 256, 256), (), ["B_WZD"])
        self.dma("pool", WZD[0][:, :, 256:260], wv(6144 + 4 * g, 4), (), ["B_WZD"])

    def load_x(g):
        self.dma("pool", WXBC[0][:, :, 0:256], wv(2048 + g * 256, 256), (), ["B_WXBC"])
        self.dma("pool", WXBC[0][:, :, 256:384], wv(4096 + g * 128, 128), (), ["B_WXBC"])
        self.dma("pool", WXBC[0][:, :, 384:512], wv(5120 + g * 128, 128), (), ["B_WXBC"])

    def load_o(g):
        self.dma("pool", WOUT[0][:], w_out[ib, g * 256:(g + 1) * 256, :].rearrange("(h p) n -> p h n", p=128), (), ["B_WOUT"])

    def chunk_front(g, b, tc, cs, ucol, st, sample=False):
        hs = slice(4 * g, 4 * g + 4)
        tri, negm = (TRIS, NEGMS) if sample else (TRI, NEGM)
        XT, BT, SM, MT, XDT, XDD = XTs[st], BTs[st], SMs[st], MTs[st], XDTs[st], XDDs[st]
        DTV, DT_, DA, NACS, EACS, DEND, CD = (SM[:, 0:4], SM[:, 4:8], SM[:, 8:12], SM[:, 12:16], SM[:, 16:20], SM[:, 20:24], SM[:, 24:28])
        K = lambda nm: ("B_" + nm, st)
        pzd, kzd = self.PS[st], ("PS", st)
        pt, kt = self.PS[2], ("PS", 2)
        pa, ka = self.PS[3], ("PS", 3)
        pl, kl = self.PS[4], ("PS", 4)
        pc, kc_ = self.PS[5], ("PS", 5)
        for kc in range(8):
            self.mm(pzd[0:cs, 0:260], (USC[:, kc, :] if sample else self.U[:, kc, ucol:ucol + cs]), WZD[b][:, kc, :], kc == 0, kc == 7, ["B_WZD", "U", "B_USC"], [kzd])
        for q in range(3):
            self.tr(pt[0:cs, q * 128:(q + 1) * 128], XCf[:, q, tc:tc + cs], self.IDF[:], ["B_XCf", "IDF"], [kt])
        self.cp("act", XT[0:cs, :], pt[0:cs, 0:256], [kt], [K("XT")])
        self.cp("dve", BT[0:cs, :], pt[0:cs, 256:384], [kt], [K("BT")])
        self.mm(pc[0:cs, 0:cs], BCb[:, 0, tc:tc + cs], BCb[:, 1, tc:tc + cs], True, True, ["B_BCb"], [kc_])
        self.tt("dve", DTV[0:cs], pzd[0:cs, 256:260], DTB[0:cs, hs], ALU.add, [kzd, "B_DTB"], [K("DTV")])
        self.act(DT_[0:cs], DTV[0:cs], AF.Exp, [K("DTV")], [K("DT")])
        self.act(DT_[0:cs], DT_[0:cs], AF.Ln, [K("DT"), ("C1", 1)], [K("DT")], bias=self.C1[0:cs, 1:2], scale=1.0)
        self.tt("dve", DA[0:cs], DT_[0:cs], AB[0:cs, hs], ALU.mult, [K("DT"), "B_AB"], [K("DA")])
        self.mm(pa[0:cs, 0:4], tri[0:cs, 0:cs], DA[0:cs], True, True, ["B_TRI", "B_TRIS", K("DA")], [ka])
        self.mm(pa[:, 4:8], (BONES[:, :] if sample else ONESF[0:cs, :]), DA[0:cs], True, True, ["B_ONESF", "B_BONES", K("DA")], [ka])
        self.ts("dve", NACS[0:cs], pa[0:cs, 0:4], -1.0, None, ALU.mult, None, [ka], [K("NACS")])
        self.act(EACS[0:cs], pa[0:cs, 0:4], AF.Exp, [ka], [K("EACS")])
        self.tt("dve", DEND[0:cs], pa[0:cs, 4:8], NACS[0:cs], ALU.add, [ka, K("NACS")], [K("DEND")])
        self.act(DEND[0:cs], DEND[0:cs], AF.Exp, [K("DEND")], [K("DEND")])
        if not sample:
            self.act(CD, pa[:, 4:8], AF.Exp, [ka], [K("CD")])
        else:
            self.tt("dve", DAS[:].rearrange("p (s h) -> p s h", h=4), DA[:].unsqueeze(1).to_broadcast([128, 16, 4]),
                    SEQM[:].unsqueeze(2).to_broadcast([128, 16, 4]), ALU.mult, [K("DA"), "B_SEQM"], ["B_DAS"])
            pcd, kcd = self.PS[6], ("PS", 6)
            self.mm(pcd[:, 0:64], ONESF[:, :], DAS[:], True, True, ["B_ONESF", "B_DAS"], [kcd])
            self.act(CDS[:], pcd[:, 0:64], AF.Exp, [kcd], ["B_CDS"])
        for h in range(4):
            self.mm(pl[0:cs, h * 128:h * 128 + cs], DA[0:cs, h:h + 1].to_broadcast([cs, cs]), tri[0:cs, 0:cs], True, False, [K("DA"), "B_TRI", "B_TRIS"], [kl])
            self.mm(pl[0:cs, h * 128:h * 128 + cs], self.IDF[0:cs, 0:cs], negm[0:cs, 0:cs], False, True, ["IDF", "B_NEGM", "B_NEGMS"], [kl])
        for h in range(4):
            self.act(LT[0:cs, h, 0:cs], pl[0:cs, h * 128:h * 128 + cs], AF.Exp, [kl, K("NACS")], [("B_LT", h)], bias=NACS[0:cs, h:h + 1], scale=1.0)
        for h in range(4):
            self.tt("dve", MT[0:cs, h, 0:cs], LT[0:cs, h, 0:cs], pc[0:cs, 0:cs], ALU.mult, [kc_, ("B_LT", h)], [("B_MT%d" % st, h)])
        self.tt("dve", v4(XDT[0:cs, :]), v4(XT[0:cs, :]), bc(DT_[0:cs], cs), ALU.mult, [K("XT"), K("DT")], [K("XDT")])
        self.tt("dve", v4(XDD[0:cs, :]), v4(XDT[0:cs, :]), bc(DEND[0:cs], cs), ALU.mult, [K("XDT"), K("DEND")], [K("XDD")])

    def chunk_back(g, b, tc, cs, ucol, st, sample=False):
        hs = slice(4 * g, 4 * g + 4)
        XT, BT, SM, MT, XDT, XDD = XTs[st], BTs[st], SMs[st], MTs[st], XDTs[st], XDDs[st]
        EACS, CD, MS = SM[:, 16:20], SM[:, 24:28], SM[:, 28:29]
        K = lambda nm: ("B_" + nm, st)
        pzd, kzd = self.PS[st], ("PS", st)
        py, ky = self.PS[6], ("PS", 6)
        p7, k7 = self.PS[7], ("PS", 7)
        self.act(SZ[0:cs, :], pzd[0:cs, 0:256], AF.Exp, [kzd], ["B_SZ"], scale=-1.0)
        self.act(SZ[0:cs, :], SZ[0:cs, :], AF.Ln, ["B_SZ", ("C1", 1)], ["B_SZ"], bias=self.C1[0:cs, 1:2], scale=1.0)
        self.act(SZ[0:cs, :], SZ[0:cs, :], AF.Exp, ["B_SZ"], ["B_SZ"], scale=-1.0)
        self.tt("dve", SZ[0:cs, :], SZ[0:cs, :], pzd[0:cs, 0:256], ALU.mult, [kzd, "B_SZ"], ["B_SZ"])
        if sample:
            self.memset("dve", YO[:], 0.0, ["B_YO"])
            for sq in range(NSQ):
                state_in(g, sq)
                po, ko = self.bank()
                self.mm(po[:, 0:256], BCb[:, 1, 0:128], STb[:], True, True, ["B_BCb", "B_STb"], [ko])
                self.stt(YO[:], po[:, 0:256], SEQM[:, sq:sq + 1], YO[:], ALU.mult, ALU.add, [ko, "B_SEQM", "B_YO"], ["B_YO"])
                self.ts("dve", BTM[:], BT[:], SEQM[:, sq:sq + 1], None, ALU.mult, None, [K("BT"), "B_SEQM"], ["B_BTM"])
                pst, kst = self.bank()
                self.mm(pst[:, 0:256], BTM[:], XDD[:], True, True, ["B_BTM", K("XDD")], [kst])
                self.tt("dve", v4(ST[:]), v4(ST[:]), bc(CDS[:, 4 * sq:4 * sq + 4], 128), ALU.mult, ["B_ST", "B_CDS"], ["B_ST"])
                self.tt("dve", ST[:], ST[:], pst[:, 0:256], ALU.add, ["B_ST", kst], ["B_ST"])
                state_out(g, self.O["ss_s"][sq, 4 * g:4 * g + 4])
        for h in range(4):
            self.mm(py[0:cs, h * 64:(h + 1) * 64], MT[0:cs, h, 0:cs], XDT[0:cs, h * 64:(h + 1) * 64], True, True, [("B_MT%d" % st, h), K("XDT")], [ky])
        if not sample:
            self.mm(p7[0:cs, 0:256], BCb[:, 1, tc:tc + cs], STb[:], True, True, ["B_BCb", "B_STb"], [k7])
            self.tt("dve", v4(Y1[0:cs, :]), v4(p7[0:cs, 0:256]), bc(EACS[0:cs], cs), ALU.mult, [k7, K("EACS")], ["B_Y1"])
        else:
            self.tt("dve", v4(Y1[:]), v4(YO[:]), bc(EACS[:], 128), ALU.mult, ["B_YO", K("EACS")], ["B_Y1"])
        self.tt("dve", Y1[0:cs, :], Y1[0:cs, :], py[0:cs, 0:256], ALU.add, [ky, "B_Y1"], ["B_Y1"])
        self.tt("pool", v4(T2[0:cs, :]), v4(XT[0:cs, :]), bc(DB[0:cs, hs], cs), ALU.mult, [K("XT"), "B_DB"], ["B_T2"])
        self.tt("dve", Y1[0:cs, :], Y1[0:cs, :], T2[0:cs, :], ALU.add, ["B_T2", "B_Y1"], ["B_Y1"])
        if not sample:
            self.mm(p7[:, 256:512], BT[0:cs, :], XDD[0:cs, :], True, True, [K("BT"), K("XDD")], [k7])
            self.tt("dve", v4(ST[:]), v4(ST[:]), bc(CD, 128), ALU.mult, ["B_ST", K("CD")], ["B_ST"])
            self.tt("dve", ST[:], ST[:], p7[:, 256:512], ALU.add, ["B_ST", k7], ["B_ST"])
            self.cp("pool", STb[:], ST[:], ["B_ST"], ["B_STb"])
        self.tt("dve", Y1[0:cs, :], Y1[0:cs, :], SZ[0:cs, :], ALU.mult, ["B_SZ", "B_Y1"], ["B_Y1"])
        self.P.op("dve", lambda e: e.scalar_tensor_tensor(out=T2[0:cs, :], in0=Y1[0:cs, :], scalar=1.0, in1=Y1[0:cs, :], op0=ALU.mult,
                                                          op1=ALU.mult, accum_out=MS[0:cs]), ["B_Y1", "B_T2"], ["B_T2", K("MS")])
        self.act(MS[0:cs], MS[0:cs], AF.Ln, [K("MS"), ("C1", 2)], [K("MS")], scale=1.0 / 256, bias=self.C1[0:cs, 2:3])
        self.act(MS[0:cs], MS[0:cs], AF.Exp, [K("MS")], [K("MS")], scale=-0.5)
        self.ts("dve", Y1[0:cs, :], Y1[0:cs, :], MS[0:cs], None, ALU.mult, None, [K("MS"), "B_Y1"], ["B_Y1"])
        pg, kg = self.PS[7], ("PS", 7)
        for hf in range(2):
            self.tr(pg[:, hf * 128:hf * 128 + cs], Y1[0:cs, hf * 128:(hf + 1) * 128], self.IDF[0:cs, 0:cs], ["B_Y1", "IDF"], [kg])
        for hf in range(2):
            ch = 2 * g + hf
            self.ts("dve", YGT[:, hf, tc:tc + cs], pg[:, hf * 128:hf * 128 + cs], self.pf("ssm_norm_w", ch // 8, ch % 8), None, ALU.mult, None,
                    [kg, "PF"], ["B_YGT"])

    def state_in(g, s):
        self.dma("sp", SIN[:], I["st_ssm"][ib, s, 4 * g:4 * g + 4].rearrange("(a h) p n -> (h p) a n", a=2), (), ["B_SIN"])
        ps, pk = self.bank()
        for a in range(2):
            self.tr(ps[:, a * 128:(a + 1) * 128], SIN[:, a, :], self.IDF[:], ["B_SIN", "IDF"], [pk])
        self.cp("dve", ST[:], ps[:, 0:256], [pk], ["B_ST"])
        self.cp("act", STb[:], ps[:, 0:256], [pk], ["B_STb"])

    def state_out(g, dst):
        ps, pk = self.bank()
        for a in range(2):
            self.tr(ps[:, a * 128:(a + 1) * 128], ST[:, a * 128:(a + 1) * 128], self.IDF[:], ["B_ST", "IDF"], [pk])
        self.cp("dve", SOUT[:].rearrange("p a n -> p (a n)"), ps[:, 0:256], [pk], ["B_SOUT"])
        self.dma("sp", dst.rearrange("(a h) p n -> (h p) a n", a=2), SOUT[:], ["B_SOUT"], ())

    import os as _os
    NG = int(_os.environ.get('BDBG_G', '8')); TLIST = [int(c) for c in _os.environ.get('BDBG_T', '01234')]; NSQ = int(_os.environ.get('BDBG_S', '16'))
    load(0)
    load_x(0)
    load_o(0)
    for g in range(NG):
        b = g % 2
        chs = [2 * g, 2 * g + 1, 16 + g, 24 + g]
        for ti, (c0, w) in enumerate(TT):
            if ti not in TLIST:
                continue
            if ti == 0:
                self.memset("dve", XF[:, :, 0:3], 0.0, [("B_XF", "st")])
            elif ti < 4:
                self.cp("dve", XF[:, :, 0:3], XF[:, :, 512:515], ["B_XF"], [("B_XF", "st")])
            else:
                for q in range(4):
                    self.cp("dve", XFSv[q][:, :, 0:3], CS0[:, chs[q], :].rearrange("p (s k) -> p s k", k=3), ["B_CS0"], [("B_XFS", "st")])
            for q in range(4):
                ps, pk = self.bank()
                for kc in range(8):
                    self.mm(self.v3(ps[:, :w], ti), WXBC[b][:, kc, q * 128:(q + 1) * 128], self.uv(kc, ti), kc == 0, kc == 7,
                            ["B_WXBC", ("U", (kc, ti))], [pk])
                if ti < 4:
                    self.cp("act", XF[:, q, 3:515], ps[:, :], [pk], [("B_XF", q)])
                else:
                    self.cp("act", XFSv[q][:, :, 3:11], self.v3(ps[:, :w], ti), [pk], [("B_XFS", q)])
            if ti == TLIST[-1] and g + 1 < NG:
                load_x(g + 1)
            for q in range(4):
                ch = chs[q]
                cw = [self.pf("ssm_conv_w", k * 4 + ch // 8, ch % 8) for k in range(4)]
                cb = self.pf("ssm_conv_b", ch // 8, ch % 8)
                if ti < 4:
                    dst = XCf[:, q, :]
                    srcf = lambda k, q=q: XF[:, q, k:k + 512]
                    rk = "B_XF"
                else:
                    dst = XCf[:, q, 0:128].rearrange("p (s t) -> p s t", t=8)
                    srcf = lambda k, q=q: XFSv[q][:, :, k:k + 8]
                    rk = "B_XFS"
                self.ts("dve", dst, srcf(0), cw[0], cb, ALU.mult, ALU.add, [rk, "PF"], [("B_XCf", q)])
                for k in range(1, 4):
                    self.stt(dst, srcf(k), cw[k], dst, ALU.mult, ALU.add, [rk, ("B_XCf", q), "PF"], [("B_XCf", q)])
                self.act(XCf[:, q, :w], XCf[:, q, :w], AF.Silu, [("B_XCf", q)], [("B_XCf", q)])
                if q >= 2:
                    self.cp("pool", BCb[:, q - 2, :w], XCf[:, q, :w], [("B_XCf", q)], ["B_BCb"])
            if ti == 3:
                for q in range(4):
                    self.cp("pool", CST[:, chs[q], 0:3], XF[:, q, 512:515], ["B_XF"], ["B_CST"])
            if ti == 4:
                for q in range(4):
                    self.cp("pool", CST[:, chs[q], 3:].rearrange("p (s k) -> p s k", k=3), XFSv[q][:, :, 8:11], ["B_XFS"], ["B_CST"])
            if ti < 4:
                if ti == 0:
                    self.memset("dve", ST[:], 0.0, ["B_ST"])
                    self.memset("pool", STb[:], 0.0, ["B_STb"])
                args = [(g, b, ck * 128, 128, 1 + c0 + ck * 128, ck % 2) for ck in range(4)]
                chunk_front(*args[0])
                for ck in range(4):
                    if ck + 1 < 4:
                        chunk_front(*args[ck + 1])
                    chunk_back(*args[ck])
                if ti == 3:
                    state_out(g, self.O["ss_p"][4 * g:4 * g + 4])
            else:
                chunk_front(g, b, 0, 128, 0, 0, sample=True)
                chunk_back(g, b, 0, 128, 0, 0, sample=True)
            for n in range(8):
                ps, pk = self.bank()
                for hf in range(2):
                    self.mm(ps[:, :w], WOUT[b][:, hf, n * 128:(n + 1) * 128], YGT[:, hf, :w], hf == 0, hf == 1, ["B_WOUT", "B_YGT"], [pk])
                self.tt("dve", self.X[:, n, c0:c0 + w], self.X[:, n, c0:c0 + w], ps[:, :w], ALU.add, [pk, ("X", (n, ti))], [("X", (n, ti))])
        if g + 1 < NG:
            load_o(g + 1)
            load(g + 1)
    pass
    for r in range(4):
        self.fm_to_rows(CST[:, r * 8:(r + 1) * 8, 0:3], "B_CST", 3, self.O["sc_p"][:, r * D:(r + 1) * D])
        self.fm_to_rows(CST[:, r * 8:(r + 1) * 8, 3:], "B_CST", NS * 3, self.O["sc_s"].rearrange("s k d -> (s k) d")[:, r * D:(r + 1) * D])


KB.layer_B = layer_B


def layer_C(self, l, ic):
    I, nc = self.I, self.nc
    E05 = float(np.exp(-0.5))
    SH0 = self.sb("C_SH0", [128, 8, NS]); USH = self.sb("C_USH", [128, 8, 1 + NS])
    self.rows_to_fm(I["st_rw_shift"][ic], NS, SH0, "C_SH0")
    Us = self.U[:, :, 1 + NTP:UC].rearrange("p c (s t) -> p c s t", t=9)
    self.cp("dve", Us[:, :, :, 0], SH0[:], ["C_SH0"], [("U", "shifts")])
    self.norm_to_U("norm_mix", l, shift_out=USH)
    XD = nc.dram_tensor("c_xspill", [128, 8, NT], F32).ap()
    import os as _os2
    CHUNKED = bool(int(_os2.environ.get("C_CHUNKED", "1")))
    HM = () if CHUNKED else ("kk", "w", "nq", "k", "r")
    SCR = {k: nc.dram_tensor("c_scr_" + k, [NT, D], F32).ap() for k in ("kk", "w", "nq", "k", "r", "v", "g", "o") if k not in HM}
    SCRH = {k: nc.dram_tensor("c_scrh_" + k, [16, NT, 64], F32).ap() for k in HM}

    def scr_tm(nm, rows):
        if nm in SCRH:
            return SCRH[nm][:, rows, :].rearrange("h t k -> t h k")
        return SCR[nm][rows, :].rearrange("t (h k) -> t h k", k=64)

    self.P.barrier()
    self.dma("sp", XD, self.X[:], ["X"], ["XD"])
    self.P.barrier()

    def xa(i):
        return self.X[:, i // 2, (i % 2) * D:(i % 2 + 1) * D], ("X", "a%d" % i)

    def h16(ap):
        return ap.rearrange("p (h k) -> p h k", k=64)

    def b16(ap):
        return ap.unsqueeze(2).to_broadcast([128, 16, 64])

    def red(out, in_, reads, writes):
        self.P.op("dve", lambda e: e.tensor_reduce(out=out, in_=in_, axis=AX.X, op=ALU.add), reads, writes)

    def pbc(i, src):
        a, k = xa(i)
        self.dma("sp", a, src.partition_broadcast(128), (), [k])

    with self.scope():
        pbc(0, I["rwkv_w0"][ic:ic + 1, :]); pbc(1, I["rwkv_a0"][ic:ic + 1, :]); pbc(2, I["rwkv_k_k"][ic:ic + 1, :]); pbc(3, I["rwkv_k_a"][ic:ic + 1, :])
        WS = [self.sb("C_WS%d" % i, [128, 8, D], BF16) for i in range(3)]
        for s_ in range(3):
            self.dma("pool", WS[s_][:], I["rwkv_w_rkv"][ic, s_].rearrange("(kc p) n -> p kc n", p=128), (), [("C_WS", s_)])
        W1 = self.sb("C_W1", [128, 8, 256], BF16)
        W2w = self.sb("C_W2w", [64, D], BF16); W2a = self.sb("C_W2a", [64, D], BF16); W2g = self.sb("C_W2g", [128, D], BF16)
        for (c0, n, nm) in ((0, 64, "rwkv_w_w1"), (64, 64, "rwkv_w_a1"), (128, 128, "rwkv_w_g1")):
            self.dma("pool", W1[:, :, c0:c0 + n], I[nm][ic].rearrange("(kc p) n -> p kc n", p=128), (), ["C_W1"])
        self.dma("pool", W2w[:], I["rwkv_w_w2"][ic], (), ["C_W2w"]); self.dma("pool", W2a[:], I["rwkv_w_a2"][ic], (), ["C_W2a"])
        self.dma("pool", W2g[:], I["rwkv_w_g2"][ic], (), ["C_W2g"])
        Dd = self.sb("C_D", [128, 8, 128]); XM = [self.sb("C_XM%d" % i, [128, 8, 128], BF16) for i in range(2)]
        TW = self.sb("C_TW", [64, 128], BF16); TA = self.sb("C_TA", [64, 128], BF16); TG = self.sb("C_TG", [128, 128], BF16)
        SS = self.sb("C_SS", [128, 16])
        (Pw0, kw0), (Pa0, ka0), (Pkk, kkk), (Pka, kka) = xa(0), xa(1), xa(2), xa(3)
        (Rt, kR), (Kt, kK), (Vt, kV), (Wt, kW), (At, kA), (KKt, kKK), (Gt, kG), (T1, kT1), (T2, kT2) = [xa(i) for i in range(7, 16)]
        wsn = 0
        for ti in range(17):
            tok0 = 128 * ti
            if ti < 16:
                cur = self.U[:, :, 1 + tok0:1 + tok0 + 128]; prev = self.U[:, :, tok0:tok0 + 128]
                dv = lambda a: a
                ukeys = [("U", (c, ti // 4)) for c in range(8)]
            else:
                cur = Us[:, :, :, 1:9]; prev = Us[:, :, :, 0:8]
                dv = lambda a: a.rearrange("p c (s t) -> p c s t", t=8) if len(a.shape) == 3 else a.rearrange("p (s t) -> p s t", t=8)
                ukeys = [("U", (c, 4)) for c in range(8)] + [("U", "shifts")]
            if ti % 4 == 0 and ti > 0 and ti < 16:
                ukeys = ukeys + [("U", (c, ti // 4 - 1)) for c in range(8)]
            if ti == 0:
                ukeys = ukeys + [("U", "shiftp")]
            self.tt("dve", dv(Dd[:]), prev, cur, ALU.subtract, ukeys, ["C_D"])

            def mix(s):
                xm = XM[s % 2]; key = "C_XM%d" % (s % 2)
                for kc in range(8):
                    self.stt(dv(xm[:, kc, :]), dv(Dd[:, kc, :]), self.pf("rwkv_mu", s, kc), cur[:, kc], ALU.mult, ALU.add, ["C_D", "PF"] + ukeys, [key])
                return xm, key

            def proj_tm(s, dst, dkey):
                b = s
                xm, key = mix(s)
                for nb in range(2):
                    ps, pk = self.bank()
                    for kc in range(8):
                        self.mm(ps[:, :], xm[:, kc, :], WS[b][:, kc, nb * 512:(nb + 1) * 512], kc == 0, kc == 7, [key, ("C_WS", b)], [pk])
                    self.cp("act", dst[:, nb * 512:(nb + 1) * 512], ps[:, :], [pk], [dkey])

            proj_tm(0, Rt, kR); proj_tm(1, Kt, kK); proj_tm(2, Vt, kV)
            for (s, c0, n, fn, dst, dk) in ((3, 0, 64, AF.Tanh, TW, "C_TW"), (4, 64, 64, AF.Copy, TA, "C_TA"), (5, 128, 128, AF.Sigmoid, TG, "C_TG")):
                xm, key = mix(s)
                ps, pk = self.bank()
                for kc in range(8):
                    self.mm(ps[0:n, 0:128], W1[:, kc, c0:c0 + n], xm[:, kc, :], kc == 0, kc == 7, [key, "C_W1"], [pk])
                self.act(dst[0:n, :], ps[0:n, 0:128], fn, [pk], [dk])
            for nb in range(2):
                cs_ = slice(nb * 512, (nb + 1) * 512)
                ps, pk = self.bank()
                self.mm(ps[:, :], TW[0:64, :], W2w[0:64, cs_], True, True, ["C_TW", "C_W2w"], [pk])
                self.tt("dve", T1[:, cs_], ps[:, :], Pw0[:, cs_], ALU.add, [pk, kw0], [kT1])
                ps, pk = self.bank()
                self.mm(ps[:, :], TA[0:64, :], W2a[0:64, cs_], True, True, ["C_TA", "C_W2a"], [pk])
                self.tt("dve", At[:, cs_], ps[:, :], Pa0[:, cs_], ALU.add, [pk, ka0], [kA])
                ps, pk = self.bank()
                self.mm(ps[:, :], TG[:, :], W2g[:, cs_], True, True, ["C_TG", "C_W2g"], [pk])
                self.cp("act", Gt[:, cs_], ps[:, :], [pk], [kG])
            self.act(T1, T1, AF.Sigmoid, [kT1], [kT1])
            self.act(Wt, T1, AF.Exp, [kT1], [kW], scale=-E05)
            self.act(At, At, AF.Sigmoid, [kA], [kA])
            self.tt("dve", KKt, Kt, Pkk, ALU.mult, [kK, kkk], [kKK])
            self.tt("dve", T1, KKt, KKt, ALU.mult, [kKK], [kT1])
            red(SS[:], h16(T1), [kT1], ["C_SS"])
            self.act(SS[:], SS[:], AF.Sqrt, ["C_SS"], ["C_SS"])
            self.ts("dve", SS[:], SS[:], 1e-12, None, ALU.max, None, ["C_SS"], ["C_SS"])
            self.P.op("dve", lambda e, SS=SS: e.reciprocal(out=SS[:], in_=SS[:]), ["C_SS"], ["C_SS"])
            self.tt("dve", h16(KKt), h16(KKt), b16(SS[:]), ALU.mult, [kKK, "C_SS"], [kKK])
            self.stt(T1, At, -1.0, Pka, ALU.add, ALU.mult, [kA, kka], [kT1])
            self.ts("dve", T1, T1, 1.0, None, ALU.add, None, [kT1], [kT1])
            self.tt("dve", Kt, Kt, T1, ALU.mult, [kK, kT1], [kK])
            self.stt(T2, KKt, -1.0, At, ALU.mult, ALU.mult, [kKK, kA], [kT2])
            rows = slice(tok0, tok0 + 128)
            for (nm, a, k) in (("kk", KKt, kKK), ("w", Wt, kW), ("nq", T2, kT2), ("k", Kt, kK), ("r", Rt, kR), ("v", Vt, kV), ("g", Gt, kG)):
                self.dma("sp", scr_tm(nm, rows), h16(a), [k], [("SCR", nm)])

    if self.debug:
        for i_, nm_ in enumerate(("kk", "w", "nq", "k", "r", "v", "g")):
            self.dma("sp", self.O["dbg_scr"][i_].rearrange("t (h k) -> t h k", k=64), scr_tm(nm_, slice(0, NT)), [("SCR", nm_)], ())

    def phase2_chunked():
        LNE = float(np.log(1.0))
        f32 = F32
        def blockmask(ap3, blk):
            self.P.op("pool", lambda e: e.affine_select(out=ap3, in_=ap3, pattern=[[-blk, 128 // blk], [0, blk]], compare_op=ALU.is_ge, fill=0.0,
                                                        base=0, channel_multiplier=1), ["C_MK"], ["C_MK"])
            self.P.op("pool", lambda e: e.affine_select(out=ap3, in_=ap3, pattern=[[blk, 128 // blk], [0, blk]], compare_op=ALU.is_ge, fill=0.0,
                                                        base=blk - 1, channel_multiplier=-1), ["C_MK"], ["C_MK"])
        MK = {}
        for blk in (32, 8):
            TRIc = self.sb("C_TRIc%d" % blk, [128, 128]); LOWs = self.sb("C_LOWs%d" % blk, [128, 128]); M3 = self.sb("C_M3_%d" % blk, [128, 3, 128])
            MSs = self.sb("C_MSs%d" % blk, [128, 128])
            self.memset("pool", TRIc[:], 1.0, ["C_MK"])
            self.P.op("pool", lambda e, T=TRIc: e.affine_select(out=T[:], in_=T[:], pattern=[[1, 128]], compare_op=ALU.is_ge, fill=0.0, base=0,
                                                                channel_multiplier=-1), ["C_MK"], ["C_MK"])
            blockmask(TRIc[:].rearrange("p (b t) -> p b t", t=blk), blk)
            self.memset("pool", LOWs[:], 1.0, ["C_MK"])
            self.P.op("pool", lambda e, T=LOWs: e.affine_select(out=T[:], in_=T[:], pattern=[[-1, 128]], compare_op=ALU.is_ge, fill=0.0, base=-1,
                                                                channel_multiplier=1), ["C_MK"], ["C_MK"])
            blockmask(LOWs[:].rearrange("p (b t) -> p b t", t=blk), blk)
            self.tt("pool", MSs[:], TRIc[:], self.IDF[:], ALU.subtract, ["C_MK", "IDF"], ["C_MK"])
            self.cp("pool", M3[:, 0, :], TRIc[:], ["C_MK"], ["C_MK"])
            self.cp("pool", M3[:, 1, :], MSs[:], ["C_MK"], ["C_MK"])
            self.cp("pool", M3[:, 2, :], TRIc[:], ["C_MK"], ["C_MK"])
            MK[blk] = (TRIc, LOWs, MSs, M3)
        SEQM = self.sb("C_SEQM", [128, 16])
        self.memset("pool", SEQM[:], 1.0, ["C_MK"])
        self.P.op("pool", lambda e: e.affine_select(out=SEQM[:], in_=SEQM[:], pattern=[[-8, 16]], compare_op=ALU.is_ge, fill=0.0, base=0,
                                                    channel_multiplier=1), ["C_MK"], ["C_MK"])
        self.P.op("pool", lambda e: e.affine_select(out=SEQM[:], in_=SEQM[:], pattern=[[8, 16]], compare_op=ALU.is_ge, fill=0.0, base=7,
                                                    channel_multiplier=-1), ["C_MK"], ["C_MK"])
        CHM = self.sb("C_CHM", [128, 4])
        self.memset("pool", CHM[:], 1.0, ["C_MK"])
        self.P.op("pool", lambda e: e.affine_select(out=CHM[:], in_=CHM[:], pattern=[[-32, 4]], compare_op=ALU.is_ge, fill=0.0, base=0,
                                                    channel_multiplier=1), ["C_MK"], ["C_MK"])
        self.P.op("pool", lambda e: e.affine_select(out=CHM[:], in_=CHM[:], pattern=[[32, 4]], compare_op=ALU.is_ge, fill=0.0, base=31,
                                                    channel_multiplier=-1), ["C_MK"], ["C_MK"])
        PRT = self.sb("C_PRT", [128, 8, 2, 128], BF16); QT = self.sb("C_QT", [128, 8, 128], BF16); KT = self.sb("C_KT", [128, 8, 128], BF16)
        QZ = self.sb("C_QZ", [128, 16, 128], BF16); KZ = self.sb("C_KZ", [128, 16, 128], BF16)
        Vb = self.sb("C_Vb", [128, D], BF16); Vm = self.sb("C_Vm", [128, D], BF16)
        ECLE = self.sb("C_ECLE", [128, 8, 16])
        XN = self.sb("C_XN", [128, 16, 128], BF16); ZN = self.sb("C_ZN", [128, 16, 128], BF16); IDB_ = self.IDB
        MB = self.sb("C_MB", [128, 16, 3, 128], BF16); TTb = self.sb("C_TTb", [128, 16, 128], BF16)
        Wsb = self.sb("C_Wsb", [128, D]); O0 = self.sb("C_O0", [128, D])
        GWb = self.sb("C_GWb", [128, D], BF16); Ub = self.sb("C_Ub", [128, D], BF16)
        S = self.sb("C_S2", [128, 8, 64]); Sb = self.sb("C_Sb", [128, 16, 64], BF16)
        SbV = Sb[:].rearrange("p (j e) v -> p j e v", e=2)

        def write_sb(src3, keys):
            self.cp("act", SbV[0:64, :, 0, :], src3[0:64], keys, ["C_Sb"])
            self.cp("act", SbV[64:128, :, 1, :], src3[64:128], keys, ["C_Sb"])

        NAT = self.sb("C_NAT", [64, 16, 64])
        self.memset("pool", QZ[:], 0.0, ["C_QZ"]); self.memset("pool", KZ[:], 0.0, ["C_KZ"])
        (KKt, kKK), (NQt, kNQ), (Kt, kK), (Rt, kR), (Vt, kV), (LW, kLW), (CL, kCL), (E1, kE1), (Ot, kO) = [xa(i) for i in range(0, 9)]

        def state_load(dram):
            self.dma("sp", NAT[:], dram.rearrange("h v k -> v h k"), (), ["C_NAT"])
            ps, pk = self.bank()
            for j in range(8):
                self.tr(ps[:, j * 64:(j + 1) * 64], NAT[:, 2 * j:2 * j + 2, :].rearrange("v h k -> v (h k)"), self.IDF[0:64, 0:64], ["C_NAT", "IDF"], [pk])
            self.cp("dve", S[:].rearrange("p j v -> p (j v)"), ps[:, :], [pk], ["C_S2"])
            write_sb(ps[:, :].rearrange("p (j v) -> p j v", v=64), [pk])

        def state_store(dram):
            for half in range(2):
                ps, pk = self.bank()
                for jj in range(4):
                    j = half * 4 + jj
                    self.tr(ps[0:64, jj * 128:(jj + 1) * 128], S[:, j, :], self.IDF[:], ["C_S2", "IDF"], [pk])
                self.cp("dve", NAT[:, half * 8:(half + 1) * 8, :].rearrange("v h k -> v (h k)"), ps[0:64, :], [pk], ["C_NAT"])
            self.dma("act", dram.rearrange("h v k -> v h k"), NAT[:], ["C_NAT"], ())

        def tile(ti, blk, seqs):
            TRIc, LOWs, MSs, M3 = MK[blk]
            STOP = int(_os2.environ.get('C2_STOP', '99'))
            nch = 128 // blk
            rows = slice(128 * ti, 128 * ti + 128)
            for (nm, a, k) in (("kk", KKt, kKK), ("nq", NQt, kNQ), ("k", Kt, kK), ("r", Rt, kR), ("v", Vt, kV), ("w", LW, kLW)):
                self.dma("sp", h16(a), scr_tm(nm, rows), [("SCR", nm)], [k])
            self.act(LW, LW, AF.Ln, [kLW], [kLW])
            for nb in range(2):
                ps, pk = self.bank()
                self.mm(ps[:, :], TRIc[:], LW[:, nb * 512:(nb + 1) * 512], True, True, ["C_MK", kLW], [pk])
                self.cp("act", CL[:, nb * 512:(nb + 1) * 512], ps[:, :], [pk], [kCL])
            self.tt("dve", E1, CL, LW, ALU.subtract, [kCL, kLW], [kE1])
            self.act(E1, E1, AF.Exp, [kE1], [kE1])
            self.tt("dve", KKt, KKt, E1, ALU.mult, [kKK, kE1], [kKK])
            self.act(E1, CL, AF.Exp, [kCL], [kE1], scale=-1.0)
            self.tt("dve", NQt, NQt, E1, ALU.mult, [kNQ, kE1], [kNQ])
            self.tt("dve", Kt, Kt, E1, ALU.mult, [kK, kE1], [kK])
            self.act(E1, CL, AF.Exp, [kCL], [kE1])
            self.tt("dve", Rt, Rt, E1, ALU.mult, [kR, kE1], [kR])
            self.cp("pool", Vb[:], Vt, [kV], ["C_Vb"])
            for (Zb, src, k, zk) in ((QZ, NQt, kNQ, "C_QZ"), (KZ, Kt, kK, "C_KZ")):
                zv = Zb[:].rearrange("p (j e) f -> p j (e f)", e=2)
                sv = src.rearrange("p (j e d) -> p j e d", e=2, d=64)
                self.cp("pool", zv[:, :, 0:64], sv[:, :, 0, :], [k], [zk])
                self.cp("pool", zv[:, :, 192:256], sv[:, :, 1, :], [k], [zk])
            if STOP <= 1:
                return
            for (src, k, dstf, dk) in ((KKt, kKK, lambda j: PRT[:, j, 0, :], "C_PRT"), (Rt, kR, lambda j: PRT[:, j, 1, :], "C_PRT"),
                                       (NQt, kNQ, lambda j: QT[:, j, :], "C_QT"), (Kt, kK, lambda j: KT[:, j, :], "C_KT")):
                for half in range(2):
                    ps, pk = self.bank()
                    for jj in range(4):
                        j = half * 4 + jj
                        self.tr(ps[:, jj * 128:(jj + 1) * 128], src[:, j * 128:(j + 1) * 128], self.IDF[:], [k, "IDF"], [pk])
                    for jj in range(4):
                        j = half * 4 + jj
                        self.cp("act" if jj % 2 else "dve", dstf(j), ps[:, jj * 128:(jj + 1) * 128], [pk], [(dk, j)])
            for half in range(2):
                ps, pk = self.bank()
                for jj in range(4):
                    j = half * 4 + jj
                    self.tr(ps[:, jj * 128:(jj + 1) * 128], E1[:, j * 128:(j + 1) * 128], self.IDF[:], [kE1, "IDF"], [pk])
                self.cp("dve", ECLE[:, half * 4:(half + 1) * 4, 0:nch], ps[:, :].rearrange("p (a c b) -> p a c b", a=4, b=blk)[:, :, :, blk - 1], [pk], ["C_ECLE"])
            if STOP <= 2:
                return
            for h in range(16):
                j, e = h // 2, h % 2
                pr = slice(e * 64, (e + 1) * 64)
                ps, pk = self.bank()
                rhsPR = PRT[pr, j, :, :].rearrange("p a t -> p (a t)")
                self.mm(ps[:, 0:256], QT[pr, j, :], rhsPR, True, True, [("C_QT", j), ("C_PRT", j)], [pk])
                self.mm(ps[:, 256:512], KT[pr, j, :], rhsPR, True, True, [("C_KT", j), ("C_PRT", j)], [pk])
                ps2, pk2 = self.bank()
                self.mm(ps2[:, 0:128], PRT[pr, j, 0, :], QT[pr, j, :], True, True, [("C_QT", j), ("C_PRT", j)], [pk2])
                self.tt("dve", ZN[:, h, :], ps[:, 0:128], MSs[:], ALU.mult, [pk, "C_MK"], [("C_ZN", h)])
                self.tt("dve", MB[:, h, :, :], ps[:, 128:512].rearrange("p (a t) -> p a t", a=3), M3[:], ALU.mult, [pk, "C_MK"], [("C_MB", h)])
                self.tt("dve", XN[:, h, :], ps2[:, 0:128], LOWs[:], ALU.mult, [pk2, "C_MK"], [("C_XN", h)])
                self.tt("pool", TTb[:, h, :], ZN[:, h, :], IDB_[:], ALU.add, [("C_ZN", h), "IDB"], [("C_TTb", h)])
            for n in range(4 if STOP > 3 else 0):
                for h in range(16):
                    ps, pk = self.bank()
                    self.mm(ps[:, 0:128], ZN[:, h, :], XN[:, h, :], True, True, [("C_ZN", h), ("C_XN", h)], [pk])
                    if n < 3:
                        self.mm(ps[:, 128:256], XN[:, h, :], ZN[:, h, :], True, True, [("C_ZN", h), ("C_XN", h)], [pk])
                    self.cp("act", XN[:, h, :], ps[:, 0:128], [pk], [("C_XN", h)])
                    if n < 3:
                        self.cp("act", ZN[:, h, :], ps[:, 128:256], [pk], [("C_ZN", h)])
                    ps2, pk2 = self.bank()
                    self.mm(ps2[:, 0:128], XN[:, h, :], TTb[:, h, :], True, True, [("C_XN", h), ("C_TTb", h)], [pk2])
                    self.tt("dve", TTb[:, h, :], TTb[:, h, :], ps2[:, 0:128], ALU.add, [pk2, ("C_TTb", h)], [("C_TTb", h)])
            if STOP <= 4:
                return
            for (ai, dst, dk) in ((1, Wsb, "C_Wsb"), (2, O0, "C_O0")):
                for half in range(2):
                    ps, pk = self.bank()
                    for hh in range(8):
                        h = half * 8 + hh
                        self.mm(ps[:, hh * 64:(hh + 1) * 64], MB[:, h, ai, :], Vb[:, h * 64:(h + 1) * 64], True, True, [("C_MB", h), "C_Vb"], [pk])
                    self.cp("act" if half else "dve", dst[:, half * 512:(half + 1) * 512], ps[:, :], [pk], [dk])
            self.cp("pool", Ot, O0[:], ["C_O0"], [kO])
            if STOP <= 5:
                return
            nrounds = nch if seqs is None else len(seqs)
            for c_ in range(nrounds):
                if seqs is None:
                    ecol = c_; mcol = CHM[:, c_:c_ + 1]
                else:
                    s_ = seqs[c_]
                    ecol = s_; mcol = SEQM[:, s_:s_ + 1]
                    state_load(I["st_rw_wkv"][ic, s_])
                self.ts("pool", Vm[:], Vb[:], mcol, None, ALU.mult, None, ["C_Vb", "C_MK"], ["C_Vm"])
                pG = [self.bank() for _ in range(2)]; pR = [self.bank() for _ in range(2)]
                for h in range(16):
                    j = h // 2
                    bq, cq = h // 8, (h % 8) * 64
                    self.mm(pG[bq][0][:, cq:cq + 64], PRT[:, j, 0, :], Sb[:, h, :], True, True, [("C_PRT", j), "C_Sb"], [pG[bq][1]])
                    self.mm(pR[bq][0][:, cq:cq + 64], PRT[:, j, 1, :], Sb[:, h, :], True, True, [("C_PRT", j), "C_Sb"], [pR[bq][1]])
                for bq in range(2):
                    cs_ = slice(bq * 512, (bq + 1) * 512)
                    self.tt("dve", GWb[:, cs_], pG[bq][0][:, :], Wsb[:, cs_], ALU.add, [pG[bq][1], "C_Wsb"], [("C_GWb", bq)])
                pU = [self.bank() for _ in range(2)]
                for h in range(16):
                    bq, cq = h // 8, (h % 8) * 64
                    self.mm(pU[bq][0][:, cq:cq + 64], TTb[:, h, :], GWb[:, h * 64:(h + 1) * 64], True, True, [("C_TTb", h), ("C_GWb", bq)], [pU[bq][1]])
                for bq in range(2):
                    cs_ = slice(bq * 512, (bq + 1) * 512)
                    self.act(Ub[:, cs_], pU[bq][0][:, :], AF.Copy, [pU[bq][1], "C_MK"], [("C_Ub", bq)], scale=mcol)
                    self.stt(Ot[:, cs_], pR[bq][0][:, :], mcol, Ot[:, cs_], ALU.mult, ALU.add, [pR[bq][1], kO, "C_MK"], [kO])
                pB = [self.bank() for _ in range(2)]
                for h in range(16):
                    bq, cq = h // 8, (h % 8) * 64
                    self.mm(pB[bq][0][:, cq:cq + 64], MB[:, h, 0, :], Ub[:, h * 64:(h + 1) * 64], True, True, [("C_MB", h), ("C_Ub", bq)], [pB[bq][1]])
                for bq in range(2):
                    cs_ = slice(bq * 512, (bq + 1) * 512)
                    self.stt(Ot[:, cs_], pB[bq][0][:, :], mcol, Ot[:, cs_], ALU.mult, ALU.add, [pB[bq][1], kO, "C_MK"], [kO])
                pS, kS = self.bank()
                for j in range(8):
                    for e in range(2):
                        h = 2 * j + e
                        self.mm(pS[:, j * 64:(j + 1) * 64], QZ[:, h, :], Ub[:, h * 64:(h + 1) * 64], e == 0, False, ["C_QZ", ("C_Ub", h // 8)], [kS])
                        self.mm(pS[:, j * 64:(j + 1) * 64], KZ[:, h, :], Vm[:, h * 64:(h + 1) * 64], False, e == 1, ["C_KZ", "C_Vm"], [kS])
                Sf = S[:].rearrange("p j v -> p (j v)")
                self.tt("dve", Sf, Sf, pS[:, :], ALU.add, [kS, "C_S2"], ["C_S2"])
                self.tt("dve", S[:], S[:], ECLE[:, :, ecol:ecol + 1].to_broadcast([128, 8, 64]), ALU.mult, ["C_S2", "C_ECLE"], ["C_S2"])
                write_sb(S, ["C_S2"])
                if seqs is not None:
                    state_store(self.O["rw_s"][s_])
            self.dma("act", scr_tm("o", rows), h16(Ot), [kO], [("SCR", "o")])

        self.memset("dve", S[:], 0.0, ["C_S2"]); self.memset("pool", Sb[:], 0.0, ["C_Sb"])
        for ti in range(16):
            tile(ti, 32, None)
        state_store(self.O["rw_p"])
        tile(16, 8, list(range(NS)))

    if CHUNKED:
        with self.scope():
            phase2_chunked()
    else:
        with self.scope():
            S = self.sb("C_S", [128, 8, 64]); Tm = self.sb("C_Tm", [128, 8, 64]); KV = [self.sb("C_KV%d" % i, [128, 8, 64]) for i in range(2)]
            SA = self.sb("C_SA", [128, 8])
            names = ("kk", "w", "nq", "k", "r")
            ST0 = {nm: self.X[:, i, 0:2048].rearrange("p (t k) -> p t k", k=64) for i, nm in enumerate(names)}
            ST1 = {nm: self.sb("C_ST1" + nm, [128, 32, 64]) for nm in names}
            VS = [self.sb("C_VS%d" % i, [128, 32, 8]) for i in range(2)]
            OS = [self.sb("C_OS%d" % i, [128, 32, 8]) for i in range(2)]

            def bk(ap):
                return ap.unsqueeze(1).to_broadcast([128, 8, 64])

            def bv(ap):
                return ap.unsqueeze(2).to_broadcast([128, 8, 64])

            nsub = 0
            import os as _os
            NOSAME = bool(int(_os.environ.get("C_NOSAME", "1")))
            SP = self.PS[7][:, :].rearrange("p (a k) -> p a k", k=64)
            SK = ("PS", 7)

            def run_segment(row0, nsteps):
                nonlocal nsub
                for t0 in range(0, nsteps, 32):
                    n = min(32, nsteps - t0)
                    sb_ = nsub % 2; nsub += 1
                    stg = ST0 if sb_ == 0 else ST1
                    skey = "C_STG%d" % sb_
                    r0 = row0 + t0
                    for nm in names:
                        src = SCRH[nm][:, r0:r0 + n, :]
                        for vb in range(8):
                            self.dma("sp", stg[nm][vb * 16:(vb + 1) * 16, 0:n, :], src, [("SCR", nm)], [(skey, nm)] + ([("X", "stg")] if sb_ == 0 else []))
                    srcv = SCR["v"][r0:r0 + n, :].rearrange("t (h v) -> h t v", v=64)
                    for vb in range(8):
                        self.dma("sp", VS[sb_][vb * 16:(vb + 1) * 16, 0:n, :], srcv[:, :, vb * 8:(vb + 1) * 8], [("SCR", "v")], [(skey, "v")])
                    xk = [("X", "stg")] if sb_ == 0 else []
                    for i in range(n):
                        kvb = i % 2
                        self.tt("pool", KV[kvb][:], bk(stg["k"][:, i, :]), bv(VS[sb_][:, i, :]), ALU.mult, [(skey, "k"), (skey, "v")] + xk, ["C_KV%d" % kvb])
                        self.P.nosame = NOSAME
                        self.tt("dve", Tm[:], SP[:], bk(stg["kk"][:, i, :]), ALU.mult, [SK, (skey, "kk")] + xk, ["C_Tm"])
                        red(SA[:], Tm[:], ["C_Tm"], ["C_SA"])
                        self.tt("dve", SP[:], SP[:], bk(stg["w"][:, i, :]), ALU.mult, [SK, (skey, "w")] + xk, [SK])
                        self.tt("dve", Tm[:], bk(stg["nq"][:, i, :]), bv(SA[:]), ALU.mult, ["C_SA", (skey, "nq")] + xk, ["C_Tm"])
                        self.tt("dve", SP[:], SP[:], Tm[:], ALU.add, [SK, "C_Tm"], [SK])
                        self.tt("dve", SP[:], SP[:], KV[kvb][:], ALU.add, [SK, "C_KV%d" % kvb], [SK])
                        self.tt("dve", Tm[:], SP[:], bk(stg["r"][:, i, :]), ALU.mult, [SK, (skey, "r")] + xk, ["C_Tm"])
                        red(OS[sb_][:, i, :], Tm[:], ["C_Tm"], ["C_OS%d" % sb_])
                        self.P.nosame = False
                    dsto = SCR["o"][r0:r0 + n, :].rearrange("t (h v) -> h t v", v=64)
                    for vb in range(8):
                        self.dma("act", dsto[:, :, vb * 8:(vb + 1) * 8], OS[sb_][vb * 16:(vb + 1) * 16, 0:n, :], ["C_OS%d" % sb_], [("SCR", "o")])

            def state_io(dram, load):
                for vb in range(8):
                    d = dram[:, vb * 8:(vb + 1) * 8, :]
                    if load:
                        self.dma("sp", S[vb * 16:(vb + 1) * 16, :, :], d, (), ["C_S"])
                    else:
                        self.dma("act", d, S[vb * 16:(vb + 1) * 16, :, :], ["C_S"], ())

            self.memset("dve", S[:], 0.0, ["C_S"])
            self.cp("dve", SP[:], S[:], ["C_S"], [SK])
            run_segment(0, NTP)
            self.cp("dve", S[:], SP[:], [SK], ["C_S"])
            state_io(self.O["rw_p"], False)
            for s in range(NS):
                state_io(I["st_rw_wkv"][ic, s], True)
                self.cp("dve", SP[:], S[:], ["C_S"], [SK])
                run_segment(NTP + 8 * s, 8)
                self.cp("dve", S[:], SP[:], [SK], ["C_S"])
                state_io(self.O["rw_s"][s], False)

    if self.debug:
        self.dma("sp", self.O["dbg_scr"][7], SCR["o"], [("SCR", "o")], ())
    with self.scope():
        pbc(0, I["rwkv_lnx_w"][ic:ic + 1, :]); pbc(1, I["rwkv_lnx_b"][ic:ic + 1, :]); pbc(2, I["rwkv_r_k"].rearrange("a h n -> a (h n)")[ic:ic + 1, :])
        (Plw, klw), (Plb, klb), (Prk, krk) = xa(0), xa(1), xa(2)
        (Ot, kO), (Rt, kR), (Kt, kK), (Vt, kV), (Gt, kG), (T1, kT1) = [xa(i) for i in range(7, 13)]
        WO = self.sb("C_WO", [128, 8, D], BF16)
        self.dma("pool", WO[:], I["rwkv_w_out"][ic].rearrange("(kc p) n -> p kc n", p=128), (), ["C_WO"])
        YT = self.sb("C_YT", [128, 8, 128], BF16); XT_ = self.sb("C_XTL", [128, 8, 128])
        M1 = self.sb("C_M1", [128, 16]); M2 = self.sb("C_M2", [128, 16])
        for ti in range(17):
            rows = slice(128 * ti, 128 * ti + 128)
            for (nm, a, k) in (("o", Ot, kO), ("r", Rt, kR), ("k", Kt, kK), ("v", Vt, kV), ("g", Gt, kG)):
                self.dma("sp", h16(a), scr_tm(nm, rows), [("SCR", nm)], [k])
            self.dma("sp", XT_[:], XD[:, :, rows], ["XD"], ["C_XTL"])
            red(M1[:], h16(Ot), [kO], ["C_M1"])
            self.ts("dve", M1[:], M1[:], -1.0 / 64, None, ALU.mult, None, ["C_M1"], ["C_M1"])
            self.tt("dve", h16(Ot), h16(Ot), b16(M1[:]), ALU.add, [kO, "C_M1"], [kO])
            self.tt("dve", T1, Ot, Ot, ALU.mult, [kO], [kT1])
            red(M2[:], h16(T1), [kT1], ["C_M2"])
            self.act(M2[:], M2[:], AF.Ln, ["C_M2", ("C1", 3)], ["C_M2"], scale=1.0 / 64, bias=self.C1[:, 3:4])
            self.act(M2[:], M2[:], AF.Exp, ["C_M2"], ["C_M2"], scale=-0.5)
            self.tt("dve", h16(Ot), h16(Ot), b16(M2[:]), ALU.mult, [kO, "C_M2"], [kO])
            self.tt("dve", Ot, Ot, Plw, ALU.mult, [kO, klw], [kO])
            self.tt("dve", Ot, Ot, Plb, ALU.add, [kO, klb], [kO])
            self.tt("dve", T1, Rt, Kt, ALU.mult, [kR, kK], [kT1])
            self.tt("dve", T1, T1, Prk, ALU.mult, [kT1, krk], [kT1])
            red(M1[:], h16(T1), [kT1], ["C_M1"])
            self.tt("dve", h16(T1), h16(Vt), b16(M1[:]), ALU.mult, [kV, "C_M1"], [kT1])
            self.tt("dve", Ot, Ot, T1, ALU.add, [kO, kT1], [kO])
            self.tt("dve", Ot, Ot, Gt, ALU.mult, [kO, kG], [kO])
            for half in range(2):
                ps, pk = self.bank()
                for cc in range(4):
                    c = half * 4 + cc
                    self.tr(ps[:, cc * 128:(cc + 1) * 128], Ot[:, c * 128:(c + 1) * 128], self.IDF[:], [kO, "IDF"], [pk])
                self.cp("act", YT[:, half * 4:(half + 1) * 4, :].rearrange("p c t -> p (c t)"), ps[:, :], [pk], ["C_YT"])
            for n in range(8):
                ps, pk = self.bank()
                for c in range(8):
                    self.mm(ps[:, 0:128], WO[:, c, n * 128:(n + 1) * 128], YT[:, c, :], c == 0, c == 7, ["C_WO", "C_YT"], [pk])
                self.tt("dve", XT_[:, n, :], XT_[:, n, :], ps[:, 0:128], ALU.add, [pk, "C_XTL"], ["C_XTL"])
            self.dma("sp", XD[:, :, rows], XT_[:], ["C_XTL"], ["XD"])
    self.dma("sp", self.X[:], XD, ["XD"], ["X"])
    self.fm_to_rows(USH[:, :, 0:1], "C_USH", 1, self.O["rs_p"])
    self.fm_to_rows(USH[:, :, 1:], "C_USH", NS, self.O["rs_s"])


KB.layer_C = layer_C


def build(layers=(0, 1, 2, 3), debug=False, mlp=True):
    kb = KB(layers, debug)
    with kb.es:
        kb.setup()
        kb.load_x()
        for l in layers:
            with kb.scope():
                kind = l % 3
                if kind == 0:
                    kb.layer_A(l, l // 3)
                elif kind == 1:
                    kb.layer_B(l, 0)
                else:
                    kb.layer_C(l, 0)
            if mlp:
                with kb.scope():
                    kb.mlp(l)
        if not getattr(kb, '_skip_final', False):
            with kb.scope():
                kb.final_out()
        if debug:
            kb.dump_x()
        kb.P.emit()
    return kb.nc


def make_in_maps(inputs, cores):
    g = {k: np.ascontiguousarray(np.asarray(v, dtype=np.float32)) for k, v in inputs.items()}
    maps = []
    for i in cores:
        sl = slice(NS * i, NS * (i + 1))
        m = {
            "xp": g["x_prompt"][i], "xs": g["x_sample"][sl].reshape(NS * LS, D),
            "st_lru_conv": g["state_lru_conv"][:, sl], "st_lru_h": g["state_lru_h"][:, sl],
            "st_ssm_conv": g["state_ssm_conv"][:, sl], "st_ssm": g["state_ssm"][:, sl],
            "st_rw_shift": g["state_rwkv_shift"][:, sl], "st_rw_wkv": g["state_rwkv_wkv"][:, sl],
        }
        for k in IN_SPECS:
            if k in m:
                continue
            v = g[k]
            if k == "norm_final":
                v = v.reshape(1, D)
            m[k] = v
        maps.append({k: np.ascontiguousarray(v) for k, v in m.items()})
    return maps


def kernel(**inputs):
    nc = build()
    cores = list(range(8))
    res = run_bass_kernel_spmd(nc, make_in_maps(inputs, cores), core_ids=cores)
    r = res.results
    def cat(name, axis=0):
        return np.concatenate([x[name] for x in r], axis=axis)
    y_p = np.stack([x["y_p"] for x in r])
    y_s = cat("y_s").reshape(128, LS, D)
    lc_p = np.stack([x["lc_p"] for x in r], axis=1)
    lc_s = cat("lc_s", 1)
    lh_p = np.stack([x["lh_p"] for x in r], axis=1)
    lh_s = cat("lh_s", 1)
    sc_p = np.stack([x["sc_p"] for x in r])[None]
    sc_s = cat("sc_s")[None]
    ss_p = np.stack([x["ss_p"] for x in r])[None]
    ss_s = cat("ss_s")[None]
    rs_p = np.stack([x["rs_p"][0] for x in r])[None]
    rs_s = cat("rs_s")[None]
    rw_p = np.stack([x["rw_p"] for x in r])[None]
    rw_s = cat("rw_s")[None]
    outs = (y_p, y_s, lc_p, lc_s, lh_p, lh_s, sc_p, sc_s, ss_p, ss_s, rs_p, rs_s, rw_p, rw_s)
    return tuple(np.ascontiguousarray(o, dtype=np.float32) for o in outs)
```

```python
import numpy as np
import concourse.bass as bass
import concourse.mybir as mybir
from concourse.bass_utils import run_bass_kernel_spmd

F32 = mybir.dt.float32
BF16 = mybir.dt.bfloat16
I32 = mybir.dt.int32
AF = mybir.ActivationFunctionType
ALU = mybir.AluOpType
AX = mybir.AxisListType

ENGS = ("pe", "act", "dve", "pool", "sp")
SAME_ENGINE_SYNC = True
import os as _os_
NFUSE = int(_os_.environ.get('NFUSE', '1'))
FUSE_WAIT = bool(int(_os_.environ.get('FUSE_WAIT', '1')))
NOSAME_ENGS = tuple(x for x in _os_.environ.get('NOSAME_ENGS', '').split(',') if x)
N_DMA_SEMS = 12


class _St:
    __slots__ = ("w", "r")

    def __init__(self):
        self.w = None
        self.r = {}


class _Op:
    __slots__ = ("eng", "fn", "waits", "dma", "signal", "tok")


class Prog:
    def __init__(self, nc):
        self.nc = nc
        self.ops = {e: [] for e in ENGS}
        self.state = {}
        self.ndma = {e: 0 for e in ENGS}
        self.dma_toks = {}
        self.pending = {}
        self.nosame = False

    def barrier(self):
        toks = set()
        for e in ENGS:
            for o in reversed(self.ops[e]):
                if not o.dma:
                    toks.add(o.tok)
                    break
            n = self.ndma[e]
            for j in range(max(0, n - N_DMA_SEMS), n):
                toks.add(("dma", e, j))
        self.pending = {e: set(toks) for e in ENGS}

    def _states(self, key, create):
        if isinstance(key, tuple):
            buf, sub = key
        else:
            buf, sub = key, None
        d = self.state.setdefault(buf, {})
        if sub is None:
            if create and "*" not in d:
                d["*"] = _St()
            return list(d.values())
        out = []
        if "*" in d:
            out.append(d["*"])
        if sub not in d:
            if create:
                d[sub] = _St()
                out.append(d[sub])
        else:
            out.append(d[sub])
        return out

    def op(self, eng, fn, reads=(), writes=(), dma=False):
        ps_r = [k for k in reads if isinstance(k, tuple) and k[0] == "PS"]
        if ps_r:
            reads = [k for k in reads if not (isinstance(k, tuple) and k[0] == "PS")]
            writes = list(writes) + ps_r
        o = _Op()
        o.eng, o.fn, o.dma, o.signal = eng, fn, dma, False
        idx = len(self.ops[eng])
        if dma:
            j = self.ndma[eng]
            self.ndma[eng] += 1
            o.tok = ("dma", eng, j)
            self.dma_toks[(eng, j)] = o
        else:
            o.tok = ("eng", eng, idx)
        waits = set()
        for k in reads:
            for st in self._states(k, True):
                if st.w is not None:
                    waits.add(st.w)
        for k in writes:
            for st in self._states(k, True):
                if st.w is not None:
                    waits.add(st.w)
                for t in st.r.values():
                    waits.add(t)
        if self.pending.get(eng):
            waits |= self.pending.pop(eng)
        w2 = set()
        for t in waits:
            if t[0] == "eng" and t[1] == eng:
                if not dma and (eng in ("pe", "sp") or not SAME_ENGINE_SYNC or self.nosame or (NOSAME_ENGS and eng in NOSAME_ENGS)):
                    continue
            w2.add(t)
        o.waits = w2
        for k in reads:
            for st in self._states(k, True):
                st.r[o.tok if dma else o.tok[1]] = o.tok
        for k in writes:
            for st in self._states(k, True):
                st.w = o.tok
                st.r = {}
        self.ops[eng].append(o)
        return o

    def emit(self, final_wait_all=True):
        nc = self.nc
        tokmap = {}
        for e in ENGS:
            for i, o in enumerate(self.ops[e]):
                tokmap[o.tok] = o
        for e in ENGS:
            for o in self.ops[e]:
                for t in o.waits:
                    tokmap[t].signal = True
        import contextlib
        with contextlib.ExitStack() as es:
            EPOCH = 16000
            nsig = {e: sum(1 for o in self.ops[e] if o.signal and not o.dma) for e in ENGS}
            esem = {e: [es.enter_context(nc.semaphore("s_%s%d" % (e, i))) for i in range(nsig[e] // EPOCH + 1)] for e in ENGS if e != "sp"}
            dsem = {e: [es.enter_context(nc.semaphore("d_%s%d" % (e, i))) for i in range(N_DMA_SEMS)]
                    for e in ENGS if self.ndma[e] > 0}
            val = {}
            for e in ENGS:
                c = 0
                for o in self.ops[e]:
                    if o.dma:
                        j = o.tok[2]
                        val[o.tok] = (dsem[e][j % N_DMA_SEMS], 16 * (j // N_DMA_SEMS + 1))
                    elif o.signal:
                        val[o.tok] = (esem[e][c // EPOCH], c % EPOCH + 1)
                        c += 1
            self.maxcount = {}
            block = es.enter_context(nc.Block())

            def run(e, eng):
                seen = {}
                for o in self.ops[e]:
                    ws = []
                    for t in o.waits:
                        ws.append(val[t])
                    if o.dma:
                        j = o.tok[2]
                        if j >= N_DMA_SEMS:
                            ws.append(val[("dma", e, j - N_DMA_SEMS)])
                    need = []
                    for (s, v) in ws:
                        if seen.get(id(s), 0) >= v:
                            continue
                        seen[id(s)] = v
                        need = [(s2, v2) for (s2, v2) in need if s2 is not s] + [(s, v)]
                    fuse = FUSE_WAIT and need and not o.dma
                    self.stats = getattr(self, "stats", {})
                    self.stats[(e, len(need))] = self.stats.get((e, len(need)), 0) + 1
                    nf = min(len(need), NFUSE) if fuse else 0
                    for (s, v) in need[:len(need) - nf]:
                        eng.wait_ge(s, v)
                    ins = o.fn(eng)
                    for (s, v) in need[len(need) - nf:]:
                        ins._wait_ge(s, v)
                    if o.dma:
                        s, v = val[o.tok]
                        ins.then_inc(s, 16)
                    elif o.signal:
                        ins.then_inc(val[o.tok][0], 1)
                if final_wait_all:
                    n = self.ndma[e]
                    for j in range(max(0, n - N_DMA_SEMS), n):
                        s, v = val[("dma", e, j)]
                        if seen.get(id(s), 0) < v:
                            eng.wait_ge(s, v)
                            seen[id(s)] = v

            @block.tensor
            def _(eng):
                run("pe", eng)

            @block.scalar
            def _(eng):
                run("act", eng)

            @block.vector
            def _(eng):
                run("dve", eng)

            @block.gpsimd
            def _(eng):
                run("pool", eng)

            @block.sync
            def _(eng):
                run("sp", eng)
import contextlib


D = 1024
NTP = 2048
NS = 16
LS = 8
NT = NTP + NS * LS
TT = [(0, 512), (512, 512), (1024, 512), (1536, 512), (2048, 128)]
UC = 1 + NTP + NS * 9
XBC = 3 + NTP + NS * 11

PROW = {}


def _prow_layout():
    r = 0
    def add(name, n):
        nonlocal r
        PROW[name] = r
        r += n
    add("norm_mix", 4); add("norm_ffn", 4); add("norm_final", 1)
    add("lru_conv_w", 8); add("lru_conv_b", 2); add("lru_b_r", 2); add("lru_b_i", 2); add("lru_lambda", 2)
    add("ssm_norm_w", 2); add("ssm_conv_w", 16); add("ssm_conv_b", 4)
    add("rwkv_mu", 6); add("rwkv_w0", 1); add("rwkv_a0", 1); add("rwkv_k_k", 1); add("rwkv_k_a", 1)
    add("rwkv_lnx_w", 1); add("rwkv_lnx_b", 1); add("rwkv_r_k", 1)
    return r


NPROW = _prow_layout()

IN_SPECS = {
    "xp": [NTP, D], "xs": [NS * LS, D],
    "st_lru_conv": [2, NS, 3, D], "st_lru_h": [2, NS, D], "st_ssm_conv": [1, NS, 3, 4096],
    "st_ssm": [1, NS, 32, 64, 128], "st_rw_shift": [1, NS, D], "st_rw_wkv": [1, NS, 16, 64, 64],
    "norm_mix": [4, D], "norm_ffn": [4, D], "norm_final": [1, D],
    "lru_w_in": [2, D, 2048], "lru_conv_w": [2, 4, D], "lru_conv_b": [2, D], "lru_w_r": [2, 8, 128, 128],
    "lru_b_r": [2, D], "lru_w_i": [2, 8, 128, 128], "lru_b_i": [2, D], "lru_lambda": [2, D], "lru_w_out": [2, D, D],
    "ssm_w_in": [1, D, 6176], "ssm_conv_w": [1, 4, 4096], "ssm_conv_b": [1, 4096], "ssm_dt_bias": [1, 32],
    "ssm_a_log": [1, 32], "ssm_d": [1, 32], "ssm_norm_w": [1, 2048], "ssm_w_out": [1, 2048, D],
    "rwkv_mu": [1, 6, D], "rwkv_w_rkv": [1, 3, D, D], "rwkv_w0": [1, D], "rwkv_w_w1": [1, D, 64], "rwkv_w_w2": [1, 64, D],
    "rwkv_a0": [1, D], "rwkv_w_a1": [1, D, 64], "rwkv_w_a2": [1, 64, D], "rwkv_w_g1": [1, D, 128], "rwkv_w_g2": [1, 128, D],
    "rwkv_k_k": [1, D], "rwkv_k_a": [1, D], "rwkv_r_k": [1, 16, 64], "rwkv_lnx_w": [1, D], "rwkv_lnx_b": [1, D],
    "rwkv_w_out": [1, D, D], "ffn_w1": [4, D, 4096], "ffn_w2": [4, 4096, D],
}
OUT_SPECS = {
    "y_p": [NTP, D], "y_s": [NS * LS, D],
    "lc_p": [2, 3, D], "lc_s": [2, NS, 3, D], "lh_p": [2, D], "lh_s": [2, NS, D],
    "sc_p": [3, 4096], "sc_s": [NS, 3, 4096], "ss_p": [32, 64, 128], "ss_s": [NS, 32, 64, 128],
    "rs_p": [1, D], "rs_s": [NS, D], "rw_p": [16, 64, 64], "rw_s": [NS, 16, 64, 64],
}


class KB:
    def __init__(self, layers=(0, 1, 2, 3), debug=False):
        self.nc = nc = bass.Bass("TRN2", target_bir_lowering=False)
        self.P = Prog(nc)
        self.es = contextlib.ExitStack()
        self.I = {k: nc.dram_tensor(k, v, F32, kind="ExternalInput").ap() for k, v in IN_SPECS.items()}
        self.O = {k: nc.dram_tensor(k, v, F32, kind="ExternalOutput").ap() for k, v in OUT_SPECS.items()}
        self.debug = debug
        if debug:
            self.O["dbg_x"] = nc.dram_tensor("dbg_x", [128, 8, NT], F32, kind="ExternalOutput").ap()
            self.O["dbg_scr"] = nc.dram_tensor("dbg_scr", [8, NT, D], F32, kind="ExternalOutput").ap()
        self.bank_i = 0
        self.layers = layers
        self._n = 0

    def sb(self, name, shape, dt=F32):
        self._n += 1
        return self.es.enter_context(self.nc.sbuf_tensor("%s_%d" % (name, self._n), shape, dt))

    @contextlib.contextmanager
    def scope(self):
        old = self.es
        self.es = contextlib.ExitStack()
        try:
            yield
        finally:
            self.es.close()
            self.es = old
            self.P.barrier()

    def bank(self):
        b = self.bank_i
        self.bank_i = (self.bank_i + 1) % 8
        return self.PS[b], ("PS", b)

    def dma(self, eng, out, in_, reads=(), writes=()):
        self.P.op(eng, lambda e: e.dma_start(out=out, in_=in_), reads, writes, dma=True)

    def act(self, out, in_, func, reads, writes, **kw):
        self.P.op("act", lambda e: e.activation(out=out, in_=in_, func=func, **kw), reads, writes)

    def mm(self, out, lhsT, rhs, start, stop, reads, writes):
        self.P.op("pe", lambda e: e.matmul(out, lhsT=lhsT, rhs=rhs, start=start, stop=stop), reads, writes)

    def tr(self, out, in_, ident, reads, writes):
        self.P.op("pe", lambda e: e.transpose(out, in_, ident), reads, writes)

    def ts(self, eng, out, in0, s1, s2, op0, op1, reads, writes):
        if op1 is None:
            self.P.op(eng, lambda e: e.tensor_scalar(out=out, in0=in0, scalar1=s1, scalar2=None, op0=op0), reads, writes)
        else:
            self.P.op(eng, lambda e: e.tensor_scalar(out=out, in0=in0, scalar1=s1, scalar2=s2, op0=op0, op1=op1), reads, writes)

    def stt(self, out, in0, scalar, in1, op0, op1, reads, writes):
        self.P.op("dve", lambda e: e.scalar_tensor_tensor(out=out, in0=in0, scalar=scalar, in1=in1, op0=op0, op1=op1), reads, writes)

    def tt(self, eng, out, in0, in1, op, reads, writes):
        self.P.op(eng, lambda e: e.tensor_tensor(out=out, in0=in0, in1=in1, op=op), reads, writes)

    def cp(self, eng, out, in_, reads, writes):
        if eng == "act":
            self.P.op("act", lambda e: e.activation(out=out, in_=in_, func=AF.Copy), reads, writes)
        else:
            self.P.op(eng, lambda e: e.tensor_copy(out=out, in_=in_), reads, writes)

    def memset(self, eng, ap, v, writes):
        self.P.op(eng, lambda e: e.memset(ap, v), (), writes)

    def scan(self, out, d0, d1, init, reads, writes):
        self.P.op("dve", lambda e: e.tensor_tensor_scan(out=out, data0=d0, data1=d1, initial=init, op0=ALU.mult, op1=ALU.add), reads, writes)

    def xv(self, c, ti):
        c0, w = TT[ti]
        return self.X[:, c, c0:c0 + w]

    def uv(self, c, ti, shift=0):
        if ti < 4:
            s = 1 + 512 * ti + shift
            return self.U[:, c, s:s + 512]
        v = self.U[:, c, 1 + NTP:UC].rearrange("p (s t) -> p s t", t=9)
        return v[:, :, 1 + shift:9 + shift]

    @staticmethod
    def v3(ap, ti):
        if ti < 4:
            return ap
        return ap.rearrange("p (s t) -> p s t", t=8)

    def setup(self):
        nc = self.nc
        self.PS = [self.es.enter_context(nc.psum_tensor("ps%d" % i, [128, 512], F32)) for i in range(8)]
        self.X = self.sb("X", [128, 8, NT])
        self.U = self.sb("U", [128, 8, UC], BF16)
        self.IDF = self.sb("IDF", [128, 128])
        self.IDB = self.sb("IDB", [128, 128], BF16)
        self.ONESB = self.sb("ONESB", [128, 128], BF16)
        self.C1 = self.sb("C1", [128, 4])
        self.PRM = self.sb("PRM", [64, D])
        self.PF = self.sb("PF", [128, 8, 64])
        self.STG = [self.sb("STG%d" % i, [128, D]) for i in range(2)]
        self.SQ = self.sb("SQ", [128, 8, 512], BF16)
        self.RS = self.sb("RS", [128, 512])
        P = self.P
        P.op("pool", lambda e: e.memset(self.IDF[:], 0.0), (), ["IDF"])
        P.op("pool", lambda e: e.affine_select(out=self.IDF[:], in_=self.IDF[:], pattern=[[-1, 128]], compare_op=ALU.not_equal,
                                               fill=1.0, base=0, channel_multiplier=1), ["IDF"], ["IDF"])
        self.cp("dve", self.IDB[:], self.IDF[:], ["IDF"], ["IDB"])
        self.memset("dve", self.ONESB[:], 1.0, ["ONESB"])
        self.memset("dve", self.C1[:, 0:1], 1e-6, [("C1", 0)])
        self.memset("dve", self.C1[:, 1:2], 1.0, [("C1", 1)])
        self.memset("dve", self.C1[:, 2:3], 1e-5, [("C1", 2)])
        self.memset("dve", self.C1[:, 3:4], 64e-5, [("C1", 3)])
        self.memset("dve", self.U[:, :, 0:1], 0.0, [("U", "shiftp")])
        self.memset("pool", self.PRM[:], 0.0, ["PRM"])
        I = self.I
        def row(name, src, n):
            r = PROW[name]
            self.dma("sp", self.PRM[r:r + n, :], src, (), ["PRM"])
        row("norm_mix", I["norm_mix"], 4); row("norm_ffn", I["norm_ffn"], 4); row("norm_final", I["norm_final"], 1)
        row("lru_conv_w", I["lru_conv_w"].rearrange("a k d -> (a k) d"), 8)
        row("lru_conv_b", I["lru_conv_b"], 2); row("lru_b_r", I["lru_b_r"], 2); row("lru_b_i", I["lru_b_i"], 2)
        row("lru_lambda", I["lru_lambda"], 2)
        row("ssm_norm_w", I["ssm_norm_w"].rearrange("a (r d) -> (a r) d", d=D), 2)
        row("ssm_conv_w", I["ssm_conv_w"].rearrange("a k (r d) -> (a k r) d", d=D), 16)
        row("ssm_conv_b", I["ssm_conv_b"].rearrange("a (r d) -> (a r) d", d=D), 4)
        row("rwkv_mu", I["rwkv_mu"].rearrange("a k d -> (a k) d"), 6)
        for nm in ("rwkv_w0", "rwkv_a0", "rwkv_k_k", "rwkv_k_a", "rwkv_lnx_w", "rwkv_lnx_b"):
            row(nm, I[nm], 1)
        row("rwkv_r_k", I["rwkv_r_k"].rearrange("a h n -> a (h n)"), 1)
        for c in range(8):
            ps, pk = self.bank()
            self.tr(ps[:, 0:64], self.PRM[:, c * 128:(c + 1) * 128], self.IDF[0:64, 0:64], ["PRM", "IDF"], [pk])
            self.cp("dve", self.PF[:, c, :], ps[:, 0:64], [pk], [("PF", c)])

    def pf(self, name, k, c):
        r = PROW[name] + k
        return self.PF[:, c, r:r + 1]

    def load_x(self):
        n = 0
        for ti, (c0, w) in enumerate(TT):
            for j in range(w // 128):
                b = n % 2
                n += 1
                src = self.I["xp"][c0 + j * 128:c0 + (j + 1) * 128, :] if ti < 4 else self.I["xs"][:, :]
                self.dma("sp", self.STG[b][:], src, (), [("STG", b)])
                for c in range(8):
                    self.tr(self.PS[c][:, j * 128:(j + 1) * 128], self.STG[b][:, c * 128:(c + 1) * 128], self.IDF[:],
                            [("STG", b), "IDF"], [("PS", c)])
            for c in range(8):
                self.cp("dve" if c % 2 == 0 else "act", self.X[:, c, c0:c0 + w], self.PS[c][:, 0:w], [("PS", c)], [("X", (c, ti))])

    def rows_to_fm(self, src, nrows, dst, dkey):
        self.dma("sp", self.STG[0][0:nrows, :], src, (), [("STG", 0)])
        for c in range(8):
            ps, pk = self.bank()
            self.tr(ps[:, 0:nrows], self.STG[0][0:nrows, c * 128:(c + 1) * 128], self.IDF[0:nrows, 0:nrows], [("STG", 0), "IDF"], [pk])
            self.cp("dve", dst[:, c, 0:nrows], ps[:, 0:nrows], [pk], [dkey])

    def fm_to_rows(self, src, skey, nrows, dst):
        for half in range(2):
            ps, pk = self.bank()
            for cc in range(4):
                c = half * 4 + cc
                self.tr(ps[0:nrows, cc * 128:(cc + 1) * 128], src[:, c, 0:nrows], self.IDF[:], [skey, "IDF"], [pk])
            self.cp("dve", self.STG[1][0:nrows, half * 512:(half + 1) * 512], ps[0:nrows, :], [pk], [("STG", 1)])
        self.dma("sp", dst, self.STG[1][0:nrows, :], [("STG", 1)], ())

    def norm_to_U(self, pname, k, shift_out=None):
        for ti, (c0, w) in enumerate(TT):
            ps, pk = self.bank()
            for c in range(8):
                self.act(self.SQ[:, c, :w], self.X[:, c, c0:c0 + w], AF.Square, [("X", (c, ti))], [("SQ", c)])
                self.mm(ps[:, :w], self.ONESB[:], self.SQ[:, c, :w], c == 0, c == 7, [("SQ", c), "ONESB"], [pk])
            self.act(self.RS[:, :w], ps[:, :w], AF.Ln, [pk, ("C1", 0)], ["RS"], scale=1.0 / D, bias=self.C1[:, 0:1])
            self.act(self.RS[:, :w], self.RS[:, :w], AF.Exp, ["RS"], ["RS"], scale=-0.5)
            for c in range(8):
                self.stt(self.uv(c, ti), self.v3(self.X[:, c, c0:c0 + w], ti), self.pf(pname, k, c), self.v3(self.RS[:, :w], ti),
                         ALU.mult, ALU.mult, [("X", (c, ti)), "RS", ("PF", c)], [("U", (c, ti))])
                if shift_out is not None and ti == 3:
                    self.stt(shift_out[:, c, 0:1], self.X[:, c, NTP - 1:NTP], self.pf(pname, k, c), self.RS[:, 511:512],
                             ALU.mult, ALU.mult, [("X", (c, ti)), "RS", ("PF", c)], ["C_USH"])
                if shift_out is not None and ti == 4:
                    self.stt(shift_out[:, c, 1:], self.X[:, c, NTP:NT].rearrange("p (s t) -> p s t", t=8)[:, :, 7],
                             self.pf(pname, k, c), self.RS[:, 0:128].rearrange("p (s t) -> p s t", t=8)[:, :, 7],
                             ALU.mult, ALU.mult, [("X", (c, ti)), "RS", ("PF", c)], ["C_USH"])

    def u_keys(self, ti):
        return [("U", (c, ti)) for c in range(8)]

    def mlp(self, l):
        self.norm_to_U("norm_ffn", l)
        self.W1S = [self.sb("W1S%d" % i, [128, 8, 512], BF16) for i in range(2)]
        self.W2S = [self.sb("W2S%d" % i, [128, 4, D], BF16) for i in range(2)]
        self.HT = [self.sb("HT%d" % i, [128, 4, 512], BF16) for i in range(2)]
        self.RT = [self.sb("RT%d" % i, [128, 512]) for i in range(2)]
        w1 = self.I["ffn_w1"]
        w2 = self.I["ffn_w2"]
        def load(s):
            b = s % 2
            self.dma("pool", self.W1S[b][:], w1[l, :, s * 512:(s + 1) * 512].rearrange("(kc p) n -> p kc n", p=128), (), [("W1S", b)])
            self.dma("pool", self.W2S[b][:], w2[l, s * 512:(s + 1) * 512, :].rearrange("(fc p) n -> p fc n", p=128), (), [("W2S", b)])
        load(0)
        hb = 0
        rb = 0
        for s in range(8):
            if s + 1 < 8:
                load(s + 1)
            b = s % 2
            for ti, (c0, w) in enumerate(TT):
                H = self.HT[hb]
                hk = "HT%d" % hb
                hb ^= 1
                for fc in range(4):
                    ps, pk = self.bank()
                    for kc in range(8):
                        self.mm(self.v3(ps[:, :w], ti), self.W1S[b][:, kc, fc * 128:(fc + 1) * 128], self.uv(kc, ti), kc == 0, kc == 7,
                                [("W1S", b), ("U", (kc, ti))], [pk])
                    R = self.RT[rb]
                    rk = "RT%d" % rb
                    rb ^= 1
                    self.act(R[:, :w], ps[:, :w], AF.Relu, [pk], [rk])
                    self.act(H[:, fc, :w], R[:, :w], AF.Square, [rk], [(hk, fc)])
                for n in range(8):
                    ps, pk = self.bank()
                    for fc in range(4):
                        self.mm(ps[:, :w], self.W2S[b][:, fc, n * 128:(n + 1) * 128], H[:, fc, :w], fc == 0, fc == 3,
                                [("W2S", b), (hk, fc)], [pk])
                    self.tt("dve", self.X[:, n, c0:c0 + w], self.X[:, n, c0:c0 + w], ps[:, :w], ALU.add,
                            [pk, ("X", (n, ti))], [("X", (n, ti))])

    def final_out(self):
        YT = self.STG
        UF = self.sb("UF", [128, 8, 512])
        n = 0
        for ti, (c0, w) in enumerate(TT):
            ps, pk = self.bank()
            for c in range(8):
                self.act(self.SQ[:, c, :w], self.X[:, c, c0:c0 + w], AF.Square, [("X", (c, ti))], [("SQ", c)])
                self.mm(ps[:, :w], self.ONESB[:], self.SQ[:, c, :w], c == 0, c == 7, [("SQ", c), "ONESB"], [pk])
            self.act(self.RS[:, :w], ps[:, :w], AF.Ln, [pk, ("C1", 0)], ["RS"], scale=1.0 / D, bias=self.C1[:, 0:1])
            self.act(self.RS[:, :w], self.RS[:, :w], AF.Exp, ["RS"], ["RS"], scale=-0.5)
            for c in range(8):
                self.stt(UF[:, c, :w], self.X[:, c, c0:c0 + w], self.pf("norm_final", 0, c), self.RS[:, :w],
                         ALU.mult, ALU.mult, [("X", (c, ti)), "RS", ("PF", c)], [("UF", c)])
            for j in range(w // 128):
                b = n % 2
                n += 1
                for half in range(2):
                    ps2, pk2 = self.bank()
                    for cc in range(4):
                        c = half * 4 + cc
                        self.tr(ps2[:, cc * 128:(cc + 1) * 128], UF[:, c, j * 128:(j + 1) * 128], self.IDF[:], [("UF", c), "IDF"], [pk2])
                    self.cp("act" if half else "dve", YT[b][:, half * 512:(half + 1) * 512], ps2[:, :], [pk2], [("STG", b)])
                dst = self.O["y_p"][c0 + j * 128:c0 + (j + 1) * 128, :] if ti < 4 else self.O["y_s"][:, :]
                self.dma("sp", dst, YT[b][:], [("STG", b)], ())

    def dump_x(self):
        self.dma("sp", self.O["dbg_x"], self.X[:], ["X"], ())


def layer_A(self, l, ia):
    I = self.I
    self.norm_to_U("norm_mix", l)
    XB = self.sb("A_XB", [128, XBC])
    XC = self.sb("A_XC", [128, NT])
    XCb = self.sb("A_XCb", [128, NT], BF16)
    GATE = self.sb("A_GATE", [128, NT], BF16)
    R = self.sb("A_R", [128, NT])
    Iq = self.sb("A_I", [128, NT])
    WIN = [self.sb("A_WIN%d" % i, [128, 8, 256], BF16) for i in range(2)]
    WR = [self.sb("A_WR%d" % i, [128, 128], BF16) for i in range(2)]
    WI = [self.sb("A_WI%d" % i, [128, 128], BF16) for i in range(2)]
    WO = [self.sb("A_WO%d" % i, [128, D], BF16) for i in range(2)]
    CL = self.sb("A_CL", [128, 8])
    H0 = self.sb("A_H0", [128, 8, NS])
    CS0 = self.sb("A_CS0", [128, 8, NS * 3])
    HST = self.sb("A_HST", [128, 8, 1 + NS])
    CST = self.sb("A_CST", [128, 8, 3 + NS * 3])
    XBs = XB[:, 3 + NTP:XBC].rearrange("p (s t) -> p s t", t=11)
    XCs = XC[:, NTP:NT].rearrange("p (s t) -> p s t", t=8)
    rl = PROW["lru_lambda"] + ia
    self.act(CL[:], self.PF[:, :, rl], AF.Exp, ["PF"], ["A_CL"], scale=-1.0)
    self.act(CL[:], CL[:], AF.Ln, ["A_CL", ("C1", 1)], ["A_CL"], bias=self.C1[:, 1:2], scale=1.0)
    self.ts("dve", CL[:], CL[:], -8.0, None, ALU.mult, None, ["A_CL"], ["A_CL"])
    self.rows_to_fm(I["st_lru_h"][ia], NS, H0, "A_H0")
    self.rows_to_fm(I["st_lru_conv"][ia].rearrange("s k d -> (s k) d"), NS * 3, CS0, "A_CS0")
    w_in, w_out = I["lru_w_in"], I["lru_w_out"]

    def load(j):
        b = j % 2
        self.dma("pool", WIN[b][:, :, 0:128], w_in[ia, :, j * 128:(j + 1) * 128].rearrange("(kc p) n -> p kc n", p=128), (), [("A_WIN", b)])
        self.dma("pool", WIN[b][:, :, 128:256], w_in[ia, :, D + j * 128:D + (j + 1) * 128].rearrange("(kc p) n -> p kc n", p=128), (), [("A_WIN", b)])
        self.dma("pool", WR[b][:], I["lru_w_r"][ia, j], (), [("A_WR", b)])
        self.dma("pool", WI[b][:], I["lru_w_i"][ia, j], (), [("A_WI", b)])
        self.dma("pool", WO[b][:], w_out[ia, j * 128:(j + 1) * 128, :], (), [("A_WO", b)])

    load(0)
    for j in range(8):
        if j + 1 < 8:
            load(j + 1)
        b = j % 2
        self.memset("dve", XB[:, 0:3], 0.0, [("A_XB", "st")])
        self.cp("dve", XBs[:, :, 0:3], CS0[:, j, :].rearrange("p (s k) -> p s k", k=3), ["A_CS0"], [("A_XB", "st")])
        for ti, (c0, w) in enumerate(TT):
            for half in range(2):
                ps, pk = self.bank()
                for kc in range(8):
                    self.mm(self.v3(ps[:, :w], ti), WIN[b][:, kc, half * 128:(half + 1) * 128], self.uv(kc, ti), kc == 0, kc == 7,
                            [("A_WIN", b), ("U", (kc, ti))], [pk])
                if half == 0:
                    dst = XB[:, 3 + c0:3 + c0 + w] if ti < 4 else XBs[:, :, 3:11]
                    self.cp("act", dst, self.v3(ps[:, :w], ti), [pk], [("A_XB", ti)])
                else:
                    self.act(GATE[:, c0:c0 + w], ps[:, :w], AF.Gelu_apprx_tanh, [pk], [("A_GATE", ti)])
        self.cp("pool", CST[:, j, 0:3], XB[:, NTP:NTP + 3], ["A_XB"], [("A_CST", j)])
        self.cp("pool", CST[:, j, 3:].rearrange("p (s k) -> p s k", k=3), XBs[:, :, 8:11], ["A_XB"], [("A_CST", j)])
        cw = [self.pf("lru_conv_w", ia * 4 + k, j) for k in range(4)]
        cb = self.pf("lru_conv_b", ia, j)
        for (dst, srcf) in ((XC[:, 0:NTP], lambda k: XB[:, k:k + NTP]), (XCs, lambda k: XBs[:, :, k:k + 8])):
            self.ts("dve", dst, srcf(0), cw[0], cb, ALU.mult, ALU.add, ["A_XB", ("PF", j)], ["A_XC"])
            for k in range(1, 4):
                self.stt(dst, srcf(k), cw[k], dst, ALU.mult, ALU.add, ["A_XB", "A_XC", ("PF", j)], ["A_XC"])
        self.cp("act", XCb[:], XC[:], ["A_XC"], ["A_XCb"])
        for ti, (c0, w) in enumerate(TT):
            for (Wg, dstb, bname, key) in ((WR, R, "lru_b_r", "A_R"), (WI, Iq, "lru_b_i", "A_I")):
                ps, pk = self.bank()
                self.mm(ps[:, :w], Wg[b][:], XCb[:, c0:c0 + w], True, True, ["A_XCb", (key.replace("A_", "A_W"), b)], [pk])
                self.act(dstb[:, c0:c0 + w], ps[:, :w], AF.Sigmoid, [pk, ("PF", j)], [key], bias=self.pf(bname, ia, j), scale=1.0)
        T1 = XB[:, 0:NT]
        self.act(R[:], R[:], AF.Exp, ["A_R", "A_CL"], ["A_R"], scale=CL[:, j:j + 1])
        self.act(T1, R[:], AF.Square, ["A_R", "A_XB"], ["A_XB"])
        self.ts("dve", T1, T1, -1.0, 1.0, ALU.mult, ALU.add, ["A_XB"], ["A_XB"])
        self.ts("dve", T1, T1, 1e-30, None, ALU.max, None, ["A_XB"], ["A_XB"])
        self.act(T1, T1, AF.Sqrt, ["A_XB"], ["A_XB"])
        self.memset("dve", T1[:, 0:1], 1.0, ["A_XB"])
        self.memset("dve", R[:, 0:1], 0.0, ["A_R"])
        self.tt("dve", Iq[:], Iq[:], T1, ALU.mult, ["A_I", "A_XB"], ["A_I"])
        self.tt("dve", Iq[:], Iq[:], XC[:], ALU.mult, ["A_I", "A_XC"], ["A_I"])
        self.scan(XC[:, 0:NTP], R[:, 0:NTP], Iq[:, 0:NTP], 0.0, ["A_R", "A_I"], ["A_XC"])
        for s in range(NS):
            c0 = NTP + s * 8
            self.scan(XC[:, c0:c0 + 8], R[:, c0:c0 + 8], Iq[:, c0:c0 + 8], H0[:, j, s:s + 1], ["A_R", "A_I", "A_H0"], ["A_XC"])
        self.cp("pool", HST[:, j, 0:1], XC[:, NTP - 1:NTP], ["A_XC"], [("A_HST", j)])
        self.cp("pool", HST[:, j, 1:], XCs[:, :, 7], ["A_XC"], [("A_HST", j)])
        self.tt("dve", XCb[:], XC[:], GATE[:], ALU.mult, ["A_XC", "A_GATE"], ["A_XCb"])
        for ti, (c0, w) in enumerate(TT):
            for n in range(8):
                ps, pk = self.bank()
                self.mm(ps[:, :w], WO[b][:, n * 128:(n + 1) * 128], XCb[:, c0:c0 + w], True, True, ["A_XCb", ("A_WO", b)], [pk])
                self.tt("dve", self.X[:, n, c0:c0 + w], self.X[:, n, c0:c0 + w], ps[:, :w], ALU.add, [pk, ("X", (n, ti))], [("X", (n, ti))])
    self.fm_to_rows(HST[:, :, 0:1], "A_HST", 1, self.O["lh_p"][ia:ia + 1, :])
    self.fm_to_rows(HST[:, :, 1:], "A_HST", NS, self.O["lh_s"][ia])
    self.fm_to_rows(CST[:, :, 0:3], "A_CST", 3, self.O["lc_p"][ia])
    self.fm_to_rows(CST[:, :, 3:], "A_CST", NS * 3, self.O["lc_s"][ia].rearrange("s k d -> (s k) d"))


KB.layer_A = layer_A


def layer_B(self, l, ib):
    I = self.I
    self.norm_to_U("norm_mix", l)
    f32 = F32
    TRI = self.sb("B_TRI", [128, 128]); NEGM = self.sb("B_NEGM", [128, 128]); ONESF = self.sb("B_ONESF", [128, 128])
    self.memset("pool", TRI[:], 1.0, ["B_TRI"])
    self.P.op("pool", lambda e: e.affine_select(out=TRI[:], in_=TRI[:], pattern=[[1, 128]], compare_op=ALU.is_ge, fill=0.0, base=0,
                                                channel_multiplier=-1), ["B_TRI"], ["B_TRI"])
    self.memset("pool", NEGM[:], 0.0, ["B_NEGM"])
    self.P.op("pool", lambda e: e.affine_select(out=NEGM[:], in_=NEGM[:], pattern=[[1, 128]], compare_op=ALU.is_ge, fill=-1.0e4, base=0,
                                                channel_multiplier=-1), ["B_NEGM"], ["B_NEGM"])
    self.memset("pool", ONESF[:], 1.0, ["B_ONESF"])
    DTB = self.sb("B_DTB", [128, 32]); AB = self.sb("B_AB", [128, 32]); DB = self.sb("B_DB", [128, 32])
    self.dma("sp", DTB[:], I["ssm_dt_bias"][ib:ib + 1, :].partition_broadcast(128), (), ["B_DTB"])
    self.dma("sp", AB[:], I["ssm_a_log"][ib:ib + 1, :].partition_broadcast(128), (), ["B_AB"])
    self.dma("sp", DB[:], I["ssm_d"][ib:ib + 1, :].partition_broadcast(128), (), ["B_DB"])
    self.act(AB[:], AB[:], AF.Exp, ["B_AB"], ["B_AB"])
    self.ts("dve", AB[:], AB[:], -1.0, None, ALU.mult, None, ["B_AB"], ["B_AB"])
    CS0 = self.sb("B_CS0", [128, 32, NS * 3]); CST = self.sb("B_CST", [128, 32, 3 + NS * 3])
    for r in range(4):
        self.rows_to_fm(I["st_ssm_conv"][ib].rearrange("s k d -> (s k) d")[:, r * D:(r + 1) * D], NS * 3, CS0[:, r * 8:(r + 1) * 8, :], "B_CS0")
    WZD = [self.sb("B_WZD0", [128, 8, 260], BF16)] * 2
    BONES = self.sb("B_BONES", [128, 128]); TRIS = self.sb("B_TRIS", [128, 128]); NEGMS = self.sb("B_NEGMS", [128, 128])
    SEQM = self.sb("B_SEQM", [128, 16]); DAS = self.sb("B_DAS", [128, 64]); CDS = self.sb("B_CDS", [128, 64])
    YO = self.sb("B_YO", [128, 256]); BTM = self.sb("B_BTM", [128, 128], BF16); USC = self.sb("B_USC", [128, 8, 128], BF16)
    def _asel(ap, pattern, base, cm, fill, keys):
        self.P.op("pool", lambda e: e.affine_select(out=ap, in_=ap, pattern=pattern, compare_op=ALU.is_ge, fill=fill, base=base,
                                                    channel_multiplier=cm), keys, keys)
    self.memset("pool", SEQM[:], 1.0, ["B_SEQM"])
    _asel(SEQM[:], [[-8, 16]], 0, 1, 0.0, ["B_SEQM"])
    _asel(SEQM[:], [[8, 16]], 7, -1, 0.0, ["B_SEQM"])
    self.memset("pool", BONES[:], 1.0, ["B_BONES"])
    _asel(BONES[:].rearrange("p (s t) -> p s t", t=8), [[-8, 16], [0, 8]], 0, 1, 0.0, ["B_BONES"])
    _asel(BONES[:].rearrange("p (s t) -> p s t", t=8), [[8, 16], [0, 8]], 7, -1, 0.0, ["B_BONES"])
    self.tt("pool", TRIS[:], TRI[:], BONES[:], ALU.mult, ["B_TRI", "B_BONES"], ["B_TRIS"])
    self.tt("pool", NEGMS[:], NEGM[:], BONES[:], ALU.mult, ["B_NEGM", "B_BONES"], ["B_NEGMS"])
    self.ts("dve", YO[:, 0:128], BONES[:], 1.0e4, -1.0e4, ALU.mult, ALU.add, ["B_BONES"], ["B_YO"])
    self.tt("dve", NEGMS[:], NEGMS[:], YO[:, 0:128], ALU.add, ["B_NEGMS", "B_YO"], ["B_NEGMS"])
    self.cp("dve", USC[:].rearrange("p c (s t) -> p c s t", t=8),
            self.U[:, :, 1 + NTP:UC].rearrange("p c (s t) -> p c s t", t=9)[:, :, :, 1:9], [("U", (c_, 4)) for c_ in range(8)], ["B_USC"])

    WXBC = [self.sb("B_WXBC0", [128, 8, 512], BF16)] * 2
    WOUT = [self.sb("B_WOUT0", [128, 2, D], BF16)] * 2
    XF = self.sb("B_XF", [128, 4, 3 + 512]); XFS = self.sb("B_XFS", [128, 4, NS * 11])
    XCf = self.sb("B_XCf", [128, 4, 512]); BCb = self.sb("B_BCb", [128, 2, 512], BF16)
    YGT = self.sb("B_YGT", [128, 2, 512], BF16)
    ST = self.sb("B_ST", [128, 256]); STb = self.sb("B_STb", [128, 256], BF16)
    SIN = self.sb("B_SIN", [128, 2, 128]); SOUT = self.sb("B_SOUT", [128, 2, 128])
    XTs = [self.sb("B_XT%d" % i, [128, 256]) for i in range(2)]; BTs = [self.sb("B_BT%d" % i, [128, 128], BF16) for i in range(2)]
    SMs = [self.sb("B_SM%d" % i, [128, 40]) for i in range(2)]
    LT = self.sb("B_LT", [128, 4, 128]); MTs = [self.sb("B_MT%d" % i, [128, 4, 128], BF16) for i in range(2)]
    XDTs = [self.sb("B_XDT%d" % i, [128, 256], BF16) for i in range(2)]; XDDs = [self.sb("B_XDD%d" % i, [128, 256], BF16) for i in range(2)]
    Y1 = self.sb("B_Y1", [128, 256]); T2 = self.sb("B_T2", [128, 256]); SZ = self.sb("B_SZ", [128, 256])
    w_in, w_out = I["ssm_w_in"], I["ssm_w_out"]
    XFSv = [XFS[:, q, :].rearrange("p (s t) -> p s t", t=11) for q in range(4)]

    def bc(ap, cs):
        return ap.unsqueeze(2).to_broadcast([cs, 4, 64])

    def v4(ap):
        return ap.rearrange("p (h q) -> p h q", q=64)

    def wv(c0, n):
        return w_in[ib, :, c0:c0 + n].rearrange("(kc p) n -> p kc n", p=128)

    def load(g):
        self.dma("pool", WZD[0][:, :, 0:256], wv(g * 256, 256), (), ["B_WZD"])
        self.dma("pool", WZD[0][:, :, 256:260], wv(6144 + 4 * g, 4), (), ["B_WZD"])

    def load_x(g):
        self.dma("pool", WXBC[0][:, :, 0:256], wv(2048 + g * 256, 256), (), ["B_WXBC"])
        self.dma("pool", WXBC[0][:, :, 256:384], wv(4096 + g * 128, 128), (), ["B_WXBC"])
        self.dma("pool", WXBC[0][:, :, 384:512], wv(5120 + g * 128, 128), (), ["B_WXBC"])

    def load_o(g):
        self.dma("pool", WOUT[0][:], w_out[ib, g * 256:(g + 1) * 256, :].rearrange("(h p) n -> p h n", p=128), (), ["B_WOUT"])

    def chunk_front(g, b, tc, cs, ucol, st, sample=False):
        hs = slice(4 * g, 4 * g + 4)
        tri, negm = (TRIS, NEGMS) if sample else (TRI, NEGM)
        XT, BT, SM, MT, XDT, XDD = XTs[st], BTs[st], SMs[st], MTs[st], XDTs[st], XDDs[st]
        DTV, DT_, DA, NACS, EACS, DEND, CD = (SM[:, 0:4], SM[:, 4:8], SM[:, 8:12], SM[:, 12:16], SM[:, 16:20], SM[:, 20:24], SM[:, 24:28])
        K = lambda nm: ("B_" + nm, st)
        pzd, kzd = self.PS[st], ("PS", st)
        pt, kt = self.PS[2], ("PS", 2)
        pa, ka = self.PS[3], ("PS", 3)
        pl, kl = self.PS[4], ("PS", 4)
        pc, kc_ = self.PS[5], ("PS", 5)
        for kc in range(8):
            self.mm(pzd[0:cs, 0:260], (USC[:, kc, :] if sample else self.U[:, kc, ucol:ucol + cs]), WZD[b][:, kc, :], kc == 0, kc == 7, ["B_WZD", "U", "B_USC"], [kzd])
        for q in range(3):
            self.tr(pt[0:cs, q * 128:(q + 1) * 128], XCf[:, q, tc:tc + cs], self.IDF[:], ["B_XCf", "IDF"], [kt])
        self.cp("act", XT[0:cs, :], pt[0:cs, 0:256], [kt], [K("XT")])
        self.cp("dve", BT[0:cs, :], pt[0:cs, 256:384], [kt], [K("BT")])
        self.mm(pc[0:cs, 0:cs], BCb[:, 0, tc:tc + cs], BCb[:, 1, tc:tc + cs], True, True, ["B_BCb"], [kc_])
        self.tt("dve", DTV[0:cs], pzd[0:cs, 256:260], DTB[0:cs, hs], ALU.add, [kzd, "B_DTB"], [K("DTV")])
        self.act(DT_[0:cs], DTV[0:cs], AF.Exp, [K("DTV")], [K("DT")])
        self.act(DT_[0:cs], DT_[0:cs], AF.Ln, [K("DT"), ("C1", 1)], [K("DT")], bias=self.C1[0:cs, 1:2], scale=1.0)
        self.tt("dve", DA[0:cs], DT_[0:cs], AB[0:cs, hs], ALU.mult, [K("DT"), "B_AB"], [K("DA")])
        self.mm(pa[0:cs, 0:4], tri[0:cs, 0:cs], DA[0:cs], True, True, ["B_TRI", "B_TRIS", K("DA")], [ka])
        self.mm(pa[:, 4:8], (BONES[:, :] if sample else ONESF[0:cs, :]), DA[0:cs], True, True, ["B_ONESF", "B_BONES", K("DA")], [ka])
        self.ts("dve", NACS[0:cs], pa[0:cs, 0:4], -1.0, None, ALU.mult, None, [ka], [K("NACS")])
        self.act(EACS[0:cs], pa[0:cs, 0:4], AF.Exp, [ka], [K("EACS")])
        self.tt("dve", DEND[0:cs], pa[0:cs, 4:8], NACS[0:cs], ALU.add, [ka, K("NACS")], [K("DEND")])
        self.act(DEND[0:cs], DEND[0:cs], AF.Exp, [K("DEND")], [K("DEND")])
        if not sample:
            self.act(CD, pa[:, 4:8], AF.Exp, [ka], [K("CD")])
        else:
            self.tt("dve", DAS[:].rearrange("p (s h) -> p s h", h=4), DA[:].unsqueeze(1).to_broadcast([128, 16, 4]),
                    SEQM[:].unsqueeze(2).to_broadcast([128, 16, 4]), ALU.mult, [K("DA"), "B_SEQM"], ["B_DAS"])
            pcd, kcd = self.PS[6], ("PS", 6)
            self.mm(pcd[:, 0:64], ONESF[:, :], DAS[:], True, True, ["B_ONESF", "B_DAS"], [kcd])
            self.act(CDS[:], pcd[:, 0:64], AF.Exp, [kcd], ["B_CDS"])
        for h in range(4):
            self.mm(pl[0:cs, h * 128:h * 128 + cs], DA[0:cs, h:h + 1].to_broadcast([cs, cs]), tri[0:cs, 0:cs], True, False, [K("DA"), "B_TRI", "B_TRIS"], [kl])
            self.mm(pl[0:cs, h * 128:h * 128 + cs], self.IDF[0:cs, 0:cs], negm[0:cs, 0:cs], False, True, ["IDF", "B_NEGM", "B_NEGMS"], [kl])
        for h in range(4):
            self.act(LT[0:cs, h, 0:cs], pl[0:cs, h * 128:h * 128 + cs], AF.Exp, [kl, K("NACS")], [("B_LT", h)], bias=NACS[0:cs, h:h + 1], scale=1.0)
        for h in range(4):
            self.tt("dve", MT[0:cs, h, 0:cs], LT[0:cs, h, 0:cs], pc[0:cs, 0:cs], ALU.mult, [kc_, ("B_LT", h)], [("B_MT%d" % st, h)])
        self.tt("dve", v4(XDT[0:cs, :]), v4(XT[0:cs, :]), bc(DT_[0:cs], cs), ALU.mult, [K("XT"), K("DT")], [K("XDT")])
        self.tt("dve", v4(XDD[0:cs, :]), v4(XDT[0:cs, :]), bc(DEND[0:cs], cs), ALU.mult, [K("XDT"), K("DEND")], [K("XDD")])

    def chunk_back(g, b, tc, cs, ucol, st, sample=False):
        hs = slice(4 * g, 4 * g + 4)
        XT, BT, SM, MT, XDT, XDD = XTs[st], BTs[st], SMs[st], MTs[st], XDTs[st], XDDs[st]
        EACS, CD, MS = SM[:, 16:20], SM[:, 24:28], SM[:, 28:29]
        K = lambda nm: ("B_" + nm, st)
        pzd, kzd = self.PS[st], ("PS", st)
        py, ky = self.PS[6], ("PS", 6)
        p7, k7 = self.PS[7], ("PS", 7)
        self.act(SZ[0:cs, :], pzd[0:cs, 0:256], AF.Exp, [kzd], ["B_SZ"], scale=-1.0)
        self.act(SZ[0:cs, :], SZ[0:cs, :], AF.Ln, ["B_SZ", ("C1", 1)], ["B_SZ"], bias=self.C1[0:cs, 1:2], scale=1.0)
        self.act(SZ[0:cs, :], SZ[0:cs, :], AF.Exp, ["B_SZ"], ["B_SZ"], scale=-1.0)
        self.tt("dve", SZ[0:cs, :], SZ[0:cs, :], pzd[0:cs, 0:256], ALU.mult, [kzd, "B_SZ"], ["B_SZ"])
        if sample:
            self.memset("dve", YO[:], 0.0, ["B_YO"])
            for sq in range(NSQ):
                state_in(g, sq)
                po, ko = self.bank()
                self.mm(po[:, 0:256], BCb[:, 1, 0:128], STb[:], True, True, ["B_BCb", "B_STb"], [ko])
                self.stt(YO[:], po[:, 0:256], SEQM[:, sq:sq + 1], YO[:], ALU.mult, ALU.add, [ko, "B_SEQM", "B_YO"], ["B_YO"])
                self.ts("dve", BTM[:], BT[:], SEQM[:, sq:sq + 1], None, ALU.mult, None, [K("BT"), "B_SEQM"], ["B_BTM"])
                pst, kst = self.bank()
                self.mm(pst[:, 0:256], BTM[:], XDD[:], True, True, ["B_BTM", K("XDD")], [kst])
                self.tt("dve", v4(ST[:]), v4(ST[:]), bc(CDS[:, 4 * sq:4 * sq + 4], 128), ALU.mult, ["B_ST", "B_CDS"], ["B_ST"])
                self.tt("dve", ST[:], ST[:], pst[:, 0:256], ALU.add, ["B_ST", kst], ["B_ST"])
                state_out(g, self.O["ss_s"][sq, 4 * g:4 * g + 4])
        for h in range(4):
            self.mm(py[0:cs, h * 64:(h + 1) * 64], MT[0:cs, h, 0:cs], XDT[0:cs, h * 64:(h + 1) * 64], True, True, [("B_MT%d" % st, h), K("XDT")], [ky])
        if not sample:
            self.mm(p7[0:cs, 0:256], BCb[:, 1, tc:tc + cs], STb[:], True, True, ["B_BCb", "B_STb"], [k7])
            self.tt("dve", v4(Y1[0:cs, :]), v4(p7[0:cs, 0:256]), bc(EACS[0:cs], cs), ALU.mult, [k7, K("EACS")], ["B_Y1"])
        else:
            self.tt("dve", v4(Y1[:]), v4(YO[:]), bc(EACS[:], 128), ALU.mult, ["B_YO", K("EACS")], ["B_Y1"])
        self.tt("dve", Y1[0:cs, :], Y1[0:cs, :], py[0:cs, 0:256], ALU.add, [ky, "B_Y1"], ["B_Y1"])
        self.tt("pool", v4(T2[0:cs, :]), v4(XT[0:cs, :]), bc(DB[0:cs, hs], cs), ALU.mult, [K("XT"), "B_DB"], ["B_T2"])
        self.tt("dve", Y1[0:cs, :], Y1[0:cs, :], T2[0:cs, :], ALU.add, ["B_T2", "B_Y1"], ["B_Y1"])
        if not sample:
            self.mm(p7[:, 256:512], BT[0:cs, :], XDD[0:cs, :], True, True, [K("BT"), K("XDD")], [k7])
            self.tt("dve", v4(ST[:]), v4(ST[:]), bc(CD, 128), ALU.mult, ["B_ST", K("CD")], ["B_ST"])
            self.tt("dve", ST[:], ST[:], p7[:, 256:512], ALU.add, ["B_ST", k7], ["B_ST"])
            self.cp("pool", STb[:], ST[:], ["B_ST"], ["B_STb"])
        self.tt("dve", Y1[0:cs, :], Y1[0:cs, :], SZ[0:cs, :], ALU.mult, ["B_SZ", "B_Y1"], ["B_Y1"])
        self.P.op("dve", lambda e: e.scalar_tensor_tensor(out=T2[0:cs, :], in0=Y1[0:cs, :], scalar=1.0, in1=Y1[0:cs, :], op0=ALU.mult,
                                                          op1=ALU.mult, accum_out=MS[0:cs]), ["B_Y1", "B_T2"], ["B_T2", K("MS")])
        self.act(MS[0:cs], MS[0:cs], AF.Ln, [K("MS"), ("C1", 2)], [K("MS")], scale=1.0 / 256, bias=self.C1[0:cs, 2:3])
        self.act(MS[0:cs], MS[0:cs], AF.Exp, [K("MS")], [K("MS")], scale=-0.5)
        self.ts("dve", Y1[0:cs, :], Y1[0:cs, :], MS[0:cs], None, ALU.mult, None, [K("MS"), "B_Y1"], ["B_Y1"])
        pg, kg = self.PS[7], ("PS", 7)
        for hf in range(2):
            self.tr(pg[:, hf * 128:hf * 128 + cs], Y1[0:cs, hf * 128:(hf + 1) * 128], self.IDF[0:cs, 0:cs], ["B_Y1", "IDF"], [kg])
        for hf in range(2):
            ch = 2 * g + hf
            self.ts("dve", YGT[:, hf, tc:tc + cs], pg[:, hf * 128:hf * 128 + cs], self.pf("ssm_norm_w", ch // 8, ch % 8), None, ALU.mult, None,
                    [kg, "PF"], ["B_YGT"])

    def state_in(g, s):
        self.dma("sp", SIN[:], I["st_ssm"][ib, s, 4 * g:4 * g + 4].rearrange("(a h) p n -> (h p) a n", a=2), (), ["B_SIN"])
        ps, pk = self.bank()
        for a in range(2):
            self.tr(ps[:, a * 128:(a + 1) * 128], SIN[:, a, :], self.IDF[:], ["B_SIN", "IDF"], [pk])
        self.cp("dve", ST[:], ps[:, 0:256], [pk], ["B_ST"])
        self.cp("act", STb[:], ps[:, 0:256], [pk], ["B_STb"])

    def state_out(g, dst):
        ps, pk = self.bank()
        for a in range(2):
            self.tr(ps[:, a * 128:(a + 1) * 128], ST[:, a * 128:(a + 1) * 128], self.IDF[:], ["B_ST", "IDF"], [pk])
        self.cp("dve", SOUT[:].rearrange("p a n -> p (a n)"), ps[:, 0:256], [pk], ["B_SOUT"])
        self.dma("sp", dst.rearrange("(a h) p n -> (h p) a n", a=2), SOUT[:], ["B_SOUT"], ())

    import os as _os
    NG = int(_os.environ.get('BDBG_G', '8')); TLIST = [int(c) for c in _os.environ.get('BDBG_T', '01234')]; NSQ = int(_os.environ.get('BDBG_S', '16'))
    load(0)
    load_x(0)
    load_o(0)
    for g in range(NG):
        b = g % 2
        chs = [2 * g, 2 * g + 1, 16 + g, 24 + g]
        for ti, (c0, w) in enumerate(TT):
            if ti not in TLIST:
                continue
            if ti == 0:
                self.memset("dve", XF[:, :, 0:3], 0.0, [("B_XF", "st")])
            elif ti < 4:
                self.cp("dve", XF[:, :, 0:3], XF[:, :, 512:515], ["B_XF"], [("B_XF", "st")])
            else:
                for q in range(4):
                    self.cp("dve", XFSv[q][:, :, 0:3], CS0[:, chs[q], :].rearrange("p (s k) -> p s k", k=3), ["B_CS0"], [("B_XFS", "st")])
            for q in range(4):
                ps, pk = self.bank()
                for kc in range(8):
                    self.mm(self.v3(ps[:, :w], ti), WXBC[b][:, kc, q * 128:(q + 1) * 128], self.uv(kc, ti), kc == 0, kc == 7,
                            ["B_WXBC", ("U", (kc, ti))], [pk])
                if ti < 4:
                    self.cp("act", XF[:, q, 3:515], ps[:, :], [pk], [("B_XF", q)])
                else:
                    self.cp("act", XFSv[q][:, :, 3:11], self.v3(ps[:, :w], ti), [pk], [("B_XFS", q)])
            if ti == TLIST[-1] and g + 1 < NG:
                load_x(g + 1)
            for q in range(4):
                ch = chs[q]
                cw = [self.pf("ssm_conv_w", k * 4 + ch // 8, ch % 8) for k in range(4)]
                cb = self.pf("ssm_conv_b", ch // 8, ch % 8)
                if ti < 4:
                    dst = XCf[:, q, :]
                    srcf = lambda k, q=q: XF[:, q, k:k + 512]
                    rk = "B_XF"
                else:
                    dst = XCf[:, q, 0:128].rearrange("p (s t) -> p s t", t=8)
                    srcf = lambda k, q=q: XFSv[q][:, :, k:k + 8]
                    rk = "B_XFS"
                self.ts("dve", dst, srcf(0), cw[0], cb, ALU.mult, ALU.add, [rk, "PF"], [("B_XCf", q)])
                for k in range(1, 4):
                    self.stt(dst, srcf(k), cw[k], dst, ALU.mult, ALU.add, [rk, ("B_XCf", q), "PF"], [("B_XCf", q)])
                self.act(XCf[:, q, :w], XCf[:, q, :w], AF.Silu, [("B_XCf", q)], [("B_XCf", q)])
                if q >= 2:
                    self.cp("pool", BCb[:, q - 2, :w], XCf[:, q, :w], [("B_XCf", q)], ["B_BCb"])
            if ti == 3:
                for q in range(4):
                    self.cp("pool", CST[:, chs[q], 0:3], XF[:, q, 512:515], ["B_XF"], ["B_CST"])
            if ti == 4:
                for q in range(4):
                    self.cp("pool", CST[:, chs[q], 3:].rearrange("p (s k) -> p s k", k=3), XFSv[q][:, :, 8:11], ["B_XFS"], ["B_CST"])
            if ti < 4:
                if ti == 0:
                    self.memset("dve", ST[:], 0.0, ["B_ST"])
                    self.memset("pool", STb[:], 0.0, ["B_STb"])
                args = [(g, b, ck * 128, 128, 1 + c0 + ck * 128, ck % 2) for ck in range(4)]
                chunk_front(*args[0])
                for ck in range(4):
                    if ck + 1 < 4:
                        chunk_front(*args[ck + 1])
                    chunk_back(*args[ck])
                if ti == 3:
                    state_out(g, self.O["ss_p"][4 * g:4 * g + 4])
            else:
                chunk_front(g, b, 0, 128, 0, 0, sample=True)
                chunk_back(g, b, 0, 128, 0, 0, sample=True)
            for n in range(8):
                ps, pk = self.bank()
                for hf in range(2):
                    self.mm(ps[:, :w], WOUT[b][:, hf, n * 128:(n + 1) * 128], YGT[:, hf, :w], hf == 0, hf == 1, ["B_WOUT", "B_YGT"], [pk])
                self.tt("dve", self.X[:, n, c0:c0 + w], self.X[:, n, c0:c0 + w], ps[:, :w], ALU.add, [pk, ("X", (n, ti))], [("X", (n, ti))])
        if g + 1 < NG:
            load_o(g + 1)
            load(g + 1)
    pass
    for r in range(4):
        self.fm_to_rows(CST[:, r * 8:(r + 1) * 8, 0:3], "B_CST", 3, self.O["sc_p"][:, r * D:(r + 1) * D])
        self.fm_to_rows(CST[:, r * 8:(r + 1) * 8, 3:], "B_CST", NS * 3, self.O["sc_s"].rearrange("s k d -> (s k) d")[:, r * D:(r + 1) * D])


KB.layer_B = layer_B


def layer_C(self, l, ic):
    I, nc = self.I, self.nc
    E05 = float(np.exp(-0.5))
    SH0 = self.sb("C_SH0", [128, 8, NS]); USH = self.sb("C_USH", [128, 8, 1 + NS])
    self.rows_to_fm(I["st_rw_shift"][ic], NS, SH0, "C_SH0")
    Us = self.U[:, :, 1 + NTP:UC].rearrange("p c (s t) -> p c s t", t=9)
    self.cp("dve", Us[:, :, :, 0], SH0[:], ["C_SH0"], [("U", "shifts")])
    self.norm_to_U("norm_mix", l, shift_out=USH)
    XD = nc.dram_tensor("c_xspill", [128, 8, NT], F32).ap()
    import os as _os2
    CHUNKED = bool(int(_os2.environ.get("C_CHUNKED", "1")))
    HM = () if CHUNKED else ("kk", "w", "nq", "k", "r")
    SCR = {k: nc.dram_tensor("c_scr_" + k, [NT, D], F32).ap() for k in ("kk", "w", "nq", "k", "r", "v", "g", "o") if k not in HM}
    SCRH = {k: nc.dram_tensor("c_scrh_" + k, [16, NT, 64], F32).ap() for k in HM}

    def scr_tm(nm, rows):
        if nm in SCRH:
            return SCRH[nm][:, rows, :].rearrange("h t k -> t h k")
        return SCR[nm][rows, :].rearrange("t (h k) -> t h k", k=64)

    self.P.barrier()
    self.dma("sp", XD, self.X[:], ["X"], ["XD"])
    self.P.barrier()

    def xa(i):
        return self.X[:, i // 2, (i % 2) * D:(i % 2 + 1) * D], ("X", "a%d" % i)

    def h16(ap):
        return ap.rearrange("p (h k) -> p h k", k=64)

    def b16(ap):
        return ap.unsqueeze(2).to_broadcast([128, 16, 64])

    def red(out, in_, reads, writes):
        self.P.op("dve", lambda e: e.tensor_reduce(out=out, in_=in_, axis=AX.X, op=ALU.add), reads, writes)

    def pbc(i, src):
        a, k = xa(i)
        self.dma("sp", a, src.partition_broadcast(128), (), [k])

    with self.scope():
        pbc(0, I["rwkv_w0"][ic:ic + 1, :]); pbc(1, I["rwkv_a0"][ic:ic + 1, :]); pbc(2, I["rwkv_k_k"][ic:ic + 1, :]); pbc(3, I["rwkv_k_a"][ic:ic + 1, :])
        WS = [self.sb("C_WS%d" % i, [128, 8, D], BF16) for i in range(3)]
        for s_ in range(3):
            self.dma("pool", WS[s_][:], I["rwkv_w_rkv"][ic, s_].rearrange("(kc p) n -> p kc n", p=128), (), [("C_WS", s_)])
        W1 = self.sb("C_W1", [128, 8, 256], BF16)
        W2w = self.sb("C_W2w", [64, D], BF16); W2a = self.sb("C_W2a", [64, D], BF16); W2g = self.sb("C_W2g", [128, D], BF16)
        for (c0, n, nm) in ((0, 64, "rwkv_w_w1"), (64, 64, "rwkv_w_a1"), (128, 128, "rwkv_w_g1")):
            self.dma("pool", W1[:, :, c0:c0 + n], I[nm][ic].rearrange("(kc p) n -> p kc n", p=128), (), ["C_W1"])
        self.dma("pool", W2w[:], I["rwkv_w_w2"][ic], (), ["C_W2w"]); self.dma("pool", W2a[:], I["rwkv_w_a2"][ic], (), ["C_W2a"])
        self.dma("pool", W2g[:], I["rwkv_w_g2"][ic], (), ["C_W2g"])
        Dd = self.sb("C_D", [128, 8, 128]); XM = [self.sb("C_XM%d" % i, [128, 8, 128], BF16) for i in range(2)]
        TW = self.sb("C_TW", [64, 128], BF16); TA = self.sb("C_TA", [64, 128], BF16); TG = self.sb("C_TG", [128, 128], BF16)
        SS = self.sb("C_SS", [128, 16])
        (Pw0, kw0), (Pa0, ka0), (Pkk, kkk), (Pka, kka) = xa(0), xa(1), xa(2), xa(3)
        (Rt, kR), (Kt, kK), (Vt, kV), (Wt, kW), (At, kA), (KKt, kKK), (Gt, kG), (T1, kT1), (T2, kT2) = [xa(i) for i in range(7, 16)]
        wsn = 0
        for ti in range(17):
            tok0 = 128 * ti
            if ti < 16:
                cur = self.U[:, :, 1 + tok0:1 + tok0 + 128]; prev = self.U[:, :, tok0:tok0 + 128]
                dv = lambda a: a
                ukeys = [("U", (c, ti // 4)) for c in range(8)]
            else:
                cur = Us[:, :, :, 1:9]; prev = Us[:, :, :, 0:8]
                dv = lambda a: a.rearrange("p c (s t) -> p c s t", t=8) if len(a.shape) == 3 else a.rearrange("p (s t) -> p s t", t=8)
                ukeys = [("U", (c, 4)) for c in range(8)] + [("U", "shifts")]
            if ti % 4 == 0 and ti > 0 and ti < 16:
                ukeys = ukeys + [("U", (c, ti // 4 - 1)) for c in range(8)]
            if ti == 0:
                ukeys = ukeys + [("U", "shiftp")]
            self.tt("dve", dv(Dd[:]), prev, cur, ALU.subtract, ukeys, ["C_D"])

            def mix(s):
                xm = XM[s % 2]; key = "C_XM%d" % (s % 2)
                for kc in range(8):
                    self.stt(dv(xm[:, kc, :]), dv(Dd[:, kc, :]), self.pf("rwkv_mu", s, kc), cur[:, kc], ALU.mult, ALU.add, ["C_D", "PF"] + ukeys, [key])
                return xm, key

            def proj_tm(s, dst, dkey):
                b = s
                xm, key = mix(s)
                for nb in range(2):
                    ps, pk = self.bank()
                    for kc in range(8):
                        self.mm(ps[:, :], xm[:, kc, :], WS[b][:, kc, nb * 512:(nb + 1) * 512], kc == 0, kc == 7, [key, ("C_WS", b)], [pk])
                    self.cp("act", dst[:, nb * 512:(nb + 1) * 512], ps[:, :], [pk], [dkey])

            proj_tm(0, Rt, kR); proj_tm(1, Kt, kK); proj_tm(2, Vt, kV)
            for (s, c0, n, fn, dst, dk) in ((3, 0, 64, AF.Tanh, TW, "C_TW"), (4, 64, 64, AF.Copy, TA, "C_TA"), (5, 128, 128, AF.Sigmoid, TG, "C_TG")):
                xm, key = mix(s)
                ps, pk = self.bank()
                for kc in range(8):
                    self.mm(ps[0:n, 0:128], W1[:, kc, c0:c0 + n], xm[:, kc, :], kc == 0, kc == 7, [key, "C_W1"], [pk])
                self.act(dst[0:n, :], ps[0:n, 0:128], fn, [pk], [dk])
            for nb in range(2):
                cs_ = slice(nb * 512, (nb + 1) * 512)
                ps, pk = self.bank()
                self.mm(ps[:, :], TW[0:64, :], W2w[0:64, cs_], True, True, ["C_TW", "C_W2w"], [pk])
                self.tt("dve", T1[:, cs_], ps[:, :], Pw0[:, cs_], ALU.add, [pk, kw0], [kT1])
                ps, pk = self.bank()
                self.mm(ps[:, :], TA[0:64, :], W2a[0:64, cs_], True, True, ["C_TA", "C_W2a"], [pk])
                self.tt("dve", At[:, cs_], ps[:, :], Pa0[:, cs_], ALU.add, [pk, ka0], [kA])
                ps, pk = self.bank()
                self.mm(ps[:, :], TG[:, :], W2g[:, cs_], True, True, ["C_TG", "C_W2g"], [pk])
                self.cp("act", Gt[:, cs_], ps[:, :], [pk], [kG])
            self.act(T1, T1, AF.Sigmoid, [kT1], [kT1])
            self.act(Wt, T1, AF.Exp, [kT1], [kW], scale=-E05)
            self.act(At, At, AF.Sigmoid, [kA], [kA])
            self.tt("dve", KKt, Kt, Pkk, ALU.mult, [kK, kkk], [kKK])
            self.tt("dve", T1, KKt, KKt, ALU.mult, [kKK], [kT1])
            red(SS[:], h16(T1), [kT1], ["C_SS"])
            self.act(SS[:], SS[:], AF.Sqrt, ["C_SS"], ["C_SS"])
            self.ts("dve", SS[:], SS[:], 1e-12, None, ALU.max, None, ["C_SS"], ["C_SS"])
            self.P.op("dve", lambda e, SS=SS: e.reciprocal(out=SS[:], in_=SS[:]), ["C_SS"], ["C_SS"])
            self.tt("dve", h16(KKt), h16(KKt), b16(SS[:]), ALU.mult, [kKK, "C_SS"], [kKK])
            self.stt(T1, At, -1.0, Pka, ALU.add, ALU.mult, [kA, kka], [kT1])
            self.ts("dve", T1, T1, 1.0, None, ALU.add, None, [kT1], [kT1])
            self.tt("dve", Kt, Kt, T1, ALU.mult, [kK, kT1], [kK])
            self.stt(T2, KKt, -1.0, At, ALU.mult, ALU.mult, [kKK, kA], [kT2])
            rows = slice(tok0, tok0 + 128)
            for (nm, a, k) in (("kk", KKt, kKK), ("w", Wt, kW), ("nq", T2, kT2), ("k", Kt, kK), ("r", Rt, kR), ("v", Vt, kV), ("g", Gt, kG)):
                self.dma("sp", scr_tm(nm, rows), h16(a), [k], [("SCR", nm)])

    if self.debug:
        for i_, nm_ in enumerate(("kk", "w", "nq", "k", "r", "v", "g")):
            self.dma("sp", self.O["dbg_scr"][i_].rearrange("t (h k) -> t h k", k=64), scr_tm(nm_, slice(0, NT)), [("SCR", nm_)], ())

    def phase2_chunked():
        LNE = float(np.log(1.0))
        f32 = F32
        def blockmask(ap3, blk):
            self.P.op("pool", lambda e: e.affine_select(out=ap3, in_=ap3, pattern=[[-blk, 128 // blk], [0, blk]], compare_op=ALU.is_ge, fill=0.0,
                                                        base=0, channel_multiplier=1), ["C_MK"], ["C_MK"])
            self.P.op("pool", lambda e: e.affine_select(out=ap3, in_=ap3, pattern=[[blk, 128 // blk], [0, blk]], compare_op=ALU.is_ge, fill=0.0,
                                                        base=blk - 1, channel_multiplier=-1), ["C_MK"], ["C_MK"])
        MK = {}
        for blk in (32, 8):
            TRIc = self.sb("C_TRIc%d" % blk, [128, 128]); LOWs = self.sb("C_LOWs%d" % blk, [128, 128]); M3 = self.sb("C_M3_%d" % blk, [128, 3, 128])
            MSs = self.sb("C_MSs%d" % blk, [128, 128])
            self.memset("pool", TRIc[:], 1.0, ["C_MK"])
            self.P.op("pool", lambda e, T=TRIc: e.affine_select(out=T[:], in_=T[:], pattern=[[1, 128]], compare_op=ALU.is_ge, fill=0.0, base=0,
                                                                channel_multiplier=-1), ["C_MK"], ["C_MK"])
            blockmask(TRIc[:].rearrange("p (b t) -> p b t", t=blk), blk)
            self.memset("pool", LOWs[:], 1.0, ["C_MK"])
            self.P.op("pool", lambda e, T=LOWs: e.affine_select(out=T[:], in_=T[:], pattern=[[-1, 128]], compare_op=ALU.is_ge, fill=0.0, base=-1,
                                                                channel_multiplier=1), ["C_MK"], ["C_MK"])
            blockmask(LOWs[:].rearrange("p (b t) -> p b t", t=blk), blk)
            self.tt("pool", MSs[:], TRIc[:], self.IDF[:], ALU.subtract, ["C_MK", "IDF"], ["C_MK"])
            self.cp("pool", M3[:, 0, :], TRIc[:], ["C_MK"], ["C_MK"])
            self.cp("pool", M3[:, 1, :], MSs[:], ["C_MK"], ["C_MK"])
            self.cp("pool", M3[:, 2, :], TRIc[:], ["C_MK"], ["C_MK"])
            MK[blk] = (TRIc, LOWs, MSs, M3)
        SEQM = self.sb("C_SEQM", [128, 16])
        self.memset("pool", SEQM[:], 1.0, ["C_MK"])
        self.P.op("pool", lambda e: e.affine_select(out=SEQM[:], in_=SEQM[:], pattern=[[-8, 16]], compare_op=ALU.is_ge, fill=0.0, base=0,
                                                    channel_multiplier=1), ["C_MK"], ["C_MK"])
        self.P.op("pool", lambda e: e.affine_select(out=SEQM[:], in_=SEQM[:], pattern=[[8, 16]], compare_op=ALU.is_ge, fill=0.0, base=7,
                                                    channel_multiplier=-1), ["C_MK"], ["C_MK"])
        CHM = self.sb("C_CHM", [128, 4])
        self.memset("pool", CHM[:], 1.0, ["C_MK"])
        self.P.op("pool", lambda e: e.affine_select(out=CHM[:], in_=CHM[:], pattern=[[-32, 4]], compare_op=ALU.is_ge, fill=0.0, base=0,
                                                    channel_multiplier=1), ["C_MK"], ["C_MK"])
        self.P.op("pool", lambda e: e.affine_select(out=CHM[:], in_=CHM[:], pattern=[[32, 4]], compare_op=ALU.is_ge, fill=0.0, base=31,
                                                    channel_multiplier=-1), ["C_MK"], ["C_MK"])
        PRT = self.sb("C_PRT", [128, 8, 2, 128], BF16); QT = self.sb("C_QT", [128, 8, 128], BF16); KT = self.sb("C_KT", [128, 8, 128], BF16)
        QZ = self.sb("C_QZ", [128, 16, 128], BF16); KZ = self.sb("C_KZ", [128, 16, 128], BF16)
        Vb = self.sb("C_Vb", [128, D], BF16); Vm = self.sb("C_Vm", [128, D], BF16)
        ECLE = self.sb("C_ECLE", [128, 8, 16])
        XN = self.sb("C_XN", [128, 16, 128], BF16); ZN = self.sb("C_ZN", [128, 16, 128], BF16); IDB_ = self.IDB
        MB = self.sb("C_MB", [128, 16, 3, 128], BF16); TTb = self.sb("C_TTb", [128, 16, 128], BF16)
        Wsb = self.sb("C_Wsb", [128, D]); O0 = self.sb("C_O0", [128, D])
        GWb = self.sb("C_GWb", [128, D], BF16); Ub = self.sb("C_Ub", [128, D], BF16)
        S = self.sb("C_S2", [128, 8, 64]); Sb = self.sb("C_Sb", [128, 16, 64], BF16)
        SbV = Sb[:].rearrange("p (j e) v -> p j e v", e=2)

        def write_sb(src3, keys):
            self.cp("act", SbV[0:64, :, 0, :], src3[0:64], keys, ["C_Sb"])
            self.cp("act", SbV[64:128, :, 1, :], src3[64:128], keys, ["C_Sb"])

        NAT = self.sb("C_NAT", [64, 16, 64])
        self.memset("pool", QZ[:], 0.0, ["C_QZ"]); self.memset("pool", KZ[:], 0.0, ["C_KZ"])
        INSETS = [[xa(i) for i in range(0, 6)], [xa(i) for i in range(6, 12)]]
        (CL, kCL), (E1, kE1), (Ot, kO) = [xa(i) for i in range(12, 15)]

        def state_load(dram):
            self.dma("sp", NAT[:], dram.rearrange("h v k -> v h k"), (), ["C_NAT"])
            ps, pk = self.bank()
            for j in range(8):
                self.tr(ps[:, j * 64:(j + 1) * 64], NAT[:, 2 * j:2 * j + 2, :].rearrange("v h k -> v (h k)"), self.IDF[0:64, 0:64], ["C_NAT", "IDF"], [pk])
            self.cp("dve", S[:].rearrange("p j v -> p (j v)"), ps[:, :], [pk], ["C_S2"])
            write_sb(ps[:, :].rearrange("p (j v) -> p j v", v=64), [pk])

        def state_store(dram):
            for half in range(2):
                ps, pk = self.bank()
                for jj in range(4):
                    j = half * 4 + jj
                    self.tr(ps[0:64, jj * 128:(jj + 1) * 128], S[:, j, :], self.IDF[:], ["C_S2", "IDF"], [pk])
                self.cp("dve", NAT[:, half * 8:(half + 1) * 8, :].rearrange("v h k -> v (h k)"), ps[0:64, :], [pk], ["C_NAT"])
            self.dma("act", dram.rearrange("h v k -> v h k"), NAT[:], ["C_NAT"], ())

        def load_inputs(ti):
            rows = slice(128 * ti, 128 * ti + 128)
            for nm, (a, k) in zip(("kk", "nq", "k", "r", "v", "w"), INSETS[ti % 2]):
                self.dma("sp", h16(a), scr_tm(nm, rows), [("SCR", nm)], [k])

        def tile(ti, blk, seqs):
            TRIc, LOWs, MSs, M3 = MK[blk]
            STOP = int(_os2.environ.get('C2_STOP', '99'))
            nch = 128 // blk
            rows = slice(128 * ti, 128 * ti + 128)
            (KKt, kKK), (NQt, kNQ), (Kt, kK), (Rt, kR), (Vt, kV), (LW, kLW) = INSETS[ti % 2]
            if ti == 0:
                load_inputs(0)
            if ti + 1 < 17:
                load_inputs(ti + 1)
            self.act(LW, LW, AF.Ln, [kLW], [kLW])
            for nb in range(2):
                ps, pk = self.bank()
                self.mm(ps[:, :], TRIc[:], LW[:, nb * 512:(nb + 1) * 512], True, True, ["C_MK", kLW], [pk])
                self.cp("act", CL[:, nb * 512:(nb + 1) * 512], ps[:, :], [pk], [kCL])
            self.tt("dve", E1, CL, LW, ALU.subtract, [kCL, kLW], [kE1])
            self.act(E1, E1, AF.Exp, [kE1], [kE1])
            self.tt("dve", KKt, KKt, E1, ALU.mult, [kKK, kE1], [kKK])
            self.act(E1, CL, AF.Exp, [kCL], [kE1], scale=-1.0)
            self.tt("dve", NQt, NQt, E1, ALU.mult, [kNQ, kE1], [kNQ])
            self.tt("dve", Kt, Kt, E1, ALU.mult, [kK, kE1], [kK])
            self.act(E1, CL, AF.Exp, [kCL], [kE1])
            self.tt("dve", Rt, Rt, E1, ALU.mult, [kR, kE1], [kR])
            self.cp("pool", Vb[:], Vt, [kV], ["C_Vb"])
            for (Zb, src, k, zk) in ((QZ, NQt, kNQ, "C_QZ"), (KZ, Kt, kK, "C_KZ")):
                zv = Zb[:].rearrange("p (j e) f -> p j (e f)", e=2)
                sv = src.rearrange("p (j e d) -> p j e d", e=2, d=64)
                self.cp("pool", zv[:, :, 0:64], sv[:, :, 0, :], [k], [zk])
                self.cp("pool", zv[:, :, 192:256], sv[:, :, 1, :], [k], [zk])
            if STOP <= 1:
                return
            for (src, k, dstf, dk) in ((KKt, kKK, lambda j: PRT[:, j, 0, :], "C_PRT"), (Rt, kR, lambda j: PRT[:, j, 1, :], "C_PRT"),
                                       (NQt, kNQ, lambda j: QT[:, j, :], "C_QT"), (Kt, kK, lambda j: KT[:, j, :], "C_KT")):
                for half in range(2):
                    ps, pk = self.bank()
                    for jj in range(4):
                        j = half * 4 + jj
                        self.tr(ps[:, jj * 128:(jj + 1) * 128], src[:, j * 128:(j + 1) * 128], self.IDF[:], [k, "IDF"], [pk])
                    for jj in range(4):
                        j = half * 4 + jj
                        self.cp("act" if jj % 2 else "dve", dstf(j), ps[:, jj * 128:(jj + 1) * 128], [pk], [(dk, j)])
            for half in range(2):
                ps, pk = self.bank()
                for jj in range(4):
                    j = half * 4 + jj
                    self.tr(ps[:, jj * 128:(jj + 1) * 128], E1[:, j * 128:(j + 1) * 128], self.IDF[:], [kE1, "IDF"], [pk])
                self.cp("dve", ECLE[:, half * 4:(half + 1) * 4, 0:nch], ps[:, :].rearrange("p (a c b) -> p a c b", a=4, b=blk)[:, :, :, blk - 1], [pk], ["C_ECLE"])
            if STOP <= 2:
                return
            for h in range(16):
                j, e = h // 2, h % 2
                pr = slice(e * 64, (e + 1) * 64)
                ps, pk = self.bank()
                rhsPR = PRT[pr, j, :, :].rearrange("p a t -> p (a t)")
                self.mm(ps[:, 0:256], QT[pr, j, :], rhsPR, True, True, [("C_QT", j), ("C_PRT", j)], [pk])
                self.mm(ps[:, 256:512], KT[pr, j, :], rhsPR, True, True, [("C_KT", j), ("C_PRT", j)], [pk])
                ps2, pk2 = self.bank()
                self.mm(ps2[:, 0:128], PRT[pr, j, 0, :], QT[pr, j, :], True, True, [("C_QT", j), ("C_PRT", j)], [pk2])
                self.tt("dve", ZN[:, h, :], ps[:, 0:128], MSs[:], ALU.mult, [pk, "C_MK"], [("C_ZN", h)])
                self.tt("dve", MB[:, h, :, :], ps[:, 128:512].rearrange("p (a t) -> p a t", a=3), M3[:], ALU.mult, [pk, "C_MK"], [("C_MB", h)])
                self.tt("dve", XN[:, h, :], ps2[:, 0:128], LOWs[:], ALU.mult, [pk2, "C_MK"], [("C_XN", h)])
                self.tt("pool", TTb[:, h, :], ZN[:, h, :], IDB_[:], ALU.add, [("C_ZN", h), "IDB"], [("C_TTb", h)])
            for n in range(4 if STOP > 3 else 0):
                for h in range(16):
                    ps, pk = self.bank()
                    self.mm(ps[:, 0:128], ZN[:, h, :], XN[:, h, :], True, True, [("C_ZN", h), ("C_XN", h)], [pk])
                    if n < 3:
                        self.mm(ps[:, 128:256], XN[:, h, :], ZN[:, h, :], True, True, [("C_ZN", h), ("C_XN", h)], [pk])
                    self.cp("act", XN[:, h, :], ps[:, 0:128], [pk], [("C_XN", h)])
                    if n < 3:
                        self.cp("act", ZN[:, h, :], ps[:, 128:256], [pk], [("C_ZN", h)])
                    ps2, pk2 = self.bank()
                    self.mm(ps2[:, 0:128], XN[:, h, :], TTb[:, h, :], True, True, [("C_XN", h), ("C_TTb", h)], [pk2])
                    self.tt("dve", TTb[:, h, :], TTb[:, h, :], ps2[:, 0:128], ALU.add, [pk2, ("C_TTb", h)], [("C_TTb", h)])
            if STOP <= 4:
                return
            for (ai, dst, dk) in ((1, Wsb, "C_Wsb"), (2, O0, "C_O0")):
                for half in range(2):
                    ps, pk = self.bank()
                    for hh in range(8):
                        h = half * 8 + hh
                        self.mm(ps[:, hh * 64:(hh + 1) * 64], MB[:, h, ai, :], Vb[:, h * 64:(h + 1) * 64], True, True, [("C_MB", h), "C_Vb"], [pk])
                    self.cp("act" if half else "dve", dst[:, half * 512:(half + 1) * 512], ps[:, :], [pk], [dk])
            self.cp("pool", Ot, O0[:], ["C_O0"], [kO])
            if STOP <= 5:
                return
            nrounds = nch if seqs is None else len(seqs)
            for c_ in range(nrounds):
                if seqs is None:
                    ecol = c_; mcol = CHM[:, c_:c_ + 1]
                else:
                    s_ = seqs[c_]
                    ecol = s_; mcol = SEQM[:, s_:s_ + 1]
                    state_load(I["st_rw_wkv"][ic, s_])
                self.ts("pool", Vm[:], Vb[:], mcol, None, ALU.mult, None, ["C_Vb", "C_MK"], ["C_Vm"])
                pG = [self.bank() for _ in range(2)]; pR = [self.bank() for _ in range(2)]
                for h in range(16):
                    j = h // 2
                    bq, cq = h // 8, (h % 8) * 64
                    self.mm(pG[bq][0][:, cq:cq + 64], PRT[:, j, 0, :], Sb[:, h, :], True, True, [("C_PRT", j), "C_Sb"], [pG[bq][1]])
                for h in range(16):
                    j = h // 2
                    bq, cq = h // 8, (h % 8) * 64
                    self.mm(pR[bq][0][:, cq:cq + 64], PRT[:, j, 1, :], Sb[:, h, :], True, True, [("C_PRT", j), "C_Sb"], [pR[bq][1]])
                for bq in range(2):
                    cs_ = slice(bq * 512, (bq + 1) * 512)
                    self.tt("dve", GWb[:, cs_], pG[bq][0][:, :], Wsb[:, cs_], ALU.add, [pG[bq][1], "C_Wsb"], [("C_GWb", bq)])
                pU = [self.bank() for _ in range(2)]
                for h in range(16):
                    bq, cq = h // 8, (h % 8) * 64
                    self.mm(pU[bq][0][:, cq:cq + 64], TTb[:, h, :], GWb[:, h * 64:(h + 1) * 64], True, True, [("C_TTb", h), ("C_GWb", bq)], [pU[bq][1]])
                for bq in range(2):
                    cs_ = slice(bq * 512, (bq + 1) * 512)
                    self.act(Ub[:, cs_], pU[bq][0][:, :], AF.Copy, [pU[bq][1], "C_MK"], [("C_Ub", bq)], scale=mcol)
                    self.stt(Ot[:, cs_], pR[bq][0][:, :], mcol, Ot[:, cs_], ALU.mult, ALU.add, [pR[bq][1], kO, "C_MK"], [kO])
                pS, kS = self.bank()
                for j in range(8):
                    for e in range(2):
                        h = 2 * j + e
                        self.mm(pS[:, j * 64:(j + 1) * 64], KZ[:, h, :], Vm[:, h * 64:(h + 1) * 64], e == 0, False, ["C_KZ", "C_Vm"], [kS])
                    for e in range(2):
                        h = 2 * j + e
                        self.mm(pS[:, j * 64:(j + 1) * 64], QZ[:, h, :], Ub[:, h * 64:(h + 1) * 64], False, e == 1, ["C_QZ", ("C_Ub", h // 8)], [kS])
                pB = [self.bank() for _ in range(2)]
                for h in range(16):
                    bq, cq = h // 8, (h % 8) * 64
                    self.mm(pB[bq][0][:, cq:cq + 64], MB[:, h, 0, :], Ub[:, h * 64:(h + 1) * 64], True, True, [("C_MB", h), ("C_Ub", bq)], [pB[bq][1]])
                Sf = S[:].rearrange("p j v -> p (j v)")
                self.tt("dve", Sf, Sf, pS[:, :], ALU.add, [kS, "C_S2"], ["C_S2"])
                self.tt("dve", S[:], S[:], ECLE[:, :, ecol:ecol + 1].to_broadcast([128, 8, 64]), ALU.mult, ["C_S2", "C_ECLE"], ["C_S2"])
                write_sb(S, ["C_S2"])
                for bq in range(2):
                    cs_ = slice(bq * 512, (bq + 1) * 512)
                    self.stt(Ot[:, cs_], pB[bq][0][:, :], mcol, Ot[:, cs_], ALU.mult, ALU.add, [pB[bq][1], kO, "C_MK"], [kO])
                if seqs is not None:
                    state_store(self.O["rw_s"][s_])
            self.dma("act", scr_tm("o", rows), h16(Ot), [kO], [("SCR", "o")])

        self.memset("dve", S[:], 0.0, ["C_S2"]); self.memset("pool", Sb[:], 0.0, ["C_Sb"])
        for ti in range(16):
            tile(ti, 32, None)
        state_store(self.O["rw_p"])
        tile(16, 8, list(range(NS)))

    if CHUNKED:
        with self.scope():
            phase2_chunked()
    else:
        with self.scope():
            S = self.sb("C_S", [128, 8, 64]); Tm = self.sb("C_Tm", [128, 8, 64]); KV = [self.sb("C_KV%d" % i, [128, 8, 64]) for i in range(2)]
            SA = self.sb("C_SA", [128, 8])
            names = ("kk", "w", "nq", "k", "r")
            ST0 = {nm: self.X[:, i, 0:2048].rearrange("p (t k) -> p t k", k=64) for i, nm in enumerate(names)}
            ST1 = {nm: self.sb("C_ST1" + nm, [128, 32, 64]) for nm in names}
            VS = [self.sb("C_VS%d" % i, [128, 32, 8]) for i in range(2)]
            OS = [self.sb("C_OS%d" % i, [128, 32, 8]) for i in range(2)]

            def bk(ap):
                return ap.unsqueeze(1).to_broadcast([128, 8, 64])

            def bv(ap):
                return ap.unsqueeze(2).to_broadcast([128, 8, 64])

            nsub = 0
            import os as _os
            NOSAME = bool(int(_os.environ.get("C_NOSAME", "1")))
            SP = self.PS[7][:, :].rearrange("p (a k) -> p a k", k=64)
            SK = ("PS", 7)

            def run_segment(row0, nsteps):
                nonlocal nsub
                for t0 in range(0, nsteps, 32):
                    n = min(32, nsteps - t0)
                    sb_ = nsub % 2; nsub += 1
                    stg = ST0 if sb_ == 0 else ST1
                    skey = "C_STG%d" % sb_
                    r0 = row0 + t0
                    for nm in names:
                        src = SCRH[nm][:, r0:r0 + n, :]
                        for vb in range(8):
                            self.dma("sp", stg[nm][vb * 16:(vb + 1) * 16, 0:n, :], src, [("SCR", nm)], [(skey, nm)] + ([("X", "stg")] if sb_ == 0 else []))
                    srcv = SCR["v"][r0:r0 + n, :].rearrange("t (h v) -> h t v", v=64)
                    for vb in range(8):
                        self.dma("sp", VS[sb_][vb * 16:(vb + 1) * 16, 0:n, :], srcv[:, :, vb * 8:(vb + 1) * 8], [("SCR", "v")], [(skey, "v")])
                    xk = [("X", "stg")] if sb_ == 0 else []
                    for i in range(n):
                        kvb = i % 2
                        self.tt("pool", KV[kvb][:], bk(stg["k"][:, i, :]), bv(VS[sb_][:, i, :]), ALU.mult, [(skey, "k"), (skey, "v")] + xk, ["C_KV%d" % kvb])
                        self.P.nosame = NOSAME
                        self.tt("dve", Tm[:], SP[:], bk(stg["kk"][:, i, :]), ALU.mult, [SK, (skey, "kk")] + xk, ["C_Tm"])
                        red(SA[:], Tm[:], ["C_Tm"], ["C_SA"])
                        self.tt("dve", SP[:], SP[:], bk(stg["w"][:, i, :]), ALU.mult, [SK, (skey, "w")] + xk, [SK])
                        self.tt("dve", Tm[:], bk(stg["nq"][:, i, :]), bv(SA[:]), ALU.mult, ["C_SA", (skey, "nq")] + xk, ["C_Tm"])
                        self.tt("dve", SP[:], SP[:], Tm[:], ALU.add, [SK, "C_Tm"], [SK])
                        self.tt("dve", SP[:], SP[:], KV[kvb][:], ALU.add, [SK, "C_KV%d" % kvb], [SK])
                        self.tt("dve", Tm[:], SP[:], bk(stg["r"][:, i, :]), ALU.mult, [SK, (skey, "r")] + xk, ["C_Tm"])
                        red(OS[sb_][:, i, :], Tm[:], ["C_Tm"], ["C_OS%d" % sb_])
                        self.P.nosame = False
                    dsto = SCR["o"][r0:r0 + n, :].rearrange("t (h v) -> h t v", v=64)
                    for vb in range(8):
                        self.dma("act", dsto[:, :, vb * 8:(vb + 1) * 8], OS[sb_][vb * 16:(vb + 1) * 16, 0:n, :], ["C_OS%d" % sb_], [("SCR", "o")])

            def state_io(dram, load):
                for vb in range(8):
                    d = dram[:, vb * 8:(vb + 1) * 8, :]
                    if load:
                        self.dma("sp", S[vb * 16:(vb + 1) * 16, :, :], d, (), ["C_S"])
                    else:
                        self.dma("act", d, S[vb * 16:(vb + 1) * 16, :, :], ["C_S"], ())

            self.memset("dve", S[:], 0.0, ["C_S"])
            self.cp("dve", SP[:], S[:], ["C_S"], [SK])
            run_segment(0, NTP)
            self.cp("dve", S[:], SP[:], [SK], ["C_S"])
            state_io(self.O["rw_p"], False)
            for s in range(NS):
                state_io(I["st_rw_wkv"][ic, s], True)
                self.cp("dve", SP[:], S[:], ["C_S"], [SK])
                run_segment(NTP + 8 * s, 8)
                self.cp("dve", S[:], SP[:], [SK], ["C_S"])
                state_io(self.O["rw_s"][s], False)

    if self.debug:
        self.dma("sp", self.O["dbg_scr"][7], SCR["o"], [("SCR", "o")], ())
    with self.scope():
        pbc(0, I["rwkv_lnx_w"][ic:ic + 1, :]); pbc(1, I["rwkv_lnx_b"][ic:ic + 1, :]); pbc(2, I["rwkv_r_k"].rearrange("a h n -> a (h n)")[ic:ic + 1, :])
        (Plw, klw), (Plb, klb), (Prk, krk) = xa(0), xa(1), xa(2)
        (Ot, kO), (Rt, kR), (Kt, kK), (Vt, kV), (Gt, kG), (T1, kT1) = [xa(i) for i in range(7, 13)]
        WO = self.sb("C_WO", [128, 8, D], BF16)
        self.dma("pool", WO[:], I["rwkv_w_out"][ic].rearrange("(kc p) n -> p kc n", p=128), (), ["C_WO"])
        YT = self.sb("C_YT", [128, 8, 128], BF16); XT_ = self.sb("C_XTL", [128, 8, 128])
        M1 = self.sb("C_M1", [128, 16]); M2 = self.sb("C_M2", [128, 16])
        for ti in range(17):
            rows = slice(128 * ti, 128 * ti + 128)
            for (nm, a, k) in (("o", Ot, kO), ("r", Rt, kR), ("k", Kt, kK), ("v", Vt, kV), ("g", Gt, kG)):
                self.dma("sp", h16(a), scr_tm(nm, rows), [("SCR", nm)], [k])
            self.dma("sp", XT_[:], XD[:, :, rows], ["XD"], ["C_XTL"])
            red(M1[:], h16(Ot), [kO], ["C_M1"])
            self.ts("dve", M1[:], M1[:], -1.0 / 64, None, ALU.mult, None, ["C_M1"], ["C_M1"])
            self.tt("dve", h16(Ot), h16(Ot), b16(M1[:]), ALU.add, [kO, "C_M1"], [kO])
            self.tt("dve", T1, Ot, Ot, ALU.mult, [kO], [kT1])
            red(M2[:], h16(T1), [kT1], ["C_M2"])
            self.act(M2[:], M2[:], AF.Ln, ["C_M2", ("C1", 3)], ["C_M2"], scale=1.0 / 64, bias=self.C1[:, 3:4])
            self.act(M2[:], M2[:], AF.Exp, ["C_M2"], ["C_M2"], scale=-0.5)
            self.tt("dve", h16(Ot), h16(Ot), b16(M2[:]), ALU.mult, [kO, "C_M2"], [kO])
            self.tt("dve", Ot, Ot, Plw, ALU.mult, [kO, klw], [kO])
            self.tt("dve", Ot, Ot, Plb, ALU.add, [kO, klb], [kO])
            self.tt("dve", T1, Rt, Kt, ALU.mult, [kR, kK], [kT1])
            self.tt("dve", T1, T1, Prk, ALU.mult, [kT1, krk], [kT1])
            red(M1[:], h16(T1), [kT1], ["C_M1"])
            self.tt("dve", h16(T1), h16(Vt), b16(M1[:]), ALU.mult, [kV, "C_M1"], [kT1])
            self.tt("dve", Ot, Ot, T1, ALU.add, [kO, kT1], [kO])
            self.tt("dve", Ot, Ot, Gt, ALU.mult, [kO, kG], [kO])
            for half in range(2):
                ps, pk = self.bank()
                for cc in range(4):
                    c = half * 4 + cc
                    self.tr(ps[:, cc * 128:(cc + 1) * 128], Ot[:, c * 128:(c + 1) * 128], self.IDF[:], [kO, "IDF"], [pk])
                self.cp("act", YT[:, half * 4:(half + 1) * 4, :].rearrange("p c t -> p (c t)"), ps[:, :], [pk], ["C_YT"])
            for n in range(8):
                ps, pk = self.bank()
                for c in range(8):
                    self.mm(ps[:, 0:128], WO[:, c, n * 128:(n + 1) * 128], YT[:, c, :], c == 0, c == 7, ["C_WO", "C_YT"], [pk])
                self.tt("dve", XT_[:, n, :], XT_[:, n, :], ps[:, 0:128], ALU.add, [pk, "C_XTL"], ["C_XTL"])
            self.dma("sp", XD[:, :, rows], XT_[:], ["C_XTL"], ["XD"])
    self.dma("sp", self.X[:], XD, ["XD"], ["X"])
    self.fm_to_rows(USH[:, :, 0:1], "C_USH", 1, self.O["rs_p"])
    self.fm_to_rows(USH[:, :, 1:], "C_USH", NS, self.O["rs_s"])


KB.layer_C = layer_C


def build(layers=(0, 1, 2, 3), debug=False, mlp=True):
    kb = KB(layers, debug)
    with kb.es:
        kb.setup()
        kb.load_x()
        for l in layers:
            with kb.scope():
                kind = l % 3
                if kind == 0:
                    kb.layer_A(l, l // 3)
                elif kind == 1:
                    kb.layer_B(l, 0)
                else:
                    kb.layer_C(l, 0)
            if mlp:
                with kb.scope():
                    kb.mlp(l)
        if not getattr(kb, '_skip_final', False):
            with kb.scope():
                kb.final_out()
        if debug:
            kb.dump_x()
        kb.P.emit()
    return kb.nc


def make_in_maps(inputs, cores):
    g = {k: np.ascontiguousarray(np.asarray(v, dtype=np.float32)) for k, v in inputs.items()}
    maps = []
    for i in cores:
        sl = slice(NS * i, NS * (i + 1))
        m = {
            "xp": g["x_prompt"][i], "xs": g["x_sample"][sl].reshape(NS * LS, D),
            "st_lru_conv": g["state_lru_conv"][:, sl], "st_lru_h": g["state_lru_h"][:, sl],
            "st_ssm_conv": g["state_ssm_conv"][:, sl], "st_ssm": g["state_ssm"][:, sl],
            "st_rw_shift": g["state_rwkv_shift"][:, sl], "st_rw_wkv": g["state_rwkv_wkv"][:, sl],
        }
        for k in IN_SPECS:
            if k in m:
                continue
            v = g[k]
            if k == "norm_final":
                v = v.reshape(1, D)
            m[k] = v
        maps.append({k: np.ascontiguousarray(v) for k, v in m.items()})
    return maps


def kernel(**inputs):
    nc = build()
    cores = list(range(8))
    res = run_bass_kernel_spmd(nc, make_in_maps(inputs, cores), core_ids=cores)
    r = res.results
    def cat(name, axis=0):
        return np.concatenate([x[name] for x in r], axis=axis)
    y_p = np.stack([x["y_p"] for x in r])
    y_s = cat("y_s").reshape(128, LS, D)
    lc_p = np.stack([x["lc_p"] for x in r], axis=1)
    lc_s = cat("lc_s", 1)
    lh_p = np.stack([x["lh_p"] for x in r], axis=1)
    lh_s = cat("lh_s", 1)
    sc_p = np.stack([x["sc_p"] for x in r])[None]
    sc_s = cat("sc_s")[None]
    ss_p = np.stack([x["ss_p"] for x in r])[None]
    ss_s = cat("ss_s")[None]
    rs_p = np.stack([x["rs_p"][0] for x in r])[None]
    rs_s = cat("rs_s")[None]
    rw_p = np.stack([x["rw_p"] for x in r])[None]
    rw_s = cat("rw_s")[None]
    outs = (y_p, y_s, lc_p, lc_s, lh_p, lh_s, sc_p, sc_s, ss_p, ss_s, rs_p, rs_s, rw_p, rw_s)
    return tuple(np.ascontiguousarray(o, dtype=np.float32) for o in outs)
```

```python
import numpy as np
import concourse.bass as bass
import concourse.mybir as mybir
from concourse.bass_utils import run_bass_kernel_spmd

F32 = mybir.dt.float32
BF16 = mybir.dt.bfloat16
I32 = mybir.dt.int32
AF = mybir.ActivationFunctionType
ALU = mybir.AluOpType
AX = mybir.AxisListType

ENGS = ("pe", "act", "dve", "pool", "sp")
SAME_ENGINE_SYNC = True
import os as _os_
NFUSE = int(_os_.environ.get('NFUSE', '1'))
FUSE_WAIT = bool(int(_os_.environ.get('FUSE_WAIT', '1')))
NOSAME_ENGS = tuple(x for x in _os_.environ.get('NOSAME_ENGS', '').split(',') if x)
N_DMA_SEMS = 12


class _St:
    __slots__ = ("w", "r")

    def __init__(self):
        self.w = None
        self.r = {}


class _Op:
    __slots__ = ("eng", "fn", "waits", "dma", "signal", "tok")


class Prog:
    def __init__(self, nc):
        self.nc = nc
        self.ops = {e: [] for e in ENGS}
        self.state = {}
        self.ndma = {e: 0 for e in ENGS}
        self.dma_toks = {}
        self.pending = {}
        self.nosame = False

    def barrier(self):
        toks = set()
        for e in ENGS:
            for o in reversed(self.ops[e]):
                if not o.dma:
                    toks.add(o.tok)
                    break
            n = self.ndma[e]
            for j in range(max(0, n - N_DMA_SEMS), n):
                toks.add(("dma", e, j))
        self.pending = {e: set(toks) for e in ENGS}

    def _states(self, key, create):
        if isinstance(key, tuple):
            buf, sub = key
        else:
            buf, sub = key, None
        d = self.state.setdefault(buf, {})
        if sub is None:
            if create and "*" not in d:
                d["*"] = _St()
            return list(d.values())
        out = []
        if "*" in d:
            out.append(d["*"])
        if sub not in d:
            if create:
                d[sub] = _St()
                out.append(d[sub])
        else:
            out.append(d[sub])
        return out

    def op(self, eng, fn, reads=(), writes=(), dma=False):
        ps_r = [k for k in reads if isinstance(k, tuple) and k[0] == "PS"]
        if ps_r:
            reads = [k for k in reads if not (isinstance(k, tuple) and k[0] == "PS")]
            writes = list(writes) + ps_r
        o = _Op()
        o.eng, o.fn, o.dma, o.signal = eng, fn, dma, False
        idx = len(self.ops[eng])
        if dma:
            j = self.ndma[eng]
            self.ndma[eng] += 1
            o.tok = ("dma", eng, j)
            self.dma_toks[(eng, j)] = o
        else:
            o.tok = ("eng", eng, idx)
        waits = set()
        for k in reads:
            for st in self._states(k, True):
                if st.w is not None:
                    waits.add(st.w)
        for k in writes:
            for st in self._states(k, True):
                if st.w is not None:
                    waits.add(st.w)
                for t in st.r.values():
                    waits.add(t)
        if self.pending.get(eng):
            waits |= self.pending.pop(eng)
        w2 = set()
        for t in waits:
            if t[0] == "eng" and t[1] == eng:
                if not dma and (eng in ("pe", "sp") or not SAME_ENGINE_SYNC or self.nosame or (NOSAME_ENGS and eng in NOSAME_ENGS)):
                    continue
            w2.add(t)
        o.waits = w2
        for k in reads:
            for st in self._states(k, True):
                st.r[o.tok if dma else o.tok[1]] = o.tok
        for k in writes:
            for st in self._states(k, True):
                st.w = o.tok
                st.r = {}
        self.ops[eng].append(o)
        return o

    def emit(self, final_wait_all=True):
        nc = self.nc
        tokmap = {}
        for e in ENGS:
            for i, o in enumerate(self.ops[e]):
                tokmap[o.tok] = o
        for e in ENGS:
            for o in self.ops[e]:
                for t in o.waits:
                    tokmap[t].signal = True
        import contextlib
        with contextlib.ExitStack() as es:
            EPOCH = 16000
            nsig = {e: sum(1 for o in self.ops[e] if o.signal and not o.dma) for e in ENGS}
            esem = {e: [es.enter_context(nc.semaphore("s_%s%d" % (e, i))) for i in range(nsig[e] // EPOCH + 1)] for e in ENGS if e != "sp"}
            dsem = {e: [es.enter_context(nc.semaphore("d_%s%d" % (e, i))) for i in range(N_DMA_SEMS)]
                    for e in ENGS if self.ndma[e] > 0}
            val = {}
            for e in ENGS:
                c = 0
                for o in self.ops[e]:
                    if o.dma:
                        j = o.tok[2]
                        val[o.tok] = (dsem[e][j % N_DMA_SEMS], 16 * (j // N_DMA_SEMS + 1))
                    elif o.signal:
                        val[o.tok] = (esem[e][c // EPOCH], c % EPOCH + 1)
                        c += 1
            self.maxcount = {}
            block = es.enter_context(nc.Block())

            def run(e, eng):
                seen = {}
                for o in self.ops[e]:
                    ws = []
                    for t in o.waits:
                        ws.append(val[t])
                    if o.dma:
                        j = o.tok[2]
                        if j >= N_DMA_SEMS:
                            ws.append(val[("dma", e, j - N_DMA_SEMS)])
                    need = []
                    for (s, v) in ws:
                        if seen.get(id(s), 0) >= v:
                            continue
                        seen[id(s)] = v
                        need = [(s2, v2) for (s2, v2) in need if s2 is not s] + [(s, v)]
                    fuse = FUSE_WAIT and need and not o.dma
                    self.stats = getattr(self, "stats", {})
                    self.stats[(e, len(need))] = self.stats.get((e, len(need)), 0) + 1
                    nf = min(len(need), NFUSE) if fuse else 0
                    for (s, v) in need[:len(need) - nf]:
                        eng.wait_ge(s, v)
                    ins = o.fn(eng)
                    for (s, v) in need[len(need) - nf:]:
                        ins._wait_ge(s, v)
                    if o.dma:
                        s, v = val[o.tok]
                        ins.then_inc(s, 16)
                    elif o.signal:
                        ins.then_inc(val[o.tok][0], 1)
                if final_wait_all:
                    n = self.ndma[e]
                    for j in range(max(0, n - N_DMA_SEMS), n):
                        s, v = val[("dma", e, j)]
                        if seen.get(id(s), 0) < v:
                            eng.wait_ge(s, v)
                            seen[id(s)] = v

            @block.tensor
            def _(eng):
                run("pe", eng)

            @block.scalar
            def _(eng):
                run("act", eng)

            @block.vector
            def _(eng):
                run("dve", eng)

            @block.gpsimd
            def _(eng):
                run("pool", eng)

            @block.sync
            def _(eng):
                run("sp", eng)
import contextlib


D = 1024
NTP = 2048
NS = 16
LS = 8
NT = NTP + NS * LS
TT = [(0, 512), (512, 512), (1024, 512), (1536, 512), (2048, 128)]
UC = 1 + NTP + NS * 9
XBC = 3 + NTP + NS * 11

PROW = {}


def _prow_layout():
    r = 0
    def add(name, n):
        nonlocal r
        PROW[name] = r
        r += n
    add("norm_mix", 4); add("norm_ffn", 4); add("norm_final", 1)
    add("lru_conv_w", 8); add("lru_conv_b", 2); add("lru_b_r", 2); add("lru_b_i", 2); add("lru_lambda", 2)
    add("ssm_norm_w", 2); add("ssm_conv_w", 16); add("ssm_conv_b", 4)
    add("rwkv_mu", 6); add("rwkv_w0", 1); add("rwkv_a0", 1); add("rwkv_k_k", 1); add("rwkv_k_a", 1)
    add("rwkv_lnx_w", 1); add("rwkv_lnx_b", 1); add("rwkv_r_k", 1)
    return r


NPROW = _prow_layout()

IN_SPECS = {
    "xp": [NTP, D], "xs": [NS * LS, D],
    "st_lru_conv": [2, NS, 3, D], "st_lru_h": [2, NS, D], "st_ssm_conv": [1, NS, 3, 4096],
    "st_ssm": [1, NS, 32, 64, 128], "st_rw_shift": [1, NS, D], "st_rw_wkv": [1, NS, 16, 64, 64],
    "norm_mix": [4, D], "norm_ffn": [4, D], "norm_final": [1, D],
    "lru_w_in": [2, D, 2048], "lru_conv_w": [2, 4, D], "lru_conv_b": [2, D], "lru_w_r": [2, 8, 128, 128],
    "lru_b_r": [2, D], "lru_w_i": [2, 8, 128, 128], "lru_b_i": [2, D], "lru_lambda": [2, D], "lru_w_out": [2, D, D],
    "ssm_w_in": [1, D, 6176], "ssm_conv_w": [1, 4, 4096], "ssm_conv_b": [1, 4096], "ssm_dt_bias": [1, 32],
    "ssm_a_log": [1, 32], "ssm_d": [1, 32], "ssm_norm_w": [1, 2048], "ssm_w_out": [1, 2048, D],
    "rwkv_mu": [1, 6, D], "rwkv_w_rkv": [1, 3, D, D], "rwkv_w0": [1, D], "rwkv_w_w1": [1, D, 64], "rwkv_w_w2": [1, 64, D],
    "rwkv_a0": [1, D], "rwkv_w_a1": [1, D, 64], "rwkv_w_a2": [1, 64, D], "rwkv_w_g1": [1, D, 128], "rwkv_w_g2": [1, 128, D],
    "rwkv_k_k": [1, D], "rwkv_k_a": [1, D], "rwkv_r_k": [1, 16, 64], "rwkv_lnx_w": [1, D], "rwkv_lnx_b": [1, D],
    "rwkv_w_out": [1, D, D], "ffn_w1": [4, D, 4096], "ffn_w2": [4, 4096, D],
}
OUT_SPECS = {
    "y_p": [NTP, D], "y_s": [NS * LS, D],
    "lc_p": [2, 3, D], "lc_s": [2, NS, 3, D], "lh_p": [2, D], "lh_s": [2, NS, D],
    "sc_p": [3, 4096], "sc_s": [NS, 3, 4096], "ss_p": [32, 64, 128], "ss_s": [NS, 32, 64, 128],
    "rs_p": [1, D], "rs_s": [NS, D], "rw_p": [16, 64, 64], "rw_s": [NS, 16, 64, 64],
}


class KB:
    def __init__(self, layers=(0, 1, 2, 3), debug=False):
        self.nc = nc = bass.Bass("TRN2", target_bir_lowering=False)
        self.P = Prog(nc)
        self.es = contextlib.ExitStack()
        self.I = {k: nc.dram_tensor(k, v, F32, kind="ExternalInput").ap() for k, v in IN_SPECS.items()}
        self.O = {k: nc.dram_tensor(k, v, F32, kind="ExternalOutput").ap() for k, v in OUT_SPECS.items()}
        self.debug = debug
        if debug:
            self.O["dbg_x"] = nc.dram_tensor("dbg_x", [128, 8, NT], F32, kind="ExternalOutput").ap()
            self.O["dbg_scr"] = nc.dram_tensor("dbg_scr", [8, NT, D], F32, kind="ExternalOutput").ap()
        self.bank_i = 0
        self.layers = layers
        self._n = 0

    def sb(self, name, shape, dt=F32):
        self._n += 1
        return self.es.enter_context(self.nc.sbuf_tensor("%s_%d" % (name, self._n), shape, dt))

    @contextlib.contextmanager
    def scope(self):
        old = self.es
        self.es = contextlib.ExitStack()
        try:
            yield
        finally:
            self.es.close()
            self.es = old
            self.P.barrier()

    def bank(self):
        b = self.bank_i
        self.bank_i = (self.bank_i + 1) % 8
        return self.PS[b], ("PS", b)

    def dma(self, eng, out, in_, reads=(), writes=()):
        self.P.op(eng, lambda e: e.dma_start(out=out, in_=in_), reads, writes, dma=True)

    def act(self, out, in_, func, reads, writes, **kw):
        self.P.op("act", lambda e: e.activation(out=out, in_=in_, func=func, **kw), reads, writes)

    def mm(self, out, lhsT, rhs, start, stop, reads, writes):
        self.P.op("pe", lambda e: e.matmul(out, lhsT=lhsT, rhs=rhs, start=start, stop=stop), reads, writes)

    def tr(self, out, in_, ident, reads, writes):
        self.P.op("pe", lambda e: e.transpose(out, in_, ident), reads, writes)

    def ts(self, eng, out, in0, s1, s2, op0, op1, reads, writes):
        if op1 is None:
            self.P.op(eng, lambda e: e.tensor_scalar(out=out, in0=in0, scalar1=s1, scalar2=None, op0=op0), reads, writes)
        else:
            self.P.op(eng, lambda e: e.tensor_scalar(out=out, in0=in0, scalar1=s1, scalar2=s2, op0=op0, op1=op1), reads, writes)

    def stt(self, out, in0, scalar, in1, op0, op1, reads, writes):
        self.P.op("dve", lambda e: e.scalar_tensor_tensor(out=out, in0=in0, scalar=scalar, in1=in1, op0=op0, op1=op1), reads, writes)

    def tt(self, eng, out, in0, in1, op, reads, writes):
        self.P.op(eng, lambda e: e.tensor_tensor(out=out, in0=in0, in1=in1, op=op), reads, writes)

    def cp(self, eng, out, in_, reads, writes):
        if eng == "act":
            self.P.op("act", lambda e: e.activation(out=out, in_=in_, func=AF.Copy), reads, writes)
        else:
            self.P.op(eng, lambda e: e.tensor_copy(out=out, in_=in_), reads, writes)

    def memset(self, eng, ap, v, writes):
        self.P.op(eng, lambda e: e.memset(ap, v), (), writes)

    def scan(self, out, d0, d1, init, reads, writes):
        self.P.op("dve", lambda e: e.tensor_tensor_scan(out=out, data0=d0, data1=d1, initial=init, op0=ALU.mult, op1=ALU.add), reads, writes)

    def xv(self, c, ti):
        c0, w = TT[ti]
        return self.X[:, c, c0:c0 + w]

    def uv(self, c, ti, shift=0):
        if ti < 4:
            s = 1 + 512 * ti + shift
            return self.U[:, c, s:s + 512]
        v = self.U[:, c, 1 + NTP:UC].rearrange("p (s t) -> p s t", t=9)
        return v[:, :, 1 + shift:9 + shift]

    @staticmethod
    def v3(ap, ti):
        if ti < 4:
            return ap
        return ap.rearrange("p (s t) -> p s t", t=8)

    def setup(self):
        nc = self.nc
        self.PS = [self.es.enter_context(nc.psum_tensor("ps%d" % i, [128, 512], F32)) for i in range(8)]
        self.X = self.sb("X", [128, 8, NT])
        self.U = self.sb("U", [128, 8, UC], BF16)
        self.IDF = self.sb("IDF", [128, 128])
        self.IDB = self.sb("IDB", [128, 128], BF16)
        self.ONESB = self.sb("ONESB", [128, 128], BF16)
        self.C1 = self.sb("C1", [128, 4])
        self.PRM = self.sb("PRM", [64, D])
        self.PF = self.sb("PF", [128, 8, 64])
        self.STG = [self.sb("STG%d" % i, [128, D]) for i in range(2)]
        self.SQ = self.sb("SQ", [128, 8, 512], BF16)
        self.RS = self.sb("RS", [128, 512])
        P = self.P
        P.op("pool", lambda e: e.memset(self.IDF[:], 0.0), (), ["IDF"])
        P.op("pool", lambda e: e.affine_select(out=self.IDF[:], in_=self.IDF[:], pattern=[[-1, 128]], compare_op=ALU.not_equal,
                                               fill=1.0, base=0, channel_multiplier=1), ["IDF"], ["IDF"])
        self.cp("dve", self.IDB[:], self.IDF[:], ["IDF"], ["IDB"])
        self.memset("dve", self.ONESB[:], 1.0, ["ONESB"])
        self.memset("dve", self.C1[:, 0:1], 1e-6, [("C1", 0)])
        self.memset("dve", self.C1[:, 1:2], 1.0, [("C1", 1)])
        self.memset("dve", self.C1[:, 2:3], 1e-5, [("C1", 2)])
        self.memset("dve", self.C1[:, 3:4], 64e-5, [("C1", 3)])
        self.memset("dve", self.U[:, :, 0:1], 0.0, [("U", "shiftp")])
        self.memset("pool", self.PRM[:], 0.0, ["PRM"])
        I = self.I
        def row(name, src, n):
            r = PROW[name]
            self.dma("sp", self.PRM[r:r + n, :], src, (), ["PRM"])
        row("norm_mix", I["norm_mix"], 4); row("norm_ffn", I["norm_ffn"], 4); row("norm_final", I["norm_final"], 1)
        row("lru_conv_w", I["lru_conv_w"].rearrange("a k d -> (a k) d"), 8)
        row("lru_conv_b", I["lru_conv_b"], 2); row("lru_b_r", I["lru_b_r"], 2); row("lru_b_i", I["lru_b_i"], 2)
        row("lru_lambda", I["lru_lambda"], 2)
        row("ssm_norm_w", I["ssm_norm_w"].rearrange("a (r d) -> (a r) d", d=D), 2)
        row("ssm_conv_w", I["ssm_conv_w"].rearrange("a k (r d) -> (a k r) d", d=D), 16)
        row("ssm_conv_b", I["ssm_conv_b"].rearrange("a (r d) -> (a r) d", d=D), 4)
        row("rwkv_mu", I["rwkv_mu"].rearrange("a k d -> (a k) d"), 6)
        for nm in ("rwkv_w0", "rwkv_a0", "rwkv_k_k", "rwkv_k_a", "rwkv_lnx_w", "rwkv_lnx_b"):
            row(nm, I[nm], 1)
        row("rwkv_r_k", I["rwkv_r_k"].rearrange("a h n -> a (h n)"), 1)
        for c in range(8):
            ps, pk = self.bank()
            self.tr(ps[:, 0:64], self.PRM[:, c * 128:(c + 1) * 128], self.IDF[0:64, 0:64], ["PRM", "IDF"], [pk])
            self.cp("dve", self.PF[:, c, :], ps[:, 0:64], [pk], [("PF", c)])

    def pf(self, name, k, c):
        r = PROW[name] + k
        return self.PF[:, c, r:r + 1]

    def load_x(self):
        n = 0
        for ti, (c0, w) in enumerate(TT):
            for j in range(w // 128):
                b = n % 2
                n += 1
                src = self.I["xp"][c0 + j * 128:c0 + (j + 1) * 128, :] if ti < 4 else self.I["xs"][:, :]
                self.dma("sp", self.STG[b][:], src, (), [("STG", b)])
                for c in range(8):
                    self.tr(self.PS[c][:, j * 128:(j + 1) * 128], self.STG[b][:, c * 128:(c + 1) * 128], self.IDF[:],
                            [("STG", b), "IDF"], [("PS", c)])
            for c in range(8):
                self.cp("dve" if c % 2 == 0 else "act", self.X[:, c, c0:c0 + w], self.PS[c][:, 0:w], [("PS", c)], [("X", (c, ti))])

    def rows_to_fm(self, src, nrows, dst, dkey):
        self.dma("sp", self.STG[0][0:nrows, :], src, (), [("STG", 0)])
        for c in range(8):
            ps, pk = self.bank()
            self.tr(ps[:, 0:nrows], self.STG[0][0:nrows, c * 128:(c + 1) * 128], self.IDF[0:nrows, 0:nrows], [("STG", 0), "IDF"], [pk])
            self.cp("dve", dst[:, c, 0:nrows], ps[:, 0:nrows], [pk], [dkey])

    def fm_to_rows(self, src, skey, nrows, dst):
        for half in range(2):
            ps, pk = self.bank()
            for cc in range(4):
                c = half * 4 + cc
                self.tr(ps[0:nrows, cc * 128:(cc + 1) * 128], src[:, c, 0:nrows], self.IDF[:], [skey, "IDF"], [pk])
            self.cp("dve", self.STG[1][0:nrows, half * 512:(half + 1) * 512], ps[0:nrows, :], [pk], [("STG", 1)])
        self.dma("sp", dst, self.STG[1][0:nrows, :], [("STG", 1)], ())

    def norm_to_U(self, pname, k, shift_out=None):
        for ti, (c0, w) in enumerate(TT):
            ps, pk = self.bank()
            for c in range(8):
                self.act(self.SQ[:, c, :w], self.X[:, c, c0:c0 + w], AF.Square, [("X", (c, ti))], [("SQ", c)])
                self.mm(ps[:, :w], self.ONESB[:], self.SQ[:, c, :w], c == 0, c == 7, [("SQ", c), "ONESB"], [pk])
            self.act(self.RS[:, :w], ps[:, :w], AF.Ln, [pk, ("C1", 0)], ["RS"], scale=1.0 / D, bias=self.C1[:, 0:1])
            self.act(self.RS[:, :w], self.RS[:, :w], AF.Exp, ["RS"], ["RS"], scale=-0.5)
            for c in range(8):
                self.stt(self.uv(c, ti), self.v3(self.X[:, c, c0:c0 + w], ti), self.pf(pname, k, c), self.v3(self.RS[:, :w], ti),
                         ALU.mult, ALU.mult, [("X", (c, ti)), "RS", ("PF", c)], [("U", (c, ti))])
                if shift_out is not None and ti == 3:
                    self.stt(shift_out[:, c, 0:1], self.X[:, c, NTP - 1:NTP], self.pf(pname, k, c), self.RS[:, 511:512],
                             ALU.mult, ALU.mult, [("X", (c, ti)), "RS", ("PF", c)], ["C_USH"])
                if shift_out is not None and ti == 4:
                    self.stt(shift_out[:, c, 1:], self.X[:, c, NTP:NT].rearrange("p (s t) -> p s t", t=8)[:, :, 7],
                             self.pf(pname, k, c), self.RS[:, 0:128].rearrange("p (s t) -> p s t", t=8)[:, :, 7],
                             ALU.mult, ALU.mult, [("X", (c, ti)), "RS", ("PF", c)], ["C_USH"])

    def u_keys(self, ti):
        return [("U", (c, ti)) for c in range(8)]

    def mlp(self, l):
        self.norm_to_U("norm_ffn", l)
        self.W1S = [self.sb("W1S%d" % i, [128, 8, 512], BF16) for i in range(2)]
        self.W2S = [self.sb("W2S%d" % i, [128, 4, D], BF16) for i in range(2)]
        self.HT = [self.sb("HT%d" % i, [128, 4, 512], BF16) for i in range(2)]
        self.RT = [self.sb("RT%d" % i, [128, 512]) for i in range(2)]
        w1 = self.I["ffn_w1"]
        w2 = self.I["ffn_w2"]
        def load(s):
            b = s % 2
            self.dma("pool", self.W1S[b][:], w1[l, :, s * 512:(s + 1) * 512].rearrange("(kc p) n -> p kc n", p=128), (), [("W1S", b)])
            self.dma("pool", self.W2S[b][:], w2[l, s * 512:(s + 1) * 512, :].rearrange("(fc p) n -> p fc n", p=128), (), [("W2S", b)])
        load(0)
        hb = 0
        rb = 0
        for s in range(8):
            if s + 1 < 8:
                load(s + 1)
            b = s % 2
            for ti, (c0, w) in enumerate(TT):
                H = self.HT[hb]
                hk = "HT%d" % hb
                hb ^= 1
                for fc in range(4):
                    ps, pk = self.bank()
                    for kc in range(8):
                        self.mm(self.v3(ps[:, :w], ti), self.W1S[b][:, kc, fc * 128:(fc + 1) * 128], self.uv(kc, ti), kc == 0, kc == 7,
                                [("W1S", b), ("U", (kc, ti))], [pk])
                    R = self.RT[rb]
                    rk = "RT%d" % rb
                    rb ^= 1
                    self.act(R[:, :w], ps[:, :w], AF.Relu, [pk], [rk])
                    self.act(H[:, fc, :w], R[:, :w], AF.Square, [rk], [(hk, fc)])
                for n in range(8):
                    ps, pk = self.bank()
                    for fc in range(4):
                        self.mm(ps[:, :w], self.W2S[b][:, fc, n * 128:(n + 1) * 128], H[:, fc, :w], fc == 0, fc == 3,
                                [("W2S", b), (hk, fc)], [pk])
                    self.tt("dve", self.X[:, n, c0:c0 + w], self.X[:, n, c0:c0 + w], ps[:, :w], ALU.add,
                            [pk, ("X", (n, ti))], [("X", (n, ti))])

    def final_out(self):
        YT = self.STG
        UF = self.sb("UF", [128, 8, 512])
        n = 0
        for ti, (c0, w) in enumerate(TT):
            ps, pk = self.bank()
            for c in range(8):
                self.act(self.SQ[:, c, :w], self.X[:, c, c0:c0 + w], AF.Square, [("X", (c, ti))], [("SQ", c)])
                self.mm(ps[:, :w], self.ONESB[:], self.SQ[:, c, :w], c == 0, c == 7, [("SQ", c), "ONESB"], [pk])
            self.act(self.RS[:, :w], ps[:, :w], AF.Ln, [pk, ("C1", 0)], ["RS"], scale=1.0 / D, bias=self.C1[:, 0:1])
            self.act(self.RS[:, :w], self.RS[:, :w], AF.Exp, ["RS"], ["RS"], scale=-0.5)
            for c in range(8):
                self.stt(UF[:, c, :w], self.X[:, c, c0:c0 + w], self.pf("norm_final", 0, c), self.RS[:, :w],
                         ALU.mult, ALU.mult, [("X", (c, ti)), "RS", ("PF", c)], [("UF", c)])
            for j in range(w // 128):
                b = n % 2
                n += 1
                for half in range(2):
                    ps2, pk2 = self.bank()
                    for cc in range(4):
                        c = half * 4 + cc
                        self.tr(ps2[:, cc * 128:(cc + 1) * 128], UF[:, c, j * 128:(j + 1) * 128], self.IDF[:], [("UF", c), "IDF"], [pk2])
                    self.cp("act" if half else "dve", YT[b][:, half * 512:(half + 1) * 512], ps2[:, :], [pk2], [("STG", b)])
                dst = self.O["y_p"][c0 + j * 128:c0 + (j + 1) * 128, :] if ti < 4 else self.O["y_s"][:, :]
                self.dma("sp", dst, YT[b][:], [("STG", b)], ())

    def dump_x(self):
        self.dma("sp", self.O["dbg_x"], self.X[:], ["X"], ())


def layer_A(self, l, ia):
    I = self.I
    self.norm_to_U("norm_mix", l)
    XB = self.sb("A_XB", [128, XBC])
    XC = self.sb("A_XC", [128, NT])
    XCb = self.sb("A_XCb", [128, NT], BF16)
    GATE = self.sb("A_GATE", [128, NT], BF16)
    R = self.sb("A_R", [128, NT])
    Iq = self.sb("A_I", [128, NT])
    WIN = [self.sb("A_WIN%d" % i, [128, 8, 256], BF16) for i in range(2)]
    WR = [self.sb("A_WR%d" % i, [128, 128], BF16) for i in range(2)]
    WI = [self.sb("A_WI%d" % i, [128, 128], BF16) for i in range(2)]
    WO = [self.sb("A_WO%d" % i, [128, D], BF16) for i in range(2)]
    CL = self.sb("A_CL", [128, 8])
    H0 = self.sb("A_H0", [128, 8, NS])
    CS0 = self.sb("A_CS0", [128, 8, NS * 3])
    HST = self.sb("A_HST", [128, 8, 1 + NS])
    CST = self.sb("A_CST", [128, 8, 3 + NS * 3])
    XBs = XB[:, 3 + NTP:XBC].rearrange("p (s t) -> p s t", t=11)
    XCs = XC[:, NTP:NT].rearrange("p (s t) -> p s t", t=8)
    rl = PROW["lru_lambda"] + ia
    self.act(CL[:], self.PF[:, :, rl], AF.Exp, ["PF"], ["A_CL"], scale=-1.0)
    self.act(CL[:], CL[:], AF.Ln, ["A_CL", ("C1", 1)], ["A_CL"], bias=self.C1[:, 1:2], scale=1.0)
    self.ts("dve", CL[:], CL[:], -8.0, None, ALU.mult, None, ["A_CL"], ["A_CL"])
    self.rows_to_fm(I["st_lru_h"][ia], NS, H0, "A_H0")
    self.rows_to_fm(I["st_lru_conv"][ia].rearrange("s k d -> (s k) d"), NS * 3, CS0, "A_CS0")
    w_in, w_out = I["lru_w_in"], I["lru_w_out"]

    def load(j):
        b = j % 2
        self.dma("pool", WIN[b][:, :, 0:128], w_in[ia, :, j * 128:(j + 1) * 128].rearrange("(kc p) n -> p kc n", p=128), (), [("A_WIN", b)])
        self.dma("pool", WIN[b][:, :, 128:256], w_in[ia, :, D + j * 128:D + (j + 1) * 128].rearrange("(kc p) n -> p kc n", p=128), (), [("A_WIN", b)])
        self.dma("pool", WR[b][:], I["lru_w_r"][ia, j], (), [("A_WR", b)])
        self.dma("pool", WI[b][:], I["lru_w_i"][ia, j], (), [("A_WI", b)])
        self.dma("pool", WO[b][:], w_out[ia, j * 128:(j + 1) * 128, :], (), [("A_WO", b)])

    load(0)
    for j in range(8):
        if j + 1 < 8:
            load(j + 1)
        b = j % 2
        self.memset("dve", XB[:, 0:3], 0.0, [("A_XB", "st")])
        self.cp("dve", XBs[:, :, 0:3], CS0[:, j, :].rearrange("p (s k) -> p s k", k=3), ["A_CS0"], [("A_XB", "st")])
        for ti, (c0, w) in enumerate(TT):
            for half in range(2):
                ps, pk = self.bank()
                for kc in range(8):
                    self.mm(self.v3(ps[:, :w], ti), WIN[b][:, kc, half * 128:(half + 1) * 128], self.uv(kc, ti), kc == 0, kc == 7,
                            [("A_WIN", b), ("U", (kc, ti))], [pk])
                if half == 0:
                    dst = XB[:, 3 + c0:3 + c0 + w] if ti < 4 else XBs[:, :, 3:11]
                    self.cp("act", dst, self.v3(ps[:, :w], ti), [pk], [("A_XB", ti)])
                else:
                    self.act(GATE[:, c0:c0 + w], ps[:, :w], AF.Gelu_apprx_tanh, [pk], [("A_GATE", ti)])
        self.cp("pool", CST[:, j, 0:3], XB[:, NTP:NTP + 3], ["A_XB"], [("A_CST", j)])
        self.cp("pool", CST[:, j, 3:].rearrange("p (s k) -> p s k", k=3), XBs[:, :, 8:11], ["A_XB"], [("A_CST", j)])
        cw = [self.pf("lru_conv_w", ia * 4 + k, j) for k in range(4)]
        cb = self.pf("lru_conv_b", ia, j)
        for (dst, srcf) in ((XC[:, 0:NTP], lambda k: XB[:, k:k + NTP]), (XCs, lambda k: XBs[:, :, k:k + 8])):
            self.ts("dve", dst, srcf(0), cw[0], cb, ALU.mult, ALU.add, ["A_XB", ("PF", j)], ["A_XC"])
            for k in range(1, 4):
                self.stt(dst, srcf(k), cw[k], dst, ALU.mult, ALU.add, ["A_XB", "A_XC", ("PF", j)], ["A_XC"])
        self.cp("act", XCb[:], XC[:], ["A_XC"], ["A_XCb"])
        for ti, (c0, w) in enumerate(TT):
            for (Wg, dstb, bname, key) in ((WR, R, "lru_b_r", "A_R"), (WI, Iq, "lru_b_i", "A_I")):
                ps, pk = self.bank()
                self.mm(ps[:, :w], Wg[b][:], XCb[:, c0:c0 + w], True, True, ["A_XCb", (key.replace("A_", "A_W"), b)], [pk])
                self.act(dstb[:, c0:c0 + w], ps[:, :w], AF.Sigmoid, [pk, ("PF", j)], [key], bias=self.pf(bname, ia, j), scale=1.0)
        T1 = XB[:, 0:NT]
        self.act(R[:], R[:], AF.Exp, ["A_R", "A_CL"], ["A_R"], scale=CL[:, j:j + 1])
        self.act(T1, R[:], AF.Square, ["A_R", "A_XB"], ["A_XB"])
        self.ts("dve", T1, T1, -1.0, 1.0, ALU.mult, ALU.add, ["A_XB"], ["A_XB"])
        self.ts("dve", T1, T1, 1e-30, None, ALU.max, None, ["A_XB"], ["A_XB"])
        self.act(T1, T1, AF.Sqrt, ["A_XB"], ["A_XB"])
        self.memset("dve", T1[:, 0:1], 1.0, ["A_XB"])
        self.memset("dve", R[:, 0:1], 0.0, ["A_R"])
        self.tt("dve", Iq[:], Iq[:], T1, ALU.mult, ["A_I", "A_XB"], ["A_I"])
        self.tt("dve", Iq[:], Iq[:], XC[:], ALU.mult, ["A_I", "A_XC"], ["A_I"])
        self.scan(XC[:, 0:NTP], R[:, 0:NTP], Iq[:, 0:NTP], 0.0, ["A_R", "A_I"], ["A_XC"])
        for s in range(NS):
            c0 = NTP + s * 8
            self.scan(XC[:, c0:c0 + 8], R[:, c0:c0 + 8], Iq[:, c0:c0 + 8], H0[:, j, s:s + 1], ["A_R", "A_I", "A_H0"], ["A_XC"])
        self.cp("pool", HST[:, j, 0:1], XC[:, NTP - 1:NTP], ["A_XC"], [("A_HST", j)])
        self.cp("pool", HST[:, j, 1:], XCs[:, :, 7], ["A_XC"], [("A_HST", j)])
        self.tt("dve", XCb[:], XC[:], GATE[:], ALU.mult, ["A_XC", "A_GATE"], ["A_XCb"])
        for ti, (c0, w) in enumerate(TT):
            for n in range(8):
                ps, pk = self.bank()
                self.mm(ps[:, :w], WO[b][:, n * 128:(n + 1) * 128], XCb[:, c0:c0 + w], True, True, ["A_XCb", ("A_WO", b)], [pk])
                self.tt("dve", self.X[:, n, c0:c0 + w], self.X[:, n, c0:c0 + w], ps[:, :w], ALU.add, [pk, ("X", (n, ti))], [("X", (n, ti))])
    self.fm_to_rows(HST[:, :, 0:1], "A_HST", 1, self.O["lh_p"][ia:ia + 1, :])
    self.fm_to_rows(HST[:, :, 1:], "A_HST", NS, self.O["lh_s"][ia])
    self.fm_to_rows(CST[:, :, 0:3], "A_CST", 3, self.O["lc_p"][ia])
    self.fm_to_rows(CST[:, :, 3:], "A_CST", NS * 3, self.O["lc_s"][ia].rearrange("s k d -> (s k) d"))


KB.layer_A = layer_A


def layer_B(self, l, ib):
    I = self.I
    self.norm_to_U("norm_mix", l)
    f32 = F32
    TRI = self.sb("B_TRI", [128, 128]); NEGM = self.sb("B_NEGM", [128, 128]); ONESF = self.sb("B_ONESF", [128, 128])
    self.memset("pool", TRI[:], 1.0, ["B_TRI"])
    self.P.op("pool", lambda e: e.affine_select(out=TRI[:], in_=TRI[:], pattern=[[1, 128]], compare_op=ALU.is_ge, fill=0.0, base=0,
                                                channel_multiplier=-1), ["B_TRI"], ["B_TRI"])
    self.memset("pool", NEGM[:], 0.0, ["B_NEGM"])
    self.P.op("pool", lambda e: e.affine_select(out=NEGM[:], in_=NEGM[:], pattern=[[1, 128]], compare_op=ALU.is_ge, fill=-1.0e4, base=0,
                                                channel_multiplier=-1), ["B_NEGM"], ["B_NEGM"])
    self.memset("pool", ONESF[:], 1.0, ["B_ONESF"])
    DTB = self.sb("B_DTB", [128, 32]); AB = self.sb("B_AB", [128, 32]); DB = self.sb("B_DB", [128, 32])
    self.dma("sp", DTB[:], I["ssm_dt_bias"][ib:ib + 1, :].partition_broadcast(128), (), ["B_DTB"])
    self.dma("sp", AB[:], I["ssm_a_log"][ib:ib + 1, :].partition_broadcast(128), (), ["B_AB"])
    self.dma("sp", DB[:], I["ssm_d"][ib:ib + 1, :].partition_broadcast(128), (), ["B_DB"])
    self.act(AB[:], AB[:], AF.Exp, ["B_AB"], ["B_AB"])
    self.ts("dve", AB[:], AB[:], -1.0, None, ALU.mult, None, ["B_AB"], ["B_AB"])
    CS0 = self.sb("B_CS0", [128, 32, NS * 3]); CST = self.sb("B_CST", [128, 32, 3 + NS * 3])
    for r in range(4):
        self.rows_to_fm(I["st_ssm_conv"][ib].rearrange("s k d -> (s k) d")[:, r * D:(r + 1) * D], NS * 3, CS0[:, r * 8:(r + 1) * 8, :], "B_CS0")
    WZD = [self.sb("B_WZD0", [128, 8, 260], BF16)] * 2
    BONES = self.sb("B_BONES", [128, 128]); TRIS = self.sb("B_TRIS", [128, 128]); NEGMS = self.sb("B_NEGMS", [128, 128])
    SEQM = self.sb("B_SEQM", [128, 16]); DAS = self.sb("B_DAS", [128, 64]); CDS = self.sb("B_CDS", [128, 64])
    YO = self.sb("B_YO", [128, 256]); BTM = self.sb("B_BTM", [128, 128], BF16); USC = self.sb("B_USC", [128, 8, 128], BF16)
    def _asel(ap, pattern, base, cm, fill, keys):
        self.P.op("pool", lambda e: e.affine_select(out=ap, in_=ap, pattern=pattern, compare_op=ALU.is_ge, fill=fill, base=base,
                                                    channel_multiplier=cm), keys, keys)
    self.memset("pool", SEQM[:], 1.0, ["B_SEQM"])
    _asel(SEQM[:], [[-8, 16]], 0, 1, 0.0, ["B_SEQM"])
    _asel(SEQM[:], [[8, 16]], 7, -1, 0.0, ["B_SEQM"])
    self.memset("pool", BONES[:], 1.0, ["B_BONES"])
    _asel(BONES[:].rearrange("p (s t) -> p s t", t=8), [[-8, 16], [0, 8]], 0, 1, 0.0, ["B_BONES"])
    _asel(BONES[:].rearrange("p (s t) -> p s t", t=8), [[8, 16], [0, 8]], 7, -1, 0.0, ["B_BONES"])
    self.tt("pool", TRIS[:], TRI[:], BONES[:], ALU.mult, ["B_TRI", "B_BONES"], ["B_TRIS"])
    self.tt("pool", NEGMS[:], NEGM[:], BONES[:], ALU.mult, ["B_NEGM", "B_BONES"], ["B_NEGMS"])
    self.ts("dve", YO[:, 0:128], BONES[:], 1.0e4, -1.0e4, ALU.mult, ALU.add, ["B_BONES"], ["B_YO"])
    self.tt("dve", NEGMS[:], NEGMS[:], YO[:, 0:128], ALU.add, ["B_NEGMS", "B_YO"], ["B_NEGMS"])
    self.cp("dve", USC[:].rearrange("p c (s t) -> p c s t", t=8),
            self.U[:, :, 1 + NTP:UC].rearrange("p c (s t) -> p c s t", t=9)[:, :, :, 1:9], [("U", (c_, 4)) for c_ in range(8)], ["B_USC"])

    WXBC = [self.sb("B_WXBC0", [128, 8, 512], BF16)] * 2
    WOUT = [self.sb("B_WOUT0", [128, 2, D], BF16)] * 2
    XF = self.sb("B_XF", [128, 4, 3 + 512]); XFS = self.sb("B_XFS", [128, 4, NS * 11])
    XCf = self.sb("B_XCf", [128, 4, 512]); BCb = self.sb("B_BCb", [128, 2, 512], BF16)
    YGT = self.sb("B_YGT", [128, 2, 512], BF16)
    ST = self.sb("B_ST", [128, 256]); STb = self.sb("B_STb", [128, 256], BF16)
    SIN = self.sb("B_SIN", [128, 2, 128]); SOUT = self.sb("B_SOUT", [128, 2, 128])
    XTs = [self.sb("B_XT%d" % i, [128, 256]) for i in range(2)]; BTs = [self.sb("B_BT%d" % i, [128, 128], BF16) for i in range(2)]
    SMs = [self.sb("B_SM%d" % i, [128, 40]) for i in range(2)]
    LT = self.sb("B_LT", [128, 4, 128]); MTs = [self.sb("B_MT%d" % i, [128, 4, 128], BF16) for i in range(2)]
    XDTs = [self.sb("B_XDT%d" % i, [128, 256], BF16) for i in range(2)]; XDDs = [self.sb("B_XDD%d" % i, [128, 256], BF16) for i in range(2)]
    Y1 = self.sb("B_Y1", [128, 256]); T2 = self.sb("B_T2", [128, 256]); SZ = self.sb("B_SZ", [128, 256])
    w_in, w_out = I["ssm_w_in"], I["ssm_w_out"]
    XFSv = [XFS[:, q, :].rearrange("p (s t) -> p s t", t=11) for q in range(4)]

    def bc(ap, cs):
        return ap.unsqueeze(2).to_broadcast([cs, 4, 64])

    def v4(ap):
        return ap.rearrange("p (h q) -> p h q", q=64)

    def wv(c0, n):
        return w_in[ib, :, c0:c0 + n].rearrange("(kc p) n -> p kc n", p=128)

    def load(g):
        self.dma("pool", WZD[0][:, :, 0:256], wv(g * 256, 256), (), ["B_WZD"])
        self.dma("pool", WZD[0][:, :, 256:260], wv(6144 + 4 * g, 4), (), ["B_WZD"])

    def load_x(g):
        self.dma("pool", WXBC[0][:, :, 0:256], wv(2048 + g * 256, 256), (), ["B_WXBC"])
        self.dma("pool", WXBC[0][:, :, 256:384], wv(4096 + g * 128, 128), (), ["B_WXBC"])
        self.dma("pool", WXBC[0][:, :, 384:512], wv(5120 + g * 128, 128), (), ["B_WXBC"])

    def load_o(g):
        self.dma("pool", WOUT[0][:], w_out[ib, g * 256:(g + 1) * 256, :].rearrange("(h p) n -> p h n", p=128), (), ["B_WOUT"])

    def chunk_front(g, b, tc, cs, ucol, st, sample=False):
        hs = slice(4 * g, 4 * g + 4)
        tri, negm = (TRIS, NEGMS) if sample else (TRI, NEGM)
        XT, BT, SM, MT, XDT, XDD = XTs[st], BTs[st], SMs[st], MTs[st], XDTs[st], XDDs[st]
        DTV, DT_, DA, NACS, EACS, DEND, CD = (SM[:, 0:4], SM[:, 4:8], SM[:, 8:12], SM[:, 12:16], SM[:, 16:20], SM[:, 20:24], SM[:, 24:28])
        K = lambda nm: ("B_" + nm, st)
        pzd, kzd = self.PS[st], ("PS", st)
        pt, kt = self.PS[2], ("PS", 2)
        pa, ka = self.PS[3], ("PS", 3)
        pl, kl = self.PS[4], ("PS", 4)
        pc, kc_ = self.PS[5], ("PS", 5)
        for kc in range(8):
            self.mm(pzd[0:cs, 0:260], (USC[:, kc, :] if sample else self.U[:, kc, ucol:ucol + cs]), WZD[b][:, kc, :], kc == 0, kc == 7, ["B_WZD", "U", "B_USC"], [kzd])
        for q in range(3):
            self.tr(pt[0:cs, q * 128:(q + 1) * 128], XCf[:, q, tc:tc + cs], self.IDF[:], ["B_XCf", "IDF"], [kt])
        self.cp("act", XT[0:cs, :], pt[0:cs, 0:256], [kt], [K("XT")])
        self.cp("dve", BT[0:cs, :], pt[0:cs, 256:384], [kt], [K("BT")])
        self.mm(pc[0:cs, 0:cs], BCb[:, 0, tc:tc + cs], BCb[:, 1, tc:tc + cs], True, True, ["B_BCb"], [kc_])
        self.tt("dve", DTV[0:cs], pzd[0:cs, 256:260], DTB[0:cs, hs], ALU.add, [kzd, "B_DTB"], [K("DTV")])
        self.act(DT_[0:cs], DTV[0:cs], AF.Exp, [K("DTV")], [K("DT")])
        self.act(DT_[0:cs], DT_[0:cs], AF.Ln, [K("DT"), ("C1", 1)], [K("DT")], bias=self.C1[0:cs, 1:2], scale=1.0)
        self.tt("dve", DA[0:cs], DT_[0:cs], AB[0:cs, hs], ALU.mult, [K("DT"), "B_AB"], [K("DA")])
        self.mm(pa[0:cs, 0:4], tri[0:cs, 0:cs], DA[0:cs], True, True, ["B_TRI", "B_TRIS", K("DA")], [ka])
        self.mm(pa[:, 4:8], (BONES[:, :] if sample else ONESF[0:cs, :]), DA[0:cs], True, True, ["B_ONESF", "B_BONES", K("DA")], [ka])
        self.ts("dve", NACS[0:cs], pa[0:cs, 0:4], -1.0, None, ALU.mult, None, [ka], [K("NACS")])
        self.act(EACS[0:cs], pa[0:cs, 0:4], AF.Exp, [ka], [K("EACS")])
        self.tt("dve", DEND[0:cs], pa[0:cs, 4:8], NACS[0:cs], ALU.add, [ka, K("NACS")], [K("DEND")])
        self.act(DEND[0:cs], DEND[0:cs], AF.Exp, [K("DEND")], [K("DEND")])
        if not sample:
            self.act(CD, pa[:, 4:8], AF.Exp, [ka], [K("CD")])
        else:
            self.tt("dve", DAS[:].rearrange("p (s h) -> p s h", h=4), DA[:].unsqueeze(1).to_broadcast([128, 16, 4]),
                    SEQM[:].unsqueeze(2).to_broadcast([128, 16, 4]), ALU.mult, [K("DA"), "B_SEQM"], ["B_DAS"])
            pcd, kcd = self.PS[6], ("PS", 6)
            self.mm(pcd[:, 0:64], ONESF[:, :], DAS[:], True, True, ["B_ONESF", "B_DAS"], [kcd])
            self.act(CDS[:], pcd[:, 0:64], AF.Exp, [kcd], ["B_CDS"])
        for h in range(4):
            self.mm(pl[0:cs, h * 128:h * 128 + cs], DA[0:cs, h:h + 1].to_broadcast([cs, cs]), tri[0:cs, 0:cs], True, False, [K("DA"), "B_TRI", "B_TRIS"], [kl])
            self.mm(pl[0:cs, h * 128:h * 128 + cs], self.IDF[0:cs, 0:cs], negm[0:cs, 0:cs], False, True, ["IDF", "B_NEGM", "B_NEGMS"], [kl])
        for h in range(4):
            self.act(LT[0:cs, h, 0:cs], pl[0:cs, h * 128:h * 128 + cs], AF.Exp, [kl, K("NACS")], [("B_LT", h)], bias=NACS[0:cs, h:h + 1], scale=1.0)
        for h in range(4):
            self.tt("dve", MT[0:cs, h, 0:cs], LT[0:cs, h, 0:cs], pc[0:cs, 0:cs], ALU.mult, [kc_, ("B_LT", h)], [("B_MT%d" % st, h)])
        self.tt("dve", v4(XDT[0:cs, :]), v4(XT[0:cs, :]), bc(DT_[0:cs], cs), ALU.mult, [K("XT"), K("DT")], [K("XDT")])
        self.tt("dve", v4(XDD[0:cs, :]), v4(XDT[0:cs, :]), bc(DEND[0:cs], cs), ALU.mult, [K("XDT"), K("DEND")], [K("XDD")])

    def chunk_back(g, b, tc, cs, ucol, st, sample=False):
        hs = slice(4 * g, 4 * g + 4)
        XT, BT, SM, MT, XDT, XDD = XTs[st], BTs[st], SMs[st], MTs[st], XDTs[st], XDDs[st]
        EACS, CD, MS = SM[:, 16:20], SM[:, 24:28], SM[:, 28:29]
        K = lambda nm: ("B_" + nm, st)
        pzd, kzd = self.PS[st], ("PS", st)
        py, ky = self.PS[6], ("PS", 6)
        p7, k7 = self.PS[7], ("PS", 7)
        self.act(SZ[0:cs, :], pzd[0:cs, 0:256], AF.Exp, [kzd], ["B_SZ"], scale=-1.0)
        self.act(SZ[0:cs, :], SZ[0:cs, :], AF.Ln, ["B_SZ", ("C1", 1)], ["B_SZ"], bias=self.C1[0:cs, 1:2], scale=1.0)
        self.act(SZ[0:cs, :], SZ[0:cs, :], AF.Exp, ["B_SZ"], ["B_SZ"], scale=-1.0)
        self.tt("dve", SZ[0:cs, :], SZ[0:cs, :], pzd[0:cs, 0:256], ALU.mult, [kzd, "B_SZ"], ["B_SZ"])
        if sample:
            self.memset("dve", YO[:], 0.0, ["B_YO"])
            for sq in range(NSQ):
                state_in(g, sq)
                po, ko = self.bank()
                self.mm(po[:, 0:256], BCb[:, 1, 0:128], STb[:], True, True, ["B_BCb", "B_STb"], [ko])
                self.stt(YO[:], po[:, 0:256], SEQM[:, sq:sq + 1], YO[:], ALU.mult, ALU.add, [ko, "B_SEQM", "B_YO"], ["B_YO"])
                self.ts("dve", BTM[:], BT[:], SEQM[:, sq:sq + 1], None, ALU.mult, None, [K("BT"), "B_SEQM"], ["B_BTM"])
                pst, kst = self.bank()
                self.mm(pst[:, 0:256], BTM[:], XDD[:], True, True, ["B_BTM", K("XDD")], [kst])
                self.tt("dve", v4(ST[:]), v4(ST[:]), bc(CDS[:, 4 * sq:4 * sq + 4], 128), ALU.mult, ["B_ST", "B_CDS"], ["B_ST"])
                self.tt("dve", ST[:], ST[:], pst[:, 0:256], ALU.add, ["B_ST", kst], ["B_ST"])
                state_out(g, self.O["ss_s"][sq, 4 * g:4 * g + 4])
        for h in range(4):
            self.mm(py[0:cs, h * 64:(h + 1) * 64], MT[0:cs, h, 0:cs], XDT[0:cs, h * 64:(h + 1) * 64], True, True, [("B_MT%d" % st, h), K("XDT")], [ky])
        if not sample:
            self.mm(p7[0:cs, 0:256], BCb[:, 1, tc:tc + cs], STb[:], True, True, ["B_BCb", "B_STb"], [k7])
            self.tt("dve", v4(Y1[0:cs, :]), v4(p7[0:cs, 0:256]), bc(EACS[0:cs], cs), ALU.mult, [k7, K("EACS")], ["B_Y1"])
        else:
            self.tt("dve", v4(Y1[:]), v4(YO[:]), bc(EACS[:], 128), ALU.mult, ["B_YO", K("EACS")], ["B_Y1"])
        self.tt("dve", Y1[0:cs, :], Y1[0:cs, :], py[0:cs, 0:256], ALU.add, [ky, "B_Y1"], ["B_Y1"])
        self.tt("pool", v4(T2[0:cs, :]), v4(XT[0:cs, :]), bc(DB[0:cs, hs], cs), ALU.mult, [K("XT"), "B_DB"], ["B_T2"])
        self.tt("dve", Y1[0:cs, :], Y1[0:cs, :], T2[0:cs, :], ALU.add, ["B_T2", "B_Y1"], ["B_Y1"])
        if not sample:
            self.mm(p7[:, 256:512], BT[0:cs, :], XDD[0:cs, :], True, True, [K("BT"), K("XDD")], [k7])
            self.tt("dve", v4(ST[:]), v4(ST[:]), bc(CD, 128), ALU.mult, ["B_ST", K("CD")], ["B_ST"])
            self.tt("dve", ST[:], ST[:], p7[:, 256:512], ALU.add, ["B_ST", k7], ["B_ST"])
            self.cp("pool", STb[:], ST[:], ["B_ST"], ["B_STb"])
        self.tt("dve", Y1[0:cs, :], Y1[0:cs, :], SZ[0:cs, :], ALU.mult, ["B_SZ", "B_Y1"], ["B_Y1"])
        self.P.op("dve", lambda e: e.scalar_tensor_tensor(out=T2[0:cs, :], in0=Y1[0:cs, :], scalar=1.0, in1=Y1[0:cs, :], op0=ALU.mult,
                                                          op1=ALU.mult, accum_out=MS[0:cs]), ["B_Y1", "B_T2"], ["B_T2", K("MS")])
        self.act(MS[0:cs], MS[0:cs], AF.Ln, [K("MS"), ("C1", 2)], [K("MS")], scale=1.0 / 256, bias=self.C1[0:cs, 2:3])
        self.act(MS[0:cs], MS[0:cs], AF.Exp, [K("MS")], [K("MS")], scale=-0.5)
        self.ts("dve", Y1[0:cs, :], Y1[0:cs, :], MS[0:cs], None, ALU.mult, None, [K("MS"), "B_Y1"], ["B_Y1"])
        pg, kg = self.PS[7], ("PS", 7)
        for hf in range(2):
            self.tr(pg[:, hf * 128:hf * 128 + cs], Y1[0:cs, hf * 128:(hf + 1) * 128], self.IDF[0:cs, 0:cs], ["B_Y1", "IDF"], [kg])
        for hf in range(2):
            ch = 2 * g + hf
            self.ts("dve", YGT[:, hf, tc:tc + cs], pg[:, hf * 128:hf * 128 + cs], self.pf("ssm_norm_w", ch // 8, ch % 8), None, ALU.mult, None,
                    [kg, "PF"], ["B_YGT"])

    def state_in(g, s):
        self.dma("sp", SIN[:], I["st_ssm"][ib, s, 4 * g:4 * g + 4].rearrange("(a h) p n -> (h p) a n", a=2), (), ["B_SIN"])
        ps, pk = self.bank()
        for a in range(2):
            self.tr(ps[:, a * 128:(a + 1) * 128], SIN[:, a, :], self.IDF[:], ["B_SIN", "IDF"], [pk])
        self.cp("dve", ST[:], ps[:, 0:256], [pk], ["B_ST"])
        self.cp("act", STb[:], ps[:, 0:256], [pk], ["B_STb"])

    def state_out(g, dst):
        ps, pk = self.bank()
        for a in range(2):
            self.tr(ps[:, a * 128:(a + 1) * 128], ST[:, a * 128:(a + 1) * 128], self.IDF[:], ["B_ST", "IDF"], [pk])
        self.cp("dve", SOUT[:].rearrange("p a n -> p (a n)"), ps[:, 0:256], [pk], ["B_SOUT"])
        self.dma("sp", dst.rearrange("(a h) p n -> (h p) a n", a=2), SOUT[:], ["B_SOUT"], ())

    import os as _os
    NG = int(_os.environ.get('BDBG_G', '8')); TLIST = [int(c) for c in _os.environ.get('BDBG_T', '01234')]; NSQ = int(_os.environ.get('BDBG_S', '16'))
    load(0)
    load_x(0)
    load_o(0)
    for g in range(NG):
        b = g % 2
        chs = [2 * g, 2 * g + 1, 16 + g, 24 + g]
        for ti, (c0, w) in enumerate(TT):
            if ti not in TLIST:
                continue
            if ti == 0:
                self.memset("dve", XF[:, :, 0:3], 0.0, [("B_XF", "st")])
            elif ti < 4:
                self.cp("dve", XF[:, :, 0:3], XF[:, :, 512:515], ["B_XF"], [("B_XF", "st")])
            else:
                for q in range(4):
                    self.cp("dve", XFSv[q][:, :, 0:3], CS0[:, chs[q], :].rearrange("p (s k) -> p s k", k=3), ["B_CS0"], [("B_XFS", "st")])
            for q in range(4):
                ps, pk = self.bank()
                for kc in range(8):
                    self.mm(self.v3(ps[:, :w], ti), WXBC[b][:, kc, q * 128:(q + 1) * 128], self.uv(kc, ti), kc == 0, kc == 7,
                            ["B_WXBC", ("U", (kc, ti))], [pk])
                if ti < 4:
                    self.cp("act", XF[:, q, 3:515], ps[:, :], [pk], [("B_XF", q)])
                else:
                    self.cp("act", XFSv[q][:, :, 3:11], self.v3(ps[:, :w], ti), [pk], [("B_XFS", q)])
            if ti == TLIST[-1] and g + 1 < NG:
                load_x(g + 1)
            for q in range(4):
                ch = chs[q]
                cw = [self.pf("ssm_conv_w", k * 4 + ch // 8, ch % 8) for k in range(4)]
                cb = self.pf("ssm_conv_b", ch // 8, ch % 8)
                if ti < 4:
                    dst = XCf[:, q, :]
                    srcf = lambda k, q=q: XF[:, q, k:k + 512]
                    rk = "B_XF"
                else:
                    dst = XCf[:, q, 0:128].rearrange("p (s t) -> p s t", t=8)
                    srcf = lambda k, q=q: XFSv[q][:, :, k:k + 8]
                    rk = "B_XFS"
                self.ts("dve", dst, srcf(0), cw[0], cb, ALU.mult, ALU.add, [rk, "PF"], [("B_XCf", q)])
                for k in range(1, 4):
                    self.stt(dst, srcf(k), cw[k], dst, ALU.mult, ALU.add, [rk, ("B_XCf", q), "PF"], [("B_XCf", q)])
                self.act(XCf[:, q, :w], XCf[:, q, :w], AF.Silu, [("B_XCf", q)], [("B_XCf", q)])
                if q >= 2:
                    self.cp("pool", BCb[:, q - 2, :w], XCf[:, q, :w], [("B_XCf", q)], ["B_BCb"])
            if ti == 3:
                for q in range(4):
                    self.cp("pool", CST[:, chs[q], 0:3], XF[:, q, 512:515], ["B_XF"], ["B_CST"])
            if ti == 4:
                for q in range(4):
                    self.cp("pool", CST[:, chs[q], 3:].rearrange("p (s k) -> p s k", k=3), XFSv[q][:, :, 8:11], ["B_XFS"], ["B_CST"])
            if ti < 4:
                if ti == 0:
                    self.memset("dve", ST[:], 0.0, ["B_ST"])
                    self.memset("pool", STb[:], 0.0, ["B_STb"])
                args = [(g, b, ck * 128, 128, 1 + c0 + ck * 128, ck % 2) for ck in range(4)]
                chunk_front(*args[0])
                for ck in range(4):
                    if ck + 1 < 4:
                        chunk_front(*args[ck + 1])
                    chunk_back(*args[ck])
                if ti == 3:
                    state_out(g, self.O["ss_p"][4 * g:4 * g + 4])
            else:
                chunk_front(g, b, 0, 128, 0, 0, sample=True)
                chunk_back(g, b, 0, 128, 0, 0, sample=True)
            for n in range(8):
                ps, pk = self.bank()
                for hf in range(2):
                    self.mm(ps[:, :w], WOUT[b][:, hf, n * 128:(n + 1) * 128], YGT[:, hf, :w], hf == 0, hf == 1, ["B_WOUT", "B_YGT"], [pk])
                self.tt("dve", self.X[:, n, c0:c0 + w], self.X[:, n, c0:c0 + w], ps[:, :w], ALU.add, [pk, ("X", (n, ti))], [("X", (n, ti))])
        if g + 1 < NG:
            load_o(g + 1)
            load(g + 1)
    pass
    for r in range(4):
        self.fm_to_rows(CST[:, r * 8:(r + 1) * 8, 0:3], "B_CST", 3, self.O["sc_p"][:, r * D:(r + 1) * D])
        self.fm_to_rows(CST[:, r * 8:(r + 1) * 8, 3:], "B_CST", NS * 3, self.O["sc_s"].rearrange("s k d -> (s k) d")[:, r * D:(r + 1) * D])


KB.layer_B = layer_B


def layer_C(self, l, ic):
    I, nc = self.I, self.nc
    E05 = float(np.exp(-0.5))
    SH0 = self.sb("C_SH0", [128, 8, NS]); USH = self.sb("C_USH", [128, 8, 1 + NS])
    self.rows_to_fm(I["st_rw_shift"][ic], NS, SH0, "C_SH0")
    Us = self.U[:, :, 1 + NTP:UC].rearrange("p c (s t) -> p c s t", t=9)
    self.cp("dve", Us[:, :, :, 0], SH0[:], ["C_SH0"], [("U", "shifts")])
    self.norm_to_U("norm_mix", l, shift_out=USH)
    XD = nc.dram_tensor("c_xspill", [128, 8, NT], F32).ap()
    import os as _os2
    CHUNKED = bool(int(_os2.environ.get("C_CHUNKED", "1")))
    HM = () if CHUNKED else ("kk", "w", "nq", "k", "r")
    SCR = {k: nc.dram_tensor("c_scr_" + k, [NT, D], F32).ap() for k in ("kk", "w", "nq", "k", "r", "v", "g", "o") if k not in HM}
    SCRH = {k: nc.dram_tensor("c_scrh_" + k, [16, NT, 64], F32).ap() for k in HM}

    def scr_tm(nm, rows):
        if nm in SCRH:
            return SCRH[nm][:, rows, :].rearrange("h t k -> t h k")
        return SCR[nm][rows, :].rearrange("t (h k) -> t h k", k=64)

    self.P.barrier()
    self.dma("sp", XD, self.X[:], ["X"], ["XD"])
    self.P.barrier()

    def xa(i):
        return self.X[:, i // 2, (i % 2) * D:(i % 2 + 1) * D], ("X", "a%d" % i)

    def h16(ap):
        return ap.rearrange("p (h k) -> p h k", k=64)

    def b16(ap):
        return ap.unsqueeze(2).to_broadcast([128, 16, 64])

    def red(out, in_, reads, writes):
        self.P.op("dve", lambda e: e.tensor_reduce(out=out, in_=in_, axis=AX.X, op=ALU.add), reads, writes)

    def pbc(i, src):
        a, k = xa(i)
        self.dma("sp", a, src.partition_broadcast(128), (), [k])

    with self.scope():
        pbc(0, I["rwkv_w0"][ic:ic + 1, :]); pbc(1, I["rwkv_a0"][ic:ic + 1, :]); pbc(2, I["rwkv_k_k"][ic:ic + 1, :]); pbc(3, I["rwkv_k_a"][ic:ic + 1, :])
        WS = [self.sb("C_WS%d" % i, [128, 8, D], BF16) for i in range(3)]
        for s_ in range(3):
            self.dma("pool", WS[s_][:], I["rwkv_w_rkv"][ic, s_].rearrange("(kc p) n -> p kc n", p=128), (), [("C_WS", s_)])
        W1 = self.sb("C_W1", [128, 8, 256], BF16)
        W2w = self.sb("C_W2w", [64, D], BF16); W2a = self.sb("C_W2a", [64, D], BF16); W2g = self.sb("C_W2g", [128, D], BF16)
        for (c0, n, nm) in ((0, 64, "rwkv_w_w1"), (64, 64, "rwkv_w_a1"), (128, 128, "rwkv_w_g1")):
            self.dma("pool", W1[:, :, c0:c0 + n], I[nm][ic].rearrange("(kc p) n -> p kc n", p=128), (), ["C_W1"])
        self.dma("pool", W2w[:], I["rwkv_w_w2"][ic], (), ["C_W2w"]); self.dma("pool", W2a[:], I["rwkv_w_a2"][ic], (), ["C_W2a"])
        self.dma("pool", W2g[:], I["rwkv_w_g2"][ic], (), ["C_W2g"])
        Dd = self.sb("C_D", [128, 8, 128]); XM = [self.sb("C_XM%d" % i, [128, 8, 128], BF16) for i in range(2)]
        TW = self.sb("C_TW", [64, 128], BF16); TA = self.sb("C_TA", [64, 128], BF16); TG = self.sb("C_TG", [128, 128], BF16)
        SS = self.sb("C_SS", [128, 16])
        (Pw0, kw0), (Pa0, ka0), (Pkk, kkk), (Pka, kka) = xa(0), xa(1), xa(2), xa(3)
        (Rt, kR), (Kt, kK), (Vt, kV), (Wt, kW), (At, kA), (KKt, kKK), (Gt, kG), (T1, kT1), (T2, kT2) = [xa(i) for i in range(7, 16)]
        wsn = 0
        for ti in range(17):
            tok0 = 128 * ti
            if ti < 16:
                cur = self.U[:, :, 1 + tok0:1 + tok0 + 128]; prev = self.U[:, :, tok0:tok0 + 128]
                dv = lambda a: a
                ukeys = [("U", (c, ti // 4)) for c in range(8)]
            else:
                cur = Us[:, :, :, 1:9]; prev = Us[:, :, :, 0:8]
                dv = lambda a: a.rearrange("p c (s t) -> p c s t", t=8) if len(a.shape) == 3 else a.rearrange("p (s t) -> p s t", t=8)
                ukeys = [("U", (c, 4)) for c in range(8)] + [("U", "shifts")]
            if ti % 4 == 0 and ti > 0 and ti < 16:
                ukeys = ukeys + [("U", (c, ti // 4 - 1)) for c in range(8)]
            if ti == 0:
                ukeys = ukeys + [("U", "shiftp")]
            self.tt("dve", dv(Dd[:]), prev, cur, ALU.subtract, ukeys, ["C_D"])

            def mix(s):
                xm = XM[s % 2]; key = "C_XM%d" % (s % 2)
                for kc in range(8):
                    self.stt(dv(xm[:, kc, :]), dv(Dd[:, kc, :]), self.pf("rwkv_mu", s, kc), cur[:, kc], ALU.mult, ALU.add, ["C_D", "PF"] + ukeys, [key])
                return xm, key

            def proj_tm(s, dst, dkey):
                b = s
                xm, key = mix(s)
                for nb in range(2):
                    ps, pk = self.bank()
                    for kc in range(8):
                        self.mm(ps[:, :], xm[:, kc, :], WS[b][:, kc, nb * 512:(nb + 1) * 512], kc == 0, kc == 7, [key, ("C_WS", b)], [pk])
                    self.cp("act", dst[:, nb * 512:(nb + 1) * 512], ps[:, :], [pk], [dkey])

            proj_tm(0, Rt, kR); proj_tm(1, Kt, kK); proj_tm(2, Vt, kV)
            for (s, c0, n, fn, dst, dk) in ((3, 0, 64, AF.Tanh, TW, "C_TW"), (4, 64, 64, AF.Copy, TA, "C_TA"), (5, 128, 128, AF.Sigmoid, TG, "C_TG")):
                xm, key = mix(s)
                ps, pk = self.bank()
                for kc in range(8):
                    self.mm(ps[0:n, 0:128], W1[:, kc, c0:c0 + n], xm[:, kc, :], kc == 0, kc == 7, [key, "C_W1"], [pk])
                self.act(dst[0:n, :], ps[0:n, 0:128], fn, [pk], [dk])
            for nb in range(2):
                cs_ = slice(nb * 512, (nb + 1) * 512)
                ps, pk = self.bank()
                self.mm(ps[:, :], TW[0:64, :], W2w[0:64, cs_], True, True, ["C_TW", "C_W2w"], [pk])
                self.tt("dve", T1[:, cs_], ps[:, :], Pw0[:, cs_], ALU.add, [pk, kw0], [kT1])
                ps, pk = self.bank()
                self.mm(ps[:, :], TA[0:64, :], W2a[0:64, cs_], True, True, ["C_TA", "C_W2a"], [pk])
                self.tt("dve", At[:, cs_], ps[:, :], Pa0[:, cs_], ALU.add, [pk, ka0], [kA])
                ps, pk = self.bank()
                self.mm(ps[:, :], TG[:, :], W2g[:, cs_], True, True, ["C_TG", "C_W2g"], [pk])
                self.cp("act", Gt[:, cs_], ps[:, :], [pk], [kG])
            self.act(T1, T1, AF.Sigmoid, [kT1], [kT1])
            self.act(Wt, T1, AF.Exp, [kT1], [kW], scale=-E05)
            self.act(At, At, AF.Sigmoid, [kA], [kA])
            self.tt("dve", KKt, Kt, Pkk, ALU.mult, [kK, kkk], [kKK])
            self.tt("dve", T1, KKt, KKt, ALU.mult, [kKK], [kT1])
            red(SS[:], h16(T1), [kT1], ["C_SS"])
            self.act(SS[:], SS[:], AF.Sqrt, ["C_SS"], ["C_SS"])
            self.ts("dve", SS[:], SS[:], 1e-12, None, ALU.max, None, ["C_SS"], ["C_SS"])
            self.P.op("dve", lambda e, SS=SS: e.reciprocal(out=SS[:], in_=SS[:]), ["C_SS"], ["C_SS"])
            self.tt("dve", h16(KKt), h16(KKt), b16(SS[:]), ALU.mult, [kKK, "C_SS"], [kKK])
            self.stt(T1, At, -1.0, Pka, ALU.add, ALU.mult, [kA, kka], [kT1])
            self.ts("dve", T1, T1, 1.0, None, ALU.add, None, [kT1], [kT1])
            self.tt("dve", Kt, Kt, T1, ALU.mult, [kK, kT1], [kK])
            self.stt(T2, KKt, -1.0, At, ALU.mult, ALU.mult, [kKK, kA], [kT2])
            rows = slice(tok0, tok0 + 128)
            for (nm, a, k) in (("kk", KKt, kKK), ("w", Wt, kW), ("nq", T2, kT2), ("k", Kt, kK), ("r", Rt, kR), ("v", Vt, kV), ("g", Gt, kG)):
                self.dma("sp", scr_tm(nm, rows), h16(a), [k], [("SCR", nm)])

    if self.debug:
        for i_, nm_ in enumerate(("kk", "w", "nq", "k", "r", "v", "g")):
            self.dma("sp", self.O["dbg_scr"][i_].rearrange("t (h k) -> t h k", k=64), scr_tm(nm_, slice(0, NT)), [("SCR", nm_)], ())

    def phase2_chunked():
        LNE = float(np.log(1.0))
        f32 = F32
        def blockmask(ap3, blk):
            self.P.op("pool", lambda e: e.affine_select(out=ap3, in_=ap3, pattern=[[-blk, 128 // blk], [0, blk]], compare_op=ALU.is_ge, fill=0.0,
                                                        base=0, channel_multiplier=1), ["C_MK"], ["C_MK"])
            self.P.op("pool", lambda e: e.affine_select(out=ap3, in_=ap3, pattern=[[blk, 128 // blk], [0, blk]], compare_op=ALU.is_ge, fill=0.0,
                                                        base=blk - 1, channel_multiplier=-1), ["C_MK"], ["C_MK"])
        MK = {}
        PBLK = int(_os2.environ.get('C_PBLK', '64'))
        for blk in (PBLK, 8):
            TRIc = self.sb("C_TRIc%d" % blk, [128, 128]); LOWs = self.sb("C_LOWs%d" % blk, [128, 128]); M3 = self.sb("C_M3_%d" % blk, [128, 3, 128])
            MSs = self.sb("C_MSs%d" % blk, [128, 128])
            self.memset("pool", TRIc[:], 1.0, ["C_MK"])
            self.P.op("pool", lambda e, T=TRIc: e.affine_select(out=T[:], in_=T[:], pattern=[[1, 128]], compare_op=ALU.is_ge, fill=0.0, base=0,
                                                                channel_multiplier=-1), ["C_MK"], ["C_MK"])
            blockmask(TRIc[:].rearrange("p (b t) -> p b t", t=blk), blk)
            self.memset("pool", LOWs[:], 1.0, ["C_MK"])
            self.P.op("pool", lambda e, T=LOWs: e.affine_select(out=T[:], in_=T[:], pattern=[[-1, 128]], compare_op=ALU.is_ge, fill=0.0, base=-1,
                                                                channel_multiplier=1), ["C_MK"], ["C_MK"])
            blockmask(LOWs[:].rearrange("p (b t) -> p b t", t=blk), blk)
            self.tt("pool", MSs[:], TRIc[:], self.IDF[:], ALU.subtract, ["C_MK", "IDF"], ["C_MK"])
            self.cp("pool", M3[:, 0, :], TRIc[:], ["C_MK"], ["C_MK"])
            self.cp("pool", M3[:, 1, :], MSs[:], ["C_MK"], ["C_MK"])
            self.cp("pool", M3[:, 2, :], TRIc[:], ["C_MK"], ["C_MK"])
            MK[blk] = (TRIc, LOWs, MSs, M3)
        SEQM = self.sb("C_SEQM", [128, 16])
        self.memset("pool", SEQM[:], 1.0, ["C_MK"])
        self.P.op("pool", lambda e: e.affine_select(out=SEQM[:], in_=SEQM[:], pattern=[[-8, 16]], compare_op=ALU.is_ge, fill=0.0, base=0,
                                                    channel_multiplier=1), ["C_MK"], ["C_MK"])
        self.P.op("pool", lambda e: e.affine_select(out=SEQM[:], in_=SEQM[:], pattern=[[8, 16]], compare_op=ALU.is_ge, fill=0.0, base=7,
                                                    channel_multiplier=-1), ["C_MK"], ["C_MK"])
        CHM = self.sb("C_CHM", [128, 4])
        self.memset("pool", CHM[:], 1.0, ["C_MK"])
        self.P.op("pool", lambda e: e.affine_select(out=CHM[:, 0:128 // PBLK], in_=CHM[:, 0:128 // PBLK], pattern=[[-PBLK, 128 // PBLK]], compare_op=ALU.is_ge,
                                                    fill=0.0, base=0, channel_multiplier=1), ["C_MK"], ["C_MK"])
        self.P.op("pool", lambda e: e.affine_select(out=CHM[:, 0:128 // PBLK], in_=CHM[:, 0:128 // PBLK], pattern=[[PBLK, 128 // PBLK]], compare_op=ALU.is_ge,
                                                    fill=0.0, base=PBLK - 1, channel_multiplier=-1), ["C_MK"], ["C_MK"])
        PRT = self.sb("C_PRT", [128, 8, 2, 128], BF16); QT = self.sb("C_QT", [128, 8, 128], BF16); KT = self.sb("C_KT", [128, 8, 128], BF16)
        QZ = self.sb("C_QZ", [128, 16, 128], BF16); KZ = self.sb("C_KZ", [128, 16, 128], BF16)
        Vb = self.sb("C_Vb", [128, D], BF16); Vm = self.sb("C_Vm", [128, D], BF16)
        ECLE = self.sb("C_ECLE", [128, 8, 16])
        XN = self.sb("C_XN", [128, 16, 128], BF16); ZN = self.sb("C_ZN", [128, 16, 128], BF16); IDB_ = self.IDB
        MB = self.sb("C_MB", [128, 16, 3, 128], BF16); TTb = self.sb("C_TTb", [128, 16, 128], BF16)
        Wsb = self.sb("C_Wsb", [128, D]); O0 = self.sb("C_O0", [128, D])
        GWb = self.sb("C_GWb", [128, D], BF16); Ub = self.sb("C_Ub", [128, D], BF16)
        S = self.sb("C_S2", [128, 8, 64]); Sb = self.sb("C_Sb", [128, 16, 64], BF16)
        SbV = Sb[:].rearrange("p (j e) v -> p j e v", e=2)

        def write_sb(src3, keys):
            self.cp("act", SbV[0:64, :, 0, :], src3[0:64], keys, ["C_Sb"])
            self.cp("act", SbV[64:128, :, 1, :], src3[64:128], keys, ["C_Sb"])

        NAT = self.sb("C_NAT", [64, 16, 64])
        self.memset("pool", QZ[:], 0.0, ["C_QZ"]); self.memset("pool", KZ[:], 0.0, ["C_KZ"])
        INSETS = [[xa(i) for i in range(0, 6)], [xa(i) for i in range(6, 12)]]
        (CL, kCL), (E1, kE1), (Ot, kO) = [xa(i) for i in range(12, 15)]

        def state_load(dram):
            self.dma("sp", NAT[:], dram.rearrange("h v k -> v h k"), (), ["C_NAT"])
            ps, pk = self.bank()
            for j in range(8):
                self.tr(ps[:, j * 64:(j + 1) * 64], NAT[:, 2 * j:2 * j + 2, :].rearrange("v h k -> v (h k)"), self.IDF[0:64, 0:64], ["C_NAT", "IDF"], [pk])
            self.cp("dve", S[:].rearrange("p j v -> p (j v)"), ps[:, :], [pk], ["C_S2"])
            write_sb(ps[:, :].rearrange("p (j v) -> p j v", v=64), [pk])

        def state_store(dram):
            for half in range(2):
                ps, pk = self.bank()
                for jj in range(4):
                    j = half * 4 + jj
                    self.tr(ps[0:64, jj * 128:(jj + 1) * 128], S[:, j, :], self.IDF[:], ["C_S2", "IDF"], [pk])
                self.cp("dve", NAT[:, half * 8:(half + 1) * 8, :].rearrange("v h k -> v (h k)"), ps[0:64, :], [pk], ["C_NAT"])
            self.dma("act", dram.rearrange("h v k -> v h k"), NAT[:], ["C_NAT"], ())

        def load_inputs(ti):
            rows = slice(128 * ti, 128 * ti + 128)
            for nm, (a, k) in zip(("kk", "nq", "k", "r", "v", "w"), INSETS[ti % 2]):
                self.dma("sp", h16(a), scr_tm(nm, rows), [("SCR", nm)], [k])

        def tile(ti, blk, seqs):
            TRIc, LOWs, MSs, M3 = MK[blk]
            STOP = int(_os2.environ.get('C2_STOP', '99'))
            nch = 128 // blk
            rows = slice(128 * ti, 128 * ti + 128)
            (KKt, kKK), (NQt, kNQ), (Kt, kK), (Rt, kR), (Vt, kV), (LW, kLW) = INSETS[ti % 2]
            if ti == 0:
                load_inputs(0)
            if ti + 1 < 17:
                load_inputs(ti + 1)
            self.act(LW, LW, AF.Ln, [kLW], [kLW])
            for nb in range(2):
                ps, pk = self.bank()
                self.mm(ps[:, :], TRIc[:], LW[:, nb * 512:(nb + 1) * 512], True, True, ["C_MK", kLW], [pk])
                self.cp("act", CL[:, nb * 512:(nb + 1) * 512], ps[:, :], [pk], [kCL])
            self.tt("dve", E1, CL, LW, ALU.subtract, [kCL, kLW], [kE1])
            self.act(E1, E1, AF.Exp, [kE1], [kE1])
            self.tt("dve", KKt, KKt, E1, ALU.mult, [kKK, kE1], [kKK])
            self.act(E1, CL, AF.Exp, [kCL], [kE1], scale=-1.0)
            self.tt("dve", NQt, NQt, E1, ALU.mult, [kNQ, kE1], [kNQ])
            self.tt("dve", Kt, Kt, E1, ALU.mult, [kK, kE1], [kK])
            self.act(E1, CL, AF.Exp, [kCL], [kE1])
            self.tt("dve", Rt, Rt, E1, ALU.mult, [kR, kE1], [kR])
            self.cp("pool", Vb[:], Vt, [kV], ["C_Vb"])
            for (Zb, src, k, zk) in ((QZ, NQt, kNQ, "C_QZ"), (KZ, Kt, kK, "C_KZ")):
                zv = Zb[:].rearrange("p (j e) f -> p j (e f)", e=2)
                sv = src.rearrange("p (j e d) -> p j e d", e=2, d=64)
                self.cp("pool", zv[:, :, 0:64], sv[:, :, 0, :], [k], [zk])
                self.cp("pool", zv[:, :, 192:256], sv[:, :, 1, :], [k], [zk])
            if STOP <= 1:
                return
            for (src, k, dstf, dk) in ((KKt, kKK, lambda j: PRT[:, j, 0, :], "C_PRT"), (Rt, kR, lambda j: PRT[:, j, 1, :], "C_PRT"),
                                       (NQt, kNQ, lambda j: QT[:, j, :], "C_QT"), (Kt, kK, lambda j: KT[:, j, :], "C_KT")):
                for half in range(2):
                    ps, pk = self.bank()
                    for jj in range(4):
                        j = half * 4 + jj
                        self.tr(ps[:, jj * 128:(jj + 1) * 128], src[:, j * 128:(j + 1) * 128], self.IDF[:], [k, "IDF"], [pk])
                    for jj in range(4):
                        j = half * 4 + jj
                        self.cp("act" if jj % 2 else "dve", dstf(j), ps[:, jj * 128:(jj + 1) * 128], [pk], [(dk, j)])
            for half in range(2):
                ps, pk = self.bank()
                for jj in range(4):
                    j = half * 4 + jj
                    self.tr(ps[:, jj * 128:(jj + 1) * 128], E1[:, j * 128:(j + 1) * 128], self.IDF[:], [kE1, "IDF"], [pk])
                self.cp("dve", ECLE[:, half * 4:(half + 1) * 4, 0:nch], ps[:, :].rearrange("p (a c b) -> p a c b", a=4, b=blk)[:, :, :, blk - 1], [pk], ["C_ECLE"])
            if STOP <= 2:
                return
            for h in range(16):
                j, e = h // 2, h % 2
                pr = slice(e * 64, (e + 1) * 64)
                ps, pk = self.bank()
                rhsPR = PRT[pr, j, :, :].rearrange("p a t -> p (a t)")
                self.mm(ps[:, 0:256], QT[pr, j, :], rhsPR, True, True, [("C_QT", j), ("C_PRT", j)], [pk])
                self.mm(ps[:, 256:512], KT[pr, j, :], rhsPR, True, True, [("C_KT", j), ("C_PRT", j)], [pk])
                ps2, pk2 = self.bank()
                self.mm(ps2[:, 0:128], PRT[pr, j, 0, :], QT[pr, j, :], True, True, [("C_QT", j), ("C_PRT", j)], [pk2])
                self.tt("dve", ZN[:, h, :], ps[:, 0:128], MSs[:], ALU.mult, [pk, "C_MK"], [("C_ZN", h)])
                self.tt("dve", MB[:, h, :, :], ps[:, 128:512].rearrange("p (a t) -> p a t", a=3), M3[:], ALU.mult, [pk, "C_MK"], [("C_MB", h)])
                self.tt("dve", XN[:, h, :], ps2[:, 0:128], LOWs[:], ALU.mult, [pk2, "C_MK"], [("C_XN", h)])
                self.tt("pool", TTb[:, h, :], ZN[:, h, :], IDB_[:], ALU.add, [("C_ZN", h), "IDB"], [("C_TTb", h)])
            NIT = {64: 5, 32: 4, 8: 3}[blk]
            for n in range(NIT if STOP > 3 else 0):
                for h in range(16):
                    ps, pk = self.bank()
                    self.mm(ps[:, 0:128], ZN[:, h, :], XN[:, h, :], True, True, [("C_ZN", h), ("C_XN", h)], [pk])
                    if n < NIT - 1:
                        self.mm(ps[:, 128:256], XN[:, h, :], ZN[:, h, :], True, True, [("C_ZN", h), ("C_XN", h)], [pk])
                    self.cp("act", XN[:, h, :], ps[:, 0:128], [pk], [("C_XN", h)])
                    if n < NIT - 1:
                        self.cp("act", ZN[:, h, :], ps[:, 128:256], [pk], [("C_ZN", h)])
                    ps2, pk2 = self.bank()
                    self.mm(ps2[:, 0:128], XN[:, h, :], TTb[:, h, :], True, True, [("C_XN", h), ("C_TTb", h)], [pk2])
                    self.tt("dve", TTb[:, h, :], TTb[:, h, :], ps2[:, 0:128], ALU.add, [pk2, ("C_TTb", h)], [("C_TTb", h)])
            if STOP <= 4:
                return
            for (ai, dst, dk) in ((1, Wsb, "C_Wsb"), (2, O0, "C_O0")):
                for half in range(2):
                    ps, pk = self.bank()
                    for hh in range(8):
                        h = half * 8 + hh
                        self.mm(ps[:, hh * 64:(hh + 1) * 64], MB[:, h, ai, :], Vb[:, h * 64:(h + 1) * 64], True, True, [("C_MB", h), "C_Vb"], [pk])
                    self.cp("act" if half else "dve", dst[:, half * 512:(half + 1) * 512], ps[:, :], [pk], [dk])
            self.cp("pool", Ot, O0[:], ["C_O0"], [kO])
            if STOP <= 5:
                return
            nrounds = nch if seqs is None else len(seqs)
            for c_ in range(nrounds):
                if seqs is None:
                    ecol = c_; mcol = CHM[:, c_:c_ + 1]
                else:
                    s_ = seqs[c_]
                    ecol = s_; mcol = SEQM[:, s_:s_ + 1]
                    state_load(I["st_rw_wkv"][ic, s_])
                self.ts("pool", Vm[:], Vb[:], mcol, None, ALU.mult, None, ["C_Vb", "C_MK"], ["C_Vm"])
                pG = [self.bank() for _ in range(2)]; pR = [self.bank() for _ in range(2)]
                for h in range(16):
                    j = h // 2
                    bq, cq = h // 8, (h % 8) * 64
                    self.mm(pG[bq][0][:, cq:cq + 64], PRT[:, j, 0, :], Sb[:, h, :], True, True, [("C_PRT", j), "C_Sb"], [pG[bq][1]])
                for h in range(16):
                    j = h // 2
                    bq, cq = h // 8, (h % 8) * 64
                    self.mm(pR[bq][0][:, cq:cq + 64], PRT[:, j, 1, :], Sb[:, h, :], True, True, [("C_PRT", j), "C_Sb"], [pR[bq][1]])
                for bq in range(2):
                    cs_ = slice(bq * 512, (bq + 1) * 512)
                    self.tt("dve", GWb[:, cs_], pG[bq][0][:, :], Wsb[:, cs_], ALU.add, [pG[bq][1], "C_Wsb"], [("C_GWb", bq)])
                pU = [self.bank() for _ in range(2)]
                for h in range(16):
                    bq, cq = h // 8, (h % 8) * 64
                    self.mm(pU[bq][0][:, cq:cq + 64], TTb[:, h, :], GWb[:, h * 64:(h + 1) * 64], True, True, [("C_TTb", h), ("C_GWb", bq)], [pU[bq][1]])
                for bq in range(2):
                    cs_ = slice(bq * 512, (bq + 1) * 512)
                    self.act(Ub[:, cs_], pU[bq][0][:, :], AF.Copy, [pU[bq][1], "C_MK"], [("C_Ub", bq)], scale=mcol)
                    self.stt(Ot[:, cs_], pR[bq][0][:, :], mcol, Ot[:, cs_], ALU.mult, ALU.add, [pR[bq][1], kO, "C_MK"], [kO])
                pS, kS = self.bank()
                for j in range(8):
                    for e in range(2):
                        h = 2 * j + e
                        self.mm(pS[:, j * 64:(j + 1) * 64], KZ[:, h, :], Vm[:, h * 64:(h + 1) * 64], e == 0, False, ["C_KZ", "C_Vm"], [kS])
                    for e in range(2):
                        h = 2 * j + e
                        self.mm(pS[:, j * 64:(j + 1) * 64], QZ[:, h, :], Ub[:, h * 64:(h + 1) * 64], False, e == 1, ["C_QZ", ("C_Ub", h // 8)], [kS])
                pB = [self.bank() for _ in range(2)]
                for h in range(16):
                    bq, cq = h // 8, (h % 8) * 64
                    self.mm(pB[bq][0][:, cq:cq + 64], MB[:, h, 0, :], Ub[:, h * 64:(h + 1) * 64], True, True, [("C_MB", h), ("C_Ub", bq)], [pB[bq][1]])
                Sf = S[:].rearrange("p j v -> p (j v)")
                self.tt("dve", Sf, Sf, pS[:, :], ALU.add, [kS, "C_S2"], ["C_S2"])
                self.tt("dve", S[:], S[:], ECLE[:, :, ecol:ecol + 1].to_broadcast([128, 8, 64]), ALU.mult, ["C_S2", "C_ECLE"], ["C_S2"])
                write_sb(S, ["C_S2"])
                for bq in range(2):
                    cs_ = slice(bq * 512, (bq + 1) * 512)
                    self.stt(Ot[:, cs_], pB[bq][0][:, :], mcol, Ot[:, cs_], ALU.mult, ALU.add, [pB[bq][1], kO, "C_MK"], [kO])
                if seqs is not None:
                    state_store(self.O["rw_s"][s_])
            self.dma("act", scr_tm("o", rows), h16(Ot), [kO], [("SCR", "o")])

        self.memset("dve", S[:], 0.0, ["C_S2"]); self.memset("pool", Sb[:], 0.0, ["C_Sb"])
        for ti in range(16):
            tile(ti, PBLK, None)
        state_store(self.O["rw_p"])
        tile(16, 8, list(range(NS)))

    if CHUNKED:
        with self.scope():
            phase2_chunked()
    else:
        with self.scope():
            S = self.sb("C_S", [128, 8, 64]); Tm = self.sb("C_Tm", [128, 8, 64]); KV = [self.sb("C_KV%d" % i, [128, 8, 64]) for i in range(2)]
            SA = self.sb("C_SA", [128, 8])
            names = ("kk", "w", "nq", "k", "r")
            ST0 = {nm: self.X[:, i, 0:2048].rearrange("p (t k) -> p t k", k=64) for i, nm in enumerate(names)}
            ST1 = {nm: self.sb("C_ST1" + nm, [128, 32, 64]) for nm in names}
            VS = [self.sb("C_VS%d" % i, [128, 32, 8]) for i in range(2)]
            OS = [self.sb("C_OS%d" % i, [128, 32, 8]) for i in range(2)]

            def bk(ap):
                return ap.unsqueeze(1).to_broadcast([128, 8, 64])

            def bv(ap):
                return ap.unsqueeze(2).to_broadcast([128, 8, 64])

            nsub = 0
            import os as _os
            NOSAME = bool(int(_os.environ.get("C_NOSAME", "1")))
            SP = self.PS[7][:, :].rearrange("p (a k) -> p a k", k=64)
            SK = ("PS", 7)

            def run_segment(row0, nsteps):
                nonlocal nsub
                for t0 in range(0, nsteps, 32):
                    n = min(32, nsteps - t0)
                    sb_ = nsub % 2; nsub += 1
                    stg = ST0 if sb_ == 0 else ST1
                    skey = "C_STG%d" % sb_
                    r0 = row0 + t0
                    for nm in names:
                        src = SCRH[nm][:, r0:r0 + n, :]
                        for vb in range(8):
                            self.dma("sp", stg[nm][vb * 16:(vb + 1) * 16, 0:n, :], src, [("SCR", nm)], [(skey, nm)] + ([("X", "stg")] if sb_ == 0 else []))
                    srcv = SCR["v"][r0:r0 + n, :].rearrange("t (h v) -> h t v", v=64)
                    for vb in range(8):
                        self.dma("sp", VS[sb_][vb * 16:(vb + 1) * 16, 0:n, :], srcv[:, :, vb * 8:(vb + 1) * 8], [("SCR", "v")], [(skey, "v")])
                    xk = [("X", "stg")] if sb_ == 0 else []
                    for i in range(n):
                        kvb = i % 2
                        self.tt("pool", KV[kvb][:], bk(stg["k"][:, i, :]), bv(VS[sb_][:, i, :]), ALU.mult, [(skey, "k"), (skey, "v")] + xk, ["C_KV%d" % kvb])
                        self.P.nosame = NOSAME
                        self.tt("dve", Tm[:], SP[:], bk(stg["kk"][:, i, :]), ALU.mult, [SK, (skey, "kk")] + xk, ["C_Tm"])
                        red(SA[:], Tm[:], ["C_Tm"], ["C_SA"])
                        self.tt("dve", SP[:], SP[:], bk(stg["w"][:, i, :]), ALU.mult, [SK, (skey, "w")] + xk, [SK])
                        self.tt("dve", Tm[:], bk(stg["nq"][:, i, :]), bv(SA[:]), ALU.mult, ["C_SA", (skey, "nq")] + xk, ["C_Tm"])
                        self.tt("dve", SP[:], SP[:], Tm[:], ALU.add, [SK, "C_Tm"], [SK])
                        self.tt("dve", SP[:], SP[:], KV[kvb][:], ALU.add, [SK, "C_KV%d" % kvb], [SK])
                        self.tt("dve", Tm[:], SP[:], bk(stg["r"][:, i, :]), ALU.mult, [SK, (skey, "r")] + xk, ["C_Tm"])
                        red(OS[sb_][:, i, :], Tm[:], ["C_Tm"], ["C_OS%d" % sb_])
                        self.P.nosame = False
                    dsto = SCR["o"][r0:r0 + n, :].rearrange("t (h v) -> h t v", v=64)
                    for vb in range(8):
                        self.dma("act", dsto[:, :, vb * 8:(vb + 1) * 8], OS[sb_][vb * 16:(vb + 1) * 16, 0:n, :], ["C_OS%d" % sb_], [("SCR", "o")])

            def state_io(dram, load):
                for vb in range(8):
                    d = dram[:, vb * 8:(vb + 1) * 8, :]
                    if load:
                        self.dma("sp", S[vb * 16:(vb + 1) * 16, :, :], d, (), ["C_S"])
                    else:
                        self.dma("act", d, S[vb * 16:(vb + 1) * 16, :, :], ["C_S"], ())

            self.memset("dve", S[:], 0.0, ["C_S"])
            self.cp("dve", SP[:], S[:], ["C_S"], [SK])
            run_segment(0, NTP)
            self.cp("dve", S[:], SP[:], [SK], ["C_S"])
            state_io(self.O["rw_p"], False)
            for s in range(NS):
                state_io(I["st_rw_wkv"][ic, s], True)
                self.cp("dve", SP[:], S[:], ["C_S"], [SK])
                run_segment(NTP + 8 * s, 8)
                self.cp("dve", S[:], SP[:], [SK], ["C_S"])
                state_io(self.O["rw_s"][s], False)

    if self.debug:
        self.dma("sp", self.O["dbg_scr"][7], SCR["o"], [("SCR", "o")], ())
    with self.scope():
        pbc(0, I["rwkv_lnx_w"][ic:ic + 1, :]); pbc(1, I["rwkv_lnx_b"][ic:ic + 1, :]); pbc(2, I["rwkv_r_k"].rearrange("a h n -> a (h n)")[ic:ic + 1, :])
        (Plw, klw), (Plb, klb), (Prk, krk) = xa(0), xa(1), xa(2)
        (Ot, kO), (Rt, kR), (Kt, kK), (Vt, kV), (Gt, kG), (T1, kT1) = [xa(i) for i in range(7, 13)]
        WO = self.sb("C_WO", [128, 8, D], BF16)
        self.dma("pool", WO[:], I["rwkv_w_out"][ic].rearrange("(kc p) n -> p kc n", p=128), (), ["C_WO"])
        YT = self.sb("C_YT", [128, 8, 128], BF16); XT_ = self.sb("C_XTL", [128, 8, 128])
        M1 = self.sb("C_M1", [128, 16]); M2 = self.sb("C_M2", [128, 16])
        for ti in range(17):
            rows = slice(128 * ti, 128 * ti + 128)
            for (nm, a, k) in (("o", Ot, kO), ("r", Rt, kR), ("k", Kt, kK), ("v", Vt, kV), ("g", Gt, kG)):
                self.dma("sp", h16(a), scr_tm(nm, rows), [("SCR", nm)], [k])
            self.dma("sp", XT_[:], XD[:, :, rows], ["XD"], ["C_XTL"])
            red(M1[:], h16(Ot), [kO], ["C_M1"])
            self.ts("dve", M1[:], M1[:], -1.0 / 64, None, ALU.mult, None, ["C_M1"], ["C_M1"])
            self.tt("dve", h16(Ot), h16(Ot), b16(M1[:]), ALU.add, [kO, "C_M1"], [kO])
            self.tt("dve", T1, Ot, Ot, ALU.mult, [kO], [kT1])
            red(M2[:], h16(T1), [kT1], ["C_M2"])
            self.act(M2[:], M2[:], AF.Ln, ["C_M2", ("C1", 3)], ["C_M2"], scale=1.0 / 64, bias=self.C1[:, 3:4])
            self.act(M2[:], M2[:], AF.Exp, ["C_M2"], ["C_M2"], scale=-0.5)
            self.tt("dve", h16(Ot), h16(Ot), b16(M2[:]), ALU.mult, [kO, "C_M2"], [kO])
            self.tt("dve", Ot, Ot, Plw, ALU.mult, [kO, klw], [kO])
            self.tt("dve", Ot, Ot, Plb, ALU.add, [kO, klb], [kO])
            self.tt("dve", T1, Rt, Kt, ALU.mult, [kR, kK], [kT1])
            self.tt("dve", T1, T1, Prk, ALU.mult, [kT1, krk], [kT1])
            red(M1[:], h16(T1), [kT1], ["C_M1"])
            self.tt("dve", h16(T1), h16(Vt), b16(M1[:]), ALU.mult, [kV, "C_M1"], [kT1])
            self.tt("dve", Ot, Ot, T1, ALU.add, [kO, kT1], [kO])
            self.tt("dve", Ot, Ot, Gt, ALU.mult, [kO, kG], [kO])
            for half in range(2):
                ps, pk = self.bank()
                for cc in range(4):
                    c = half * 4 + cc
                    self.tr(ps[:, cc * 128:(cc + 1) * 128], Ot[:, c * 128:(c + 1) * 128], self.IDF[:], [kO, "IDF"], [pk])
                self.cp("act", YT[:, half * 4:(half + 1) * 4, :].rearrange("p c t -> p (c t)"), ps[:, :], [pk], ["C_YT"])
            for n in range(8):
                ps, pk = self.bank()
                for c in range(8):
                    self.mm(ps[:, 0:128], WO[:, c, n * 128:(n + 1) * 128], YT[:, c, :], c == 0, c == 7, ["C_WO", "C_YT"], [pk])
                self.tt("dve", XT_[:, n, :], XT_[:, n, :], ps[:, 0:128], ALU.add, [pk, "C_XTL"], ["C_XTL"])
            self.dma("sp", XD[:, :, rows], XT_[:], ["C_XTL"], ["XD"])
    self.dma("sp", self.X[:], XD, ["XD"], ["X"])
    self.fm_to_rows(USH[:, :, 0:1], "C_USH", 1, self.O["rs_p"])
    self.fm_to_rows(USH[:, :, 1:], "C_USH", NS, self.O["rs_s"])


KB.layer_C = layer_C


def build(layers=(0, 1, 2, 3), debug=False, mlp=True):
    kb = KB(layers, debug)
    with kb.es:
        kb.setup()
        kb.load_x()
        for l in layers:
            with kb.scope():
                kind = l % 3
                if kind == 0:
                    kb.layer_A(l, l // 3)
                elif kind == 1:
                    kb.layer_B(l, 0)
                else:
                    kb.layer_C(l, 0)
            if mlp:
                with kb.scope():
                    kb.mlp(l)
        if not getattr(kb, '_skip_final', False):
            with kb.scope():
                kb.final_out()
        if debug:
            kb.dump_x()
        kb.P.emit()
    return kb.nc


def make_in_maps(inputs, cores):
    g = {k: np.ascontiguousarray(np.asarray(v, dtype=np.float32)) for k, v in inputs.items()}
    maps = []
    for i in cores:
        sl = slice(NS * i, NS * (i + 1))
        m = {
            "xp": g["x_prompt"][i], "xs": g["x_sample"][sl].reshape(NS * LS, D),
            "st_lru_conv": g["state_lru_conv"][:, sl], "st_lru_h": g["state_lru_h"][:, sl],
            "st_ssm_conv": g["state_ssm_conv"][:, sl], "st_ssm": g["state_ssm"][:, sl],
            "st_rw_shift": g["state_rwkv_shift"][:, sl], "st_rw_wkv": g["state_rwkv_wkv"][:, sl],
        }
        for k in IN_SPECS:
            if k in m:
                continue
            v = g[k]
            if k == "norm_final":
                v = v.reshape(1, D)
            m[k] = v
        maps.append({k: np.ascontiguousarray(v) for k, v in m.items()})
    return maps


def kernel(**inputs):
    nc = build()
    cores = list(range(8))
    res = run_bass_kernel_spmd(nc, make_in_maps(inputs, cores), core_ids=cores)
    r = res.results
    def cat(name, axis=0):
        return np.concatenate([x[name] for x in r], axis=axis)
    y_p = np.stack([x["y_p"] for x in r])
    y_s = cat("y_s").reshape(128, LS, D)
    lc_p = np.stack([x["lc_p"] for x in r], axis=1)
    lc_s = cat("lc_s", 1)
    lh_p = np.stack([x["lh_p"] for x in r], axis=1)
    lh_s = cat("lh_s", 1)
    sc_p = np.stack([x["sc_p"] for x in r])[None]
    sc_s = cat("sc_s")[None]
    ss_p = np.stack([x["ss_p"] for x in r])[None]
    ss_s = cat("ss_s")[None]
    rs_p = np.stack([x["rs_p"][0] for x in r])[None]
    rs_s = cat("rs_s")[None]
    rw_p = np.stack([x["rw_p"] for x in r])[None]
    rw_s = cat("rw_s")[None]
    outs = (y_p, y_s, lc_p, lc_s, lh_p, lh_s, sc_p, sc_s, ss_p, ss_s, rs_p, rs_s, rw_p, rw_s)
    return tuple(np.ascontiguousarray(o, dtype=np.float32) for o in outs)
```
